# Optimizing a Trainium2 kernel written in Bass

```python
import math
import jax, jax.numpy as jnp
from jax import lax
import numpy as np

D_MODEL = 1024
BATCH = 8
SEQ = 4096
DEPTH = 2

EPS = 1e-6
MIX_WIDTH = D_MODEL
A_WIDTH = MIX_WIDTH // 2
B_WIDTH = MIX_WIDTH - A_WIDTH
A_GROUPS = 4
A_GROUP_DIM = A_WIDTH // A_GROUPS
CHUNK = 128
B_GROUPS = 8
CONV_W = 3
EVEN_IN_WIDTH = 2 * A_WIDTH + 3 * B_WIDTH
N_HEADS = 16
N_KV_HEADS = 4
Q_PER_KV = N_HEADS // N_KV_HEADS
HEAD_DIM = D_MODEL // N_HEADS
QKV_WIDTH = (N_HEADS + 2 * N_KV_HEADS) * HEAD_DIM
WINDOW = 128
BLOCK = 128
N_BUCKETS = 32
MAX_DISTANCE = 128
D_FF = ((8 * D_MODEL // 3 + 255) // 256) * 256
N_EVEN = (DEPTH + 1) // 2
N_ODD = DEPTH // 2

kernel_name = "hybrid_gmlp_shortconv_swa_encoder"


def rmsnorm(x, g):
    xf = x.astype(jnp.float32)
    y = xf * lax.rsqrt(jnp.mean(xf * xf, axis=-1, keepdims=True) + EPS)
    return (y * g.astype(jnp.float32)).astype(x.dtype)


def layernorm(x, g, b):
    xf = x.astype(jnp.float32)
    mu = jnp.mean(xf, axis=-1, keepdims=True)
    xc = xf - mu
    y = xc * lax.rsqrt(jnp.mean(xc * xc, axis=-1, keepdims=True) + EPS)
    return (y * g.astype(jnp.float32) + b.astype(jnp.float32)).astype(x.dtype)


def t5_buckets(rel):
    nb = N_BUCKETS // 2
    ret = jnp.where(rel > 0, nb, 0)
    n = jnp.abs(rel)
    max_exact = nb // 2
    nf = jnp.maximum(n, 1).astype(jnp.float32)
    large = max_exact + (jnp.log(nf / max_exact) / math.log(MAX_DISTANCE / max_exact)
                         * (nb - max_exact)).astype(jnp.int32)
    large = jnp.minimum(large, nb - 1)
    return ret + jnp.where(n < max_exact, n, large)


def even_mixer(h, w_in, v_ln_g, v_ln_b, w_spatial, b_spatial, conv_w, w_out):
    bsz, s, _ = h.shape
    proj = jnp.einsum('bsd,de->bse', h, w_in)
    a_u, a_v, b_b, b_c, b_h = jnp.split(
        proj, [A_WIDTH, 2 * A_WIDTH, 2 * A_WIDTH + B_WIDTH, 2 * A_WIDTH + 2 * B_WIDTH], axis=-1)
    a_u = jax.nn.gelu(a_u, approximate=False)
    a_v = layernorm(jax.nn.gelu(a_v, approximate=False), v_ln_g, v_ln_b)
    n_chunks = s // CHUNK
    v = a_v.reshape(bsz, n_chunks, CHUNK, A_GROUPS, A_GROUP_DIM)
    mixed = jnp.einsum('gpq,bcqgd->bcpgd', w_spatial, v) + b_spatial.T[None, None, :, :, None]
    a_out = a_u * mixed.reshape(bsz, s, A_WIDTH)
    z = b_c * b_h
    pad = CONV_W // 2
    zp = jnp.pad(z, ((0, 0), (pad, pad), (0, 0)))
    conv = zp[:, 0:s] * conv_w[0]
    for tap in range(1, CONV_W):
        conv = conv + zp[:, tap:tap + s] * conv_w[tap]
    b_out = b_b * conv
    y = jnp.concatenate([a_out, b_out], axis=-1)
    return jnp.einsum('bse,ed->bsd', y, w_out)


def window_attention(h, w_qkv, sink, rel_bias, w_out):
    bsz, s, _ = h.shape
    n_blocks = s // BLOCK
    qkv = jnp.einsum('bsd,de->bse', h, w_qkv)
    q, k, v = jnp.split(qkv, [N_HEADS * HEAD_DIM, (N_HEADS + N_KV_HEADS) * HEAD_DIM], axis=-1)
    q = q.reshape(bsz, n_blocks, BLOCK, N_KV_HEADS, Q_PER_KV, HEAD_DIM).transpose(1, 0, 2, 3, 4, 5)
    k = k.reshape(bsz, s, N_KV_HEADS, HEAD_DIM)
    v = v.reshape(bsz, s, N_KV_HEADS, HEAD_DIM)
    kp = jnp.pad(k, ((0, 0), (BLOCK, BLOCK), (0, 0), (0, 0)))
    vp = jnp.pad(v, ((0, 0), (BLOCK, BLOCK), (0, 0), (0, 0)))
    qi = jnp.arange(BLOCK, dtype=jnp.int32)[:, None]
    kj = jnp.arange(3 * BLOCK, dtype=jnp.int32)[None, :]
    rel = kj - BLOCK - qi
    band = jnp.abs(rel) <= WINDOW
    bias = rel_bias[t5_buckets(rel)].astype(jnp.float32)
    bias = bias.transpose(2, 0, 1).reshape(N_KV_HEADS, Q_PER_KV, BLOCK, 3 * BLOCK)
    sink_l = sink.astype(jnp.float32).reshape(N_KV_HEADS, Q_PER_KV, 1, 1)
    scale = HEAD_DIM ** -0.5

    def attend_block(args):
        n, qb = args
        start = n * BLOCK
        kb = lax.dynamic_slice_in_dim(kp, start, 3 * BLOCK, axis=1)
        vb = lax.dynamic_slice_in_dim(vp, start, 3 * BLOCK, axis=1)
        key_pos = start - BLOCK + kj
        mask = band & (key_pos >= 0) & (key_pos < s)
        sc = jnp.einsum('bqhgd,bkhd->bhgqk', qb, kb).astype(jnp.float32) * scale + bias
        sc = jnp.where(mask, sc, -jnp.inf)
        m = jnp.maximum(jnp.max(sc, axis=-1, keepdims=True), sink_l)
        p = jnp.exp(sc - m)
        p = p / (jnp.sum(p, axis=-1, keepdims=True) + jnp.exp(sink_l - m))
        return jnp.einsum('bhgqk,bkhd->bqhgd', p.astype(vb.dtype), vb)

    out = lax.map(attend_block, (jnp.arange(n_blocks, dtype=jnp.int32), q))
    out = out.transpose(1, 0, 2, 3, 4, 5).reshape(bsz, s, N_HEADS * HEAD_DIM)
    return jnp.einsum('bse,ed->bsd', out, w_out)


def swiglu(h, w_gate, w_up, w_down):
    g = jnp.einsum('bsd,df->bsf', h, w_gate)
    u = jnp.einsum('bsd,df->bsf', h, w_up)
    return jnp.einsum('bsf,fd->bsd', jax.nn.silu(g) * u, w_down)


def setup_inputs(seed: int = 0) -> dict:
    key = jax.random.key(seed)
    ks = jax.random.split(key, 20)
    f32 = jnp.float32
    nrm = lambda k, shape, scale: jax.random.normal(k, shape, f32) * scale
    return {
        "x": nrm(ks[0], (BATCH, SEQ, D_MODEL), 1.0),
        "norm_mix": 1.0 + nrm(ks[1], (DEPTH, D_MODEL), 0.02),
        "norm_ffn": 1.0 + nrm(ks[2], (DEPTH, D_MODEL), 0.02),
        "even_w_in": nrm(ks[3], (N_EVEN, D_MODEL, EVEN_IN_WIDTH), D_MODEL ** -0.5),
        "even_v_ln_g": 1.0 + nrm(ks[4], (N_EVEN, A_WIDTH), 0.02),
        "even_v_ln_b": nrm(ks[5], (N_EVEN, A_WIDTH), 0.02),
        "even_w_spatial": nrm(ks[6], (N_EVEN, A_GROUPS, CHUNK, CHUNK), CHUNK ** -0.5),
        "even_b_spatial": 1.0 + nrm(ks[7], (N_EVEN, A_GROUPS, CHUNK), 0.1),
        "even_conv_w": nrm(ks[8], (N_EVEN, CONV_W, B_WIDTH), CONV_W ** -0.5),
        "even_w_out": nrm(ks[9], (N_EVEN, MIX_WIDTH, D_MODEL), MIX_WIDTH ** -0.5),
        "attn_w_qkv": nrm(ks[10], (N_ODD, D_MODEL, QKV_WIDTH), D_MODEL ** -0.5),
        "attn_sink": nrm(ks[11], (N_ODD, N_HEADS), 0.5),
        "rel_bias": nrm(ks[12], (N_BUCKETS, N_HEADS), 0.5),
        "attn_w_out": nrm(ks[13], (N_ODD, N_HEADS * HEAD_DIM, D_MODEL), (N_HEADS * HEAD_DIM) ** -0.5),
        "ffn_w_gate": nrm(ks[14], (DEPTH, D_MODEL, D_FF), D_MODEL ** -0.5),
        "ffn_w_up": nrm(ks[15], (DEPTH, D_MODEL, D_FF), D_MODEL ** -0.5),
        "ffn_w_down": nrm(ks[16], (DEPTH, D_FF, D_MODEL), D_FF ** -0.5),
        "final_norm": 1.0 + nrm(ks[17], (D_MODEL,), 0.02),
    }


def reference(x, norm_mix, norm_ffn, even_w_in, even_v_ln_g, even_v_ln_b, even_w_spatial,
              even_b_spatial, even_conv_w, even_w_out, attn_w_qkv, attn_sink, rel_bias,
              attn_w_out, ffn_w_gate, ffn_w_up, ffn_w_down, final_norm):
    for layer in range(DEPTH):
        i = layer // 2
        h = rmsnorm(x, norm_mix[layer])
        if layer % 2 == 0:
            x = x + even_mixer(h, even_w_in[i], even_v_ln_g[i], even_v_ln_b[i], even_w_spatial[i],
                               even_b_spatial[i], even_conv_w[i], even_w_out[i])
        else:
            x = x + window_attention(h, attn_w_qkv[i], attn_sink[i], rel_bias, attn_w_out[i])
        h = rmsnorm(x, norm_ffn[layer])
        x = x + swiglu(h, ffn_w_gate[layer], ffn_w_up[layer], ffn_w_down[layer])
    return rmsnorm(x, final_norm)
```

```python
import numpy as np
from contextlib import ExitStack
import concourse.bass as bass
import concourse.mybir as mybir
from concourse.bass_utils import run_bass_kernel_spmd

F32 = mybir.dt.float32
BF16 = mybir.dt.bfloat16
AF = mybir.ActivationFunctionType
ALU = mybir.AluOpType
AX = mybir.AxisListType

ENGS = ("pe", "act", "dve", "pool", "sp")


class Buf:
    __slots__ = ("name", "w", "r", "rd", "sem", "semcnt")

    def __init__(self, name, sem=None):
        self.name = name
        self.w = None
        self.r = {}
        self.rd = []
        self.sem = sem
        self.semcnt = 0


class Op:
    __slots__ = ("eng", "emit", "deps", "mile", "mileno", "sem", "semval", "is_dma", "seq")


class Prog:
    def __init__(self):
        self.ops = {e: [] for e in ENGS}
        self.final_waits = []

    def add(self, eng, emit, reads=(), writes=(), dma_buf=None):
        op = Op()
        op.eng = eng
        op.emit = emit
        op.mile = False
        op.mileno = 0
        op.is_dma = dma_buf is not None
        op.sem = None
        op.semval = 0
        op.seq = len(self.ops[eng])
        deps = []
        wset = set(id(b) for b in writes)
        for b in reads:
            if b.w is not None:
                deps.append(b.w)
        for b in writes:
            if b.w is not None:
                deps.append(b.w)
            deps.extend(b.r.values())
            deps.extend(b.rd)
        best = {}
        dl = []
        seen = set()
        for d in deps:
            if d.is_dma:
                if id(d) not in seen:
                    seen.add(id(d))
                    dl.append(d)
            else:
                if d.eng == "pe" and eng == "pe" and not op.is_dma:
                    continue
                cur = best.get(d.eng)
                if cur is None or d.seq > cur.seq:
                    best[d.eng] = d
        for d in best.values():
            d.mile = True
            dl.append(d)
        op.deps = dl
        if op.is_dma:
            dma_buf.semcnt += 16
            op.sem = dma_buf.sem
            op.semval = dma_buf.semcnt
        for b in writes:
            b.w = op
            b.r = {}
            b.rd = []
        for b in reads:
            if id(b) in wset:
                continue
            if op.is_dma:
                b.rd.append(op)
            else:
                b.r[eng] = op
        self.ops[eng].append(op)
        return op

    def emit_all(self, nc, block, engsem):
        for e in ENGS:
            n = 0
            for op in self.ops[e]:
                if op.mile and not op.is_dma:
                    n += 1
                    op.mileno = n
        prog = self

        def run(ename, eobj):
            known = {}
            for op in prog.ops[ename]:
                need = {}
                for d in op.deps:
                    if d.is_dma:
                        s, v = d.sem, d.semval
                    else:
                        s, v = engsem[d.eng], d.mileno
                    k = s.num
                    if k not in need or need[k][1] < v:
                        need[k] = (s, v)
                for k, (s, v) in need.items():
                    if known.get(k, 0) < v:
                        eobj.wait_ge(s, v)
                        known[k] = v
                ins = op.emit(eobj)
                if op.is_dma:
                    ins.then_inc(op.sem, 16)
                elif op.mile:
                    ins.then_inc(engsem[ename], 1)
            if ename == "sp":
                for (s, v) in prog.final_waits:
                    eobj.wait_ge(s, v)

        @block.tensor
        def _(e):
            run("pe", e)

        @block.scalar
        def _(e):
            run("act", e)

        @block.vector
        def _(e):
            run("dve", e)

        @block.gpsimd
        def _(e):
            run("pool", e)

        @block.sync
        def _(e):
            run("sp", e)


class Ctx:
    def __init__(self, nc, st):
        self.nc = nc
        self.st = st
        self.P = Prog()
        self.nsem = 0
        self.engsem = {}
        for e in ("pe", "act", "dve", "pool"):
            self.engsem[e] = st.enter_context(nc.semaphore("prog_" + e))

    def sbuf(self, name, shape, dt):
        return self.st.enter_context(self.nc.sbuf_tensor("sb_" + name, list(shape), dt))

    def psum(self, name, shape, dt):
        return self.st.enter_context(self.nc.psum_tensor("pp_" + name, list(shape), dt))

    def dbuf(self, name):
        self.nsem += 1
        s = self.st.enter_context(self.nc.semaphore("d_" + name))
        return Buf(name, sem=s)

    def mm(self, out, lhsT, rhs, start, stop, reads, writes):
        return self.P.add("pe", lambda e: e.matmul(out, lhsT, rhs, start=start, stop=stop), reads, writes)

    def tr(self, out, in_, ident, reads, writes):
        return self.P.add("pe", lambda e: e.transpose(out, in_, ident), reads, writes)

    def act(self, out, in_, func, reads, writes, bias=None, scale=None, accum_out=None, eng="act"):
        kw = {}
        if bias is not None:
            kw["bias"] = bias
        if scale is not None:
            kw["scale"] = scale
        if accum_out is not None:
            kw["accum_out"] = accum_out
        return self.P.add("act", lambda e: e.activation(out, in_, func, **kw), reads, writes)

    def tt(self, eng, out, in0, in1, op, reads, writes):
        return self.P.add(eng, lambda e: e.tensor_tensor(out, in0, in1, op), reads, writes)

    def ts(self, eng, out, in0, s1, s2, op0, op1, reads, writes):
        if s2 is None:
            return self.P.add(eng, lambda e: e.tensor_scalar(out, in0, s1, None, op0), reads, writes)
        return self.P.add(eng, lambda e: e.tensor_scalar(out, in0, s1, s2, op0, op1), reads, writes)

    def stt(self, eng, out, in0, scalar, in1, op0, op1, reads, writes):
        return self.P.add(eng, lambda e: e.scalar_tensor_tensor(out, in0, scalar, in1, op0, op1), reads, writes)

    def copy(self, eng, out, in_, reads, writes):
        if eng == "act":
            return self.P.add("act", lambda e: e.copy(out, in_), reads, writes)
        return self.P.add(eng, lambda e: e.tensor_copy(out, in_), reads, writes)

    def memset(self, eng, ap, val, writes):
        return self.P.add(eng, lambda e: e.memset(ap, val), (), writes)

    def dma(self, eng, out, in_, reads, writes, dma_buf, **kw):
        return self.P.add(eng, lambda e: e.dma_start(out, in_, **kw), reads, writes, dma_buf=dma_buf)


D = 1024
KC = 8
S = 4096
T = 512
NT = S // T
FF = 2816
FC = 22
EPS = 1e-6
NRING = 3
HEAD_OF_SLOT = []
for _c in range(8):
    HEAD_OF_SLOT.append([0, 1, 2, 3, 8, 9, 10, 11][_c])
    HEAD_OF_SLOT.append([4, 5, 6, 7, 12, 13, 14, 15][_c])
NEG_MASK = -30000.0

WS_IN = 0
WS_OUT0 = 16
WS_QK = 24
WS_OUT1 = 34
NWS = 42


class WPool:
    def __init__(self, C, name, nslots, shape, eng="sp"):
        self.C = C
        self.n = nslots
        self.t = [C.sbuf("%s%d" % (name, i), shape, BF16) for i in range(nslots)]
        self.b = [C.dbuf("%s%d" % (name, i)) for i in range(nslots)]
        self.req = []
        self.emitted = 0
        self.cons = 0
        self.eng = eng

    def _top(self, upto):
        upto = min(upto, len(self.req))
        while self.emitted < upto:
            i = self.emitted
            src, srcbuf = self.req[i]
            k = i % self.n
            self.C.dma(self.eng, self.t[k][:], src, (srcbuf,), (self.b[k],), self.b[k])
            self.emitted += 1

    def prefetch(self):
        self._top(self.cons + self.n)

    def next(self):
        i = self.cons
        assert i < len(self.req), "weight pool underflow"
        self._top(i + self.n)
        self.cons += 1
        self._last = i
        return self.t[i % self.n], self.b[i % self.n]


def build_program(debug=False, NT=NT, stages=3):
    nc = bass.Bass("TRN2", target_bir_lowering=False)
    dt_in = lambda name, shape: nc.dram_tensor(name, list(shape), F32, kind="ExternalInput").ap()
    x_d = dt_in("x", [S, D])
    ws_d = dt_in("ws", [NWS, 128, 1024])
    gu_d = dt_in("gu", [2 * FC, 2, 128, 1024])
    wd_d = dt_in("wd", [16, 128, FC, 128])
    wv0_d = dt_in("wv0", [128, 8, 512])
    wv1_d = dt_in("wv1", [128, 8, 256])
    gall_d = dt_in("gall", [128, 4, 8])
    gfin_d = dt_in("gfin", [128, 1024])
    lng_d = dt_in("lng", [128, 512])
    lnb_d = dt_in("lnb", [128, 512])
    bsp_d = dt_in("bsp", [128, 4, 128])
    wsp_d = dt_in("wsp", [128, 4, 128])
    cw_d = dt_in("cw", [128, 3, 4])
    bias_d = dt_in("biasT", [128, 16, 384])
    sink_d = dt_in("sinkb", [128, 16])
    id_d = dt_in("ident", [128, 128])
    out_d = nc.dram_tensor("out", [S, D], F32, kind="ExternalOutput").ap()
    if debug:
        dbg_d = nc.dram_tensor("dbg", [5, NT, 128, 8, 512], F32, kind="ExternalOutput").ap()
    ws_s = nc.dram_tensor("ws_s", [NWS, 128, 1024], BF16).ap()
    gu_s = nc.dram_tensor("gu_s", [2 * FC, 2, 128, 1024], BF16).ap()
    wd_s = nc.dram_tensor("wd_s", [16, 128, FC, 128], BF16).ap()
    wv0_s = nc.dram_tensor("wv0_s", [128, 8, 512], BF16).ap()
    wv1_s = nc.dram_tensor("wv1_s", [128, 8, 256], BF16).ap()

    with ExitStack() as st:
        C = Ctx(nc, st)
        P = C.P
        xr = C.sbuf("xr", [128, NRING, KC, T], F32)
        xr_b = [[Buf("xr%d_%d" % (r, c)) for c in range(KC)] for r in range(NRING)]
        hT = C.sbuf("hT", [128, KC, T + 2], BF16)
        hT_b = [Buf("hT%d" % c) for c in range(KC)]
        hTx_b = Buf("hTx")
        arena = C.sbuf("arena", [128, 32 * 256], F32)
        pg = [Buf("pg%d" % i) for i in range(32)]

        def av_bf(p0, np_):
            return arena[:, p0 * 256:(p0 + np_) * 256].bitcast(BF16)

        def av_f32(p0, np_):
            return arena[:, p0 * 256:(p0 + np_) * 256]

        kring = C.sbuf("kring", [128, 2, 4 * T], BF16)
        kr_b = [[Buf("k%d_%d" % (c, b)) for b in range(16)] for c in range(2)]
        vring = C.sbuf("vring", [128, 16, 4, 65], BF16)
        vr_b = [Buf("v%d" % b) for b in range(16)]
        qT = C.sbuf("qT", [128, KC, T], BF16)
        qT_b = [Buf("qT%d" % c) for c in range(KC)]
        biasT = C.sbuf("biasTs", [128, 16, 384], BF16)
        biasT_b = C.dbuf("biasT")
        xstage = C.sbuf("xstage", [128, D], F32)
        xstage_b = C.dbuf("xstage")
        ostage = C.sbuf("ostage", [128, D], F32)
        ostage_b = Buf("ostage")
        ostore_b = C.dbuf("ostore")
        sq = [C.sbuf("sq%d" % i, [128, T], BF16) for i in range(3)]
        sq_b = [Buf("sq%d" % i) for i in range(3)]
        sqx = C.sbuf("sqx", [128, 8], BF16)
        sqx_b = Buf("sqx")
        tbuf = C.sbuf("tbuf", [128, T], F32)
        tbuf_b = Buf("tbuf")
        rsb = C.sbuf("rsb", [128, T], F32)
        rsb_b = Buf("rsb")
        small = C.sbuf("small", [128, 256], F32)
        zext = [C.sbuf("zext%d" % i, [128, T + 4], F32) for i in range(2)]
        zext_b = [Buf("zext%d" % i) for i in range(2)]
        zprev = C.sbuf("zprev", [128, 4], F32)
        zprev_b = [Buf("zprev%d" % j) for j in range(4)]
        ident_f = C.sbuf("ident_f", [128, 128], F32)
        ident_fb = C.dbuf("ident_f")
        ident_b = C.sbuf("ident_b", [128, 128], BF16)
        ident_bb = Buf("ident_b")
        ones_b = C.sbuf("ones_b", [128, 128], BF16)
        ones_bb = Buf("ones_b")
        neghalf = C.sbuf("neghalf", [128, T], F32)
        neghalf_b = Buf("neghalf")
        gfin = C.sbuf("gfin", [128, D], F32)
        lng = C.sbuf("lng", [128, 512], F32)
        lnb = C.sbuf("lnb", [128, 512], F32)
        bsp = C.sbuf("bsp", [128, 4, 128], F32)
        wsp_f = C.sbuf("wsp_f", [128, 4, 128], F32)
        wsp_b = C.sbuf("wsp_b", [128, 4, 128], BF16)
        wsp_bb = Buf("wsp_b")
        gall = C.sbuf("gall", [128, 4, 8], F32)
        cw = C.sbuf("cw", [128, 3, 4], F32)
        sinkb = C.sbuf("sinkb", [128, 16], F32)
        negc = C.sbuf("negc", [128, 16], F32)
        sinkterm = C.sbuf("sinkterm", [128, 16], F32)
        att_b = Buf("attconst")
        cst_b = C.dbuf("consts")
        junk = C.sbuf("junk", [128, T], BF16)
        junk_b = Buf("junk")

        _sm = [0]

        def sm(n):
            a = small[:, _sm[0]:_sm[0] + n]
            _sm[0] += n
            assert _sm[0] <= 256
            return a

        psb = [C.psum("ps%d" % i, [128, 512], F32) for i in range(8)]
        ps_b = [Buf("ps%d" % i) for i in range(8)]
        _pr = [0]

        def next_ps():
            k = _pr[0] % 5
            _pr[0] += 1
            return psb[k], ps_b[k]

        acc_t = psb[5:8]
        acc_b = ps_b[5:8]
        xps_b = Buf("xps")

        def cast_piece(name, dst, src):
            b = C.dbuf(name)
            C.dma("pool", dst, src, (), (b,), b)
            return b

        wv0_sb = cast_piece("c_wv0", wv0_s, wv0_d)
        ws_in_sb = cast_piece("c_wsin", ws_s[WS_IN:WS_IN + 16], ws_d[WS_IN:WS_IN + 16])
        ws_out0_sb = cast_piece("c_wsout0", ws_s[WS_OUT0:WS_OUT0 + 8], ws_d[WS_OUT0:WS_OUT0 + 8])
        gu_sb = {}
        wd_sb = {}
        gu_sb[(0, 0)] = cast_piece("c_gu0a", gu_s[0:11], gu_d[0:11])
        gu_sb[(0, 1)] = cast_piece("c_gu0b", gu_s[11:22], gu_d[11:22])
        wd_sb[0] = cast_piece("c_wd0", wd_s[0:8], wd_d[0:8])
        ws_qk_sb = cast_piece("c_wsqk", ws_s[WS_QK:WS_QK + 10], ws_d[WS_QK:WS_QK + 10])
        wv1_sb = cast_piece("c_wv1", wv1_s, wv1_d)
        ws_out1_sb = cast_piece("c_wsout1", ws_s[WS_OUT1:WS_OUT1 + 8], ws_d[WS_OUT1:WS_OUT1 + 8])
        gu_sb[(1, 0)] = cast_piece("c_gu1a", gu_s[22:33], gu_d[22:33])
        gu_sb[(1, 1)] = cast_piece("c_gu1b", gu_s[33:44], gu_d[33:44])
        wd_sb[1] = cast_piece("c_wd1", wd_s[8:16], wd_d[8:16])
        C.dma("pool", biasT[:], bias_d, (), (biasT_b,), biasT_b)

        for (dst, src) in ((ident_f, id_d), (gfin, gfin_d), (lng, lng_d), (lnb, lnb_d), (bsp, bsp_d),
                           (wsp_f, wsp_d), (gall, gall_d), (cw, cw_d), (sinkb, sink_d)):
            C.dma("sp", dst[:], src, (), (cst_b,), cst_b)
        C.copy("dve", ident_b[:], ident_f[:], (cst_b,), (ident_bb,))
        C.memset("pool", ones_b[:], 1.0, (ones_bb,))
        C.memset("pool", neghalf[:], -0.5, (neghalf_b,))
        C.copy("dve", wsp_b[:], wsp_f[:], (cst_b,), (wsp_bb,))
        C.memset("pool", vring[:], 1.0, vr_b)
        C.ts("dve", negc[:], sinkb[:], 0.0, -1.0, ALU.max, ALU.mult, (cst_b,), (att_b,))
        C.tt("dve", sinkterm[:], sinkb[:], negc[:], ALU.add, (cst_b, att_b), (att_b,))
        C.act(sinkterm[:], sinkterm[:], AF.Exp, (att_b,), (att_b,))

        proj = WPool(C, "wproj", 4, [128, KC, 128])
        gup = WPool(C, "wgu", 3, [128, 2, KC, 128])
        dnp = WPool(C, "wdn", 2, [128, FC, 128])
        wv0p = WPool(C, "wv0p", 1, [128, KC, 512])
        wv1p = WPool(C, "wv1p", 1, [128, KC, 256])

        def ws_req(idx, b):
            return (ws_s[idx].rearrange("p (k j) -> p k j", k=KC), b)

        def ffn_reqs(layer):
            for fc in range(FC):
                gup.req.append((gu_s[layer * FC + fc].rearrange("t p (k j) -> p t k j", k=KC), gu_sb[(layer, fc // 11)]))
            for dc in range(8):
                dnp.req.append((wd_s[layer * 8 + dc], wd_sb[layer]))

        for s in range(NT + 1):
            if s < NT:
                wv0p.req.append((wv0_s, wv0_sb))
                for g in range(4):
                    proj.req.append(ws_req(WS_IN + g, ws_in_sb))
                for j in range(4):
                    for t3 in range(3):
                        proj.req.append(ws_req(WS_IN + 4 + 4 * t3 + j, ws_in_sb))
                for dc in range(8):
                    proj.req.append(ws_req(WS_OUT0 + dc, ws_out0_sb))
                ffn_reqs(0)
                if stages >= 2:
                    wv1p.req.append((wv1_s, wv1_sb))
                    for k2 in range(2):
                        proj.req.append(ws_req(WS_QK + 8 + k2, ws_qk_sb))
            if s >= 1 and stages >= 2:
                for dc in range(8):
                    proj.req.append(ws_req(WS_OUT1 + dc, ws_out1_sb))
                ffn_reqs(1)
            if s < NT and stages >= 2:
                for c in range(8):
                    proj.req.append(ws_req(WS_QK + c, ws_qk_sb))

        def proj_fm(w_t, w_b, rhs_of_kc, rhs_bufs, n, nk=KC, ps=None):
            if ps is None:
                ps = next_ps()
            pt, pb = ps
            for kc in range(nk):
                C.mm(pt[:, 0:n], w_t[:, kc, :], rhs_of_kc(kc), kc == 0, kc == nk - 1,
                     (w_b, rhs_bufs[kc]), (pb,))
            return pt, pb

        def resid_add(slot, dc, pt, pb):
            xa = xr[:, slot, dc, :]
            C.tt("dve", xa, xa, pt[:, 0:T], ALU.add, (pb,), (xr_b[slot][dc],))

        def norm(slot, gi, extra_slot=None):
            pt, pb = next_ps()
            for c in range(KC):
                k = c % 3
                xa = xr[:, slot, c, :]
                C.tt("pool", sq[k][:], xa, xa, ALU.mult, (xr_b[slot][c],), (sq_b[k],))
                C.mm(pt[:, 0:T], ones_b[:], sq[k][:], c == 0, c == KC - 1, (ones_bb, sq_b[k]), (pb,))
            C.ts("dve", tbuf[:], pt[:, 0:T], 1.0 / D, EPS, ALU.mult, ALU.add, (pb,), (tbuf_b,))
            C.tt("pool", rsb[:], tbuf[:], neghalf[:], ALU.pow, (tbuf_b, neghalf_b), (rsb_b,))
            for c in range(KC):
                C.stt("dve", hT[:, c, 0:T], xr[:, slot, c, :], gall[:, gi, c:c + 1], rsb[:], ALU.mult, ALU.mult,
                      (xr_b[slot][c], rsb_b, cst_b), (hT_b[c],))
            if extra_slot is not None:
                xcol = xr[:, extra_slot, :, 0]
                xb = xr_b[extra_slot]
                C.tt("pool", sqx[:], xcol, xcol, ALU.mult, xb, (sqx_b,))
                pt2, pb2 = next_ps()
                for c in range(KC):
                    C.mm(pt2[:, 0:1], ones_b[:], sqx[:, c:c + 1], c == 0, c == KC - 1, (ones_bb, sqx_b), (pb2,))
                t1 = sm_t1
                C.ts("dve", t1, pt2[:, 0:1], 1.0 / D, EPS, ALU.mult, ALU.add, (pb2,), (smx_b,))
                C.tt("pool", sm_rs1, t1, neghalf[:, 0:1], ALU.pow, (smx_b, neghalf_b), (smx_b,))
                C.tt("dve", sm_x8, xcol, gall[:, gi, :], ALU.mult, tuple(xb) + (cst_b,), (smx_b,))
                C.ts("dve", hT[:, :, T], sm_x8, sm_rs1, None, ALU.mult, None, (smx_b,), (hTx_b,))

        sm_t1 = sm(1)
        sm_rs1 = sm(1)
        sm_x8 = sm(8)
        smx_b = Buf("smx")

        def load_tile(t):
            slot = t % NRING
            for tb in range(4):
                r0 = t * T + tb * 128
                C.dma("sp", xstage[:], x_d[r0:r0 + 128, :], (), (xstage_b,), xstage_b)
                for half in range(2):
                    pt, pb = next_ps()
                    for c4 in range(4):
                        c = half * 4 + c4
                        C.tr(pt[:, c4 * 128:(c4 + 1) * 128], xstage[:, c * 128:(c + 1) * 128], ident_f[:],
                             (xstage_b, ident_fb), (pb,))
                    dst = xr[:, slot, half * 4:half * 4 + 4, tb * 128:(tb + 1) * 128]
                    C.copy("act", dst, pt[:, 0:512].rearrange("p (c n) -> p c n", c=4), (pb,),
                           tuple(xr_b[slot][half * 4:half * 4 + 4]))

        def tmp_slot(i):
            p0 = 18 + 2 * (i % 6)
            return av_f32(p0, 2), (pg[p0], pg[p0 + 1])

        _tmpi = [0]

        def next_tmp():
            r = tmp_slot(_tmpi[0])
            _tmpi[0] += 1
            return r

        ln_st = [sm(6) for _ in range(4)]
        ln_mv = [sm(2) for _ in range(4)]
        ln_rt = [sm(1) for _ in range(4)]
        ln_rs = [sm(1) for _ in range(4)]
        ln_b = [Buf("ln%d" % i) for i in range(4)]
        xc_sb = sm(4)
        xc_b = [Buf("xc%d" % j) for j in range(4)]

        def mixer0(s):
            slot = s % NRING
            nslot = (s + 1) % NRING if s + 1 < NT else None
            norm(slot, 0, nslot)
            gup.prefetch()
            hb = hT_b
            wv_t, wv_b = wv0p.next()
            for tb in range(4):
                pt, pb = next_ps()
                for kc in range(KC):
                    C.mm(pt[:, 0:512], hT[:, kc, tb * 128:(tb + 1) * 128], wv_t[:, kc, :], kc == 0, kc == KC - 1,
                         (hb[kc], wv_b), (pb,))
                gv, gvb = next_tmp()
                C.act(gv, pt[:, 0:512], AF.Gelu, (pb,), gvb)
                C.P.add("dve", lambda e, o=ln_st[tb], i=gv: e.bn_stats(o, i), gvb, (ln_b[tb],))
                C.P.add("dve", lambda e, o=ln_mv[tb], i=ln_st[tb]: e.bn_aggr(o, i), (ln_b[tb],), (ln_b[tb],))
                C.ts("dve", ln_rt[tb], ln_mv[tb][:, 1:2], EPS, None, ALU.add, None, (ln_b[tb],), (ln_b[tb],))
                C.tt("pool", ln_rs[tb], ln_rt[tb], neghalf[:, 0:1], ALU.pow, (ln_b[tb], neghalf_b), (ln_b[tb],))
                C.ts("dve", gv, gv, ln_mv[tb][:, 0:1], ln_rs[tb], ALU.subtract, ALU.mult, (ln_b[tb],), gvb)
                C.tt("pool", gv, gv, lng[:], ALU.mult, (cst_b,), gvb)
                vt = av_bf(8 + tb, 1)
                C.tt("pool", vt, gv, lnb[:], ALU.add, gvb + (cst_b,), (pg[8 + tb],))
            for g in range(4):
                w_t, w_b = proj.next()
                pt, pb = proj_fm(w_t, w_b, lambda kc: hT[:, kc, 0:T], hb, T)
                au = av_bf(12 + g % 2, 1)
                aub = pg[12 + g % 2]
                C.act(au, pt[:, 0:T], AF.Gelu, (pb,), (aub,))
                pt2, pb2 = next_ps()
                for tb in range(4):
                    vt = av_bf(8 + tb, 1)
                    C.mm(pt2[:, tb * 128:(tb + 1) * 128], vt[:, g * 128:(g + 1) * 128], wsp_b[:, g, :], True, True,
                         (pg[8 + tb], wsp_bb), (pb2,))
                t1, t1b = next_tmp()
                C.tt("dve", t1.rearrange("p (t q) -> p t q", t=4), pt2[:, 0:512].rearrange("p (t q) -> p t q", t=4),
                     bsp[:, g:g + 1, :].broadcast_to([128, 4, 128]), ALU.add, (pb2, cst_b), t1b)
                C.tt("pool", av_bf(g, 1), t1, au, ALU.mult, t1b + (aub,), (pg[g],))
            for j in range(4):
                z = zext[j % 2]
                zb = zext_b[j % 2]
                bb = av_f32(14 + 2 * (j % 2), 2)
                bbb = (pg[14 + 2 * (j % 2)], pg[15 + 2 * (j % 2)])
                xp = acc_t[2]

                def halo(col, wt_, wb_):
                    if nslot is None:
                        return
                    for kc in range(KC):
                        C.mm(xp[:, col:col + 1], wt_[:, kc, :], hT[:, kc, T:T + 1], kc == 0, kc == KC - 1,
                             (wb_, hTx_b), (xps_b,))

                wb_t, wb_b = proj.next()
                pt, pb = proj_fm(wb_t, wb_b, lambda kc: hT[:, kc, 0:T], hb, T)
                C.copy("act", bb, pt[:, 0:T], (pb,), bbb)
                wc_t, wc_b = proj.next()
                ptc, pbc = proj_fm(wc_t, wc_b, lambda kc: hT[:, kc, 0:T], hb, T)
                halo(496 + 2 * j, wc_t, wc_b)
                tc_, tcb = next_tmp()
                C.copy("act", tc_, ptc[:, 0:T], (pbc,), tcb)
                wh_t, wh_b = proj.next()
                pth, pbh = proj_fm(wh_t, wh_b, lambda kc: hT[:, kc, 0:T], hb, T)
                halo(497 + 2 * j, wh_t, wh_b)
                C.tt("dve", z[:, 1:T + 1], tc_, pth[:, 0:T], ALU.mult, tcb + (pbh,), (zb,))
                if s > 0:
                    C.copy("pool", z[:, 0:1], zprev[:, j:j + 1], (zprev_b[j],), (zb,))
                else:
                    C.memset("pool", z[:, 0:1], 0.0, (zb,))
                if nslot is not None:
                    C.copy("act", xc_sb[:, j:j + 1], xp[:, 496 + 2 * j:497 + 2 * j], (xps_b,), (xc_b[j],))
                    C.tt("dve", z[:, T + 1:T + 2], xc_sb[:, j:j + 1], xp[:, 497 + 2 * j:498 + 2 * j], ALU.mult,
                         (xc_b[j], xps_b), (zb,))
                else:
                    C.memset("pool", z[:, T + 1:T + 2], 0.0, (zb,))
                C.copy("pool", zprev[:, j:j + 1], z[:, T:T + 1], (zb,), (zprev_b[j],))
                ct, ctb = next_tmp()
                C.ts("pool", ct, z[:, 0:T], cw[:, 0, j:j + 1], None, ALU.mult, None, (zb, cst_b), ctb)
                C.stt("dve", ct, z[:, 1:T + 1], cw[:, 1, j:j + 1], ct, ALU.mult, ALU.add, (zb, cst_b), ctb)
                C.stt("dve", ct, z[:, 2:T + 2], cw[:, 2, j:j + 1], ct, ALU.mult, ALU.add, (zb, cst_b), ctb)
                C.tt("pool", av_bf(4 + j, 1), ct, bb, ALU.mult, ctb + bbb, (pg[4 + j],))
            for dc in range(8):
                w_t, w_b = proj.next()
                pt, pb = proj_fm(w_t, w_b, lambda kc: av_bf(kc, 1), pg[0:8], T)
                resid_add(slot, dc, pt, pb)

        def ffn(t, layer):
            slot = t % NRING
            norm(slot, 1 + 2 * layer)
            dnp.prefetch()
            for fc in range(FC):
                gu_t, gu_b = gup.next()
                ptg, pbg = next_ps()
                ptu, pbu = next_ps()
                for kc in range(KC):
                    C.mm(ptg[:, 0:T], gu_t[:, 0, kc, :], hT[:, kc, 0:T], kc == 0, kc == KC - 1, (gu_b, hT_b[kc]), (pbg,))
                for kc in range(KC):
                    C.mm(ptu[:, 0:T], gu_t[:, 1, kc, :], hT[:, kc, 0:T], kc == 0, kc == KC - 1, (gu_b, hT_b[kc]), (pbu,))
                p0 = 22 + 2 * (fc % 2)
                sg = av_f32(p0, 2)
                sgb = (pg[p0], pg[p0 + 1])
                C.act(sg, ptg[:, 0:T], AF.Silu, (pbg,), sgb)
                C.tt("dve", av_bf(fc, 1), sg, ptu[:, 0:T], ALU.mult, sgb + (pbu,), (pg[fc],))
            proj.prefetch()
            for dc in range(8):
                wd_t, wd_b = dnp.next()
                pt, pb = next_ps()
                for fc in range(FC):
                    C.mm(pt[:, 0:T], wd_t[:, fc, :], av_bf(fc, 1), fc == 0, fc == FC - 1, (wd_b, pg[fc]), (pb,))
                resid_add(slot, dc, pt, pb)

        def l1_kv(s):
            slot = s % NRING
            norm(slot, 2)
            wv_t, wv_b = wv1p.next()
            rb = (s % 4) * 4
            for k2 in range(2):
                w_t, w_b = proj.next()
                pt, pb = proj_fm(w_t, w_b, lambda kc: hT[:, kc, 0:T], hT_b, T)
                C.copy("act", kring[:, k2, rb * 128:rb * 128 + T], pt[:, 0:T], (pb,), tuple(kr_b[k2][rb:rb + 4]))
            for tb in range(4):
                pt, pb = next_ps()
                for kc in range(KC):
                    C.mm(pt[:, 0:256], hT[:, kc, tb * 128:(tb + 1) * 128], wv_t[:, kc, :], kc == 0, kc == KC - 1,
                         (hT_b[kc], wv_b), (pb,))
                C.copy("dve", vring[:, rb + tb, :, 0:64], pt[:, 0:256].rearrange("p (g d) -> p g d", g=4), (pb,),
                       (vr_b[rb + tb],))

        def l1_q(s):
            slot = s % NRING
            norm(slot, 2)
            for c in range(8):
                w_t, w_b = proj.next()
                pt, pb = proj_fm(w_t, w_b, lambda kc: hT[:, kc, 0:T], hT_b, T)
                C.act(qT[:, c, :], pt[:, 0:T], AF.Copy, (pb,), (qT_b[c],), scale=0.125)

        den = sm(16)
        rden = sm(16)
        den_b = Buf("den")

        def attention(t):
            slot = t % NRING
            proj.prefetch()
            gup.prefetch()
            for qb in range(4):
                gb = 4 * t + qb
                js = [j for j in range(3) if 0 <= gb - 1 + j < 32]
                nj = len(js)
                j0 = js[0]
                ao = av_bf(8 + 2 * (qb % 2), 2)
                aob = (pg[8 + 2 * (qb % 2)], pg[9 + 2 * (qb % 2)])
                for hs in range(16):
                    c = hs // 2
                    half = hs % 2
                    g = HEAD_OF_SLOT[hs] // 4
                    assert g % 2 == half
                    k2 = g // 2
                    st_t, st_b = next_ps()
                    C.mm(st_t[:, 0:nj * 128], ident_b[:], biasT[:, hs, j0 * 128:(j0 + nj) * 128], True, False,
                         (ident_bb, biasT_b), (st_b,))
                    for idx, j in enumerate(js):
                        kb = (gb - 1 + j) % 16
                        C.mm(st_t[:, idx * 128:(idx + 1) * 128],
                             kring[half * 64:(half + 1) * 64, k2, kb * 128:(kb + 1) * 128],
                             qT[half * 64:(half + 1) * 64, c, qb * 128:(qb + 1) * 128],
                             False, idx == nj - 1, (kr_b[k2][kb], qT_b[c]), (st_b,))
                    ptile = av_bf(12 + hs % 3, 1)
                    ptb = pg[12 + hs % 3]
                    C.act(ptile[:, 0:nj * 128], st_t[:, 0:nj * 128], AF.Exp, (st_b, att_b), (ptb,), bias=negc[:, hs:hs + 1],
                          scale=1.0)
                    bank = hs // 7
                    col = (hs % 7) * 65
                    for idx, j in enumerate(js):
                        kb = (gb - 1 + j) % 16
                        C.mm(acc_t[bank][:, col:col + 65], ptile[:, idx * 128:(idx + 1) * 128], vring[:, kb, g, :],
                             idx == 0, idx == nj - 1, (ptb, vr_b[kb]), (acc_b[bank],))
                for bank in range(3):
                    h0 = bank * 7
                    h1 = min(16, h0 + 7)
                    nh = h1 - h0
                    a3 = acc_t[bank][:, 0:nh * 65].rearrange("p (h e) -> p h e", e=65)
                    C.tt("dve", den[:, h0:h1], a3[:, :, 64], sinkterm[:, h0:h1], ALU.add, (acc_b[bank], att_b), (den_b,))
                    C.P.add("dve", lambda e, o=rden[:, h0:h1], i=den[:, h0:h1]: e.reciprocal(o, i), (den_b,), (den_b,))
                    C.tt("dve", ao[:, h0 * 64:h1 * 64].rearrange("p (h d) -> p h d", d=64), a3[:, :, 0:64],
                         rden[:, h0:h1].unsqueeze(2).broadcast_to([128, nh, 64]), ALU.mult, (acc_b[bank], den_b), aob)
                tp, tpb = next_ps()
                tpv = tp[:, 0:512].bitcast(BF16)
                for c in range(8):
                    C.tr(tpv[:, c * 128:(c + 1) * 128], ao[:, c * 128:(c + 1) * 128], ident_b[:], aob + (ident_bb,), (tpb,))
                dst = av_bf(0, 8).rearrange("p (c n) -> p c n", c=8)[:, :, qb * 128:(qb + 1) * 128]
                C.copy("act", dst, tpv.rearrange("p (c n) -> p c n", c=8), (tpb,), tuple(pg[0:8]))
            for dc in range(8):
                w_t, w_b = proj.next()
                pt, pb = proj_fm(w_t, w_b, lambda kc: av_bf(kc, 1), pg[0:8], T)
                resid_add(slot, dc, pt, pb)

        fss = [sm(1), sm(1)]
        fs_t = sm(1)
        fs_r = sm(1)
        fs_b = Buf("fs")

        def final(t):
            slot = t % NRING
            for tb in range(4):
                pts = [next_ps(), next_ps()]
                for c in range(8):
                    pt, pb = pts[c // 4]
                    C.tr(pt[:, (c % 4) * 128:(c % 4 + 1) * 128], xr[:, slot, c, tb * 128:(tb + 1) * 128], ident_f[:],
                         (xr_b[slot][c], ident_fb), (pb,))
                for h2 in range(2):
                    pt, pb = pts[h2]
                    C.act(junk[:], pt[:, 0:512], AF.Square, (pb,), (junk_b, fs_b), accum_out=fss[h2])
                C.tt("dve", fs_t, fss[0], fss[1], ALU.add, (fs_b,), (fs_b,))
                C.ts("dve", fs_t, fs_t, 1.0 / D, EPS, ALU.mult, ALU.add, (fs_b,), (fs_b,))
                C.tt("pool", fs_r, fs_t, neghalf[:, 0:1], ALU.pow, (fs_b, neghalf_b), (fs_b,))
                for h2 in range(2):
                    pt, pb = pts[h2]
                    C.stt("dve", ostage[:, h2 * 512:(h2 + 1) * 512], pt[:, 0:512], fs_r, gfin[:, h2 * 512:(h2 + 1) * 512],
                          ALU.mult, ALU.mult, (pb, fs_b, cst_b, ostore_b), (ostage_b,))
                r0 = t * T + tb * 128
                C.dma("sp", out_d[r0:r0 + 128, :], ostage[:], (ostage_b,), (ostore_b,), ostore_b)

        def dump(k, t):
            if not debug:
                return
            slot = t % NRING
            b = C.dbuf("dbg%d_%d" % (k, t))
            C.dma("sp", dbg_d[k, t], xr[:, slot], tuple(xr_b[slot]), (), b)
            P.final_waits.append(b)

        load_tile(0)
        for s in range(NT + 1):
            if s + 1 < NT:
                load_tile(s + 1)
            if s < NT:
                dump(0, s)
                mixer0(s)
                dump(1, s)
                ffn(s, 0)
                dump(2, s)
                if stages >= 2:
                    l1_kv(s)
            if s >= 1 and stages >= 2:
                attention(s - 1)
                dump(3, s - 1)
                ffn(s - 1, 1)
                dump(4, s - 1)
                final(s - 1)
            if s < NT and stages >= 2:
                l1_q(s)

        fw = [(ostore_b.sem, ostore_b.semcnt)]
        for b in P.final_waits:
            fw.append((b.sem, b.semcnt))
        P.final_waits = fw
        with nc.Block() as block:
            P.emit_all(nc, block, C.engsem)
    return nc


def _t5_bucket_table():
    nb = 16
    max_exact = 8
    rel = np.arange(-255, 256)
    ret = np.where(rel > 0, nb, 0)
    n = np.abs(rel)
    nf = np.maximum(n, 1).astype(np.float32)
    large = max_exact + (np.log(nf / max_exact) / np.log(128 / max_exact) * (nb - max_exact)).astype(np.int32)
    large = np.minimum(large, nb - 1)
    return ret + np.where(n < max_exact, n, large)


def _chunks_kmajor(W):
    K, E = W.shape
    a = W.reshape(K // 128, 128, E // 128, 128)
    return np.ascontiguousarray(a.transpose(2, 1, 0, 3)).reshape(E // 128, 128, (K // 128) * 128)


def prep_shared(inp):
    f = lambda a: np.ascontiguousarray(np.asarray(a, dtype=np.float32))
    w_in = f(inp["even_w_in"])[0]
    cols = np.concatenate([np.arange(0, 512), np.arange(1024, 2560)])
    ws_in = _chunks_kmajor(w_in[:, cols])
    ws_out0 = _chunks_kmajor(f(inp["even_w_out"])[0])
    wqkv = f(inp["attn_w_qkv"])[0]
    qcols = np.concatenate([np.arange(h * 64, h * 64 + 64) for h in HEAD_OF_SLOT])
    ws_q = _chunks_kmajor(wqkv[:, qcols])
    ws_k = _chunks_kmajor(wqkv[:, 1024:1280])
    wo1 = f(inp["attn_w_out"])[0][qcols, :]
    ws_out1 = _chunks_kmajor(wo1)
    ws = np.concatenate([ws_in, ws_out0, ws_q, ws_k, ws_out1], axis=0)
    assert ws.shape == (NWS, 128, 1024)
    gate = f(inp["ffn_w_gate"])
    up = f(inp["ffn_w_up"])
    down = f(inp["ffn_w_down"])
    gu = np.stack([np.stack([_chunks_kmajor(gate[l]), _chunks_kmajor(up[l])], axis=1) for l in range(2)], axis=0)
    gu = np.ascontiguousarray(gu.reshape(2 * FC, 2, 128, 1024))
    wd = np.stack([_chunks_kmajor(down[l]) for l in range(2)], axis=0).reshape(16, 128, FC, 128)
    wv0 = np.ascontiguousarray(w_in[:, 512:1024].reshape(8, 128, 512).transpose(1, 0, 2))
    wv1 = np.ascontiguousarray(wqkv[:, 1280:1536].reshape(8, 128, 256).transpose(1, 0, 2))
    nm = f(inp["norm_mix"])
    nf_ = f(inp["norm_ffn"])
    gl = np.stack([nm[0], nf_[0], nm[1], nf_[1]], axis=0)
    gall = np.ascontiguousarray(gl.reshape(4, 8, 128).transpose(2, 0, 1))
    rep = lambda v: np.ascontiguousarray(np.broadcast_to(v, (128,) + v.shape))
    gfin = rep(f(inp["final_norm"]))
    lng = rep(f(inp["even_v_ln_g"])[0])
    lnb = rep(f(inp["even_v_ln_b"])[0])
    bsp = rep(f(inp["even_b_spatial"])[0])
    wsp = np.ascontiguousarray(f(inp["even_w_spatial"])[0].transpose(2, 0, 1))
    cw = np.ascontiguousarray(f(inp["even_conv_w"])[0].reshape(3, 4, 128).transpose(2, 0, 1))
    tab = _t5_bucket_table()
    k = np.arange(128)[:, None, None]
    j = np.arange(3)[None, :, None]
    q = np.arange(128)[None, None, :]
    rel = (j - 1) * 128 + k - q
    bucket = tab[rel + 255]
    rb = f(inp["rel_bias"])[:, HEAD_OF_SLOT]
    bias = rb[bucket]
    band = (np.abs(rel) <= 128)[..., None]
    bias = np.where(band, bias, np.float32(NEG_MASK)).astype(np.float32)
    biasT = np.ascontiguousarray(bias.transpose(0, 3, 1, 2)).reshape(128, 16, 384)
    sinkb = rep(f(inp["attn_sink"])[0][HEAD_OF_SLOT])
    return {
        "ws": ws, "gu": gu, "wd": np.ascontiguousarray(wd), "wv0": wv0, "wv1": wv1, "gall": gall, "gfin": gfin,
        "lng": lng, "lnb": lnb, "bsp": bsp, "wsp": wsp, "cw": cw, "biasT": biasT, "sinkb": sinkb,
        "ident": np.eye(128, dtype=np.float32),
    }


_NC_CACHE = {}


def kernel(**inputs):
    x = np.ascontiguousarray(np.asarray(inputs["x"], dtype=np.float32))
    shared = prep_shared(inputs)
    if "nc" not in _NC_CACHE:
        _NC_CACHE["nc"] = build_program(False)
    nc = _NC_CACHE["nc"]
    in_maps = []
    for b in range(8):
        m = dict(shared)
        m["x"] = x[b]
        in_maps.append(m)
    res = run_bass_kernel_spmd(nc, in_maps, core_ids=list(range(8)))
    out = np.stack([np.asarray(r["out"], dtype=np.float32) for r in res.results], axis=0)
    return out
```

```python
import numpy as np
from contextlib import ExitStack
import concourse.bass as bass
import concourse.mybir as mybir
from concourse.bass_utils import run_bass_kernel_spmd

F32 = mybir.dt.float32
BF16 = mybir.dt.bfloat16
AF = mybir.ActivationFunctionType
ALU = mybir.AluOpType
AX = mybir.AxisListType

ENGS = ("pe", "act", "dve", "pool", "sp")


class Buf:
    __slots__ = ("name", "w", "r", "rd", "sem", "semcnt")

    def __init__(self, name, sem=None):
        self.name = name
        self.w = None
        self.r = {}
        self.rd = []
        self.sem = sem
        self.semcnt = 0


class Op:
    __slots__ = ("eng", "emit", "deps", "mile", "mileno", "sem", "semval", "is_dma", "seq")


class Prog:
    def __init__(self):
        self.ops = {e: [] for e in ENGS}
        self.final_waits = []

    def add(self, eng, emit, reads=(), writes=(), dma_buf=None, indep=False):
        op = Op()
        op.eng = eng
        op.emit = emit
        op.mile = False
        op.mileno = 0
        op.is_dma = dma_buf is not None
        op.sem = None
        op.semval = 0
        op.seq = len(self.ops[eng])
        deps = []
        wset = set(id(b) for b in writes)
        for b in reads:
            if b.w is not None:
                deps.append(b.w)
        for b in writes:
            if b.w is not None and not indep:
                deps.append(b.w)
            deps.extend(b.r.values())
            deps.extend(b.rd)
        best = {}
        dl = []
        seen = set()
        for d in deps:
            if d.is_dma:
                if id(d) not in seen:
                    seen.add(id(d))
                    dl.append(d)
            else:
                if d.eng == "pe" and eng == "pe" and not op.is_dma:
                    continue
                cur = best.get(d.eng)
                if cur is None or d.seq > cur.seq:
                    best[d.eng] = d
        for d in best.values():
            d.mile = True
            dl.append(d)
        op.deps = dl
        if op.is_dma:
            dma_buf.semcnt += 16
            op.sem = dma_buf.sem
            op.semval = dma_buf.semcnt
        for b in writes:
            b.w = op
            b.r = {}
            b.rd = []
        for b in reads:
            if id(b) in wset:
                continue
            if op.is_dma:
                b.rd.append(op)
            else:
                b.r[eng] = op
        self.ops[eng].append(op)
        return op

    def emit_all(self, nc, block, engsem):
        for e in ENGS:
            n = 0
            for op in self.ops[e]:
                if op.mile and not op.is_dma:
                    n += 1
                    op.mileno = n
        prog = self

        def run(ename, eobj):
            known = {}
            for op in prog.ops[ename]:
                need = {}
                for d in op.deps:
                    if d.is_dma:
                        s, v = d.sem, d.semval
                    else:
                        s, v = engsem[d.eng], d.mileno
                    k = s.num
                    if k not in need or need[k][1] < v:
                        need[k] = (s, v)
                for k, (s, v) in need.items():
                    if known.get(k, 0) < v:
                        eobj.wait_ge(s, v)
                        known[k] = v
                ins = op.emit(eobj)
                if op.is_dma:
                    ins.then_inc(op.sem, 16)
                elif op.mile:
                    ins.then_inc(engsem[ename], 1)
            if ename == "sp":
                for (s, v) in prog.final_waits:
                    eobj.wait_ge(s, v)

        @block.tensor
        def _(e):
            run("pe", e)

        @block.scalar
        def _(e):
            run("act", e)

        @block.vector
        def _(e):
            run("dve", e)

        @block.gpsimd
        def _(e):
            run("pool", e)

        @block.sync
        def _(e):
            run("sp", e)


class Ctx:
    def __init__(self, nc, st):
        self.nc = nc
        self.st = st
        self.P = Prog()
        self.nsem = 0
        self.engsem = {}
        for e in ("pe", "act", "dve", "pool"):
            self.engsem[e] = st.enter_context(nc.semaphore("prog_" + e))

    def sbuf(self, name, shape, dt):
        return self.st.enter_context(self.nc.sbuf_tensor("sb_" + name, list(shape), dt))

    def psum(self, name, shape, dt):
        return self.st.enter_context(self.nc.psum_tensor("pp_" + name, list(shape), dt))

    def dbuf(self, name):
        self.nsem += 1
        s = self.st.enter_context(self.nc.semaphore("d_" + name))
        return Buf(name, sem=s)

    def mm(self, out, lhsT, rhs, start, stop, reads, writes):
        return self.P.add("pe", lambda e: e.matmul(out, lhsT, rhs, start=start, stop=stop), reads, writes)

    def tr(self, out, in_, ident, reads, writes):
        return self.P.add("pe", lambda e: e.transpose(out, in_, ident), reads, writes)

    def act(self, out, in_, func, reads, writes, bias=None, scale=None, accum_out=None, eng="act"):
        kw = {}
        if bias is not None:
            kw["bias"] = bias
        if scale is not None:
            kw["scale"] = scale
        if accum_out is not None:
            kw["accum_out"] = accum_out
        return self.P.add("act", lambda e: e.activation(out, in_, func, **kw), reads, writes)

    def tt(self, eng, out, in0, in1, op, reads, writes):
        return self.P.add(eng, lambda e: e.tensor_tensor(out, in0, in1, op), reads, writes)

    def ts(self, eng, out, in0, s1, s2, op0, op1, reads, writes):
        if s2 is None:
            return self.P.add(eng, lambda e: e.tensor_scalar(out, in0, s1, None, op0), reads, writes)
        return self.P.add(eng, lambda e: e.tensor_scalar(out, in0, s1, s2, op0, op1), reads, writes)

    def stt(self, eng, out, in0, scalar, in1, op0, op1, reads, writes):
        return self.P.add(eng, lambda e: e.scalar_tensor_tensor(out, in0, scalar, in1, op0, op1), reads, writes)

    def copy(self, eng, out, in_, reads, writes):
        if eng == "act":
            return self.P.add("act", lambda e: e.copy(out, in_), reads, writes)
        return self.P.add(eng, lambda e: e.tensor_copy(out, in_), reads, writes)

    def memset(self, eng, ap, val, writes):
        return self.P.add(eng, lambda e: e.memset(ap, val), (), writes)

    def dma(self, eng, out, in_, reads, writes, dma_buf, indep=False, **kw):
        return self.P.add(eng, lambda e: e.dma_start(out, in_, **kw), reads, writes, dma_buf=dma_buf, indep=indep)


D = 1024
KC = 8
S = 4096
T = 512
NT = S // T
FF = 2816
FC = 22
EPS = 1e-6
NRING = 3
HEAD_OF_SLOT = []
for _c in range(8):
    HEAD_OF_SLOT.append([0, 1, 2, 3, 8, 9, 10, 11][_c])
    HEAD_OF_SLOT.append([4, 5, 6, 7, 12, 13, 14, 15][_c])
NEG_MASK = -30000.0

WS_IN = 0
WS_OUT0 = 16
WS_QK = 24
WS_OUT1 = 34
NWS = 42


class WPool:
    def __init__(self, C, name, nslots, shape, eng="sp"):
        self.C = C
        self.n = nslots
        self.t = [C.sbuf("%s%d" % (name, i), shape, BF16) for i in range(nslots)]
        self.b = [C.dbuf("%s%d" % (name, i)) for i in range(nslots)]
        self.req = []
        self.emitted = 0
        self.cons = 0
        self.eng = eng

    def _top(self, upto):
        upto = min(upto, len(self.req))
        while self.emitted < upto:
            i = self.emitted
            src, srcbuf = self.req[i]
            k = i % self.n
            self.C.dma(self.eng, self.t[k][:], src, (srcbuf,), (self.b[k],), self.b[k])
            self.emitted += 1

    def prefetch(self):
        self._top(self.cons + self.n)

    def next(self):
        i = self.cons
        assert i < len(self.req), "weight pool underflow"
        self._top(i + self.n)
        self.cons += 1
        self._last = i
        return self.t[i % self.n], self.b[i % self.n]


def build_program(debug=False, NT=NT, stages=3):
    nc = bass.Bass("TRN2", target_bir_lowering=False)
    dt_in = lambda name, shape: nc.dram_tensor(name, list(shape), F32, kind="ExternalInput").ap()
    x_d = dt_in("x", [S, D])
    ws_d = dt_in("ws", [NWS, 128, 1024])
    gu_d = dt_in("gu", [2 * FC, 2, 128, 1024])
    wd_d = dt_in("wd", [16, 128, FC, 128])
    wv0_d = dt_in("wv0", [128, 8, 512])
    wv1_d = dt_in("wv1", [128, 8, 256])
    gall_d = dt_in("gall", [128, 4, 8])
    gfin_d = dt_in("gfin", [128, 1024])
    lng_d = dt_in("lng", [128, 512])
    lnb_d = dt_in("lnb", [128, 512])
    bsp_d = dt_in("bsp", [128, 4, 128])
    wsp_d = dt_in("wsp", [128, 4, 128])
    cw_d = dt_in("cw", [128, 3, 4])
    bias_d = dt_in("biasT", [128, 16, 384])
    sink_d = dt_in("sinkb", [128, 16])
    id_d = dt_in("ident", [128, 128])
    out_d = nc.dram_tensor("out", [S, D], F32, kind="ExternalOutput").ap()
    if debug:
        dbg_d = nc.dram_tensor("dbg", [5, NT, 128, 8, 512], F32, kind="ExternalOutput").ap()
    ws_s = nc.dram_tensor("ws_s", [NWS, 128, 1024], BF16).ap()
    gu_s = nc.dram_tensor("gu_s", [2 * FC, 2, 128, 1024], BF16).ap()
    wd_s = nc.dram_tensor("wd_s", [16, 128, FC, 128], BF16).ap()
    wv0_s = nc.dram_tensor("wv0_s", [128, 8, 512], BF16).ap()
    wv1_s = nc.dram_tensor("wv1_s", [128, 8, 256], BF16).ap()

    with ExitStack() as st:
        C = Ctx(nc, st)
        P = C.P
        xr = C.sbuf("xr", [128, NRING, KC, T], F32)
        xr_b = [[Buf("xr%d_%d" % (r, c)) for c in range(KC)] for r in range(NRING)]
        hT = C.sbuf("hT", [128, KC, T + 2], BF16)
        hT_b = [Buf("hT%d" % c) for c in range(KC)]
        hTx_b = Buf("hTx")
        arena = C.sbuf("arena", [128, 32 * 256], F32)
        pg = [Buf("pg%d" % i) for i in range(32)]

        def av_bf(p0, np_):
            return arena[:, p0 * 256:(p0 + np_) * 256].bitcast(BF16)

        def av_f32(p0, np_):
            return arena[:, p0 * 256:(p0 + np_) * 256]

        kring = C.sbuf("kring", [128, 2, 4 * T], BF16)
        kr_b = [[Buf("k%d_%d" % (c, b)) for b in range(16)] for c in range(2)]
        vring = C.sbuf("vring", [128, 16, 4, 65], BF16)
        vr_b = [Buf("v%d" % b) for b in range(16)]
        qT = C.sbuf("qT", [128, KC, T], BF16)
        qT_b = [Buf("qT%d" % c) for c in range(KC)]
        biasT = C.sbuf("biasTs", [128, 16, 384], BF16)
        biasT_b = C.dbuf("biasT")
        xstage2 = [C.sbuf("xstage%d" % i, [128, D], F32) for i in range(2)]
        xstage2_b = [C.dbuf("xstage%d" % i) for i in range(2)]
        ostage = C.sbuf("ostage", [128, D], F32)
        ostage_b = Buf("ostage")
        ostore_b = C.dbuf("ostore")
        sq = [C.sbuf("sq%d" % i, [128, T], BF16) for i in range(3)]
        sq_b = [Buf("sq%d" % i) for i in range(3)]
        sqx = C.sbuf("sqx", [128, 8], BF16)
        sqx_b = Buf("sqx")
        tbuf = C.sbuf("tbuf", [128, T], F32)
        tbuf_b = Buf("tbuf")
        rsb = C.sbuf("rsb", [128, T], F32)
        rsb_b = Buf("rsb")
        small = C.sbuf("small", [128, 256], F32)
        zext = [C.sbuf("zext%d" % i, [128, T + 4], F32) for i in range(2)]
        zext_b = [Buf("zext%d" % i) for i in range(2)]
        zprev = C.sbuf("zprev", [128, 4], F32)
        zprev_b = [Buf("zprev%d" % j) for j in range(4)]
        ident_f = C.sbuf("ident_f", [128, 128], F32)
        ident_fb = C.dbuf("ident_f")
        ident_b = C.sbuf("ident_b", [128, 128], BF16)
        ident_bb = Buf("ident_b")
        ones_b = C.sbuf("ones_b", [128, 128], BF16)
        ones_bb = Buf("ones_b")
        epsc = C.sbuf("epsc", [128, 1], F32)
        eps_b = Buf("epsc")
        neghalf = C.sbuf("neghalf", [128, 8], F32)
        neghalf_b = Buf("neghalf")
        gfin = C.sbuf("gfin", [128, D], F32)
        lng = C.sbuf("lng", [128, 512], F32)
        lnb = C.sbuf("lnb", [128, 512], F32)
        bsp = C.sbuf("bsp", [128, 4, 128], F32)
        wsp_f = C.sbuf("wsp_f", [128, 4, 128], F32)
        wsp_b = C.sbuf("wsp_b", [128, 4, 128], BF16)
        wsp_bb = Buf("wsp_b")
        gall = C.sbuf("gall", [128, 4, 8], F32)
        cw = C.sbuf("cw", [128, 3, 4], F32)
        sinkb = C.sbuf("sinkb", [128, 16], F32)
        negc = C.sbuf("negc", [128, 16], F32)
        sinkterm = C.sbuf("sinkterm", [128, 16], F32)
        att_b = Buf("attconst")
        cst_b = C.dbuf("consts")
        junk = C.sbuf("junk", [128, T], BF16)
        junk_b = Buf("junk")

        _sm = [0]

        def sm(n):
            a = small[:, _sm[0]:_sm[0] + n]
            _sm[0] += n
            assert _sm[0] <= 256
            return a

        psb = [C.psum("ps%d" % i, [128, 512], F32) for i in range(8)]
        ps_b = [Buf("ps%d" % i) for i in range(8)]
        _pr = [0]

        def next_ps():
            k = _pr[0] % 5
            _pr[0] += 1
            return psb[k], ps_b[k]

        acc_t = psb[5:8]
        acc_b = ps_b[5:8]
        xps_b = Buf("xps")

        def cast_piece(name, dst, src):
            b = C.dbuf(name)
            C.dma("pool", dst, src, (), (b,), b)
            return b

        wv0_sb = cast_piece("c_wv0", wv0_s, wv0_d)
        ws_in_sb = cast_piece("c_wsin", ws_s[WS_IN:WS_IN + 16], ws_d[WS_IN:WS_IN + 16])
        ws_out0_sb = cast_piece("c_wsout0", ws_s[WS_OUT0:WS_OUT0 + 8], ws_d[WS_OUT0:WS_OUT0 + 8])
        gu_sb = {}
        wd_sb = {}
        gu_sb[(0, 0)] = cast_piece("c_gu0a", gu_s[0:11], gu_d[0:11])
        gu_sb[(0, 1)] = cast_piece("c_gu0b", gu_s[11:22], gu_d[11:22])
        wd_sb[0] = cast_piece("c_wd0", wd_s[0:8], wd_d[0:8])
        ws_qk_sb = cast_piece("c_wsqk", ws_s[WS_QK:WS_QK + 10], ws_d[WS_QK:WS_QK + 10])
        wv1_sb = cast_piece("c_wv1", wv1_s, wv1_d)
        ws_out1_sb = cast_piece("c_wsout1", ws_s[WS_OUT1:WS_OUT1 + 8], ws_d[WS_OUT1:WS_OUT1 + 8])
        gu_sb[(1, 0)] = cast_piece("c_gu1a", gu_s[22:33], gu_d[22:33])
        gu_sb[(1, 1)] = cast_piece("c_gu1b", gu_s[33:44], gu_d[33:44])
        wd_sb[1] = cast_piece("c_wd1", wd_s[8:16], wd_d[8:16])
        C.dma("pool", biasT[:], bias_d, (), (biasT_b,), biasT_b)

        for (dst, src) in ((ident_f, id_d), (gfin, gfin_d), (lng, lng_d), (lnb, lnb_d), (bsp, bsp_d),
                           (wsp_f, wsp_d), (gall, gall_d), (cw, cw_d), (sinkb, sink_d)):
            C.dma("sp", dst[:], src, (), (cst_b,), cst_b, indep=True)
        C.copy("dve", ident_b[:], ident_f[:], (cst_b,), (ident_bb,))
        C.memset("pool", ones_b[:], 1.0, (ones_bb,))
        C.memset("pool", neghalf[:], -0.5, (neghalf_b,))
        C.memset("pool", epsc[:], EPS, (eps_b,))
        C.copy("dve", wsp_b[:], wsp_f[:], (cst_b,), (wsp_bb,))
        C.memset("pool", vring[:], 1.0, vr_b)
        C.ts("dve", negc[:], sinkb[:], 0.0, -1.0, ALU.max, ALU.mult, (cst_b,), (att_b,))
        C.tt("dve", sinkterm[:], sinkb[:], negc[:], ALU.add, (cst_b, att_b), (att_b,))
        C.act(sinkterm[:], sinkterm[:], AF.Exp, (att_b,), (att_b,))

        proj = WPool(C, "wproj", 4, [128, KC, 128])
        gup = WPool(C, "wgu", 3, [128, 2, KC, 128])
        dnp = WPool(C, "wdn", 2, [128, FC, 128])
        wv0p = WPool(C, "wv0p", 1, [128, KC, 512])
        wv1p = WPool(C, "wv1p", 1, [128, KC, 256])

        def ws_req(idx, b):
            return (ws_s[idx].rearrange("p (k j) -> p k j", k=KC), b)

        def ffn_reqs(layer):
            for fc in range(FC):
                gup.req.append((gu_s[layer * FC + fc].rearrange("t p (k j) -> p t k j", k=KC), gu_sb[(layer, fc // 11)]))
            for dc in range(8):
                dnp.req.append((wd_s[layer * 8 + dc], wd_sb[layer]))

        for s in range(NT + 1):
            if s < NT:
                wv0p.req.append((wv0_s, wv0_sb))
                for g in range(4):
                    proj.req.append(ws_req(WS_IN + g, ws_in_sb))
                for j in range(4):
                    for t3 in range(3):
                        proj.req.append(ws_req(WS_IN + 4 + 4 * t3 + j, ws_in_sb))
                for dc in range(8):
                    proj.req.append(ws_req(WS_OUT0 + dc, ws_out0_sb))
                ffn_reqs(0)
                if stages >= 2:
                    wv1p.req.append((wv1_s, wv1_sb))
                    for k2 in range(2):
                        proj.req.append(ws_req(WS_QK + 8 + k2, ws_qk_sb))
            if s >= 1 and stages >= 2:
                for dc in range(8):
                    proj.req.append(ws_req(WS_OUT1 + dc, ws_out1_sb))
                ffn_reqs(1)
            if s < NT and stages >= 2:
                for c in range(8):
                    proj.req.append(ws_req(WS_QK + c, ws_qk_sb))

        def proj_fm(w_t, w_b, rhs_of_kc, rhs_bufs, n, nk=KC, ps=None):
            if ps is None:
                ps = next_ps()
            pt, pb = ps
            for kc in range(nk):
                C.mm(pt[:, 0:n], w_t[:, kc, :], rhs_of_kc(kc), kc == 0, kc == nk - 1,
                     (w_b, rhs_bufs[kc]), (pb,))
            return pt, pb

        def resid_add(slot, dc, pt, pb):
            xa = xr[:, slot, dc, :]
            C.tt("dve", xa, xa, pt[:, 0:T], ALU.add, (pb,), (xr_b[slot][dc],))

        def norm(slot, gi, extra_slot=None):
            pt, pb = next_ps()
            for c in range(KC):
                k = c % 3
                xa = xr[:, slot, c, :]
                if c % 2 == 0:
                    C.act(sq[k][:], xa, AF.Square, (xr_b[slot][c],), (sq_b[k],))
                else:
                    C.tt("dve", sq[k][:], xa, xa, ALU.mult, (xr_b[slot][c],), (sq_b[k],))
                C.mm(pt[:, 0:T], ones_b[:], sq[k][:], c == 0, c == KC - 1, (ones_bb, sq_b[k]), (pb,))
            C.act(tbuf[:], pt[:, 0:T], AF.Sqrt, (pb, eps_b), (tbuf_b,), bias=epsc[:, 0:1], scale=1.0 / D)
            C.P.add("dve", lambda e: e.reciprocal(rsb[:], tbuf[:]), (tbuf_b,), (rsb_b,))
            for c in range(KC):
                C.stt("dve", hT[:, c, 0:T], xr[:, slot, c, :], gall[:, gi, c:c + 1], rsb[:], ALU.mult, ALU.mult,
                      (xr_b[slot][c], rsb_b, cst_b), (hT_b[c],))
            if extra_slot is not None:
                xcol = xr[:, extra_slot, :, 0]
                xb = xr_b[extra_slot]
                C.tt("dve", sqx[:], xcol, xcol, ALU.mult, xb, (sqx_b,))
                pt2, pb2 = next_ps()
                for c in range(KC):
                    C.mm(pt2[:, 0:1], ones_b[:], sqx[:, c:c + 1], c == 0, c == KC - 1, (ones_bb, sqx_b), (pb2,))
                t1 = sm_t1
                C.act(t1, pt2[:, 0:1], AF.Sqrt, (pb2, eps_b), (smx_b,), bias=epsc[:, 0:1], scale=1.0 / D)
                C.P.add("dve", lambda e: e.reciprocal(sm_rs1, t1), (smx_b,), (smx_b,))
                C.tt("dve", sm_x8, xcol, gall[:, gi, :], ALU.mult, tuple(xb) + (cst_b,), (smx_b,))
                C.ts("dve", hT[:, :, T], sm_x8, sm_rs1, None, ALU.mult, None, (smx_b,), (hTx_b,))

        sm_t1 = sm(1)
        sm_rs1 = sm(1)
        sm_x8 = sm(8)
        smx_b = Buf("smx")

        def load_tile(t):
            slot = t % NRING
            for tb in range(4):
                r0 = t * T + tb * 128
                xstage = xstage2[tb % 2]
                xstage_b = xstage2_b[tb % 2]
                C.dma("sp", xstage[:], x_d[r0:r0 + 128, :], (), (xstage_b,), xstage_b)
                for half in range(2):
                    pt, pb = next_ps()
                    for c4 in range(4):
                        c = half * 4 + c4
                        C.tr(pt[:, c4 * 128:(c4 + 1) * 128], xstage[:, c * 128:(c + 1) * 128], ident_f[:],
                             (xstage_b, ident_fb), (pb,))
                    dst = xr[:, slot, half * 4:half * 4 + 4, tb * 128:(tb + 1) * 128]
                    C.copy("act", dst, pt[:, 0:512].rearrange("p (c n) -> p c n", c=4), (pb,),
                           tuple(xr_b[slot][half * 4:half * 4 + 4]))

        def tmp_slot(i):
            p0 = 18 + 2 * (i % 6)
            return av_f32(p0, 2), (pg[p0], pg[p0 + 1])

        _tmpi = [0]

        def next_tmp():
            r = tmp_slot(_tmpi[0])
            _tmpi[0] += 1
            return r

        ln_st = [sm(6) for _ in range(4)]
        ln_mv = [sm(2) for _ in range(4)]
        ln_rt = [sm(1) for _ in range(4)]
        ln_rs = [sm(1) for _ in range(4)]
        ln_b = [Buf("ln%d" % i) for i in range(4)]
        xc_sb = sm(4)
        xc_b = [Buf("xc%d" % j) for j in range(4)]

        def mixer0(s):
            slot = s % NRING
            nslot = (s + 1) % NRING if s + 1 < NT else None
            norm(slot, 0, nslot)
            gup.prefetch()
            hb = hT_b
            wv_t, wv_b = wv0p.next()
            for tb in range(4):
                pt, pb = next_ps()
                for kc in range(KC):
                    C.mm(pt[:, 0:512], hT[:, kc, tb * 128:(tb + 1) * 128], wv_t[:, kc, :], kc == 0, kc == KC - 1,
                         (hb[kc], wv_b), (pb,))
                gv, gvb = next_tmp()
                C.act(gv, pt[:, 0:512], AF.Gelu, (pb,), gvb)
                C.P.add("dve", lambda e, o=ln_st[tb], i=gv: e.bn_stats(o, i), gvb, (ln_b[tb],))
                C.P.add("dve", lambda e, o=ln_mv[tb], i=ln_st[tb]: e.bn_aggr(o, i), (ln_b[tb],), (ln_b[tb],))
                C.ts("dve", ln_rt[tb], ln_mv[tb][:, 1:2], EPS, None, ALU.add, None, (ln_b[tb],), (ln_b[tb],))
                C.tt("pool", ln_rs[tb], ln_rt[tb], neghalf[:, 0:1], ALU.pow, (ln_b[tb], neghalf_b), (ln_b[tb],))
                C.ts("dve", gv, gv, ln_mv[tb][:, 0:1], ln_rs[tb], ALU.subtract, ALU.mult, (ln_b[tb],), gvb)
                C.tt("pool", gv, gv, lng[:], ALU.mult, (cst_b,), gvb)
                vt = av_bf(8 + tb, 1)
                C.tt("pool", vt, gv, lnb[:], ALU.add, gvb + (cst_b,), (pg[8 + tb],))
            for g in range(4):
                w_t, w_b = proj.next()
                pt, pb = proj_fm(w_t, w_b, lambda kc: hT[:, kc, 0:T], hb, T)
                au = av_bf(12 + g % 2, 1)
                aub = pg[12 + g % 2]
                C.act(au, pt[:, 0:T], AF.Gelu, (pb,), (aub,))
                pt2, pb2 = next_ps()
                for tb in range(4):
                    vt = av_bf(8 + tb, 1)
                    C.mm(pt2[:, tb * 128:(tb + 1) * 128], vt[:, g * 128:(g + 1) * 128], wsp_b[:, g, :], True, True,
                         (pg[8 + tb], wsp_bb), (pb2,))
                t1, t1b = next_tmp()
                C.tt("dve", t1.rearrange("p (t q) -> p t q", t=4), pt2[:, 0:512].rearrange("p (t q) -> p t q", t=4),
                     bsp[:, g:g + 1, :].broadcast_to([128, 4, 128]), ALU.add, (pb2, cst_b), t1b)
                C.tt("pool", av_bf(g, 1), t1, au, ALU.mult, t1b + (aub,), (pg[g],))
            for j in range(4):
                z = zext[j % 2]
                zb = zext_b[j % 2]
                bb = av_f32(14 + 2 * (j % 2), 2)
                bbb = (pg[14 + 2 * (j % 2)], pg[15 + 2 * (j % 2)])
                xp = acc_t[2]

                def halo(col, wt_, wb_):
                    if nslot is None:
                        return
                    for kc in range(KC):
                        C.mm(xp[:, col:col + 1], wt_[:, kc, :], hT[:, kc, T:T + 1], kc == 0, kc == KC - 1,
                             (wb_, hTx_b), (xps_b,))

                wb_t, wb_b = proj.next()
                pt, pb = proj_fm(wb_t, wb_b, lambda kc: hT[:, kc, 0:T], hb, T)
                C.copy("act", bb, pt[:, 0:T], (pb,), bbb)
                wc_t, wc_b = proj.next()
                ptc, pbc = proj_fm(wc_t, wc_b, lambda kc: hT[:, kc, 0:T], hb, T)
                halo(496 + 2 * j, wc_t, wc_b)
                tc_, tcb = next_tmp()
                C.copy("act", tc_, ptc[:, 0:T], (pbc,), tcb)
                wh_t, wh_b = proj.next()
                pth, pbh = proj_fm(wh_t, wh_b, lambda kc: hT[:, kc, 0:T], hb, T)
                halo(497 + 2 * j, wh_t, wh_b)
                C.tt("dve", z[:, 1:T + 1], tc_, pth[:, 0:T], ALU.mult, tcb + (pbh,), (zb,))
                if s > 0:
                    C.copy("pool", z[:, 0:1], zprev[:, j:j + 1], (zprev_b[j],), (zb,))
                else:
                    C.memset("pool", z[:, 0:1], 0.0, (zb,))
                if nslot is not None:
                    C.copy("act", xc_sb[:, j:j + 1], xp[:, 496 + 2 * j:497 + 2 * j], (xps_b,), (xc_b[j],))
                    C.tt("dve", z[:, T + 1:T + 2], xc_sb[:, j:j + 1], xp[:, 497 + 2 * j:498 + 2 * j], ALU.mult,
                         (xc_b[j], xps_b), (zb,))
                else:
                    C.memset("pool", z[:, T + 1:T + 2], 0.0, (zb,))
                C.copy("pool", zprev[:, j:j + 1], z[:, T:T + 1], (zb,), (zprev_b[j],))
                ct, ctb = next_tmp()
                C.ts("dve", ct, z[:, 0:T], cw[:, 0, j:j + 1], None, ALU.mult, None, (zb, cst_b), ctb)
                C.stt("dve", ct, z[:, 1:T + 1], cw[:, 1, j:j + 1], ct, ALU.mult, ALU.add, (zb, cst_b), ctb)
                C.stt("dve", ct, z[:, 2:T + 2], cw[:, 2, j:j + 1], ct, ALU.mult, ALU.add, (zb, cst_b), ctb)
                C.tt("pool", av_bf(4 + j, 1), ct, bb, ALU.mult, ctb + bbb, (pg[4 + j],))
            for dc in range(8):
                w_t, w_b = proj.next()
                pt, pb = proj_fm(w_t, w_b, lambda kc: av_bf(kc, 1), pg[0:8], T)
                resid_add(slot, dc, pt, pb)

        def ffn(t, layer):
            slot = t % NRING
            norm(slot, 1 + 2 * layer)
            dnp.prefetch()
            for fc in range(FC):
                gu_t, gu_b = gup.next()
                ptg, pbg = next_ps()
                ptu, pbu = next_ps()
                for kc in range(KC):
                    C.mm(ptg[:, 0:T], gu_t[:, 0, kc, :], hT[:, kc, 0:T], kc == 0, kc == KC - 1, (gu_b, hT_b[kc]), (pbg,))
                for kc in range(KC):
                    C.mm(ptu[:, 0:T], gu_t[:, 1, kc, :], hT[:, kc, 0:T], kc == 0, kc == KC - 1, (gu_b, hT_b[kc]), (pbu,))
                p0 = 22 + 2 * (fc % 2)
                sg = av_f32(p0, 2)
                sgb = (pg[p0], pg[p0 + 1])
                C.act(sg, ptg[:, 0:T], AF.Silu, (pbg,), sgb)
                C.tt("dve", av_bf(fc, 1), sg, ptu[:, 0:T], ALU.mult, sgb + (pbu,), (pg[fc],))
            proj.prefetch()
            for dc in range(8):
                wd_t, wd_b = dnp.next()
                pt, pb = next_ps()
                for fc in range(FC):
                    C.mm(pt[:, 0:T], wd_t[:, fc, :], av_bf(fc, 1), fc == 0, fc == FC - 1, (wd_b, pg[fc]), (pb,))
                resid_add(slot, dc, pt, pb)

        def l1_kv(s):
            slot = s % NRING
            norm(slot, 2)
            wv_t, wv_b = wv1p.next()
            rb = (s % 4) * 4
            for k2 in range(2):
                w_t, w_b = proj.next()
                pt, pb = proj_fm(w_t, w_b, lambda kc: hT[:, kc, 0:T], hT_b, T)
                C.copy("act", kring[:, k2, rb * 128:rb * 128 + T], pt[:, 0:T], (pb,), tuple(kr_b[k2][rb:rb + 4]))
            for tb in range(4):
                pt, pb = next_ps()
                for kc in range(KC):
                    C.mm(pt[:, 0:256], hT[:, kc, tb * 128:(tb + 1) * 128], wv_t[:, kc, :], kc == 0, kc == KC - 1,
                         (hT_b[kc], wv_b), (pb,))
                C.copy("dve", vring[:, rb + tb, :, 0:64], pt[:, 0:256].rearrange("p (g d) -> p g d", g=4), (pb,),
                       (vr_b[rb + tb],))

        def l1_q(s):
            slot = s % NRING
            norm(slot, 2)
            for c in range(8):
                w_t, w_b = proj.next()
                pt, pb = proj_fm(w_t, w_b, lambda kc: hT[:, kc, 0:T], hT_b, T)
                C.act(qT[:, c, :], pt[:, 0:T], AF.Copy, (pb,), (qT_b[c],), scale=0.125)

        den = sm(16)
        rden = sm(16)
        den_b = Buf("den")

        def attention(t):
            slot = t % NRING
            proj.prefetch()
            gup.prefetch()
            for qb in range(4):
                gb = 4 * t + qb
                js = [j for j in range(3) if 0 <= gb - 1 + j < 32]
                nj = len(js)
                j0 = js[0]
                ao = av_bf(8 + 2 * (qb % 2), 2)
                aob = (pg[8 + 2 * (qb % 2)], pg[9 + 2 * (qb % 2)])
                for hs in range(16):
                    c = hs // 2
                    half = hs % 2
                    g = HEAD_OF_SLOT[hs] // 4
                    assert g % 2 == half
                    k2 = g // 2
                    st_t, st_b = next_ps()
                    C.mm(st_t[:, 0:nj * 128], ident_b[:], biasT[:, hs, j0 * 128:(j0 + nj) * 128], True, False,
                         (ident_bb, biasT_b), (st_b,))
                    for idx, j in enumerate(js):
                        kb = (gb - 1 + j) % 16
                        C.mm(st_t[:, idx * 128:(idx + 1) * 128],
                             kring[half * 64:(half + 1) * 64, k2, kb * 128:(kb + 1) * 128],
                             qT[half * 64:(half + 1) * 64, c, qb * 128:(qb + 1) * 128],
                             False, idx == nj - 1, (kr_b[k2][kb], qT_b[c]), (st_b,))
                    ptile = av_bf(12 + hs % 3, 1)
                    ptb = pg[12 + hs % 3]
                    C.act(ptile[:, 0:nj * 128], st_t[:, 0:nj * 128], AF.Exp, (st_b, att_b), (ptb,), bias=negc[:, hs:hs + 1],
                          scale=1.0)
                    bank = hs // 7
                    col = (hs % 7) * 65
                    for idx, j in enumerate(js):
                        kb = (gb - 1 + j) % 16
                        C.mm(acc_t[bank][:, col:col + 65], ptile[:, idx * 128:(idx + 1) * 128], vring[:, kb, g, :],
                             idx == 0, idx == nj - 1, (ptb, vr_b[kb]), (acc_b[bank],))
                for bank in range(3):
                    h0 = bank * 7
                    h1 = min(16, h0 + 7)
                    nh = h1 - h0
                    a3 = acc_t[bank][:, 0:nh * 65].rearrange("p (h e) -> p h e", e=65)
                    C.tt("dve", den[:, h0:h1], a3[:, :, 64], sinkterm[:, h0:h1], ALU.add, (acc_b[bank], att_b), (den_b,))
                    C.P.add("dve", lambda e, o=rden[:, h0:h1], i=den[:, h0:h1]: e.reciprocal(o, i), (den_b,), (den_b,))
                    C.tt("dve", ao[:, h0 * 64:h1 * 64].rearrange("p (h d) -> p h d", d=64), a3[:, :, 0:64],
                         rden[:, h0:h1].unsqueeze(2).broadcast_to([128, nh, 64]), ALU.mult, (acc_b[bank], den_b), aob)
                tp, tpb = next_ps()
                tpv = tp[:, 0:512].bitcast(BF16)
                for c in range(8):
                    C.tr(tpv[:, c * 128:(c + 1) * 128], ao[:, c * 128:(c + 1) * 128], ident_b[:], aob + (ident_bb,), (tpb,))
                dst = av_bf(0, 8).rearrange("p (c n) -> p c n", c=8)[:, :, qb * 128:(qb + 1) * 128]
                C.copy("act", dst, tpv.rearrange("p (c n) -> p c n", c=8), (tpb,), tuple(pg[0:8]))
            for dc in range(8):
                w_t, w_b = proj.next()
                pt, pb = proj_fm(w_t, w_b, lambda kc: av_bf(kc, 1), pg[0:8], T)
                resid_add(slot, dc, pt, pb)

        fss = [sm(1), sm(1)]
        fs_t = sm(1)
        fs_r = sm(1)
        fs_b = Buf("fs")

        def final(t):
            slot = t % NRING
            for tb in range(4):
                pts = [next_ps(), next_ps()]
                for c in range(8):
                    pt, pb = pts[c // 4]
                    C.tr(pt[:, (c % 4) * 128:(c % 4 + 1) * 128], xr[:, slot, c, tb * 128:(tb + 1) * 128], ident_f[:],
                         (xr_b[slot][c], ident_fb), (pb,))
                for h2 in range(2):
                    pt, pb = pts[h2]
                    C.act(junk[:], pt[:, 0:512], AF.Square, (pb,), (junk_b, fs_b), accum_out=fss[h2])
                C.tt("dve", fs_t, fss[0], fss[1], ALU.add, (fs_b,), (fs_b,))
                C.ts("dve", fs_t, fs_t, 1.0 / D, EPS, ALU.mult, ALU.add, (fs_b,), (fs_b,))
                C.tt("pool", fs_r, fs_t, neghalf[:, 0:1], ALU.pow, (fs_b, neghalf_b), (fs_b,))
                for h2 in range(2):
                    pt, pb = pts[h2]
                    C.stt("dve", ostage[:, h2 * 512:(h2 + 1) * 512], pt[:, 0:512], fs_r, gfin[:, h2 * 512:(h2 + 1) * 512],
                          ALU.mult, ALU.mult, (pb, fs_b, cst_b, ostore_b), (ostage_b,))
                r0 = t * T + tb * 128
                C.dma("sp", out_d[r0:r0 + 128, :], ostage[:], (ostage_b,), (ostore_b,), ostore_b)

        def dump(k, t):
            if not debug:
                return
            slot = t % NRING
            b = C.dbuf("dbg%d_%d" % (k, t))
            C.dma("sp", dbg_d[k, t], xr[:, slot], tuple(xr_b[slot]), (), b)
            P.final_waits.append(b)

        load_tile(0)
        for s in range(NT + 1):
            if s + 1 < NT:
                load_tile(s + 1)
            if s < NT:
                dump(0, s)
                mixer0(s)
                dump(1, s)
                ffn(s, 0)
                dump(2, s)
                if stages >= 2:
                    l1_kv(s)
            if s >= 1 and stages >= 2:
                attention(s - 1)
                dump(3, s - 1)
                ffn(s - 1, 1)
                dump(4, s - 1)
                final(s - 1)
            if s < NT and stages >= 2:
                l1_q(s)

        fw = [(ostore_b.sem, ostore_b.semcnt)]
        for b in P.final_waits:
            fw.append((b.sem, b.semcnt))
        P.final_waits = fw
        print("SBUF bytes remaining per partition:", nc.sbuf_bytes_remaining, "ops:", {e: len(P.ops[e]) for e in ENGS})
        with nc.Block() as block:
            P.emit_all(nc, block, C.engsem)
    return nc


def _t5_bucket_table():
    nb = 16
    max_exact = 8
    rel = np.arange(-255, 256)
    ret = np.where(rel > 0, nb, 0)
    n = np.abs(rel)
    nf = np.maximum(n, 1).astype(np.float32)
    large = max_exact + (np.log(nf / max_exact) / np.log(128 / max_exact) * (nb - max_exact)).astype(np.int32)
    large = np.minimum(large, nb - 1)
    return ret + np.where(n < max_exact, n, large)


def _chunks_kmajor(W):
    K, E = W.shape
    a = W.reshape(K // 128, 128, E // 128, 128)
    return np.ascontiguousarray(a.transpose(2, 1, 0, 3)).reshape(E // 128, 128, (K // 128) * 128)


def prep_shared(inp):
    f = lambda a: np.ascontiguousarray(np.asarray(a, dtype=np.float32))
    w_in = f(inp["even_w_in"])[0]
    cols = np.concatenate([np.arange(0, 512), np.arange(1024, 2560)])
    ws_in = _chunks_kmajor(w_in[:, cols])
    ws_out0 = _chunks_kmajor(f(inp["even_w_out"])[0])
    wqkv = f(inp["attn_w_qkv"])[0]
    qcols = np.concatenate([np.arange(h * 64, h * 64 + 64) for h in HEAD_OF_SLOT])
    ws_q = _chunks_kmajor(wqkv[:, qcols])
    ws_k = _chunks_kmajor(wqkv[:, 1024:1280])
    wo1 = f(inp["attn_w_out"])[0][qcols, :]
    ws_out1 = _chunks_kmajor(wo1)
    ws = np.concatenate([ws_in, ws_out0, ws_q, ws_k, ws_out1], axis=0)
    assert ws.shape == (NWS, 128, 1024)
    gate = f(inp["ffn_w_gate"])
    up = f(inp["ffn_w_up"])
    down = f(inp["ffn_w_down"])
    gu = np.stack([np.stack([_chunks_kmajor(gate[l]), _chunks_kmajor(up[l])], axis=1) for l in range(2)], axis=0)
    gu = np.ascontiguousarray(gu.reshape(2 * FC, 2, 128, 1024))
    wd = np.stack([_chunks_kmajor(down[l]) for l in range(2)], axis=0).reshape(16, 128, FC, 128)
    wv0 = np.ascontiguousarray(w_in[:, 512:1024].reshape(8, 128, 512).transpose(1, 0, 2))
    wv1 = np.ascontiguousarray(wqkv[:, 1280:1536].reshape(8, 128, 256).transpose(1, 0, 2))
    nm = f(inp["norm_mix"])
    nf_ = f(inp["norm_ffn"])
    gl = np.stack([nm[0], nf_[0], nm[1], nf_[1]], axis=0)
    gall = np.ascontiguousarray(gl.reshape(4, 8, 128).transpose(2, 0, 1))
    rep = lambda v: np.ascontiguousarray(np.broadcast_to(v, (128,) + v.shape))
    gfin = rep(f(inp["final_norm"]))
    lng = rep(f(inp["even_v_ln_g"])[0])
    lnb = rep(f(inp["even_v_ln_b"])[0])
    bsp = rep(f(inp["even_b_spatial"])[0])
    wsp = np.ascontiguousarray(f(inp["even_w_spatial"])[0].transpose(2, 0, 1))
    cw = np.ascontiguousarray(f(inp["even_conv_w"])[0].reshape(3, 4, 128).transpose(2, 0, 1))
    tab = _t5_bucket_table()
    k = np.arange(128)[:, None, None]
    j = np.arange(3)[None, :, None]
    q = np.arange(128)[None, None, :]
    rel = (j - 1) * 128 + k - q
    bucket = tab[rel + 255]
    rb = f(inp["rel_bias"])[:, HEAD_OF_SLOT]
    bias = rb[bucket]
    band = (np.abs(rel) <= 128)[..., None]
    bias = np.where(band, bias, np.float32(NEG_MASK)).astype(np.float32)
    biasT = np.ascontiguousarray(bias.transpose(0, 3, 1, 2)).reshape(128, 16, 384)
    sinkb = rep(f(inp["attn_sink"])[0][HEAD_OF_SLOT])
    return {
        "ws": ws, "gu": gu, "wd": np.ascontiguousarray(wd), "wv0": wv0, "wv1": wv1, "gall": gall, "gfin": gfin,
        "lng": lng, "lnb": lnb, "bsp": bsp, "wsp": wsp, "cw": cw, "biasT": biasT, "sinkb": sinkb,
        "ident": np.eye(128, dtype=np.float32),
    }


_NC_CACHE = {}


def kernel(**inputs):
    x = np.ascontiguousarray(np.asarray(inputs["x"], dtype=np.float32))
    shared = prep_shared(inputs)
    if "nc" not in _NC_CACHE:
        _NC_CACHE["nc"] = build_program(False)
    nc = _NC_CACHE["nc"]
    in_maps = []
    for b in range(8):
        m = dict(shared)
        m["x"] = x[b]
        in_maps.append(m)
    res = run_bass_kernel_spmd(nc, in_maps, core_ids=list(range(8)))
    out = np.stack([np.asarray(r["out"], dtype=np.float32) for r in res.results], axis=0)
    return out
```

```python
import numpy as np
from contextlib import ExitStack
import concourse.bass as bass
import concourse.mybir as mybir
from concourse.bass_utils import run_bass_kernel_spmd

F32 = mybir.dt.float32
BF16 = mybir.dt.bfloat16
AF = mybir.ActivationFunctionType
ALU = mybir.AluOpType
AX = mybir.AxisListType

ENGS = ("pe", "act", "dve", "pool", "sp")


class Buf:
    __slots__ = ("name", "w", "r", "rd", "sem", "semcnt")

    def __init__(self, name, sem=None):
        self.name = name
        self.w = None
        self.r = {}
        self.rd = []
        self.sem = sem
        self.semcnt = 0


class Op:
    __slots__ = ("eng", "emit", "deps", "mile", "mileno", "sem", "semval", "is_dma", "seq")


class Prog:
    def __init__(self):
        self.ops = {e: [] for e in ENGS}
        self.final_waits = []

    def add(self, eng, emit, reads=(), writes=(), dma_buf=None, indep=False):
        op = Op()
        op.eng = eng
        op.emit = emit
        op.mile = False
        op.mileno = 0
        op.is_dma = dma_buf is not None
        op.sem = None
        op.semval = 0
        op.seq = len(self.ops[eng])
        deps = []
        wset = set(id(b) for b in writes)
        for b in reads:
            if b.w is not None:
                deps.append(b.w)
        for b in writes:
            if b.w is not None and not indep:
                deps.append(b.w)
            deps.extend(b.r.values())
            deps.extend(b.rd)
        best = {}
        dl = []
        seen = set()
        for d in deps:
            if d.is_dma:
                if id(d) not in seen:
                    seen.add(id(d))
                    dl.append(d)
            else:
                if d.eng == "pe" and eng == "pe" and not op.is_dma:
                    continue
                cur = best.get(d.eng)
                if cur is None or d.seq > cur.seq:
                    best[d.eng] = d
        for d in best.values():
            d.mile = True
            dl.append(d)
        op.deps = dl
        if op.is_dma:
            dma_buf.semcnt += 16
            op.sem = dma_buf.sem
            op.semval = dma_buf.semcnt
        for b in writes:
            b.w = op
            b.r = {}
            b.rd = []
        for b in reads:
            if id(b) in wset:
                continue
            if op.is_dma:
                b.rd.append(op)
            else:
                b.r[eng] = op
        self.ops[eng].append(op)
        return op

    def emit_all(self, nc, block, engsem):
        for e in ENGS:
            n = 0
            for op in self.ops[e]:
                if op.mile and not op.is_dma:
                    n += 1
                    op.mileno = n
        prog = self

        def run(ename, eobj):
            known = {}
            for op in prog.ops[ename]:
                need = {}
                for d in op.deps:
                    if d.is_dma:
                        s, v = d.sem, d.semval
                    else:
                        s, v = engsem[d.eng], d.mileno
                    k = s.num
                    if k not in need or need[k][1] < v:
                        need[k] = (s, v)
                for k, (s, v) in need.items():
                    if known.get(k, 0) < v:
                        eobj.wait_ge(s, v)
                        known[k] = v
                ins = op.emit(eobj)
                if op.is_dma:
                    ins.then_inc(op.sem, 16)
                elif op.mile:
                    ins.then_inc(engsem[ename], 1)
            if ename == "sp":
                for (s, v) in prog.final_waits:
                    eobj.wait_ge(s, v)

        @block.tensor
        def _(e):
            run("pe", e)

        @block.scalar
        def _(e):
            run("act", e)

        @block.vector
        def _(e):
            run("dve", e)

        @block.gpsimd
        def _(e):
            run("pool", e)

        @block.sync
        def _(e):
            run("sp", e)


class Ctx:
    def __init__(self, nc, st):
        self.nc = nc
        self.st = st
        self.P = Prog()
        self.nsem = 0
        self.engsem = {}
        for e in ("pe", "act", "dve", "pool"):
            self.engsem[e] = st.enter_context(nc.semaphore("prog_" + e))

    def sbuf(self, name, shape, dt):
        return self.st.enter_context(self.nc.sbuf_tensor("sb_" + name, list(shape), dt))

    def psum(self, name, shape, dt):
        return self.st.enter_context(self.nc.psum_tensor("pp_" + name, list(shape), dt))

    def dbuf(self, name):
        self.nsem += 1
        s = self.st.enter_context(self.nc.semaphore("d_" + name))
        return Buf(name, sem=s)

    def mm(self, out, lhsT, rhs, start, stop, reads, writes):
        return self.P.add("pe", lambda e: e.matmul(out, lhsT, rhs, start=start, stop=stop), reads, writes)

    def tr(self, out, in_, ident, reads, writes):
        return self.P.add("pe", lambda e: e.transpose(out, in_, ident), reads, writes)

    def act(self, out, in_, func, reads, writes, bias=None, scale=None, accum_out=None, eng="act"):
        kw = {}
        if bias is not None:
            kw["bias"] = bias
        if scale is not None:
            kw["scale"] = scale
        if accum_out is not None:
            kw["accum_out"] = accum_out
        return self.P.add("act", lambda e: e.activation(out, in_, func, **kw), reads, writes)

    def tt(self, eng, out, in0, in1, op, reads, writes):
        return self.P.add(eng, lambda e: e.tensor_tensor(out, in0, in1, op), reads, writes)

    def ts(self, eng, out, in0, s1, s2, op0, op1, reads, writes):
        if s2 is None:
            return self.P.add(eng, lambda e: e.tensor_scalar(out, in0, s1, None, op0), reads, writes)
        return self.P.add(eng, lambda e: e.tensor_scalar(out, in0, s1, s2, op0, op1), reads, writes)

    def stt(self, eng, out, in0, scalar, in1, op0, op1, reads, writes):
        return self.P.add(eng, lambda e: e.scalar_tensor_tensor(out, in0, scalar, in1, op0, op1), reads, writes)

    def copy(self, eng, out, in_, reads, writes):
        if eng == "act":
            return self.P.add("act", lambda e: e.copy(out, in_), reads, writes)
        return self.P.add(eng, lambda e: e.tensor_copy(out, in_), reads, writes)

    def memset(self, eng, ap, val, writes):
        return self.P.add(eng, lambda e: e.memset(ap, val), (), writes)

    def dma(self, eng, out, in_, reads, writes, dma_buf, indep=False, **kw):
        return self.P.add(eng, lambda e: e.dma_start(out, in_, **kw), reads, writes, dma_buf=dma_buf, indep=indep)


D = 1024
KC = 8
S = 4096
T = 512
NT = S // T
FF = 2816
FC = 22
EPS = 1e-6
NRING = 3
HEAD_OF_SLOT = []
for _c in range(8):
    HEAD_OF_SLOT.append([0, 1, 2, 3, 8, 9, 10, 11][_c])
    HEAD_OF_SLOT.append([4, 5, 6, 7, 12, 13, 14, 15][_c])
NEG_MASK = -30000.0

WS_IN = 0
WS_OUT0 = 16
WS_QK = 24
WS_OUT1 = 34
NWS = 42


class WPool:
    def __init__(self, C, name, nslots, shape):
        self.C = C
        self.n = nslots
        self.t = [C.sbuf("%s%d" % (name, i), shape, BF16) for i in range(nslots)]
        self.b = [C.dbuf("%s%d" % (name, i)) for i in range(nslots)]
        self.sb = [C.dbuf("%s%dst" % (name, i)) for i in range(nslots)]
        self.req = []
        self.chunk = {}
        self.emitted = 0
        self.cons = 0

    def _top(self, upto):
        C = self.C
        upto = min(upto, len(self.req))
        while self.emitted < upto:
            i = self.emitted
            key, src32, scr = self.req[i]
            k = i % self.n
            if key not in self.chunk:
                cb = Buf("chunk")
                self.chunk[key] = cb
                C.dma("pool", self.t[k][:], src32, (), (self.b[k],), self.b[k])
                C.dma("sp", scr, self.t[k][:], (self.b[k],), (cb,), self.sb[k])
            else:
                C.dma("sp", self.t[k][:], scr, (self.chunk[key],), (self.b[k],), self.b[k])
            self.emitted += 1

    def prefetch(self):
        self._top(self.cons + self.n)

    def next(self):
        i = self.cons
        assert i < len(self.req), "weight pool underflow"
        self._top(i + self.n)
        self.cons += 1
        return self.t[i % self.n], self.b[i % self.n]


def build_program(debug=False, NT=NT, stages=3):
    nc = bass.Bass("TRN2", target_bir_lowering=False)
    dt_in = lambda name, shape: nc.dram_tensor(name, list(shape), F32, kind="ExternalInput").ap()
    x_d = dt_in("x", [S, D])
    ws_d = dt_in("ws", [NWS, 128, 1024])
    gu_d = dt_in("gu", [2 * FC, 2, 128, 1024])
    wd_d = dt_in("wd", [16, 128, FC, 128])
    wv0_d = dt_in("wv0", [128, 8, 512])
    wv1_d = dt_in("wv1", [128, 8, 256])
    gall_d = dt_in("gall", [128, 4, 8])
    gfin_d = dt_in("gfin", [128, 1024])
    lng_d = dt_in("lng", [128, 512])
    lnb_d = dt_in("lnb", [128, 512])
    bsp_d = dt_in("bsp", [128, 4, 128])
    wsp_d = dt_in("wsp", [128, 4, 128])
    cw_d = dt_in("cw", [128, 3, 4])
    bias_d = dt_in("biasT", [128, 16, 384])
    sink_d = dt_in("sinkb", [128, 16])
    id_d = dt_in("ident", [128, 128])
    out_d = nc.dram_tensor("out", [S, D], F32, kind="ExternalOutput").ap()
    if debug:
        dbg_d = nc.dram_tensor("dbg", [5, NT, 128, 8, 512], F32, kind="ExternalOutput").ap()
    ws_s = nc.dram_tensor("ws_s", [NWS, 128, 1024], BF16).ap()
    gu_s = nc.dram_tensor("gu_s", [2 * FC, 2, 128, 1024], BF16).ap()
    wd_s = nc.dram_tensor("wd_s", [16, 128, FC, 128], BF16).ap()
    wv0_s = nc.dram_tensor("wv0_s", [128, 8, 512], BF16).ap()
    wv1_s = nc.dram_tensor("wv1_s", [128, 8, 256], BF16).ap()

    with ExitStack() as st:
        C = Ctx(nc, st)
        P = C.P
        xr = C.sbuf("xr", [128, NRING, KC, T], F32)
        xr_b = [[Buf("xr%d_%d" % (r, c)) for c in range(KC)] for r in range(NRING)]
        hT = C.sbuf("hT", [128, KC, T + 2], BF16)
        hT_b = [Buf("hT%d" % c) for c in range(KC)]
        hTx_b = Buf("hTx")
        arena = C.sbuf("arena", [128, 32 * 256], F32)
        pg = [Buf("pg%d" % i) for i in range(32)]

        def av_bf(p0, np_):
            return arena[:, p0 * 256:(p0 + np_) * 256].bitcast(BF16)

        def av_f32(p0, np_):
            return arena[:, p0 * 256:(p0 + np_) * 256]

        kring = C.sbuf("kring", [128, 2, 4 * T], BF16)
        kr_b = [[Buf("k%d_%d" % (c, b)) for b in range(16)] for c in range(2)]
        vring = C.sbuf("vring", [128, 16, 4, 65], BF16)
        vr_b = [Buf("v%d" % b) for b in range(16)]
        qT = C.sbuf("qT", [128, KC, T], BF16)
        qT_b = [Buf("qT%d" % c) for c in range(KC)]
        biasT = C.sbuf("biasTs", [128, 16, 384], BF16)
        biasT_b = C.dbuf("biasT")
        xstage2 = [C.sbuf("xstage%d" % i, [128, D], F32) for i in range(2)]
        xstage2_b = [C.dbuf("xstage%d" % i) for i in range(2)]
        ostage = C.sbuf("ostage", [128, D], F32)
        ostage_b = Buf("ostage")
        ostore_b = C.dbuf("ostore")
        sq = [C.sbuf("sq%d" % i, [128, T], BF16) for i in range(3)]
        sq_b = [Buf("sq%d" % i) for i in range(3)]
        sqx = C.sbuf("sqx", [128, 8], BF16)
        sqx_b = Buf("sqx")
        tbuf = C.sbuf("tbuf", [128, T], F32)
        tbuf_b = Buf("tbuf")
        rsb = C.sbuf("rsb", [128, T], F32)
        rsb_b = Buf("rsb")
        small = C.sbuf("small", [128, 256], F32)
        zext = [C.sbuf("zext%d" % i, [128, T + 4], F32) for i in range(2)]
        zext_b = [Buf("zext%d" % i) for i in range(2)]
        zprev = C.sbuf("zprev", [128, 4], F32)
        zprev_b = [Buf("zprev%d" % j) for j in range(4)]
        ident_f = C.sbuf("ident_f", [128, 128], F32)
        ident_fb = C.dbuf("ident_f")
        ident_b = C.sbuf("ident_b", [128, 128], BF16)
        ident_bb = Buf("ident_b")
        ones_b = C.sbuf("ones_b", [128, 128], BF16)
        ones_bb = Buf("ones_b")
        epsc = C.sbuf("epsc", [128, 1], F32)
        eps_b = Buf("epsc")
        neghalf = C.sbuf("neghalf", [128, 8], F32)
        neghalf_b = Buf("neghalf")
        gfin = C.sbuf("gfin", [128, D], F32)
        lng = C.sbuf("lng", [128, 512], F32)
        lnb = C.sbuf("lnb", [128, 512], F32)
        bsp = C.sbuf("bsp", [128, 4, 128], F32)
        wsp_f = C.sbuf("wsp_f", [128, 4, 128], F32)
        wsp_b = C.sbuf("wsp_b", [128, 4, 128], BF16)
        wsp_bb = Buf("wsp_b")
        gall = C.sbuf("gall", [128, 4, 8], F32)
        cw = C.sbuf("cw", [128, 3, 4], F32)
        sinkb = C.sbuf("sinkb", [128, 16], F32)
        negc = C.sbuf("negc", [128, 16], F32)
        sinkterm = C.sbuf("sinkterm", [128, 16], F32)
        att_b = Buf("attconst")
        cst_b = C.dbuf("consts")
        junk = C.sbuf("junk", [128, T], BF16)
        junk_b = Buf("junk")

        _sm = [0]

        def sm(n):
            a = small[:, _sm[0]:_sm[0] + n]
            _sm[0] += n
            assert _sm[0] <= 256
            return a

        psb = [C.psum("ps%d" % i, [128, 512], F32) for i in range(8)]
        ps_b = [Buf("ps%d" % i) for i in range(8)]
        _pr = [0]

        def next_ps():
            k = _pr[0] % 5
            _pr[0] += 1
            return psb[k], ps_b[k]

        acc_t = psb[5:8]
        acc_b = ps_b[5:8]
        xps_b = Buf("xps")

        C.dma("pool", biasT[:], bias_d, (), (biasT_b,), biasT_b)

        for (dst, src) in ((ident_f, id_d), (gfin, gfin_d), (lng, lng_d), (lnb, lnb_d), (bsp, bsp_d),
                           (wsp_f, wsp_d), (gall, gall_d), (cw, cw_d), (sinkb, sink_d)):
            C.dma("sp", dst[:], src, (), (cst_b,), cst_b, indep=True)
        C.copy("dve", ident_b[:], ident_f[:], (cst_b,), (ident_bb,))
        C.memset("pool", ones_b[:], 1.0, (ones_bb,))
        C.memset("pool", neghalf[:], -0.5, (neghalf_b,))
        C.memset("pool", epsc[:], EPS, (eps_b,))
        C.copy("dve", wsp_b[:], wsp_f[:], (cst_b,), (wsp_bb,))
        C.memset("pool", vring[:], 1.0, vr_b)
        C.ts("dve", negc[:], sinkb[:], 0.0, -1.0, ALU.max, ALU.mult, (cst_b,), (att_b,))
        C.tt("dve", sinkterm[:], sinkb[:], negc[:], ALU.add, (cst_b, att_b), (att_b,))
        C.act(sinkterm[:], sinkterm[:], AF.Exp, (att_b,), (att_b,))

        proj = WPool(C, "wproj", 4, [128, KC, 128])
        gup = WPool(C, "wgu", 3, [128, 2, KC, 128])
        dnp = WPool(C, "wdn", 2, [128, 2, 11 * 128])
        wv0p = WPool(C, "wv0p", 1, [128, KC, 512])
        wv1p = WPool(C, "wv1p", 1, [128, KC, 256])

        def ws_req(idx):
            return (("ws", idx), ws_d[idx].rearrange("p (k j) -> p k j", k=KC), ws_s[idx].rearrange("p (k j) -> p k j", k=KC))

        def ffn_reqs(layer):
            for fc in range(FC):
                i = layer * FC + fc
                gup.req.append((("gu", i), gu_d[i].rearrange("t p (k j) -> p t k j", k=KC),
                                gu_s[i].rearrange("t p (k j) -> p t k j", k=KC)))
            for dc in range(8):
                i = layer * 8 + dc
                dnp.req.append((("wd", i), wd_d[i].rearrange("p (a b) j -> p a (b j)", a=2),
                                wd_s[i].rearrange("p (a b) j -> p a (b j)", a=2)))

        for s in range(NT + 1):
            if s < NT:
                wv0p.req.append((("wv0", 0), wv0_d, wv0_s))
                for j in range(4):
                    for t3 in range(3):
                        proj.req.append(ws_req(WS_IN + 4 + 4 * t3 + j))
                for g in range(4):
                    proj.req.append(ws_req(WS_IN + g))
                for dc in range(8):
                    proj.req.append(ws_req(WS_OUT0 + dc))
                ffn_reqs(0)
                if stages >= 2:
                    wv1p.req.append((("wv1", 0), wv1_d, wv1_s))
                    for k2 in range(2):
                        proj.req.append(ws_req(WS_QK + 8 + k2))
            if s >= 1 and stages >= 2:
                for dc in range(8):
                    proj.req.append(ws_req(WS_OUT1 + dc))
                ffn_reqs(1)
            if s < NT and stages >= 2:
                for c in range(8):
                    proj.req.append(ws_req(WS_QK + c))

        def proj_fm(w_t, w_b, rhs_of_kc, rhs_bufs, n, nk=KC, ps=None):
            if ps is None:
                ps = next_ps()
            pt, pb = ps
            for kc in range(nk):
                C.mm(pt[:, 0:n], w_t[:, kc, :], rhs_of_kc(kc), kc == 0, kc == nk - 1,
                     (w_b, rhs_bufs[kc]), (pb,))
            return pt, pb

        def resid_add(slot, dc, pt, pb):
            xa = xr[:, slot, dc, :]
            C.tt("dve", xa, xa, pt[:, 0:T], ALU.add, (pb,), (xr_b[slot][dc],))

        def norm(slot, gi, extra_slot=None, filler=None):
            pt, pb = next_ps()
            for c in range(KC):
                k = c % 3
                xa = xr[:, slot, c, :]
                if c % 2 == 0:
                    C.act(sq[k][:], xa, AF.Square, (xr_b[slot][c],), (sq_b[k],))
                else:
                    C.tt("dve", sq[k][:], xa, xa, ALU.mult, (xr_b[slot][c],), (sq_b[k],))
                C.mm(pt[:, 0:T], ones_b[:], sq[k][:], c == 0, c == KC - 1, (ones_bb, sq_b[k]), (pb,))
            C.act(tbuf[:], pt[:, 0:T], AF.Ln, (pb, eps_b), (tbuf_b,), bias=epsc[:, 0:1], scale=1.0 / D)
            C.act(rsb[:], tbuf[:], AF.Exp, (tbuf_b,), (rsb_b,), scale=-0.5)
            for c in range(KC):
                C.stt("dve", hT[:, c, 0:T], xr[:, slot, c, :], gall[:, gi, c:c + 1], rsb[:], ALU.mult, ALU.mult,
                      (xr_b[slot][c], rsb_b, cst_b), (hT_b[c],))
            if filler is not None:
                filler()
            if extra_slot is not None:
                xcol = xr[:, extra_slot, :, 0]
                xb = xr_b[extra_slot]
                C.tt("dve", sqx[:], xcol, xcol, ALU.mult, xb, (sqx_b,))
                pt2, pb2 = next_ps()
                for c in range(KC):
                    C.mm(pt2[:, 0:1], ones_b[:], sqx[:, c:c + 1], c == 0, c == KC - 1, (ones_bb, sqx_b), (pb2,))
                t1 = sm_t1
                C.act(t1, pt2[:, 0:1], AF.Ln, (pb2, eps_b), (smx_b,), bias=epsc[:, 0:1], scale=1.0 / D)
                C.act(sm_rs1, t1, AF.Exp, (smx_b,), (smx_b,), scale=-0.5)
                C.tt("dve", sm_x8, xcol, gall[:, gi, :], ALU.mult, tuple(xb) + (cst_b,), (smx_b,))
                C.ts("dve", hT[:, :, T], sm_x8, sm_rs1, None, ALU.mult, None, (smx_b,), (hTx_b,))

        sm_t1 = sm(1)
        sm_rs1 = sm(1)
        sm_x8 = sm(8)
        smx_b = Buf("smx")

        def load_tile(t):
            slot = t % NRING
            for tb in range(4):
                r0 = t * T + tb * 128
                xstage = xstage2[tb % 2]
                xstage_b = xstage2_b[tb % 2]
                C.dma("sp", xstage[:], x_d[r0:r0 + 128, :], (), (xstage_b,), xstage_b)
                for half in range(2):
                    pt, pb = next_ps()
                    for c4 in range(4):
                        c = half * 4 + c4
                        C.tr(pt[:, c4 * 128:(c4 + 1) * 128], xstage[:, c * 128:(c + 1) * 128], ident_f[:],
                             (xstage_b, ident_fb), (pb,))
                    dst = xr[:, slot, half * 4:half * 4 + 4, tb * 128:(tb + 1) * 128]
                    C.copy("act", dst, pt[:, 0:512].rearrange("p (c n) -> p c n", c=4), (pb,),
                           tuple(xr_b[slot][half * 4:half * 4 + 4]))

        def tmp_slot(i):
            p0 = 18 + 2 * (i % 6)
            return av_f32(p0, 2), (pg[p0], pg[p0 + 1])

        _tmpi = [0]

        def next_tmp():
            r = tmp_slot(_tmpi[0])
            _tmpi[0] += 1
            return r

        ln_st = [sm(6) for _ in range(4)]
        ln_mv = [sm(2) for _ in range(4)]
        ln_rt = [sm(1) for _ in range(4)]
        ln_rs = [sm(1) for _ in range(4)]
        ln_b = [Buf("ln%d" % i) for i in range(4)]
        xc_sb = sm(4)
        xc_b = [Buf("xc%d" % j) for j in range(4)]

        def mixer0(s):
            slot = s % NRING
            nslot = (s + 1) % NRING if s + 1 < NT else None
            norm(slot, 0, nslot, filler=(lambda: load_tile(s + 1)) if s + 1 < NT else None)
            gup.prefetch()
            hb = hT_b
            wv_t, wv_b = wv0p.next()
            for tb in range(4):
                pt, pb = next_ps()
                for kc in range(KC):
                    C.mm(pt[:, 0:512], hT[:, kc, tb * 128:(tb + 1) * 128], wv_t[:, kc, :], kc == 0, kc == KC - 1,
                         (hb[kc], wv_b), (pb,))
                gv, gvb = next_tmp()
                C.act(gv, pt[:, 0:512], AF.Gelu, (pb,), gvb)
                C.P.add("dve", lambda e, o=ln_st[tb], i=gv: e.bn_stats(o, i), gvb, (ln_b[tb],))
                C.P.add("dve", lambda e, o=ln_mv[tb], i=ln_st[tb]: e.bn_aggr(o, i), (ln_b[tb],), (ln_b[tb],))
                C.ts("dve", ln_rt[tb], ln_mv[tb][:, 1:2], EPS, None, ALU.add, None, (ln_b[tb],), (ln_b[tb],))
                C.tt("pool", ln_rs[tb], ln_rt[tb], neghalf[:, 0:1], ALU.pow, (ln_b[tb], neghalf_b), (ln_b[tb],))
                C.ts("dve", gv, gv, ln_mv[tb][:, 0:1], ln_rs[tb], ALU.subtract, ALU.mult, (ln_b[tb],), gvb)
                C.tt("pool", gv, gv, lng[:], ALU.mult, (cst_b,), gvb)
                vt = av_bf(8 + tb, 1)
                C.tt("pool", vt, gv, lnb[:], ALU.add, gvb + (cst_b,), (pg[8 + tb],))
            for j in range(4):
                z = zext[j % 2]
                zb = zext_b[j % 2]
                bb = av_f32(14 + 2 * (j % 2), 2)
                bbb = (pg[14 + 2 * (j % 2)], pg[15 + 2 * (j % 2)])
                xp = acc_t[2]

                def halo(col, wt_, wb_):
                    if nslot is None:
                        return
                    for kc in range(KC):
                        C.mm(xp[:, col:col + 1], wt_[:, kc, :], hT[:, kc, T:T + 1], kc == 0, kc == KC - 1,
                             (wb_, hTx_b), (xps_b,))

                wb_t, wb_b = proj.next()
                pt, pb = proj_fm(wb_t, wb_b, lambda kc: hT[:, kc, 0:T], hb, T)
                C.copy("act", bb, pt[:, 0:T], (pb,), bbb)
                wc_t, wc_b = proj.next()
                ptc, pbc = proj_fm(wc_t, wc_b, lambda kc: hT[:, kc, 0:T], hb, T)
                halo(496 + 2 * j, wc_t, wc_b)
                tc_, tcb = next_tmp()
                C.copy("act", tc_, ptc[:, 0:T], (pbc,), tcb)
                wh_t, wh_b = proj.next()
                pth, pbh = proj_fm(wh_t, wh_b, lambda kc: hT[:, kc, 0:T], hb, T)
                halo(497 + 2 * j, wh_t, wh_b)
                C.tt("dve", z[:, 1:T + 1], tc_, pth[:, 0:T], ALU.mult, tcb + (pbh,), (zb,))
                if s > 0:
                    C.copy("pool", z[:, 0:1], zprev[:, j:j + 1], (zprev_b[j],), (zb,))
                else:
                    C.memset("pool", z[:, 0:1], 0.0, (zb,))
                if nslot is not None:
                    C.copy("act", xc_sb[:, j:j + 1], xp[:, 496 + 2 * j:497 + 2 * j], (xps_b,), (xc_b[j],))
                    C.tt("dve", z[:, T + 1:T + 2], xc_sb[:, j:j + 1], xp[:, 497 + 2 * j:498 + 2 * j], ALU.mult,
                         (xc_b[j], xps_b), (zb,))
                else:
                    C.memset("pool", z[:, T + 1:T + 2], 0.0, (zb,))
                C.copy("pool", zprev[:, j:j + 1], z[:, T:T + 1], (zb,), (zprev_b[j],))
                ct, ctb = next_tmp()
                C.ts("dve", ct, z[:, 0:T], cw[:, 0, j:j + 1], None, ALU.mult, None, (zb, cst_b), ctb)
                C.stt("dve", ct, z[:, 1:T + 1], cw[:, 1, j:j + 1], ct, ALU.mult, ALU.add, (zb, cst_b), ctb)
                C.stt("dve", ct, z[:, 2:T + 2], cw[:, 2, j:j + 1], ct, ALU.mult, ALU.add, (zb, cst_b), ctb)
                C.tt("pool", av_bf(4 + j, 1), ct, bb, ALU.mult, ctb + bbb, (pg[4 + j],))
            for g in range(4):
                w_t, w_b = proj.next()
                pt, pb = proj_fm(w_t, w_b, lambda kc: hT[:, kc, 0:T], hb, T)
                au = av_bf(12 + g % 2, 1)
                aub = pg[12 + g % 2]
                C.act(au, pt[:, 0:T], AF.Gelu, (pb,), (aub,))
                pt2, pb2 = next_ps()
                for tb in range(4):
                    vt = av_bf(8 + tb, 1)
                    C.mm(pt2[:, tb * 128:(tb + 1) * 128], vt[:, g * 128:(g + 1) * 128], wsp_b[:, g, :], True, True,
                         (pg[8 + tb], wsp_bb), (pb2,))
                t1, t1b = next_tmp()
                C.tt("dve", t1.rearrange("p (t q) -> p t q", t=4), pt2[:, 0:512].rearrange("p (t q) -> p t q", t=4),
                     bsp[:, g:g + 1, :].broadcast_to([128, 4, 128]), ALU.add, (pb2, cst_b), t1b)
                C.tt("pool", av_bf(g, 1), t1, au, ALU.mult, t1b + (aub,), (pg[g],))
            for dc in range(8):
                w_t, w_b = proj.next()
                pt, pb = proj_fm(w_t, w_b, lambda kc: av_bf(kc, 1), pg[0:8], T)
                resid_add(slot, dc, pt, pb)

        def ffn(t, layer):
            slot = t % NRING
            norm(slot, 1 + 2 * layer)
            dnp.prefetch()
            for fc in range(FC):
                gu_t, gu_b = gup.next()
                ptg, pbg = next_ps()
                ptu, pbu = next_ps()
                for kc in range(KC):
                    C.mm(ptg[:, 0:T], gu_t[:, 0, kc, :], hT[:, kc, 0:T], kc == 0, kc == KC - 1, (gu_b, hT_b[kc]), (pbg,))
                for kc in range(KC):
                    C.mm(ptu[:, 0:T], gu_t[:, 1, kc, :], hT[:, kc, 0:T], kc == 0, kc == KC - 1, (gu_b, hT_b[kc]), (pbu,))
                p0 = 22 + 2 * (fc % 2)
                sg = av_f32(p0, 2)
                sgb = (pg[p0], pg[p0 + 1])
                C.act(sg, ptg[:, 0:T], AF.Silu, (pbg,), sgb)
                C.tt("dve", av_bf(fc, 1), sg, ptu[:, 0:T], ALU.mult, sgb + (pbu,), (pg[fc],))
            proj.prefetch()
            for dc in range(8):
                wd_t, wd_b = dnp.next()
                pt, pb = next_ps()
                for fc in range(FC):
                    C.mm(pt[:, 0:T], wd_t[:, fc // 11, (fc % 11) * 128:(fc % 11 + 1) * 128], av_bf(fc, 1), fc == 0, fc == FC - 1, (wd_b, pg[fc]), (pb,))
                resid_add(slot, dc, pt, pb)

        def l1_kv(s):
            slot = s % NRING
            norm(slot, 2)
            wv_t, wv_b = wv1p.next()
            rb = (s % 4) * 4
            for k2 in range(2):
                w_t, w_b = proj.next()
                pt, pb = proj_fm(w_t, w_b, lambda kc: hT[:, kc, 0:T], hT_b, T)
                C.copy("act", kring[:, k2, rb * 128:rb * 128 + T], pt[:, 0:T], (pb,), tuple(kr_b[k2][rb:rb + 4]))
            for tb in range(4):
                pt, pb = next_ps()
                for kc in range(KC):
                    C.mm(pt[:, 0:256], hT[:, kc, tb * 128:(tb + 1) * 128], wv_t[:, kc, :], kc == 0, kc == KC - 1,
                         (hT_b[kc], wv_b), (pb,))
                C.copy("dve", vring[:, rb + tb, :, 0:64], pt[:, 0:256].rearrange("p (g d) -> p g d", g=4), (pb,),
                       (vr_b[rb + tb],))

        def l1_q(s):
            slot = s % NRING
            norm(slot, 2, filler=(lambda: final(s - 1)) if s >= 1 else None)
            for c in range(8):
                w_t, w_b = proj.next()
                pt, pb = proj_fm(w_t, w_b, lambda kc: hT[:, kc, 0:T], hT_b, T)
                C.act(qT[:, c, :], pt[:, 0:T], AF.Copy, (pb,), (qT_b[c],), scale=0.125)

        den = sm(16)
        rden = sm(16)
        den_b = Buf("den")

        def attention(t):
            slot = t % NRING
            proj.prefetch()
            gup.prefetch()
            for qb in range(4):
                gb = 4 * t + qb
                js = [j for j in range(3) if 0 <= gb - 1 + j < 32]
                nj = len(js)
                j0 = js[0]
                ao = av_bf(8 + 2 * (qb % 2), 2)
                aob = (pg[8 + 2 * (qb % 2)], pg[9 + 2 * (qb % 2)])
                def emit_st(hs):
                    c = hs // 2
                    half = hs % 2
                    g = HEAD_OF_SLOT[hs] // 4
                    assert g % 2 == half
                    k2 = g // 2
                    st_t, st_b = next_ps()
                    C.mm(st_t[:, 0:nj * 128], ident_b[:], biasT[:, hs, j0 * 128:(j0 + nj) * 128], True, False,
                         (ident_bb, biasT_b), (st_b,))
                    for idx, j in enumerate(js):
                        kb = (gb - 1 + j) % 16
                        C.mm(st_t[:, idx * 128:(idx + 1) * 128],
                             kring[half * 64:(half + 1) * 64, k2, kb * 128:(kb + 1) * 128],
                             qT[half * 64:(half + 1) * 64, c, qb * 128:(qb + 1) * 128],
                             False, idx == nj - 1, (kr_b[k2][kb], qT_b[c]), (st_b,))
                    ptile = av_bf(12 + hs % 3, 1)
                    ptb = pg[12 + hs % 3]
                    C.act(ptile[:, 0:nj * 128], st_t[:, 0:nj * 128], AF.Exp, (st_b, att_b), (ptb,), bias=negc[:, hs:hs + 1],
                          scale=1.0)

                def emit_pv(hs):
                    g = HEAD_OF_SLOT[hs] // 4
                    ptile = av_bf(12 + hs % 3, 1)
                    ptb = pg[12 + hs % 3]
                    bank = hs // 7
                    col = (hs % 7) * 65
                    for idx, j in enumerate(js):
                        kb = (gb - 1 + j) % 16
                        C.mm(acc_t[bank][:, col:col + 65], ptile[:, idx * 128:(idx + 1) * 128], vring[:, kb, g, :],
                             idx == 0, idx == nj - 1, (ptb, vr_b[kb]), (acc_b[bank],))

                emit_st(0)
                for hs in range(1, 16):
                    emit_st(hs)
                    emit_pv(hs - 1)
                emit_pv(15)
                for bank in range(3):
                    h0 = bank * 7
                    h1 = min(16, h0 + 7)
                    nh = h1 - h0
                    a3 = acc_t[bank][:, 0:nh * 65].rearrange("p (h e) -> p h e", e=65)
                    C.tt("dve", den[:, h0:h1], a3[:, :, 64], sinkterm[:, h0:h1], ALU.add, (acc_b[bank], att_b), (den_b,))
                    C.P.add("dve", lambda e, o=rden[:, h0:h1], i=den[:, h0:h1]: e.reciprocal(o, i), (den_b,), (den_b,))
                    C.tt("dve", ao[:, h0 * 64:h1 * 64].rearrange("p (h d) -> p h d", d=64), a3[:, :, 0:64],
                         rden[:, h0:h1].unsqueeze(2).broadcast_to([128, nh, 64]), ALU.mult, (acc_b[bank], den_b), aob)
                tp, tpb = next_ps()
                tpv = tp[:, 0:512].bitcast(BF16)
                for c in range(8):
                    C.tr(tpv[:, c * 128:(c + 1) * 128], ao[:, c * 128:(c + 1) * 128], ident_b[:], aob + (ident_bb,), (tpb,))
                dst = av_bf(0, 8).rearrange("p (c n) -> p c n", c=8)[:, :, qb * 128:(qb + 1) * 128]
                C.copy("act", dst, tpv.rearrange("p (c n) -> p c n", c=8), (tpb,), tuple(pg[0:8]))
            for dc in range(8):
                w_t, w_b = proj.next()
                pt, pb = proj_fm(w_t, w_b, lambda kc: av_bf(kc, 1), pg[0:8], T)
                resid_add(slot, dc, pt, pb)

        fss = [sm(1), sm(1)]
        fs_t = sm(1)
        fs_r = sm(1)
        fs_b = Buf("fs")

        def final(t):
            slot = t % NRING
            for tb in range(4):
                pts = [next_ps(), next_ps()]
                for c in range(8):
                    pt, pb = pts[c // 4]
                    C.tr(pt[:, (c % 4) * 128:(c % 4 + 1) * 128], xr[:, slot, c, tb * 128:(tb + 1) * 128], ident_f[:],
                         (xr_b[slot][c], ident_fb), (pb,))
                for h2 in range(2):
                    pt, pb = pts[h2]
                    C.act(junk[:], pt[:, 0:512], AF.Square, (pb,), (junk_b, fs_b), accum_out=fss[h2])
                C.tt("dve", fs_t, fss[0], fss[1], ALU.add, (fs_b,), (fs_b,))
                C.ts("dve", fs_t, fs_t, 1.0 / D, EPS, ALU.mult, ALU.add, (fs_b,), (fs_b,))
                C.tt("pool", fs_r, fs_t, neghalf[:, 0:1], ALU.pow, (fs_b, neghalf_b), (fs_b,))
                for h2 in range(2):
                    pt, pb = pts[h2]
                    C.stt("dve", ostage[:, h2 * 512:(h2 + 1) * 512], pt[:, 0:512], fs_r, gfin[:, h2 * 512:(h2 + 1) * 512],
                          ALU.mult, ALU.mult, (pb, fs_b, cst_b, ostore_b), (ostage_b,))
                r0 = t * T + tb * 128
                C.dma("sp", out_d[r0:r0 + 128, :], ostage[:], (ostage_b,), (ostore_b,), ostore_b)

        def dump(k, t):
            if not debug:
                return
            slot = t % NRING
            b = C.dbuf("dbg%d_%d" % (k, t))
            C.dma("sp", dbg_d[k, t], xr[:, slot], tuple(xr_b[slot]), (), b)
            P.final_waits.append(b)

        load_tile(0)
        for s in range(NT + 1):
            if s < NT:
                dump(0, s)
                mixer0(s)
                dump(1, s)
                ffn(s, 0)
                dump(2, s)
                if stages >= 2:
                    l1_kv(s)
            if s >= 1 and stages >= 2:
                attention(s - 1)
                dump(3, s - 1)
                ffn(s - 1, 1)
                dump(4, s - 1)
                if s == NT:
                    final(s - 1)
            if s < NT and stages >= 2:
                l1_q(s)

        fw = [(ostore_b.sem, ostore_b.semcnt)]
        for b in P.final_waits:
            fw.append((b.sem, b.semcnt))
        P.final_waits = fw
        print("SBUF bytes remaining per partition:", nc.sbuf_bytes_remaining, "ops:", {e: len(P.ops[e]) for e in ENGS})
        with nc.Block() as block:
            P.emit_all(nc, block, C.engsem)
    return nc


def _t5_bucket_table():
    nb = 16
    max_exact = 8
    rel = np.arange(-255, 256)
    ret = np.where(rel > 0, nb, 0)
    n = np.abs(rel)
    nf = np.maximum(n, 1).astype(np.float32)
    large = max_exact + (np.log(nf / max_exact) / np.log(128 / max_exact) * (nb - max_exact)).astype(np.int32)
    large = np.minimum(large, nb - 1)
    return ret + np.where(n < max_exact, n, large)


def _chunks_kmajor(W):
    K, E = W.shape
    a = W.reshape(K // 128, 128, E // 128, 128)
    return np.ascontiguousarray(a.transpose(2, 1, 0, 3)).reshape(E // 128, 128, (K // 128) * 128)


def prep_shared(inp):
    f = lambda a: np.ascontiguousarray(np.asarray(a, dtype=np.float32))
    w_in = f(inp["even_w_in"])[0]
    cols = np.concatenate([np.arange(0, 512), np.arange(1024, 2560)])
    ws_in = _chunks_kmajor(w_in[:, cols])
    ws_out0 = _chunks_kmajor(f(inp["even_w_out"])[0])
    wqkv = f(inp["attn_w_qkv"])[0]
    qcols = np.concatenate([np.arange(h * 64, h * 64 + 64) for h in HEAD_OF_SLOT])
    ws_q = _chunks_kmajor(wqkv[:, qcols])
    ws_k = _chunks_kmajor(wqkv[:, 1024:1280])
    wo1 = f(inp["attn_w_out"])[0][qcols, :]
    ws_out1 = _chunks_kmajor(wo1)
    ws = np.concatenate([ws_in, ws_out0, ws_q, ws_k, ws_out1], axis=0)
    assert ws.shape == (NWS, 128, 1024)
    gate = f(inp["ffn_w_gate"])
    up = f(inp["ffn_w_up"])
    down = f(inp["ffn_w_down"])
    gu = np.stack([np.stack([_chunks_kmajor(gate[l]), _chunks_kmajor(up[l])], axis=1) for l in range(2)], axis=0)
    gu = np.ascontiguousarray(gu.reshape(2 * FC, 2, 128, 1024))
    wd = np.stack([_chunks_kmajor(down[l]) for l in range(2)], axis=0).reshape(16, 128, FC, 128)
    wv0 = np.ascontiguousarray(w_in[:, 512:1024].reshape(8, 128, 512).transpose(1, 0, 2))
    wv1 = np.ascontiguousarray(wqkv[:, 1280:1536].reshape(8, 128, 256).transpose(1, 0, 2))
    nm = f(inp["norm_mix"])
    nf_ = f(inp["norm_ffn"])
    gl = np.stack([nm[0], nf_[0], nm[1], nf_[1]], axis=0)
    gall = np.ascontiguousarray(gl.reshape(4, 8, 128).transpose(2, 0, 1))
    rep = lambda v: np.ascontiguousarray(np.broadcast_to(v, (128,) + v.shape))
    gfin = rep(f(inp["final_norm"]))
    lng = rep(f(inp["even_v_ln_g"])[0])
    lnb = rep(f(inp["even_v_ln_b"])[0])
    bsp = rep(f(inp["even_b_spatial"])[0])
    wsp = np.ascontiguousarray(f(inp["even_w_spatial"])[0].transpose(2, 0, 1))
    cw = np.ascontiguousarray(f(inp["even_conv_w"])[0].reshape(3, 4, 128).transpose(2, 0, 1))
    tab = _t5_bucket_table()
    k = np.arange(128)[:, None, None]
    j = np.arange(3)[None, :, None]
    q = np.arange(128)[None, None, :]
    rel = (j - 1) * 128 + k - q
    bucket = tab[rel + 255]
    rb = f(inp["rel_bias"])[:, HEAD_OF_SLOT]
    bias = rb[bucket]
    band = (np.abs(rel) <= 128)[..., None]
    bias = np.where(band, bias, np.float32(NEG_MASK)).astype(np.float32)
    biasT = np.ascontiguousarray(bias.transpose(0, 3, 1, 2)).reshape(128, 16, 384)
    sinkb = rep(f(inp["attn_sink"])[0][HEAD_OF_SLOT])
    return {
        "ws": ws, "gu": gu, "wd": np.ascontiguousarray(wd), "wv0": wv0, "wv1": wv1, "gall": gall, "gfin": gfin,
        "lng": lng, "lnb": lnb, "bsp": bsp, "wsp": wsp, "cw": cw, "biasT": biasT, "sinkb": sinkb,
        "ident": np.eye(128, dtype=np.float32),
    }


_NC_CACHE = {}


def kernel(**inputs):
    x = np.ascontiguousarray(np.asarray(inputs["x"], dtype=np.float32))
    shared = prep_shared(inputs)
    if "nc" not in _NC_CACHE:
        _NC_CACHE["nc"] = build_program(False)
    nc = _NC_CACHE["nc"]
    in_maps = []
    for b in range(8):
        m = dict(shared)
        m["x"] = x[b]
        in_maps.append(m)
    res = run_bass_kernel_spmd(nc, in_maps, core_ids=list(range(8)))
    out = np.stack([np.asarray(r["out"], dtype=np.float32) for r in res.results], axis=0)
    return out
```

```python
import numpy as np
from contextlib import ExitStack
import concourse.bass as bass
import concourse.mybir as mybir
from concourse.bass_utils import run_bass_kernel_spmd

F32 = mybir.dt.float32
BF16 = mybir.dt.bfloat16
AF = mybir.ActivationFunctionType
ALU = mybir.AluOpType
AX = mybir.AxisListType

ENGS = ("pe", "act", "dve", "pool", "sp")


class Buf:
    __slots__ = ("name", "w", "r", "rd", "sem", "semcnt")

    def __init__(self, name, sem=None):
        self.name = name
        self.w = None
        self.r = {}
        self.rd = []
        self.sem = sem
        self.semcnt = 0


class Op:
    __slots__ = ("eng", "emit", "deps", "mile", "mileno", "sem", "semval", "is_dma", "seq")


class Prog:
    def __init__(self):
        self.ops = {e: [] for e in ENGS}
        self.final_waits = []

    def add(self, eng, emit, reads=(), writes=(), dma_buf=None, indep=False):
        op = Op()
        op.eng = eng
        op.emit = emit
        op.mile = False
        op.mileno = 0
        op.is_dma = dma_buf is not None
        op.sem = None
        op.semval = 0
        op.seq = len(self.ops[eng])
        deps = []
        wset = set(id(b) for b in writes)
        for b in reads:
            if b.w is not None:
                deps.append(b.w)
        for b in writes:
            if b.w is not None and not indep:
                deps.append(b.w)
            deps.extend(b.r.values())
            deps.extend(b.rd)
        best = {}
        dl = []
        seen = set()
        for d in deps:
            if d.is_dma:
                if id(d) not in seen:
                    seen.add(id(d))
                    dl.append(d)
            else:
                if d.eng == "pe" and eng == "pe" and not op.is_dma:
                    continue
                cur = best.get(d.eng)
                if cur is None or d.seq > cur.seq:
                    best[d.eng] = d
        for d in best.values():
            d.mile = True
            dl.append(d)
        op.deps = dl
        if op.is_dma:
            dma_buf.semcnt += 16
            op.sem = dma_buf.sem
            op.semval = dma_buf.semcnt
        for b in writes:
            b.w = op
            b.r = {}
            b.rd = []
        for b in reads:
            if id(b) in wset:
                continue
            if op.is_dma:
                b.rd.append(op)
            else:
                b.r[eng] = op
        self.ops[eng].append(op)
        return op

    def emit_all(self, nc, block, engsem):
        for e in ENGS:
            n = 0
            for op in self.ops[e]:
                if op.mile and not op.is_dma:
                    n += 1
                    op.mileno = n
        prog = self

        def run(ename, eobj):
            known = {}
            for op in prog.ops[ename]:
                need = {}
                for d in op.deps:
                    if d.is_dma:
                        s, v = d.sem, d.semval
                    else:
                        s, v = engsem[d.eng], d.mileno
                    k = s.num
                    if k not in need or need[k][1] < v:
                        need[k] = (s, v)
                for k, (s, v) in need.items():
                    if known.get(k, 0) < v:
                        eobj.wait_ge(s, v)
                        known[k] = v
                ins = op.emit(eobj)
                if op.is_dma:
                    ins.then_inc(op.sem, 16)
                elif op.mile:
                    ins.then_inc(engsem[ename], 1)
            if ename == "sp":
                for (s, v) in prog.final_waits:
                    eobj.wait_ge(s, v)

        @block.tensor
        def _(e):
            run("pe", e)

        @block.scalar
        def _(e):
            run("act", e)

        @block.vector
        def _(e):
            run("dve", e)

        @block.gpsimd
        def _(e):
            run("pool", e)

        @block.sync
        def _(e):
            run("sp", e)


class Ctx:
    def __init__(self, nc, st):
        self.nc = nc
        self.st = st
        self.P = Prog()
        self.nsem = 0
        self.engsem = {}
        for e in ("pe", "act", "dve", "pool"):
            self.engsem[e] = st.enter_context(nc.semaphore("prog_" + e))

    def sbuf(self, name, shape, dt):
        return self.st.enter_context(self.nc.sbuf_tensor("sb_" + name, list(shape), dt))

    def psum(self, name, shape, dt):
        return self.st.enter_context(self.nc.psum_tensor("pp_" + name, list(shape), dt))

    def dbuf(self, name):
        self.nsem += 1
        s = self.st.enter_context(self.nc.semaphore("d_" + name))
        return Buf(name, sem=s)

    def mm(self, out, lhsT, rhs, start, stop, reads, writes):
        return self.P.add("pe", lambda e: e.matmul(out, lhsT, rhs, start=start, stop=stop), reads, writes)

    def tr(self, out, in_, ident, reads, writes):
        return self.P.add("pe", lambda e: e.transpose(out, in_, ident), reads, writes)

    def act(self, out, in_, func, reads, writes, bias=None, scale=None, accum_out=None, eng="act"):
        kw = {}
        if bias is not None:
            kw["bias"] = bias
        if scale is not None:
            kw["scale"] = scale
        if accum_out is not None:
            kw["accum_out"] = accum_out
        return self.P.add("act", lambda e: e.activation(out, in_, func, **kw), reads, writes)

    def tt(self, eng, out, in0, in1, op, reads, writes):
        return self.P.add(eng, lambda e: e.tensor_tensor(out, in0, in1, op), reads, writes)

    def ts(self, eng, out, in0, s1, s2, op0, op1, reads, writes):
        if s2 is None:
            return self.P.add(eng, lambda e: e.tensor_scalar(out, in0, s1, None, op0), reads, writes)
        return self.P.add(eng, lambda e: e.tensor_scalar(out, in0, s1, s2, op0, op1), reads, writes)

    def stt(self, eng, out, in0, scalar, in1, op0, op1, reads, writes):
        return self.P.add(eng, lambda e: e.scalar_tensor_tensor(out, in0, scalar, in1, op0, op1), reads, writes)

    def copy(self, eng, out, in_, reads, writes):
        if eng == "act":
            return self.P.add("act", lambda e: e.copy(out, in_), reads, writes)
        return self.P.add(eng, lambda e: e.tensor_copy(out, in_), reads, writes)

    def memset(self, eng, ap, val, writes):
        return self.P.add(eng, lambda e: e.memset(ap, val), (), writes)

    def dma(self, eng, out, in_, reads, writes, dma_buf, indep=False, **kw):
        return self.P.add(eng, lambda e: e.dma_start(out, in_, **kw), reads, writes, dma_buf=dma_buf, indep=indep)


D = 1024
KC = 8
S = 4096
T = 512
NT = S // T
FF = 2816
FC = 22
EPS = 1e-6
NRING = 3
HEAD_OF_SLOT = []
for _c in range(8):
    HEAD_OF_SLOT.append([0, 1, 2, 3, 8, 9, 10, 11][_c])
    HEAD_OF_SLOT.append([4, 5, 6, 7, 12, 13, 14, 15][_c])
NEG_MASK = -30000.0

WS_IN = 0
WS_OUT0 = 16
WS_QK = 24
WS_OUT1 = 34
NWS = 42


class WPool:
    def __init__(self, C, name, nslots, shape):
        self.C = C
        self.n = nslots
        self.t = [C.sbuf("%s%d" % (name, i), shape, BF16) for i in range(nslots)]
        self.b = [C.dbuf("%s%d" % (name, i)) for i in range(nslots)]
        self.sb = [C.dbuf("%s%dst" % (name, i)) for i in range(nslots)]
        self.req = []
        self.chunk = {}
        self.emitted = 0
        self.cons = 0

    def _top(self, upto):
        C = self.C
        upto = min(upto, len(self.req))
        while self.emitted < upto:
            i = self.emitted
            key, src32, scr = self.req[i]
            k = i % self.n
            if key not in self.chunk:
                cb = Buf("chunk")
                self.chunk[key] = cb
                C.dma("pool", self.t[k][:], src32, (), (self.b[k],), self.b[k])
                C.dma("sp", scr, self.t[k][:], (self.b[k],), (cb,), self.sb[k])
            else:
                C.dma("sp", self.t[k][:], scr, (self.chunk[key],), (self.b[k],), self.b[k])
            self.emitted += 1

    def prefetch(self):
        self._top(self.cons + self.n)

    def next(self):
        i = self.cons
        assert i < len(self.req), "weight pool underflow"
        self._top(i + self.n)
        self.cons += 1
        return self.t[i % self.n], self.b[i % self.n]


def build_program(debug=False, NT=NT, stages=3):
    nc = bass.Bass("TRN2", target_bir_lowering=False)
    dt_in = lambda name, shape: nc.dram_tensor(name, list(shape), F32, kind="ExternalInput").ap()
    x_d = dt_in("x", [S, D])
    ws_d = dt_in("ws", [NWS, 128, 1024])
    gu_d = dt_in("gu", [2 * FC, 2, 128, 1024])
    wd_d = dt_in("wd", [16, 128, FC, 128])
    wv0_d = dt_in("wv0", [128, 8, 512])
    wv1_d = dt_in("wv1", [128, 8, 256])
    gall_d = dt_in("gall", [128, 4, 8])
    gfin_d = dt_in("gfin", [128, 1024])
    lng_d = dt_in("lng", [128, 512])
    lnb_d = dt_in("lnb", [128, 512])
    bsp_d = dt_in("bsp", [128, 4, 128])
    wsp_d = dt_in("wsp", [128, 4, 128])
    cw_d = dt_in("cw", [128, 3, 4])
    bias_d = dt_in("biasT", [128, 16, 384])
    sink_d = dt_in("sinkb", [128, 16])
    id_d = dt_in("ident", [128, 128])
    out_d = nc.dram_tensor("out", [S, D], F32, kind="ExternalOutput").ap()
    if debug:
        dbg_d = nc.dram_tensor("dbg", [5, NT, 128, 8, 512], F32, kind="ExternalOutput").ap()
    ws_s = nc.dram_tensor("ws_s", [NWS, 128, 1024], BF16).ap()
    gu_s = nc.dram_tensor("gu_s", [2 * FC, 2, 128, 1024], BF16).ap()
    wd_s = nc.dram_tensor("wd_s", [16, 128, FC, 128], BF16).ap()
    wv0_s = nc.dram_tensor("wv0_s", [128, 8, 512], BF16).ap()
    wv1_s = nc.dram_tensor("wv1_s", [128, 8, 256], BF16).ap()

    with ExitStack() as st:
        C = Ctx(nc, st)
        P = C.P
        xr = C.sbuf("xr", [128, NRING, KC, T], F32)
        xr_b = [[Buf("xr%d_%d" % (r, c)) for c in range(KC)] for r in range(NRING)]
        hT = C.sbuf("hT", [128, KC, T + 2], BF16)
        hT_b = [Buf("hT%d" % c) for c in range(KC)]
        hTx_b = Buf("hTx")
        arena = C.sbuf("arena", [128, 32 * 256], F32)
        pg = [Buf("pg%d" % i) for i in range(32)]

        def av_bf(p0, np_):
            return arena[:, p0 * 256:(p0 + np_) * 256].bitcast(BF16)

        def av_f32(p0, np_):
            return arena[:, p0 * 256:(p0 + np_) * 256]

        kring = C.sbuf("kring", [128, 2, 4 * T], BF16)
        kr_b = [[Buf("k%d_%d" % (c, b)) for b in range(16)] for c in range(2)]
        vring = C.sbuf("vring", [128, 16, 4, 65], BF16)
        vr_b = [Buf("v%d" % b) for b in range(16)]
        qT = C.sbuf("qT", [128, KC, T], BF16)
        qT_b = [Buf("qT%d" % c) for c in range(KC)]
        biasT = C.sbuf("biasTs", [128, 16, 384], BF16)
        biasT_b = C.dbuf("biasT")
        xstage2 = [C.sbuf("xstage%d" % i, [128, D], F32) for i in range(2)]
        xstage2_b = [C.dbuf("xstage%d" % i) for i in range(2)]
        ostage = C.sbuf("ostage", [128, D], F32)
        ostage_b = Buf("ostage")
        ostore_b = C.dbuf("ostore")
        sq = [C.sbuf("sq%d" % i, [128, T], BF16) for i in range(3)]
        sq_b = [Buf("sq%d" % i) for i in range(3)]
        nrm_t = sq
        nrm_b = sq_b
        sqx = C.sbuf("sqx", [128, 8], BF16)
        sqx_b = Buf("sqx")
        tbuf = C.sbuf("tbuf", [128, T], F32)
        tbuf_b = Buf("tbuf")
        rsb = C.sbuf("rsb", [128, T], F32)
        rsb_b = Buf("rsb")
        small = C.sbuf("small", [128, 256], F32)
        zext = [C.sbuf("zext%d" % i, [128, T + 4], F32) for i in range(2)]
        zext_b = [Buf("zext%d" % i) for i in range(2)]
        zprev = C.sbuf("zprev", [128, 4], F32)
        zprev_b = [Buf("zprev%d" % j) for j in range(4)]
        ident_f = C.sbuf("ident_f", [128, 128], F32)
        ident_fb = C.dbuf("ident_f")
        ident_b = C.sbuf("ident_b", [128, 128], BF16)
        ident_bb = Buf("ident_b")
        ones_b = C.sbuf("ones_b", [128, 128], BF16)
        ones_bb = Buf("ones_b")
        epsc = C.sbuf("epsc", [128, 1], F32)
        eps_b = Buf("epsc")
        neghalf = C.sbuf("neghalf", [128, 8], F32)
        neghalf_b = Buf("neghalf")
        gfin = C.sbuf("gfin", [128, D], F32)
        lng = C.sbuf("lng", [128, 512], F32)
        lnb = C.sbuf("lnb", [128, 512], F32)
        bsp = C.sbuf("bsp", [128, 4, 128], F32)
        wsp_f = C.sbuf("wsp_f", [128, 4, 128], F32)
        wsp_b = C.sbuf("wsp_b", [128, 4, 128], BF16)
        wsp_bb = Buf("wsp_b")
        gall = C.sbuf("gall", [128, 4, 8], F32)
        cw = C.sbuf("cw", [128, 3, 4], F32)
        sinkb = C.sbuf("sinkb", [128, 16], F32)
        negc = C.sbuf("negc", [128, 16], F32)
        sinkterm = C.sbuf("sinkterm", [128, 16], F32)
        att_b = Buf("attconst")
        cst_b = C.dbuf("consts")
        junk = C.sbuf("junk", [128, T], BF16)
        junk_b = Buf("junk")

        _sm = [0]

        def sm(n):
            a = small[:, _sm[0]:_sm[0] + n]
            _sm[0] += n
            assert _sm[0] <= 256
            return a

        psb = [C.psum("ps%d" % i, [128, 512], F32) for i in range(8)]
        ps_b = [Buf("ps%d" % i) for i in range(8)]
        _pr = [0]

        def next_ps():
            k = _pr[0] % 5
            _pr[0] += 1
            return psb[k], ps_b[k]

        acc_t = psb[5:8]
        acc_b = ps_b[5:8]
        xps_b = Buf("xps")

        C.dma("pool", biasT[:], bias_d, (), (biasT_b,), biasT_b)

        for (dst, src) in ((ident_f, id_d), (gfin, gfin_d), (lng, lng_d), (lnb, lnb_d), (bsp, bsp_d),
                           (wsp_f, wsp_d), (gall, gall_d), (cw, cw_d), (sinkb, sink_d)):
            C.dma("sp", dst[:], src, (), (cst_b,), cst_b, indep=True)
        C.copy("dve", ident_b[:], ident_f[:], (cst_b,), (ident_bb,))
        C.memset("pool", ones_b[:], 1.0, (ones_bb,))
        C.memset("pool", neghalf[:], -0.5, (neghalf_b,))
        C.memset("pool", epsc[:], EPS, (eps_b,))
        C.copy("dve", wsp_b[:], wsp_f[:], (cst_b,), (wsp_bb,))
        C.memset("pool", vring[:], 1.0, vr_b)
        C.ts("dve", negc[:], sinkb[:], 0.0, -1.0, ALU.max, ALU.mult, (cst_b,), (att_b,))
        C.tt("dve", sinkterm[:], sinkb[:], negc[:], ALU.add, (cst_b, att_b), (att_b,))
        C.act(sinkterm[:], sinkterm[:], AF.Exp, (att_b,), (att_b,))

        proj = WPool(C, "wproj", 4, [128, KC, 128])
        gup = WPool(C, "wgu", 3, [128, 2, KC, 128])
        dnp = WPool(C, "wdn", 2, [128, 2, 11 * 128])
        wv0p = WPool(C, "wv0p", 1, [128, KC, 512])
        wv1p = WPool(C, "wv1p", 1, [128, KC, 256])

        def ws_req(idx):
            return (("ws", idx), ws_d[idx].rearrange("p (k j) -> p k j", k=KC), ws_s[idx].rearrange("p (k j) -> p k j", k=KC))

        def ffn_reqs(layer):
            for fc in range(FC):
                i = layer * FC + fc
                gup.req.append((("gu", i), gu_d[i].rearrange("t p (k j) -> p t k j", k=KC),
                                gu_s[i].rearrange("t p (k j) -> p t k j", k=KC)))
            for dc in range(8):
                i = layer * 8 + dc
                dnp.req.append((("wd", i), wd_d[i].rearrange("p (a b) j -> p a (b j)", a=2),
                                wd_s[i].rearrange("p (a b) j -> p a (b j)", a=2)))

        for s in range(NT + 1):
            if s < NT:
                wv0p.req.append((("wv0", 0), wv0_d, wv0_s))
                for j in range(4):
                    for t3 in range(3):
                        proj.req.append(ws_req(WS_IN + 4 + 4 * t3 + j))
                for g in range(4):
                    proj.req.append(ws_req(WS_IN + g))
                for dc in range(8):
                    proj.req.append(ws_req(WS_OUT0 + dc))
                ffn_reqs(0)
                if stages >= 2:
                    wv1p.req.append((("wv1", 0), wv1_d, wv1_s))
                    for k2 in range(2):
                        proj.req.append(ws_req(WS_QK + 8 + k2))
            if s >= 1 and stages >= 2:
                for dc in range(8):
                    proj.req.append(ws_req(WS_OUT1 + dc))
                ffn_reqs(1)
            if s < NT and stages >= 2:
                for c in range(8):
                    proj.req.append(ws_req(WS_QK + c))

        def proj_fm(w_t, w_b, rhs_of_kc, rhs_bufs, n, nk=KC, ps=None):
            if ps is None:
                ps = next_ps()
            pt, pb = ps
            for kc in range(nk):
                C.mm(pt[:, 0:n], w_t[:, kc, :], rhs_of_kc(kc), kc == 0, kc == nk - 1,
                     (w_b, rhs_bufs[kc]), (pb,))
            return pt, pb

        def resid_add(slot, dc, pt, pb):
            xa = xr[:, slot, dc, :]
            C.tt("dve", xa, xa, pt[:, 0:T], ALU.add, (pb,), (xr_b[slot][dc],))

        def norm(slot, gi, extra_slot=None, filler=None):
            pt, pb = next_ps()
            for c in range(KC):
                k = c % 3
                xa = xr[:, slot, c, :]
                if c % 2 == 0:
                    C.act(sq[k][:], xa, AF.Square, (xr_b[slot][c],), (sq_b[k],))
                else:
                    C.tt("dve", sq[k][:], xa, xa, ALU.mult, (xr_b[slot][c],), (sq_b[k],))
                C.mm(pt[:, 0:T], ones_b[:], sq[k][:], c == 0, c == KC - 1, (ones_bb, sq_b[k]), (pb,))
            C.act(tbuf[:], pt[:, 0:T], AF.Ln, (pb, eps_b), (tbuf_b,), bias=epsc[:, 0:1], scale=1.0 / D)
            C.act(rsb[:], tbuf[:], AF.Exp, (tbuf_b,), (rsb_b,), scale=-0.5)
            for c in range(KC):
                if c < 5:
                    C.stt("dve", hT[:, c, 0:T], xr[:, slot, c, :], gall[:, gi, c:c + 1], rsb[:], ALU.mult, ALU.mult,
                          (xr_b[slot][c], rsb_b, cst_b), (hT_b[c],))
                else:
                    k = c - 5
                    C.tt("pool", nrm_t[k][:], xr[:, slot, c, :], rsb[:], ALU.mult, (xr_b[slot][c], rsb_b), (nrm_b[k],))
                    C.act(hT[:, c, 0:T], nrm_t[k][:], AF.Copy, (nrm_b[k], cst_b), (hT_b[c],), scale=gall[:, gi, c:c + 1])
            if filler is not None:
                filler()
            if extra_slot is not None:
                xcol = xr[:, extra_slot, :, 0]
                xb = xr_b[extra_slot]
                C.tt("dve", sqx[:], xcol, xcol, ALU.mult, xb, (sqx_b,))
                pt2, pb2 = next_ps()
                for c in range(KC):
                    C.mm(pt2[:, 0:1], ones_b[:], sqx[:, c:c + 1], c == 0, c == KC - 1, (ones_bb, sqx_b), (pb2,))
                t1 = sm_t1
                C.act(t1, pt2[:, 0:1], AF.Ln, (pb2, eps_b), (smx_b,), bias=epsc[:, 0:1], scale=1.0 / D)
                C.act(sm_rs1, t1, AF.Exp, (smx_b,), (smx_b,), scale=-0.5)
                C.tt("dve", sm_x8, xcol, gall[:, gi, :], ALU.mult, tuple(xb) + (cst_b,), (smx_b,))
                C.ts("dve", hT[:, :, T], sm_x8, sm_rs1, None, ALU.mult, None, (smx_b,), (hTx_b,))

        sm_t1 = sm(1)
        sm_rs1 = sm(1)
        sm_x8 = sm(8)
        smx_b = Buf("smx")

        def load_dma(t, tb):
            r0 = t * T + tb * 128
            C.dma("sp", xstage2[tb % 2][:], x_d[r0:r0 + 128, :], (), (xstage2_b[tb % 2],), xstage2_b[tb % 2])

        def load_tr(t, tb):
            slot = t % NRING
            xstage = xstage2[tb % 2]
            xstage_b = xstage2_b[tb % 2]
            for half in range(2):
                pt, pb = next_ps()
                for c4 in range(4):
                    c = half * 4 + c4
                    C.tr(pt[:, c4 * 128:(c4 + 1) * 128], xstage[:, c * 128:(c + 1) * 128], ident_f[:],
                         (xstage_b, ident_fb), (pb,))
                dst = xr[:, slot, half * 4:half * 4 + 4, tb * 128:(tb + 1) * 128]
                C.copy("act", dst, pt[:, 0:512].rearrange("p (c n) -> p c n", c=4), (pb,),
                       tuple(xr_b[slot][half * 4:half * 4 + 4]))

        def load_a(t):
            load_tr(t, 0)
            load_tr(t, 1)
            load_dma(t, 2)
            load_dma(t, 3)

        def load_b(t):
            load_tr(t, 2)
            load_tr(t, 3)
            if t + 1 < NT:
                load_dma(t + 1, 0)
                load_dma(t + 1, 1)

        def tmp_slot(i):
            p0 = 18 + 2 * (i % 6)
            return av_f32(p0, 2), (pg[p0], pg[p0 + 1])

        _tmpi = [0]

        def next_tmp():
            r = tmp_slot(_tmpi[0])
            _tmpi[0] += 1
            return r

        ln_st = [sm(6) for _ in range(4)]
        ln_mv = [sm(2) for _ in range(4)]
        ln_rt = [sm(1) for _ in range(4)]
        ln_rs = [sm(1) for _ in range(4)]
        ln_b = [Buf("ln%d" % i) for i in range(4)]
        xc_sb = sm(4)
        xc_b = [Buf("xc%d" % j) for j in range(4)]

        def mixer0(s):
            slot = s % NRING
            nslot = (s + 1) % NRING if s + 1 < NT else None
            norm(slot, 0, nslot, filler=(lambda: load_a(s + 1)) if s + 1 < NT else None)
            gup.prefetch()
            hb = hT_b
            wv_t, wv_b = wv0p.next()
            for tb in range(4):
                pt, pb = next_ps()
                for kc in range(KC):
                    C.mm(pt[:, 0:512], hT[:, kc, tb * 128:(tb + 1) * 128], wv_t[:, kc, :], kc == 0, kc == KC - 1,
                         (hb[kc], wv_b), (pb,))
                gv, gvb = next_tmp()
                C.act(gv, pt[:, 0:512], AF.Gelu, (pb,), gvb)
                C.P.add("dve", lambda e, o=ln_st[tb], i=gv: e.bn_stats(o, i), gvb, (ln_b[tb],))
                C.P.add("dve", lambda e, o=ln_mv[tb], i=ln_st[tb]: e.bn_aggr(o, i), (ln_b[tb],), (ln_b[tb],))
                C.ts("dve", ln_rt[tb], ln_mv[tb][:, 1:2], EPS, None, ALU.add, None, (ln_b[tb],), (ln_b[tb],))
                C.tt("pool", ln_rs[tb], ln_rt[tb], neghalf[:, 0:1], ALU.pow, (ln_b[tb], neghalf_b), (ln_b[tb],))
                C.ts("dve", gv, gv, ln_mv[tb][:, 0:1], ln_rs[tb], ALU.subtract, ALU.mult, (ln_b[tb],), gvb)
                C.tt("pool", gv, gv, lng[:], ALU.mult, (cst_b,), gvb)
                vt = av_bf(8 + tb, 1)
                C.tt("pool", vt, gv, lnb[:], ALU.add, gvb + (cst_b,), (pg[8 + tb],))
            for j in range(4):
                z = zext[j % 2]
                zb = zext_b[j % 2]
                bb = av_f32(14 + 2 * (j % 2), 2)
                bbb = (pg[14 + 2 * (j % 2)], pg[15 + 2 * (j % 2)])
                xp = acc_t[2]

                def halo(col, wt_, wb_):
                    if nslot is None:
                        return
                    for kc in range(KC):
                        C.mm(xp[:, col:col + 1], wt_[:, kc, :], hT[:, kc, T:T + 1], kc == 0, kc == KC - 1,
                             (wb_, hTx_b), (xps_b,))

                wb_t, wb_b = proj.next()
                pt, pb = proj_fm(wb_t, wb_b, lambda kc: hT[:, kc, 0:T], hb, T)
                C.copy("act", bb, pt[:, 0:T], (pb,), bbb)
                wc_t, wc_b = proj.next()
                ptc, pbc = proj_fm(wc_t, wc_b, lambda kc: hT[:, kc, 0:T], hb, T)
                halo(496 + 2 * j, wc_t, wc_b)
                tc_, tcb = next_tmp()
                C.copy("act", tc_, ptc[:, 0:T], (pbc,), tcb)
                wh_t, wh_b = proj.next()
                pth, pbh = proj_fm(wh_t, wh_b, lambda kc: hT[:, kc, 0:T], hb, T)
                halo(497 + 2 * j, wh_t, wh_b)
                C.tt("dve", z[:, 1:T + 1], tc_, pth[:, 0:T], ALU.mult, tcb + (pbh,), (zb,))
                if s > 0:
                    C.copy("pool", z[:, 0:1], zprev[:, j:j + 1], (zprev_b[j],), (zb,))
                else:
                    C.memset("pool", z[:, 0:1], 0.0, (zb,))
                if nslot is not None:
                    C.copy("act", xc_sb[:, j:j + 1], xp[:, 496 + 2 * j:497 + 2 * j], (xps_b,), (xc_b[j],))
                    C.tt("dve", z[:, T + 1:T + 2], xc_sb[:, j:j + 1], xp[:, 497 + 2 * j:498 + 2 * j], ALU.mult,
                         (xc_b[j], xps_b), (zb,))
                else:
                    C.memset("pool", z[:, T + 1:T + 2], 0.0, (zb,))
                C.copy("pool", zprev[:, j:j + 1], z[:, T:T + 1], (zb,), (zprev_b[j],))
                ct, ctb = next_tmp()
                C.ts("dve", ct, z[:, 0:T], cw[:, 0, j:j + 1], None, ALU.mult, None, (zb, cst_b), ctb)
                C.stt("dve", ct, z[:, 1:T + 1], cw[:, 1, j:j + 1], ct, ALU.mult, ALU.add, (zb, cst_b), ctb)
                C.stt("dve", ct, z[:, 2:T + 2], cw[:, 2, j:j + 1], ct, ALU.mult, ALU.add, (zb, cst_b), ctb)
                C.tt("pool", av_bf(4 + j, 1), ct, bb, ALU.mult, ctb + bbb, (pg[4 + j],))
            for g in range(4):
                w_t, w_b = proj.next()
                pt, pb = proj_fm(w_t, w_b, lambda kc: hT[:, kc, 0:T], hb, T)
                au = av_bf(12 + g % 2, 1)
                aub = pg[12 + g % 2]
                C.act(au, pt[:, 0:T], AF.Gelu, (pb,), (aub,))
                pt2, pb2 = next_ps()
                for tb in range(4):
                    vt = av_bf(8 + tb, 1)
                    C.mm(pt2[:, tb * 128:(tb + 1) * 128], vt[:, g * 128:(g + 1) * 128], wsp_b[:, g, :], True, True,
                         (pg[8 + tb], wsp_bb), (pb2,))
                t1, t1b = next_tmp()
                C.tt("dve", t1.rearrange("p (t q) -> p t q", t=4), pt2[:, 0:512].rearrange("p (t q) -> p t q", t=4),
                     bsp[:, g:g + 1, :].broadcast_to([128, 4, 128]), ALU.add, (pb2, cst_b), t1b)
                C.tt("pool", av_bf(g, 1), t1, au, ALU.mult, t1b + (aub,), (pg[g],))
            for dc in range(8):
                w_t, w_b = proj.next()
                pt, pb = proj_fm(w_t, w_b, lambda kc: av_bf(kc, 1), pg[0:8], T)
                resid_add(slot, dc, pt, pb)

        def ffn(t, layer):
            slot = t % NRING
            norm(slot, 1 + 2 * layer, filler=(lambda: load_b(t + 1)) if (layer == 0 and t + 1 < NT) else None)
            dnp.prefetch()
            for fc in range(FC):
                gu_t, gu_b = gup.next()
                ptg, pbg = next_ps()
                ptu, pbu = next_ps()
                for kc in range(KC):
                    C.mm(ptg[:, 0:T], gu_t[:, 0, kc, :], hT[:, kc, 0:T], kc == 0, kc == KC - 1, (gu_b, hT_b[kc]), (pbg,))
                for kc in range(KC):
                    C.mm(ptu[:, 0:T], gu_t[:, 1, kc, :], hT[:, kc, 0:T], kc == 0, kc == KC - 1, (gu_b, hT_b[kc]), (pbu,))
                p0 = 22 + 2 * (fc % 2)
                sg = av_f32(p0, 2)
                sgb = (pg[p0], pg[p0 + 1])
                C.act(sg, ptg[:, 0:T], AF.Silu, (pbg,), sgb)
                C.tt("dve", av_bf(fc, 1), sg, ptu[:, 0:T], ALU.mult, sgb + (pbu,), (pg[fc],))
            proj.prefetch()
            for dc in range(8):
                wd_t, wd_b = dnp.next()
                pt, pb = next_ps()
                for fc in range(FC):
                    C.mm(pt[:, 0:T], wd_t[:, fc // 11, (fc % 11) * 128:(fc % 11 + 1) * 128], av_bf(fc, 1), fc == 0, fc == FC - 1, (wd_b, pg[fc]), (pb,))
                resid_add(slot, dc, pt, pb)

        def l1_kv(s):
            slot = s % NRING
            norm(slot, 2)
            wv_t, wv_b = wv1p.next()
            rb = (s % 4) * 4
            for k2 in range(2):
                w_t, w_b = proj.next()
                pt, pb = proj_fm(w_t, w_b, lambda kc: hT[:, kc, 0:T], hT_b, T)
                C.copy("act", kring[:, k2, rb * 128:rb * 128 + T], pt[:, 0:T], (pb,), tuple(kr_b[k2][rb:rb + 4]))
            for tb in range(4):
                pt, pb = next_ps()
                for kc in range(KC):
                    C.mm(pt[:, 0:256], hT[:, kc, tb * 128:(tb + 1) * 128], wv_t[:, kc, :], kc == 0, kc == KC - 1,
                         (hT_b[kc], wv_b), (pb,))
                C.copy("dve", vring[:, rb + tb, :, 0:64], pt[:, 0:256].rearrange("p (g d) -> p g d", g=4), (pb,),
                       (vr_b[rb + tb],))

        def l1_q(s):
            slot = s % NRING
            norm(slot, 2, filler=(lambda: final(s - 1)) if s >= 1 else None)
            for c in range(8):
                w_t, w_b = proj.next()
                pt, pb = proj_fm(w_t, w_b, lambda kc: hT[:, kc, 0:T], hT_b, T)
                C.act(qT[:, c, :], pt[:, 0:T], AF.Copy, (pb,), (qT_b[c],), scale=0.125)

        den = sm(16)
        rden = sm(16)
        den_b = Buf("den")

        def attention(t):
            slot = t % NRING
            proj.prefetch()
            gup.prefetch()
            for qb in range(4):
                gb = 4 * t + qb
                js = [j for j in range(3) if 0 <= gb - 1 + j < 32]
                nj = len(js)
                j0 = js[0]
                ao = av_bf(8 + 2 * (qb % 2), 2)
                aob = (pg[8 + 2 * (qb % 2)], pg[9 + 2 * (qb % 2)])
                def emit_st2(cq):
                    sts = []
                    for half in range(2):
                        hs = 2 * cq + half
                        st_t, st_b = next_ps()
                        sts.append((st_t, st_b))
                        C.mm(st_t[:, 0:nj * 128], ident_b[:], biasT[:, hs, j0 * 128:(j0 + nj) * 128], True, False,
                             (ident_bb, biasT_b), (st_b,))
                    for idx, j in enumerate(js):
                        kb = (gb - 1 + j) % 16
                        for half in range(2):
                            hs = 2 * cq + half
                            g = HEAD_OF_SLOT[hs] // 4
                            assert g % 2 == half
                            k2 = g // 2
                            st_t, st_b = sts[half]
                            C.mm(st_t[:, idx * 128:(idx + 1) * 128],
                                 kring[half * 64:(half + 1) * 64, k2, kb * 128:(kb + 1) * 128],
                                 qT[half * 64:(half + 1) * 64, cq, qb * 128:(qb + 1) * 128],
                                 False, idx == nj - 1, (kr_b[k2][kb], qT_b[cq]), (st_b,))
                    for half in range(2):
                        hs = 2 * cq + half
                        st_t, st_b = sts[half]
                        ptile = av_bf(12 + hs % 4, 1)
                        ptb = pg[12 + hs % 4]
                        C.act(ptile[:, 0:nj * 128], st_t[:, 0:nj * 128], AF.Exp, (st_b, att_b), (ptb,),
                              bias=negc[:, hs:hs + 1], scale=1.0)

                def emit_pv(hs):
                    g = HEAD_OF_SLOT[hs] // 4
                    ptile = av_bf(12 + hs % 4, 1)
                    ptb = pg[12 + hs % 4]
                    bank = hs // 7
                    col = (hs % 7) * 65
                    for idx, j in enumerate(js):
                        kb = (gb - 1 + j) % 16
                        C.mm(acc_t[bank][:, col:col + 65], ptile[:, idx * 128:(idx + 1) * 128], vring[:, kb, g, :],
                             idx == 0, idx == nj - 1, (ptb, vr_b[kb]), (acc_b[bank],))

                emit_st2(0)
                for cq in range(1, 8):
                    emit_st2(cq)
                    emit_pv(2 * cq - 2)
                    emit_pv(2 * cq - 1)
                emit_pv(14)
                emit_pv(15)
                for bank in range(3):
                    h0 = bank * 7
                    h1 = min(16, h0 + 7)
                    nh = h1 - h0
                    a3 = acc_t[bank][:, 0:nh * 65].rearrange("p (h e) -> p h e", e=65)
                    C.tt("dve", den[:, h0:h1], a3[:, :, 64], sinkterm[:, h0:h1], ALU.add, (acc_b[bank], att_b), (den_b,))
                    C.P.add("dve", lambda e, o=rden[:, h0:h1], i=den[:, h0:h1]: e.reciprocal(o, i), (den_b,), (den_b,))
                    C.tt("dve", ao[:, h0 * 64:h1 * 64].rearrange("p (h d) -> p h d", d=64), a3[:, :, 0:64],
                         rden[:, h0:h1].unsqueeze(2).broadcast_to([128, nh, 64]), ALU.mult, (acc_b[bank], den_b), aob)
                tp, tpb = next_ps()
                tpv = tp[:, 0:512].bitcast(BF16)
                for c in range(8):
                    C.tr(tpv[:, c * 128:(c + 1) * 128], ao[:, c * 128:(c + 1) * 128], ident_b[:], aob + (ident_bb,), (tpb,))
                dst = av_bf(0, 8).rearrange("p (c n) -> p c n", c=8)[:, :, qb * 128:(qb + 1) * 128]
                C.copy("act", dst, tpv.rearrange("p (c n) -> p c n", c=8), (tpb,), tuple(pg[0:8]))
            for dc in range(8):
                w_t, w_b = proj.next()
                pt, pb = proj_fm(w_t, w_b, lambda kc: av_bf(kc, 1), pg[0:8], T)
                resid_add(slot, dc, pt, pb)

        fss = [sm(1), sm(1)]
        fs_t = sm(1)
        fs_r = sm(1)
        fs_b = Buf("fs")

        def final(t):
            slot = t % NRING
            for tb in range(4):
                pts = [next_ps(), next_ps()]
                for c in range(8):
                    pt, pb = pts[c // 4]
                    C.tr(pt[:, (c % 4) * 128:(c % 4 + 1) * 128], xr[:, slot, c, tb * 128:(tb + 1) * 128], ident_f[:],
                         (xr_b[slot][c], ident_fb), (pb,))
                for h2 in range(2):
                    pt, pb = pts[h2]
                    C.act(junk[:], pt[:, 0:512], AF.Square, (pb,), (junk_b, fs_b), accum_out=fss[h2])
                C.tt("dve", fs_t, fss[0], fss[1], ALU.add, (fs_b,), (fs_b,))
                C.ts("dve", fs_t, fs_t, 1.0 / D, EPS, ALU.mult, ALU.add, (fs_b,), (fs_b,))
                C.tt("pool", fs_r, fs_t, neghalf[:, 0:1], ALU.pow, (fs_b, neghalf_b), (fs_b,))
                for h2 in range(2):
                    pt, pb = pts[h2]
                    C.stt("dve", ostage[:, h2 * 512:(h2 + 1) * 512], pt[:, 0:512], fs_r, gfin[:, h2 * 512:(h2 + 1) * 512],
                          ALU.mult, ALU.mult, (pb, fs_b, cst_b, ostore_b), (ostage_b,))
                r0 = t * T + tb * 128
                C.dma("sp", out_d[r0:r0 + 128, :], ostage[:], (ostage_b,), (ostore_b,), ostore_b)

        def dump(k, t):
            if not debug:
                return
            slot = t % NRING
            b = C.dbuf("dbg%d_%d" % (k, t))
            C.dma("sp", dbg_d[k, t], xr[:, slot], tuple(xr_b[slot]), (), b)
            P.final_waits.append(b)

        load_dma(0, 0)
        load_dma(0, 1)
        load_a(0)
        load_b(0)
        for s in range(NT + 1):
            if s < NT:
                dump(0, s)
                mixer0(s)
                dump(1, s)
                ffn(s, 0)
                dump(2, s)
                if stages >= 2:
                    l1_kv(s)
            if s >= 1 and stages >= 2:
                attention(s - 1)
                dump(3, s - 1)
                ffn(s - 1, 1)
                dump(4, s - 1)
                if s == NT:
                    final(s - 1)
            if s < NT and stages >= 2:
                l1_q(s)

        fw = [(ostore_b.sem, ostore_b.semcnt)]
        for b in P.final_waits:
            fw.append((b.sem, b.semcnt))
        P.final_waits = fw
        print("SBUF bytes remaining per partition:", nc.sbuf_bytes_remaining, "ops:", {e: len(P.ops[e]) for e in ENGS})
        with nc.Block() as block:
            P.emit_all(nc, block, C.engsem)
    return nc


def _t5_bucket_table():
    nb = 16
    max_exact = 8
    rel = np.arange(-255, 256)
    ret = np.where(rel > 0, nb, 0)
    n = np.abs(rel)
    nf = np.maximum(n, 1).astype(np.float32)
    large = max_exact + (np.log(nf / max_exact) / np.log(128 / max_exact) * (nb - max_exact)).astype(np.int32)
    large = np.minimum(large, nb - 1)
    return ret + np.where(n < max_exact, n, large)


def _chunks_kmajor(W):
    K, E = W.shape
    a = W.reshape(K // 128, 128, E // 128, 128)
    return np.ascontiguousarray(a.transpose(2, 1, 0, 3)).reshape(E // 128, 128, (K // 128) * 128)


def prep_shared(inp):
    f = lambda a: np.ascontiguousarray(np.asarray(a, dtype=np.float32))
    w_in = f(inp["even_w_in"])[0]
    cols = np.concatenate([np.arange(0, 512), np.arange(1024, 2560)])
    ws_in = _chunks_kmajor(w_in[:, cols])
    ws_out0 = _chunks_kmajor(f(inp["even_w_out"])[0])
    wqkv = f(inp["attn_w_qkv"])[0]
    qcols = np.concatenate([np.arange(h * 64, h * 64 + 64) for h in HEAD_OF_SLOT])
    ws_q = _chunks_kmajor(wqkv[:, qcols])
    ws_k = _chunks_kmajor(wqkv[:, 1024:1280])
    wo1 = f(inp["attn_w_out"])[0][qcols, :]
    ws_out1 = _chunks_kmajor(wo1)
    ws = np.concatenate([ws_in, ws_out0, ws_q, ws_k, ws_out1], axis=0)
    assert ws.shape == (NWS, 128, 1024)
    gate = f(inp["ffn_w_gate"])
    up = f(inp["ffn_w_up"])
    down = f(inp["ffn_w_down"])
    gu = np.stack([np.stack([_chunks_kmajor(gate[l]), _chunks_kmajor(up[l])], axis=1) for l in range(2)], axis=0)
    gu = np.ascontiguousarray(gu.reshape(2 * FC, 2, 128, 1024))
    wd = np.stack([_chunks_kmajor(down[l]) for l in range(2)], axis=0).reshape(16, 128, FC, 128)
    wv0 = np.ascontiguousarray(w_in[:, 512:1024].reshape(8, 128, 512).transpose(1, 0, 2))
    wv1 = np.ascontiguousarray(wqkv[:, 1280:1536].reshape(8, 128, 256).transpose(1, 0, 2))
    nm = f(inp["norm_mix"])
    nf_ = f(inp["norm_ffn"])
    gl = np.stack([nm[0], nf_[0], nm[1], nf_[1]], axis=0)
    gall = np.ascontiguousarray(gl.reshape(4, 8, 128).transpose(2, 0, 1))
    rep = lambda v: np.ascontiguousarray(np.broadcast_to(v, (128,) + v.shape))
    gfin = rep(f(inp["final_norm"]))
    lng = rep(f(inp["even_v_ln_g"])[0])
    lnb = rep(f(inp["even_v_ln_b"])[0])
    bsp = rep(f(inp["even_b_spatial"])[0])
    wsp = np.ascontiguousarray(f(inp["even_w_spatial"])[0].transpose(2, 0, 1))
    cw = np.ascontiguousarray(f(inp["even_conv_w"])[0].reshape(3, 4, 128).transpose(2, 0, 1))
    tab = _t5_bucket_table()
    k = np.arange(128)[:, None, None]
    j = np.arange(3)[None, :, None]
    q = np.arange(128)[None, None, :]
    rel = (j - 1) * 128 + k - q
    bucket = tab[rel + 255]
    rb = f(inp["rel_bias"])[:, HEAD_OF_SLOT]
    bias = rb[bucket]
    band = (np.abs(rel) <= 128)[..., None]
    bias = np.where(band, bias, np.float32(NEG_MASK)).astype(np.float32)
    biasT = np.ascontiguousarray(bias.transpose(0, 3, 1, 2)).reshape(128, 16, 384)
    sinkb = rep(f(inp["attn_sink"])[0][HEAD_OF_SLOT])
    return {
        "ws": ws, "gu": gu, "wd": np.ascontiguousarray(wd), "wv0": wv0, "wv1": wv1, "gall": gall, "gfin": gfin,
        "lng": lng, "lnb": lnb, "bsp": bsp, "wsp": wsp, "cw": cw, "biasT": biasT, "sinkb": sinkb,
        "ident": np.eye(128, dtype=np.float32),
    }


_NC_CACHE = {}


def kernel(**inputs):
    x = np.ascontiguousarray(np.asarray(inputs["x"], dtype=np.float32))
    shared = prep_shared(inputs)
    if "nc" not in _NC_CACHE:
        _NC_CACHE["nc"] = build_program(False)
    nc = _NC_CACHE["nc"]
    in_maps = []
    for b in range(8):
        m = dict(shared)
        m["x"] = x[b]
        in_maps.append(m)
    res = run_bass_kernel_spmd(nc, in_maps, core_ids=list(range(8)))
    out = np.stack([np.asarray(r["out"], dtype=np.float32) for r in res.results], axis=0)
    return out
```

```python
import numpy as np
from contextlib import ExitStack
import concourse.bass as bass
import concourse.mybir as mybir
from concourse.bass_utils import run_bass_kernel_spmd

F32 = mybir.dt.float32
BF16 = mybir.dt.bfloat16
AF = mybir.ActivationFunctionType
ALU = mybir.AluOpType
AX = mybir.AxisListType

ENGS = ("pe", "act", "dve", "pool", "sp")


class Buf:
    __slots__ = ("name", "w", "r", "rd", "sem", "semcnt")

    def __init__(self, name, sem=None):
        self.name = name
        self.w = None
        self.r = {}
        self.rd = []
        self.sem = sem
        self.semcnt = 0


class Op:
    __slots__ = ("eng", "emit", "deps", "mile", "mileno", "sem", "semval", "is_dma", "seq")


class Prog:
    def __init__(self):
        self.ops = {e: [] for e in ENGS}
        self.final_waits = []

    def add(self, eng, emit, reads=(), writes=(), dma_buf=None, indep=False):
        op = Op()
        op.eng = eng
        op.emit = emit
        op.mile = False
        op.mileno = 0
        op.is_dma = dma_buf is not None
        op.sem = None
        op.semval = 0
        op.seq = len(self.ops[eng])
        deps = []
        wset = set(id(b) for b in writes)
        for b in reads:
            if b.w is not None:
                deps.append(b.w)
        for b in writes:
            if b.w is not None and not indep:
                deps.append(b.w)
            deps.extend(b.r.values())
            deps.extend(b.rd)
        best = {}
        dl = []
        seen = set()
        for d in deps:
            if d.is_dma:
                if id(d) not in seen:
                    seen.add(id(d))
                    dl.append(d)
            else:
                if d.eng == "pe" and eng == "pe" and not op.is_dma:
                    continue
                cur = best.get(d.eng)
                if cur is None or d.seq > cur.seq:
                    best[d.eng] = d
        for d in best.values():
            d.mile = True
            dl.append(d)
        op.deps = dl
        if op.is_dma:
            dma_buf.semcnt += 16
            op.sem = dma_buf.sem
            op.semval = dma_buf.semcnt
        for b in writes:
            b.w = op
            b.r = {}
            b.rd = []
        for b in reads:
            if id(b) in wset:
                continue
            if op.is_dma:
                b.rd.append(op)
            else:
                b.r[eng] = op
        self.ops[eng].append(op)
        return op

    def emit_all(self, nc, block, engsem):
        for e in ENGS:
            n = 0
            for op in self.ops[e]:
                if op.mile and not op.is_dma:
                    n += 1
                    op.mileno = n
        prog = self

        def run(ename, eobj):
            known = {}
            for op in prog.ops[ename]:
                need = {}
                for d in op.deps:
                    if d.is_dma:
                        s, v = d.sem, d.semval
                    else:
                        s, v = engsem[d.eng], d.mileno
                    k = s.num
                    if k not in need or need[k][1] < v:
                        need[k] = (s, v)
                for k, (s, v) in need.items():
                    if known.get(k, 0) < v:
                        eobj.wait_ge(s, v)
                        known[k] = v
                ins = op.emit(eobj)
                if op.is_dma:
                    ins.then_inc(op.sem, 16)
                elif op.mile:
                    ins.then_inc(engsem[ename], 1)
            if ename == "sp":
                for (s, v) in prog.final_waits:
                    eobj.wait_ge(s, v)

        @block.tensor
        def _(e):
            run("pe", e)

        @block.scalar
        def _(e):
            run("act", e)

        @block.vector
        def _(e):
            run("dve", e)

        @block.gpsimd
        def _(e):
            run("pool", e)

        @block.sync
        def _(e):
            run("sp", e)


class Ctx:
    def __init__(self, nc, st):
        self.nc = nc
        self.st = st
        self.P = Prog()
        self.nsem = 0
        self.engsem = {}
        for e in ("pe", "act", "dve", "pool"):
            self.engsem[e] = st.enter_context(nc.semaphore("prog_" + e))

    def sbuf(self, name, shape, dt):
        return self.st.enter_context(self.nc.sbuf_tensor("sb_" + name, list(shape), dt))

    def psum(self, name, shape, dt):
        return self.st.enter_context(self.nc.psum_tensor("pp_" + name, list(shape), dt))

    def dbuf(self, name):
        self.nsem += 1
        s = self.st.enter_context(self.nc.semaphore("d_" + name))
        return Buf(name, sem=s)

    def mm(self, out, lhsT, rhs, start, stop, reads, writes):
        return self.P.add("pe", lambda e: e.matmul(out, lhsT, rhs, start=start, stop=stop), reads, writes)

    def tr(self, out, in_, ident, reads, writes):
        return self.P.add("pe", lambda e: e.transpose(out, in_, ident), reads, writes)

    def act(self, out, in_, func, reads, writes, bias=None, scale=None, accum_out=None, eng="act"):
        kw = {}
        if bias is not None:
            kw["bias"] = bias
        if scale is not None:
            kw["scale"] = scale
        if accum_out is not None:
            kw["accum_out"] = accum_out
        return self.P.add("act", lambda e: e.activation(out, in_, func, **kw), reads, writes)

    def tt(self, eng, out, in0, in1, op, reads, writes):
        return self.P.add(eng, lambda e: e.tensor_tensor(out, in0, in1, op), reads, writes)

    def ts(self, eng, out, in0, s1, s2, op0, op1, reads, writes):
        if s2 is None:
            return self.P.add(eng, lambda e: e.tensor_scalar(out, in0, s1, None, op0), reads, writes)
        return self.P.add(eng, lambda e: e.tensor_scalar(out, in0, s1, s2, op0, op1), reads, writes)

    def stt(self, eng, out, in0, scalar, in1, op0, op1, reads, writes):
        return self.P.add(eng, lambda e: e.scalar_tensor_tensor(out, in0, scalar, in1, op0, op1), reads, writes)

    def copy(self, eng, out, in_, reads, writes):
        if eng == "act":
            return self.P.add("act", lambda e: e.copy(out, in_), reads, writes)
        return self.P.add(eng, lambda e: e.tensor_copy(out, in_), reads, writes)

    def memset(self, eng, ap, val, writes):
        return self.P.add(eng, lambda e: e.memset(ap, val), (), writes)

    def dma(self, eng, out, in_, reads, writes, dma_buf, indep=False, **kw):
        return self.P.add(eng, lambda e: e.dma_start(out, in_, **kw), reads, writes, dma_buf=dma_buf, indep=indep)


D = 1024
KC = 8
S = 4096
T = 512
NT = S // T
FF = 2816
FC = 22
EPS = 1e-6
NRING = 3
HEAD_OF_SLOT = []
for _c in range(8):
    HEAD_OF_SLOT.append([0, 1, 2, 3, 8, 9, 10, 11][_c])
    HEAD_OF_SLOT.append([4, 5, 6, 7, 12, 13, 14, 15][_c])
NEG_MASK = -30000.0

WS_IN = 0
WS_OUT0 = 16
WS_QK = 24
WS_OUT1 = 34
NWS = 42


class WPool:
    def __init__(self, C, name, nslots, shape):
        self.C = C
        self.n = nslots
        self.t = [C.sbuf("%s%d" % (name, i), shape, BF16) for i in range(nslots)]
        self.b = [C.dbuf("%s%d" % (name, i)) for i in range(nslots)]
        self.sb = [C.dbuf("%s%dst" % (name, i)) for i in range(nslots)]
        self.req = []
        self.chunk = {}
        self.emitted = 0
        self.cons = 0

    def _top(self, upto):
        C = self.C
        upto = min(upto, len(self.req))
        while self.emitted < upto:
            i = self.emitted
            key, src32, scr = self.req[i]
            k = i % self.n
            if key not in self.chunk:
                cb = Buf("chunk")
                self.chunk[key] = cb
                C.dma("pool", self.t[k][:], src32, (), (self.b[k],), self.b[k])
                C.dma("sp", scr, self.t[k][:], (self.b[k],), (cb,), self.sb[k])
            else:
                C.dma("sp", self.t[k][:], scr, (self.chunk[key],), (self.b[k],), self.b[k])
            self.emitted += 1

    def prefetch(self):
        self._top(self.cons + self.n)

    def next(self):
        i = self.cons
        assert i < len(self.req), "weight pool underflow"
        self._top(i + self.n)
        self.cons += 1
        return self.t[i % self.n], self.b[i % self.n]


def build_program(debug=False, NT=NT, stages=3):
    nc = bass.Bass("TRN2", target_bir_lowering=False)
    dt_in = lambda name, shape: nc.dram_tensor(name, list(shape), F32, kind="ExternalInput").ap()
    x_d = dt_in("x", [S, D])
    ws_d = dt_in("ws", [NWS, 128, 1024])
    gu_d = dt_in("gu", [2 * FC, 2, 128, 1024])
    wd_d = dt_in("wd", [16, 128, FC, 128])
    wv0_d = dt_in("wv0", [128, 8, 512])
    wv1_d = dt_in("wv1", [128, 8, 256])
    gall_d = dt_in("gall", [128, 4, 8])
    gfin_d = dt_in("gfin", [128, 1024])
    lng_d = dt_in("lng", [128, 512])
    lnb_d = dt_in("lnb", [128, 512])
    bsp_d = dt_in("bsp", [128, 4, 128])
    wsp_d = dt_in("wsp", [128, 4, 128])
    cw_d = dt_in("cw", [128, 3, 4])
    bias_d = dt_in("biasT", [128, 16, 384])
    sink_d = dt_in("sinkb", [128, 16])
    id_d = dt_in("ident", [128, 128])
    out_d = nc.dram_tensor("out", [S, D], F32, kind="ExternalOutput").ap()
    if debug:
        dbg_d = nc.dram_tensor("dbg", [5, NT, 128, 8, 512], F32, kind="ExternalOutput").ap()
    ws_s = nc.dram_tensor("ws_s", [NWS, 128, 1024], BF16).ap()
    gu_s = nc.dram_tensor("gu_s", [2 * FC, 2, 128, 1024], BF16).ap()
    wd_s = nc.dram_tensor("wd_s", [16, 128, FC, 128], BF16).ap()
    wv0_s = nc.dram_tensor("wv0_s", [128, 8, 512], BF16).ap()
    wv1_s = nc.dram_tensor("wv1_s", [128, 8, 256], BF16).ap()

    with ExitStack() as st:
        C = Ctx(nc, st)
        P = C.P
        xr = C.sbuf("xr", [128, NRING, KC, T], F32)
        xr_b = [[Buf("xr%d_%d" % (r, c)) for c in range(KC)] for r in range(NRING)]
        hT = C.sbuf("hT", [128, KC, T + 2], BF16)
        hT_b = [Buf("hT%d" % c) for c in range(KC)]
        hTx_b = Buf("hTx")
        arena = C.sbuf("arena", [128, 32 * 256], F32)
        pg = [Buf("pg%d" % i) for i in range(32)]

        def av_bf(p0, np_):
            return arena[:, p0 * 256:(p0 + np_) * 256].bitcast(BF16)

        def av_f32(p0, np_):
            return arena[:, p0 * 256:(p0 + np_) * 256]

        kring = C.sbuf("kring", [128, 2, 4 * T], BF16)
        kr_b = [[Buf("k%d_%d" % (c, b)) for b in range(16)] for c in range(2)]
        vring = C.sbuf("vring", [128, 16, 4, 65], BF16)
        vr_b = [Buf("v%d" % b) for b in range(16)]
        qT = C.sbuf("qT", [128, KC, T], BF16)
        qT_b = [Buf("qT%d" % c) for c in range(KC)]
        biasT = C.sbuf("biasTs", [128, 16, 384], BF16)
        biasT_b = C.dbuf("biasT")
        xstage2 = [C.sbuf("xstage%d" % i, [128, D], F32) for i in range(2)]
        xstage2_b = [C.dbuf("xstage%d" % i) for i in range(2)]
        ostage = C.sbuf("ostage", [128, D], F32)
        ostage_b = Buf("ostage")
        ostore_b = C.dbuf("ostore")
        sq = [C.sbuf("sq%d" % i, [128, T], BF16) for i in range(3)]
        sq_b = [Buf("sq%d" % i) for i in range(3)]
        nrm_t = sq
        nrm_b = sq_b
        sqx = C.sbuf("sqx", [128, 8], BF16)
        sqx_b = Buf("sqx")
        tbuf = C.sbuf("tbuf", [128, T], F32)
        tbuf_b = Buf("tbuf")
        rsb = C.sbuf("rsb", [128, T], F32)
        rsb_b = Buf("rsb")
        small = C.sbuf("small", [128, 256], F32)
        zext = [C.sbuf("zext%d" % i, [128, T + 4], F32) for i in range(2)]
        zext_b = [Buf("zext%d" % i) for i in range(2)]
        zprev = C.sbuf("zprev", [128, 4], F32)
        zprev_b = [Buf("zprev%d" % j) for j in range(4)]
        ident_f = C.sbuf("ident_f", [128, 128], F32)
        ident_fb = C.dbuf("ident_f")
        ident_b = C.sbuf("ident_b", [128, 128], BF16)
        ident_bb = Buf("ident_b")
        ones_b = C.sbuf("ones_b", [128, 128], BF16)
        ones_bb = Buf("ones_b")
        epsc = C.sbuf("epsc", [128, 1], F32)
        eps_b = Buf("epsc")
        neghalf = C.sbuf("neghalf", [128, 8], F32)
        neghalf_b = Buf("neghalf")
        gfin = C.sbuf("gfin", [128, D], F32)
        lng = C.sbuf("lng", [128, 512], F32)
        lnb = C.sbuf("lnb", [128, 512], F32)
        bsp = C.sbuf("bsp", [128, 4, 128], F32)
        wsp_f = C.sbuf("wsp_f", [128, 4, 128], F32)
        wsp_b = C.sbuf("wsp_b", [128, 4, 128], BF16)
        wsp_bb = Buf("wsp_b")
        gall = C.sbuf("gall", [128, 4, 8], F32)
        cw = C.sbuf("cw", [128, 3, 4], F32)
        sinkb = C.sbuf("sinkb", [128, 16], F32)
        negc = C.sbuf("negc", [128, 16], F32)
        sinkterm = C.sbuf("sinkterm", [128, 16], F32)
        att_b = Buf("attconst")
        cst_b = C.dbuf("consts")
        junk = C.sbuf("junk", [128, T], BF16)
        junk_b = Buf("junk")

        _sm = [0]

        def sm(n):
            a = small[:, _sm[0]:_sm[0] + n]
            _sm[0] += n
            assert _sm[0] <= 256
            return a

        psb = [C.psum("ps%d" % i, [128, 512], F32) for i in range(8)]
        ps_b = [Buf("ps%d" % i) for i in range(8)]
        _pr = [0]

        def next_ps():
            k = _pr[0] % 5
            _pr[0] += 1
            return psb[k], ps_b[k]

        acc_t = psb[5:8]
        acc_b = ps_b[5:8]
        xps_b = Buf("xps")

        C.dma("pool", biasT[:], bias_d, (), (biasT_b,), biasT_b)

        for (dst, src) in ((ident_f, id_d), (gfin, gfin_d), (lng, lng_d), (lnb, lnb_d), (bsp, bsp_d),
                           (wsp_f, wsp_d), (gall, gall_d), (cw, cw_d), (sinkb, sink_d)):
            C.dma("sp", dst[:], src, (), (cst_b,), cst_b, indep=True)
        C.copy("dve", ident_b[:], ident_f[:], (cst_b,), (ident_bb,))
        C.memset("pool", ones_b[:], 1.0, (ones_bb,))
        C.memset("pool", neghalf[:], -0.5, (neghalf_b,))
        C.memset("pool", epsc[:], EPS, (eps_b,))
        C.copy("dve", wsp_b[:], wsp_f[:], (cst_b,), (wsp_bb,))
        C.memset("pool", vring[:], 1.0, vr_b)
        C.ts("dve", negc[:], sinkb[:], 0.0, -1.0, ALU.max, ALU.mult, (cst_b,), (att_b,))
        C.tt("dve", sinkterm[:], sinkb[:], negc[:], ALU.add, (cst_b, att_b), (att_b,))
        C.act(sinkterm[:], sinkterm[:], AF.Exp, (att_b,), (att_b,))

        proj = WPool(C, "wproj", 4, [128, KC, 128])
        gup = WPool(C, "wgu", 3, [128, 2, KC, 128])
        dnp = WPool(C, "wdn", 2, [128, 2, 11 * 128])
        wv0p = WPool(C, "wv0p", 1, [128, KC, 512])
        wv1p = WPool(C, "wv1p", 1, [128, KC, 256])

        def ws_req(idx):
            return (("ws", idx), ws_d[idx].rearrange("p (k j) -> p k j", k=KC), ws_s[idx].rearrange("p (k j) -> p k j", k=KC))

        def ffn_reqs(layer):
            for fc in range(FC):
                i = layer * FC + fc
                gup.req.append((("gu", i), gu_d[i].rearrange("t p (k j) -> p t k j", k=KC),
                                gu_s[i].rearrange("t p (k j) -> p t k j", k=KC)))
            for dc in range(8):
                i = layer * 8 + dc
                dnp.req.append((("wd", i), wd_d[i].rearrange("p (a b) j -> p a (b j)", a=2),
                                wd_s[i].rearrange("p (a b) j -> p a (b j)", a=2)))

        for s in range(NT + 1):
            if s < NT:
                wv0p.req.append((("wv0", 0), wv0_d, wv0_s))
                for j in range(4):
                    for t3 in range(3):
                        proj.req.append(ws_req(WS_IN + 4 + 4 * t3 + j))
                for g in range(4):
                    proj.req.append(ws_req(WS_IN + g))
                for dc in range(8):
                    proj.req.append(ws_req(WS_OUT0 + dc))
                ffn_reqs(0)
                if stages >= 2:
                    wv1p.req.append((("wv1", 0), wv1_d, wv1_s))
                    for k2 in range(2):
                        proj.req.append(ws_req(WS_QK + 8 + k2))
            if s >= 1 and stages >= 2:
                for dc in range(8):
                    proj.req.append(ws_req(WS_OUT1 + dc))
                ffn_reqs(1)
            if s < NT and stages >= 2:
                for c in range(8):
                    proj.req.append(ws_req(WS_QK + c))

        def proj_fm(w_t, w_b, rhs_of_kc, rhs_bufs, n, nk=KC, ps=None):
            if ps is None:
                ps = next_ps()
            pt, pb = ps
            for kc in range(nk):
                C.mm(pt[:, 0:n], w_t[:, kc, :], rhs_of_kc(kc), kc == 0, kc == nk - 1,
                     (w_b, rhs_bufs[kc]), (pb,))
            return pt, pb

        def resid_add(slot, dc, pt, pb):
            xa = xr[:, slot, dc, :]
            C.tt("dve", xa, xa, pt[:, 0:T], ALU.add, (pb,), (xr_b[slot][dc],))

        def norm(slot, gi, extra_slot=None, filler=None):
            pt, pb = next_ps()
            for c in range(KC):
                k = c % 3
                xa = xr[:, slot, c, :]
                if c % 2 == 0:
                    C.act(sq[k][:], xa, AF.Square, (xr_b[slot][c],), (sq_b[k],))
                else:
                    C.tt("dve", sq[k][:], xa, xa, ALU.mult, (xr_b[slot][c],), (sq_b[k],))
                C.mm(pt[:, 0:T], ones_b[:], sq[k][:], c == 0, c == KC - 1, (ones_bb, sq_b[k]), (pb,))
            C.act(tbuf[:], pt[:, 0:T], AF.Ln, (pb, eps_b), (tbuf_b,), bias=epsc[:, 0:1], scale=1.0 / D)
            C.act(rsb[:], tbuf[:], AF.Exp, (tbuf_b,), (rsb_b,), scale=-0.5)
            for c in range(KC):
                C.stt("dve", hT[:, c, 0:T], xr[:, slot, c, :], gall[:, gi, c:c + 1], rsb[:], ALU.mult, ALU.mult,
                      (xr_b[slot][c], rsb_b, cst_b), (hT_b[c],))
            if filler is not None:
                filler()
            if extra_slot is not None:
                return lambda: norm_extra(gi, extra_slot)
            return None

        def norm_extra(gi, extra_slot):
            if True:
                xcol = xr[:, extra_slot, :, 0]
                xb = xr_b[extra_slot]
                C.tt("dve", sqx[:], xcol, xcol, ALU.mult, xb, (sqx_b,))
                pt2, pb2 = next_ps()
                for c in range(KC):
                    C.mm(pt2[:, 0:1], ones_b[:], sqx[:, c:c + 1], c == 0, c == KC - 1, (ones_bb, sqx_b), (pb2,))
                t1 = sm_t1
                C.act(t1, pt2[:, 0:1], AF.Ln, (pb2, eps_b), (smx_b,), bias=epsc[:, 0:1], scale=1.0 / D)
                C.act(sm_rs1, t1, AF.Exp, (smx_b,), (smx_b,), scale=-0.5)
                C.tt("dve", sm_x8, xcol, gall[:, gi, :], ALU.mult, tuple(xb) + (cst_b,), (smx_b,))
                C.ts("dve", hT[:, :, T], sm_x8, sm_rs1, None, ALU.mult, None, (smx_b,), (hTx_b,))

        sm_t1 = sm(1)
        sm_rs1 = sm(1)
        sm_x8 = sm(8)
        smx_b = Buf("smx")

        def load_dma(t, tb):
            r0 = t * T + tb * 128
            C.dma("sp", xstage2[tb % 2][:], x_d[r0:r0 + 128, :], (), (xstage2_b[tb % 2],), xstage2_b[tb % 2])

        def load_tr(t, tb):
            slot = t % NRING
            xstage = xstage2[tb % 2]
            xstage_b = xstage2_b[tb % 2]
            for half in range(2):
                pt, pb = next_ps()
                for c4 in range(4):
                    c = half * 4 + c4
                    C.tr(pt[:, c4 * 128:(c4 + 1) * 128], xstage[:, c * 128:(c + 1) * 128], ident_f[:],
                         (xstage_b, ident_fb), (pb,))
                dst = xr[:, slot, half * 4:half * 4 + 4, tb * 128:(tb + 1) * 128]
                C.copy("act", dst, pt[:, 0:512].rearrange("p (c n) -> p c n", c=4), (pb,),
                       tuple(xr_b[slot][half * 4:half * 4 + 4]))

        def load_a(t):
            load_tr(t, 0)
            load_tr(t, 1)
            load_dma(t, 2)
            load_dma(t, 3)

        def load_b(t):
            load_tr(t, 2)
            load_tr(t, 3)
            if t + 1 < NT:
                load_dma(t + 1, 0)
                load_dma(t + 1, 1)

        def tmp_slot(i):
            p0 = 18 + 2 * (i % 6)
            return av_f32(p0, 2), (pg[p0], pg[p0 + 1])

        _tmpi = [0]

        def next_tmp():
            r = tmp_slot(_tmpi[0])
            _tmpi[0] += 1
            return r

        ln_st = [sm(6) for _ in range(4)]
        ln_mv = [sm(2) for _ in range(4)]
        ln_rt = [sm(1) for _ in range(4)]
        ln_rs = [sm(1) for _ in range(4)]
        ln_b = [Buf("ln%d" % i) for i in range(4)]
        xc_sb = sm(4)
        xc_b = [Buf("xc%d" % j) for j in range(4)]

        def mixer0(s):
            slot = s % NRING
            nslot = (s + 1) % NRING if s + 1 < NT else None
            extra = norm(slot, 0, nslot, filler=(lambda: load_a(s + 1)) if s + 1 < NT else None)
            gup.prefetch()
            hb = hT_b
            wv_t, wv_b = wv0p.next()
            for tb in range(4):
                pt, pb = next_ps()
                for kc in range(KC):
                    C.mm(pt[:, 0:512], hT[:, kc, tb * 128:(tb + 1) * 128], wv_t[:, kc, :], kc == 0, kc == KC - 1,
                         (hb[kc], wv_b), (pb,))
                gv, gvb = next_tmp()
                C.act(gv, pt[:, 0:512], AF.Gelu, (pb,), gvb)
                C.P.add("dve", lambda e, o=ln_st[tb], i=gv: e.bn_stats(o, i), gvb, (ln_b[tb],))
                C.P.add("dve", lambda e, o=ln_mv[tb], i=ln_st[tb]: e.bn_aggr(o, i), (ln_b[tb],), (ln_b[tb],))
                C.ts("dve", ln_rt[tb], ln_mv[tb][:, 1:2], EPS, None, ALU.add, None, (ln_b[tb],), (ln_b[tb],))
                C.tt("pool", ln_rs[tb], ln_rt[tb], neghalf[:, 0:1], ALU.pow, (ln_b[tb], neghalf_b), (ln_b[tb],))
                C.ts("dve", gv, gv, ln_mv[tb][:, 0:1], ln_rs[tb], ALU.subtract, ALU.mult, (ln_b[tb],), gvb)
                C.tt("pool", gv, gv, lng[:], ALU.mult, (cst_b,), gvb)
                vt = av_bf(8 + tb, 1)
                C.tt("pool", vt, gv, lnb[:], ALU.add, gvb + (cst_b,), (pg[8 + tb],))
            if extra is not None:
                extra()
            for j in range(4):
                z = zext[j % 2]
                zb = zext_b[j % 2]
                bb = av_f32(14 + 2 * (j % 2), 2)
                bbb = (pg[14 + 2 * (j % 2)], pg[15 + 2 * (j % 2)])
                xp = acc_t[2]

                def halo(col, wt_, wb_):
                    if nslot is None:
                        return
                    for kc in range(KC):
                        C.mm(xp[:, col:col + 1], wt_[:, kc, :], hT[:, kc, T:T + 1], kc == 0, kc == KC - 1,
                             (wb_, hTx_b), (xps_b,))

                wb_t, wb_b = proj.next()
                pt, pb = proj_fm(wb_t, wb_b, lambda kc: hT[:, kc, 0:T], hb, T)
                C.copy("act", bb, pt[:, 0:T], (pb,), bbb)
                wc_t, wc_b = proj.next()
                ptc, pbc = proj_fm(wc_t, wc_b, lambda kc: hT[:, kc, 0:T], hb, T)
                halo(496 + 2 * j, wc_t, wc_b)
                tc_, tcb = next_tmp()
                C.copy("act", tc_, ptc[:, 0:T], (pbc,), tcb)
                wh_t, wh_b = proj.next()
                pth, pbh = proj_fm(wh_t, wh_b, lambda kc: hT[:, kc, 0:T], hb, T)
                halo(497 + 2 * j, wh_t, wh_b)
                C.tt("dve", z[:, 1:T + 1], tc_, pth[:, 0:T], ALU.mult, tcb + (pbh,), (zb,))
                if s > 0:
                    C.copy("pool", z[:, 0:1], zprev[:, j:j + 1], (zprev_b[j],), (zb,))
                else:
                    C.memset("pool", z[:, 0:1], 0.0, (zb,))
                if nslot is not None:
                    C.copy("act", xc_sb[:, j:j + 1], xp[:, 496 + 2 * j:497 + 2 * j], (xps_b,), (xc_b[j],))
                    C.tt("dve", z[:, T + 1:T + 2], xc_sb[:, j:j + 1], xp[:, 497 + 2 * j:498 + 2 * j], ALU.mult,
                         (xc_b[j], xps_b), (zb,))
                else:
                    C.memset("pool", z[:, T + 1:T + 2], 0.0, (zb,))
                C.copy("pool", zprev[:, j:j + 1], z[:, T:T + 1], (zb,), (zprev_b[j],))
                ct, ctb = next_tmp()
                C.ts("dve", ct, z[:, 0:T], cw[:, 0, j:j + 1], None, ALU.mult, None, (zb, cst_b), ctb)
                C.stt("dve", ct, z[:, 1:T + 1], cw[:, 1, j:j + 1], ct, ALU.mult, ALU.add, (zb, cst_b), ctb)
                C.stt("dve", ct, z[:, 2:T + 2], cw[:, 2, j:j + 1], ct, ALU.mult, ALU.add, (zb, cst_b), ctb)
                C.tt("pool", av_bf(4 + j, 1), ct, bb, ALU.mult, ctb + bbb, (pg[4 + j],))
            for g in range(4):
                w_t, w_b = proj.next()
                pt, pb = proj_fm(w_t, w_b, lambda kc: hT[:, kc, 0:T], hb, T)
                au = av_bf(12 + g % 2, 1)
                aub = pg[12 + g % 2]
                C.act(au, pt[:, 0:T], AF.Gelu, (pb,), (aub,))
                pt2, pb2 = next_ps()
                for tb in range(4):
                    vt = av_bf(8 + tb, 1)
                    C.mm(pt2[:, tb * 128:(tb + 1) * 128], vt[:, g * 128:(g + 1) * 128], wsp_b[:, g, :], True, True,
                         (pg[8 + tb], wsp_bb), (pb2,))
                t1, t1b = next_tmp()
                C.tt("dve", t1.rearrange("p (t q) -> p t q", t=4), pt2[:, 0:512].rearrange("p (t q) -> p t q", t=4),
                     bsp[:, g:g + 1, :].broadcast_to([128, 4, 128]), ALU.add, (pb2, cst_b), t1b)
                C.tt("pool", av_bf(g, 1), t1, au, ALU.mult, t1b + (aub,), (pg[g],))
            for dc in range(8):
                w_t, w_b = proj.next()
                pt, pb = proj_fm(w_t, w_b, lambda kc: av_bf(kc, 1), pg[0:8], T)
                resid_add(slot, dc, pt, pb)

        def ffn(t, layer):
            slot = t % NRING
            norm(slot, 1 + 2 * layer, filler=(lambda: load_b(t + 1)) if (layer == 0 and t + 1 < NT) else None)
            dnp.prefetch()
            for fc in range(FC):
                gu_t, gu_b = gup.next()
                ptg, pbg = next_ps()
                ptu, pbu = next_ps()
                for kc in range(KC):
                    C.mm(ptg[:, 0:T], gu_t[:, 0, kc, :], hT[:, kc, 0:T], kc == 0, kc == KC - 1, (gu_b, hT_b[kc]), (pbg,))
                for kc in range(KC):
                    C.mm(ptu[:, 0:T], gu_t[:, 1, kc, :], hT[:, kc, 0:T], kc == 0, kc == KC - 1, (gu_b, hT_b[kc]), (pbu,))
                p0 = 22 + 2 * (fc % 2)
                sg = av_f32(p0, 2)
                sgb = (pg[p0], pg[p0 + 1])
                C.act(sg, ptg[:, 0:T], AF.Silu, (pbg,), sgb)
                C.tt("dve", av_bf(fc, 1), sg, ptu[:, 0:T], ALU.mult, sgb + (pbu,), (pg[fc],))
            proj.prefetch()
            for dc in range(8):
                wd_t, wd_b = dnp.next()
                pt, pb = next_ps()
                for fc in range(FC):
                    C.mm(pt[:, 0:T], wd_t[:, fc // 11, (fc % 11) * 128:(fc % 11 + 1) * 128], av_bf(fc, 1), fc == 0, fc == FC - 1, (wd_b, pg[fc]), (pb,))
                resid_add(slot, dc, pt, pb)

        def l1_kv(s):
            slot = s % NRING
            norm(slot, 2)
            wv_t, wv_b = wv1p.next()
            rb = (s % 4) * 4
            for k2 in range(2):
                w_t, w_b = proj.next()
                pt, pb = proj_fm(w_t, w_b, lambda kc: hT[:, kc, 0:T], hT_b, T)
                C.copy("act", kring[:, k2, rb * 128:rb * 128 + T], pt[:, 0:T], (pb,), tuple(kr_b[k2][rb:rb + 4]))
            for tb in range(4):
                pt, pb = next_ps()
                for kc in range(KC):
                    C.mm(pt[:, 0:256], hT[:, kc, tb * 128:(tb + 1) * 128], wv_t[:, kc, :], kc == 0, kc == KC - 1,
                         (hT_b[kc], wv_b), (pb,))
                C.copy("dve", vring[:, rb + tb, :, 0:64], pt[:, 0:256].rearrange("p (g d) -> p g d", g=4), (pb,),
                       (vr_b[rb + tb],))

        def l1_q(s):
            slot = s % NRING
            fin = s >= 1
            if fin:
                final_block(s - 1, 0)
            norm(slot, 2)
            if fin:
                final_block(s - 1, 1)
            for c in range(8):
                w_t, w_b = proj.next()
                pt, pb = proj_fm(w_t, w_b, lambda kc: hT[:, kc, 0:T], hT_b, T)
                C.act(qT[:, c, :], pt[:, 0:T], AF.Copy, (pb,), (qT_b[c],), scale=0.125)
                if fin and c == 1:
                    final_block(s - 1, 2)
                if fin and c == 3:
                    final_block(s - 1, 3)

        den = sm(16)
        rden = sm(16)
        den_b = Buf("den")

        def attention(t):
            slot = t % NRING
            proj.prefetch()
            gup.prefetch()
            for qb in range(4):
                gb = 4 * t + qb
                js = [j for j in range(3) if 0 <= gb - 1 + j < 32]
                nj = len(js)
                j0 = js[0]
                ao = av_bf(8 + 2 * (qb % 2), 2)
                aob = (pg[8 + 2 * (qb % 2)], pg[9 + 2 * (qb % 2)])
                def emit_st2(cq):
                    sts = []
                    for half in range(2):
                        hs = 2 * cq + half
                        st_t, st_b = next_ps()
                        sts.append((st_t, st_b))
                        C.mm(st_t[:, 0:nj * 128], ident_b[:], biasT[:, hs, j0 * 128:(j0 + nj) * 128], True, False,
                             (ident_bb, biasT_b), (st_b,))
                    for idx, j in enumerate(js):
                        kb = (gb - 1 + j) % 16
                        for half in range(2):
                            hs = 2 * cq + half
                            g = HEAD_OF_SLOT[hs] // 4
                            assert g % 2 == half
                            k2 = g // 2
                            st_t, st_b = sts[half]
                            C.mm(st_t[:, idx * 128:(idx + 1) * 128],
                                 kring[half * 64:(half + 1) * 64, k2, kb * 128:(kb + 1) * 128],
                                 qT[half * 64:(half + 1) * 64, cq, qb * 128:(qb + 1) * 128],
                                 False, idx == nj - 1, (kr_b[k2][kb], qT_b[cq]), (st_b,))
                    for half in range(2):
                        hs = 2 * cq + half
                        st_t, st_b = sts[half]
                        ptile = av_bf(12 + hs % 4, 1)
                        ptb = pg[12 + hs % 4]
                        C.act(ptile[:, 0:nj * 128], st_t[:, 0:nj * 128], AF.Exp, (st_b, att_b), (ptb,),
                              bias=negc[:, hs:hs + 1], scale=1.0)

                def emit_pv(hs):
                    g = HEAD_OF_SLOT[hs] // 4
                    ptile = av_bf(12 + hs % 4, 1)
                    ptb = pg[12 + hs % 4]
                    bank = hs // 7
                    col = (hs % 7) * 65
                    for idx, j in enumerate(js):
                        kb = (gb - 1 + j) % 16
                        C.mm(acc_t[bank][:, col:col + 65], ptile[:, idx * 128:(idx + 1) * 128], vring[:, kb, g, :],
                             idx == 0, idx == nj - 1, (ptb, vr_b[kb]), (acc_b[bank],))

                emit_st2(0)
                for cq in range(1, 8):
                    emit_st2(cq)
                    emit_pv(2 * cq - 2)
                    emit_pv(2 * cq - 1)
                emit_pv(14)
                emit_pv(15)
                for bank in range(3):
                    h0 = bank * 7
                    h1 = min(16, h0 + 7)
                    nh = h1 - h0
                    a3 = acc_t[bank][:, 0:nh * 65].rearrange("p (h e) -> p h e", e=65)
                    C.tt("dve", den[:, h0:h1], a3[:, :, 64], sinkterm[:, h0:h1], ALU.add, (acc_b[bank], att_b), (den_b,))
                    C.P.add("dve", lambda e, o=rden[:, h0:h1], i=den[:, h0:h1]: e.reciprocal(o, i), (den_b,), (den_b,))
                    C.tt("dve", ao[:, h0 * 64:h1 * 64].rearrange("p (h d) -> p h d", d=64), a3[:, :, 0:64],
                         rden[:, h0:h1].unsqueeze(2).broadcast_to([128, nh, 64]), ALU.mult, (acc_b[bank], den_b), aob)
                tp, tpb = next_ps()
                tpv = tp[:, 0:512].bitcast(BF16)
                for c in range(8):
                    C.tr(tpv[:, c * 128:(c + 1) * 128], ao[:, c * 128:(c + 1) * 128], ident_b[:], aob + (ident_bb,), (tpb,))
                dst = av_bf(0, 8).rearrange("p (c n) -> p c n", c=8)[:, :, qb * 128:(qb + 1) * 128]
                C.copy("act", dst, tpv.rearrange("p (c n) -> p c n", c=8), (tpb,), tuple(pg[0:8]))
            for dc in range(8):
                w_t, w_b = proj.next()
                pt, pb = proj_fm(w_t, w_b, lambda kc: av_bf(kc, 1), pg[0:8], T)
                resid_add(slot, dc, pt, pb)

        fss = [sm(1), sm(1)]
        fs_t = sm(1)
        fs_r = sm(1)
        fs_b = Buf("fs")

        def final_block(t, tb):
            slot = t % NRING
            pts = [(acc_t[0], acc_b[0]), (acc_t[1], acc_b[1])]
            for c in range(8):
                pt, pb = pts[c // 4]
                C.tr(pt[:, (c % 4) * 128:(c % 4 + 1) * 128], xr[:, slot, c, tb * 128:(tb + 1) * 128], ident_f[:],
                     (xr_b[slot][c], ident_fb), (pb,))
            for h2 in range(2):
                pt, pb = pts[h2]
                C.act(junk[:], pt[:, 0:512], AF.Square, (pb,), (junk_b, fs_b), accum_out=fss[h2])
            C.tt("dve", fs_t, fss[0], fss[1], ALU.add, (fs_b,), (fs_b,))
            C.ts("dve", fs_t, fs_t, 1.0 / D, EPS, ALU.mult, ALU.add, (fs_b,), (fs_b,))
            C.tt("pool", fs_r, fs_t, neghalf[:, 0:1], ALU.pow, (fs_b, neghalf_b), (fs_b,))
            for h2 in range(2):
                pt, pb = pts[h2]
                C.stt("dve", ostage[:, h2 * 512:(h2 + 1) * 512], pt[:, 0:512], fs_r, gfin[:, h2 * 512:(h2 + 1) * 512],
                      ALU.mult, ALU.mult, (pb, fs_b, cst_b, ostore_b), (ostage_b,))
            r0 = t * T + tb * 128
            C.dma("sp", out_d[r0:r0 + 128, :], ostage[:], (ostage_b,), (ostore_b,), ostore_b)

        def final(t):
            for tb in range(4):
                final_block(t, tb)

        def dump(k, t):
            if not debug:
                return
            slot = t % NRING
            b = C.dbuf("dbg%d_%d" % (k, t))
            C.dma("sp", dbg_d[k, t], xr[:, slot], tuple(xr_b[slot]), (), b)
            P.final_waits.append(b)

        load_dma(0, 0)
        load_dma(0, 1)
        load_a(0)
        load_b(0)
        for s in range(NT + 1):
            if s < NT:
                dump(0, s)
                mixer0(s)
                dump(1, s)
                ffn(s, 0)
                dump(2, s)
                if stages >= 2:
                    l1_kv(s)
            if s >= 1 and stages >= 2:
                attention(s - 1)
                dump(3, s - 1)
                ffn(s - 1, 1)
                dump(4, s - 1)
                if s == NT:
                    final(s - 1)
            if s < NT and stages >= 2:
                l1_q(s)

        fw = [(ostore_b.sem, ostore_b.semcnt)]
        for b in P.final_waits:
            fw.append((b.sem, b.semcnt))
        P.final_waits = fw
        print("SBUF bytes remaining per partition:", nc.sbuf_bytes_remaining, "ops:", {e: len(P.ops[e]) for e in ENGS})
        with nc.Block() as block:
            P.emit_all(nc, block, C.engsem)
    return nc


def _t5_bucket_table():
    nb = 16
    max_exact = 8
    rel = np.arange(-255, 256)
    ret = np.where(rel > 0, nb, 0)
    n = np.abs(rel)
    nf = np.maximum(n, 1).astype(np.float32)
    large = max_exact + (np.log(nf / max_exact) / np.log(128 / max_exact) * (nb - max_exact)).astype(np.int32)
    large = np.minimum(large, nb - 1)
    return ret + np.where(n < max_exact, n, large)


def _chunks_kmajor(W):
    K, E = W.shape
    a = W.reshape(K // 128, 128, E // 128, 128)
    return np.ascontiguousarray(a.transpose(2, 1, 0, 3)).reshape(E // 128, 128, (K // 128) * 128)


def prep_shared(inp):
    f = lambda a: np.ascontiguousarray(np.asarray(a, dtype=np.float32))
    w_in = f(inp["even_w_in"])[0]
    cols = np.concatenate([np.arange(0, 512), np.arange(1024, 2560)])
    ws_in = _chunks_kmajor(w_in[:, cols])
    ws_out0 = _chunks_kmajor(f(inp["even_w_out"])[0])
    wqkv = f(inp["attn_w_qkv"])[0]
    qcols = np.concatenate([np.arange(h * 64, h * 64 + 64) for h in HEAD_OF_SLOT])
    ws_q = _chunks_kmajor(wqkv[:, qcols])
    ws_k = _chunks_kmajor(wqkv[:, 1024:1280])
    wo1 = f(inp["attn_w_out"])[0][qcols, :]
    ws_out1 = _chunks_kmajor(wo1)
    ws = np.concatenate([ws_in, ws_out0, ws_q, ws_k, ws_out1], axis=0)
    assert ws.shape == (NWS, 128, 1024)
    gate = f(inp["ffn_w_gate"])
    up = f(inp["ffn_w_up"])
    down = f(inp["ffn_w_down"])
    gu = np.stack([np.stack([_chunks_kmajor(gate[l]), _chunks_kmajor(up[l])], axis=1) for l in range(2)], axis=0)
    gu = np.ascontiguousarray(gu.reshape(2 * FC, 2, 128, 1024))
    wd = np.stack([_chunks_kmajor(down[l]) for l in range(2)], axis=0).reshape(16, 128, FC, 128)
    wv0 = np.ascontiguousarray(w_in[:, 512:1024].reshape(8, 128, 512).transpose(1, 0, 2))
    wv1 = np.ascontiguousarray(wqkv[:, 1280:1536].reshape(8, 128, 256).transpose(1, 0, 2))
    nm = f(inp["norm_mix"])
    nf_ = f(inp["norm_ffn"])
    gl = np.stack([nm[0], nf_[0], nm[1], nf_[1]], axis=0)
    gall = np.ascontiguousarray(gl.reshape(4, 8, 128).transpose(2, 0, 1))
    rep = lambda v: np.ascontiguousarray(np.broadcast_to(v, (128,) + v.shape))
    gfin = rep(f(inp["final_norm"]))
    lng = rep(f(inp["even_v_ln_g"])[0])
    lnb = rep(f(inp["even_v_ln_b"])[0])
    bsp = rep(f(inp["even_b_spatial"])[0])
    wsp = np.ascontiguousarray(f(inp["even_w_spatial"])[0].transpose(2, 0, 1))
    cw = np.ascontiguousarray(f(inp["even_conv_w"])[0].reshape(3, 4, 128).transpose(2, 0, 1))
    tab = _t5_bucket_table()
    k = np.arange(128)[:, None, None]
    j = np.arange(3)[None, :, None]
    q = np.arange(128)[None, None, :]
    rel = (j - 1) * 128 + k - q
    bucket = tab[rel + 255]
    rb = f(inp["rel_bias"])[:, HEAD_OF_SLOT]
    bias = rb[bucket]
    band = (np.abs(rel) <= 128)[..., None]
    bias = np.where(band, bias, np.float32(NEG_MASK)).astype(np.float32)
    biasT = np.ascontiguousarray(bias.transpose(0, 3, 1, 2)).reshape(128, 16, 384)
    sinkb = rep(f(inp["attn_sink"])[0][HEAD_OF_SLOT])
    return {
        "ws": ws, "gu": gu, "wd": np.ascontiguousarray(wd), "wv0": wv0, "wv1": wv1, "gall": gall, "gfin": gfin,
        "lng": lng, "lnb": lnb, "bsp": bsp, "wsp": wsp, "cw": cw, "biasT": biasT, "sinkb": sinkb,
        "ident": np.eye(128, dtype=np.float32),
    }


_NC_CACHE = {}


def kernel(**inputs):
    x = np.ascontiguousarray(np.asarray(inputs["x"], dtype=np.float32))
    shared = prep_shared(inputs)
    if "nc" not in _NC_CACHE:
        _NC_CACHE["nc"] = build_program(False)
    nc = _NC_CACHE["nc"]
    in_maps = []
    for b in range(8):
        m = dict(shared)
        m["x"] = x[b]
        in_maps.append(m)
    res = run_bass_kernel_spmd(nc, in_maps, core_ids=list(range(8)))
    out = np.stack([np.asarray(r["out"], dtype=np.float32) for r in res.results], axis=0)
    return out
```

```python
import numpy as np
from contextlib import ExitStack
import concourse.bass as bass
import concourse.mybir as mybir
from concourse.bass_utils import run_bass_kernel_spmd

F32 = mybir.dt.float32
BF16 = mybir.dt.bfloat16
AF = mybir.ActivationFunctionType
ALU = mybir.AluOpType
AX = mybir.AxisListType

ENGS = ("pe", "act", "dve", "pool", "sp")


class Buf:
    __slots__ = ("name", "w", "r", "rd", "sem", "semcnt")

    def __init__(self, name, sem=None):
        self.name = name
        self.w = None
        self.r = {}
        self.rd = []
        self.sem = sem
        self.semcnt = 0


class Op:
    __slots__ = ("eng", "emit", "deps", "mile", "mileno", "sem", "semval", "is_dma", "seq")


class Prog:
    def __init__(self):
        self.ops = {e: [] for e in ENGS}
        self.final_waits = []

    def add(self, eng, emit, reads=(), writes=(), dma_buf=None, indep=False):
        op = Op()
        op.eng = eng
        op.emit = emit
        op.mile = False
        op.mileno = 0
        op.is_dma = dma_buf is not None
        op.sem = None
        op.semval = 0
        op.seq = len(self.ops[eng])
        deps = []
        wset = set(id(b) for b in writes)
        for b in reads:
            if b.w is not None:
                deps.append(b.w)
        for b in writes:
            if b.w is not None and not indep:
                deps.append(b.w)
            deps.extend(b.r.values())
            deps.extend(b.rd)
        best = {}
        dl = []
        seen = set()
        for d in deps:
            if d.is_dma:
                if id(d) not in seen:
                    seen.add(id(d))
                    dl.append(d)
            else:
                if d.eng == "pe" and eng == "pe" and not op.is_dma:
                    continue
                cur = best.get(d.eng)
                if cur is None or d.seq > cur.seq:
                    best[d.eng] = d
        for d in best.values():
            d.mile = True
            dl.append(d)
        op.deps = dl
        if op.is_dma:
            dma_buf.semcnt += 16
            op.sem = dma_buf.sem
            op.semval = dma_buf.semcnt
        for b in writes:
            b.w = op
            b.r = {}
            b.rd = []
        for b in reads:
            if id(b) in wset:
                continue
            if op.is_dma:
                b.rd.append(op)
            else:
                b.r[eng] = op
        self.ops[eng].append(op)
        return op

    def emit_all(self, nc, block, engsem):
        for e in ENGS:
            n = 0
            for op in self.ops[e]:
                if op.mile and not op.is_dma:
                    n += 1
                    op.mileno = n
        prog = self

        def run(ename, eobj):
            known = {}
            for op in prog.ops[ename]:
                need = {}
                for d in op.deps:
                    if d.is_dma:
                        s, v = d.sem, d.semval
                    else:
                        s, v = engsem[d.eng], d.mileno
                    k = s.num
                    if k not in need or need[k][1] < v:
                        need[k] = (s, v)
                for k, (s, v) in need.items():
                    if known.get(k, 0) < v:
                        eobj.wait_ge(s, v)
                        known[k] = v
                ins = op.emit(eobj)
                if op.is_dma:
                    ins.then_inc(op.sem, 16)
                elif op.mile:
                    ins.then_inc(engsem[ename], 1)
            if ename == "sp":
                for (s, v) in prog.final_waits:
                    eobj.wait_ge(s, v)

        @block.tensor
        def _(e):
            run("pe", e)

        @block.scalar
        def _(e):
            run("act", e)

        @block.vector
        def _(e):
            run("dve", e)

        @block.gpsimd
        def _(e):
            run("pool", e)

        @block.sync
        def _(e):
            run("sp", e)


class Ctx:
    def __init__(self, nc, st):
        self.nc = nc
        self.st = st
        self.P = Prog()
        self.nsem = 0
        self.engsem = {}
        for e in ("pe", "act", "dve", "pool"):
            self.engsem[e] = st.enter_context(nc.semaphore("prog_" + e))

    def sbuf(self, name, shape, dt):
        return self.st.enter_context(self.nc.sbuf_tensor("sb_" + name, list(shape), dt))

    def psum(self, name, shape, dt):
        return self.st.enter_context(self.nc.psum_tensor("pp_" + name, list(shape), dt))

    def dbuf(self, name):
        self.nsem += 1
        s = self.st.enter_context(self.nc.semaphore("d_" + name))
        return Buf(name, sem=s)

    def mm(self, out, lhsT, rhs, start, stop, reads, writes):
        return self.P.add("pe", lambda e: e.matmul(out, lhsT, rhs, start=start, stop=stop), reads, writes)

    def tr(self, out, in_, ident, reads, writes):
        return self.P.add("pe", lambda e: e.transpose(out, in_, ident), reads, writes)

    def act(self, out, in_, func, reads, writes, bias=None, scale=None, accum_out=None, eng="act"):
        kw = {}
        if bias is not None:
            kw["bias"] = bias
        if scale is not None:
            kw["scale"] = scale
        if accum_out is not None:
            kw["accum_out"] = accum_out
        return self.P.add("act", lambda e: e.activation(out, in_, func, **kw), reads, writes)

    def tt(self, eng, out, in0, in1, op, reads, writes):
        return self.P.add(eng, lambda e: e.tensor_tensor(out, in0, in1, op), reads, writes)

    def ts(self, eng, out, in0, s1, s2, op0, op1, reads, writes):
        if s2 is None:
            return self.P.add(eng, lambda e: e.tensor_scalar(out, in0, s1, None, op0), reads, writes)
        return self.P.add(eng, lambda e: e.tensor_scalar(out, in0, s1, s2, op0, op1), reads, writes)

    def stt(self, eng, out, in0, scalar, in1, op0, op1, reads, writes):
        return self.P.add(eng, lambda e: e.scalar_tensor_tensor(out, in0, scalar, in1, op0, op1), reads, writes)

    def copy(self, eng, out, in_, reads, writes):
        if eng == "act":
            return self.P.add("act", lambda e: e.copy(out, in_), reads, writes)
        return self.P.add(eng, lambda e: e.tensor_copy(out, in_), reads, writes)

    def memset(self, eng, ap, val, writes):
        return self.P.add(eng, lambda e: e.memset(ap, val), (), writes)

    def dma(self, eng, out, in_, reads, writes, dma_buf, indep=False, **kw):
        return self.P.add(eng, lambda e: e.dma_start(out, in_, **kw), reads, writes, dma_buf=dma_buf, indep=indep)


D = 1024
KC = 8
S = 4096
T = 512
NT = S // T
FF = 2816
FC = 22
EPS = 1e-6
NRING = 3
HEAD_OF_SLOT = []
for _c in range(8):
    HEAD_OF_SLOT.append([0, 1, 2, 3, 8, 9, 10, 11][_c])
    HEAD_OF_SLOT.append([4, 5, 6, 7, 12, 13, 14, 15][_c])
NEG_MASK = -30000.0

WS_IN = 0
WS_OUT0 = 16
WS_QK = 24
WS_OUT1 = 34
NWS = 42


class WPool:
    def __init__(self, C, name, nslots, shape):
        self.C = C
        self.n = nslots
        self.t = [C.sbuf("%s%d" % (name, i), shape, BF16) for i in range(nslots)]
        self.b = [C.dbuf("%s%d" % (name, i)) for i in range(nslots)]
        self.sb = [C.dbuf("%s%dst" % (name, i)) for i in range(nslots)]
        self.req = []
        self.chunk = {}
        self.emitted = 0
        self.cons = 0

    def _top(self, upto):
        C = self.C
        upto = min(upto, len(self.req))
        while self.emitted < upto:
            i = self.emitted
            key, src32, scr = self.req[i]
            k = i % self.n
            if key not in self.chunk:
                cb = Buf("chunk")
                self.chunk[key] = cb
                C.dma("pool", self.t[k][:], src32, (), (self.b[k],), self.b[k])
                C.dma("sp", scr, self.t[k][:], (self.b[k],), (cb,), self.sb[k])
            else:
                C.dma("sp", self.t[k][:], scr, (self.chunk[key],), (self.b[k],), self.b[k])
            self.emitted += 1

    def prefetch(self):
        self._top(self.cons + self.n)

    def next(self):
        i = self.cons
        assert i < len(self.req), "weight pool underflow"
        self._top(i + self.n)
        self.cons += 1
        return self.t[i % self.n], self.b[i % self.n]


def build_program(debug=False, NT=NT, stages=3):
    nc = bass.Bass("TRN2", target_bir_lowering=False)
    dt_in = lambda name, shape: nc.dram_tensor(name, list(shape), F32, kind="ExternalInput").ap()
    x_d = dt_in("x", [S, D])
    ws_d = dt_in("ws", [NWS, 128, 1024])
    gu_d = dt_in("gu", [2 * FC, 2, 128, 1024])
    wd_d = dt_in("wd", [16, 128, FC, 128])
    wv0_d = dt_in("wv0", [128, 8, 512])
    wv1_d = dt_in("wv1", [128, 8, 256])
    gall_d = dt_in("gall", [128, 4, 8])
    gfin_d = dt_in("gfin", [128, 1024])
    lng_d = dt_in("lng", [128, 512])
    lnb_d = dt_in("lnb", [128, 512])
    bsp_d = dt_in("bsp", [128, 4, 128])
    wsp_d = dt_in("wsp", [128, 4, 128])
    cw_d = dt_in("cw", [128, 3, 4])
    bias_d = dt_in("biasT", [128, 16, 384])
    sink_d = dt_in("sinkb", [128, 16])
    id_d = dt_in("ident", [128, 128])
    out_d = nc.dram_tensor("out", [S, D], F32, kind="ExternalOutput").ap()
    if debug:
        dbg_d = nc.dram_tensor("dbg", [5, NT, 128, 8, 512], F32, kind="ExternalOutput").ap()
    ws_s = nc.dram_tensor("ws_s", [NWS, 128, 1024], BF16).ap()
    gu_s = nc.dram_tensor("gu_s", [2 * FC, 2, 128, 1024], BF16).ap()
    wd_s = nc.dram_tensor("wd_s", [16, 128, FC, 128], BF16).ap()
    wv0_s = nc.dram_tensor("wv0_s", [128, 8, 512], BF16).ap()
    wv1_s = nc.dram_tensor("wv1_s", [128, 8, 256], BF16).ap()

    with ExitStack() as st:
        C = Ctx(nc, st)
        P = C.P
        xr = C.sbuf("xr", [128, NRING, KC, T], F32)
        xr_b = [[Buf("xr%d_%d" % (r, c)) for c in range(KC)] for r in range(NRING)]
        hT = C.sbuf("hT", [128, KC, T + 2], BF16)
        hT_b = [Buf("hT%d" % c) for c in range(KC)]
        hTx_b = Buf("hTx")
        arena = C.sbuf("arena", [128, 32 * 256], F32)
        pg = [Buf("pg%d" % i) for i in range(32)]

        def av_bf(p0, np_):
            return arena[:, p0 * 256:(p0 + np_) * 256].bitcast(BF16)

        def av_f32(p0, np_):
            return arena[:, p0 * 256:(p0 + np_) * 256]

        kring = C.sbuf("kring", [128, 2, 4 * T], BF16)
        kr_b = [[Buf("k%d_%d" % (c, b)) for b in range(16)] for c in range(2)]
        vring = C.sbuf("vring", [128, 16, 4, 65], BF16)
        vr_b = [Buf("v%d" % b) for b in range(16)]
        qT = C.sbuf("qT", [128, KC, T], BF16)
        qT_b = [Buf("qT%d" % c) for c in range(KC)]
        biasT = C.sbuf("biasTs", [128, 16, 384], BF16)
        biasT_b = C.dbuf("biasT")
        xstage2 = [C.sbuf("xstage%d" % i, [128, D], F32) for i in range(2)]
        xstage2_b = [C.dbuf("xstage%d" % i) for i in range(2)]
        ostage = C.sbuf("ostage", [128, D], F32)
        ostage_b = Buf("ostage")
        ostore_b = C.dbuf("ostore")
        sq = [C.sbuf("sq%d" % i, [128, T], BF16) for i in range(3)]
        sq_b = [Buf("sq%d" % i) for i in range(3)]
        nrm_t = sq
        nrm_b = sq_b
        sqx = C.sbuf("sqx", [128, 8], BF16)
        sqx_b = Buf("sqx")
        tbuf = C.sbuf("tbuf", [128, T], F32)
        tbuf_b = Buf("tbuf")
        rsb = C.sbuf("rsb", [128, T], F32)
        rsb_b = Buf("rsb")
        small = C.sbuf("small", [128, 256], F32)
        zext = [C.sbuf("zext%d" % i, [128, T + 4], F32) for i in range(2)]
        zext_b = [Buf("zext%d" % i) for i in range(2)]
        zprev = C.sbuf("zprev", [128, 4], F32)
        zprev_b = [Buf("zprev%d" % j) for j in range(4)]
        ident_f = C.sbuf("ident_f", [128, 128], F32)
        ident_fb = C.dbuf("ident_f")
        ident_b = C.sbuf("ident_b", [128, 128], BF16)
        ident_bb = Buf("ident_b")
        ones_b = C.sbuf("ones_b", [128, 128], BF16)
        ones_bb = Buf("ones_b")
        epsc = C.sbuf("epsc", [128, 1], F32)
        eps_b = Buf("epsc")
        neghalf = C.sbuf("neghalf", [128, 8], F32)
        neghalf_b = Buf("neghalf")
        gfin = C.sbuf("gfin", [128, D], F32)
        lng = C.sbuf("lng", [128, 512], F32)
        lnb = C.sbuf("lnb", [128, 512], F32)
        bsp = C.sbuf("bsp", [128, 4, 128], F32)
        wsp_f = C.sbuf("wsp_f", [128, 4, 128], F32)
        wsp_b = C.sbuf("wsp_b", [128, 4, 128], BF16)
        wsp_bb = Buf("wsp_b")
        gall = C.sbuf("gall", [128, 4, 8], F32)
        cw = C.sbuf("cw", [128, 3, 4], F32)
        sinkb = C.sbuf("sinkb", [128, 16], F32)
        negc = C.sbuf("negc", [128, 16], F32)
        sinkterm = C.sbuf("sinkterm", [128, 16], F32)
        att_b = Buf("attconst")
        cst_b = C.dbuf("consts")
        junk = C.sbuf("junk", [128, T], BF16)
        junk_b = Buf("junk")

        _sm = [0]

        def sm(n):
            a = small[:, _sm[0]:_sm[0] + n]
            _sm[0] += n
            assert _sm[0] <= 256
            return a

        psb = [C.psum("ps%d" % i, [128, 512], F32) for i in range(8)]
        ps_b = [Buf("ps%d" % i) for i in range(8)]
        _pr = [0]

        def next_ps():
            k = _pr[0] % 5
            _pr[0] += 1
            return psb[k], ps_b[k]

        acc_t = psb[5:8]
        acc_b = ps_b[5:8]
        xps_b = Buf("xps")

        C.dma("pool", biasT[:], bias_d, (), (biasT_b,), biasT_b)

        for (dst, src) in ((ident_f, id_d), (gfin, gfin_d), (lng, lng_d), (lnb, lnb_d), (bsp, bsp_d),
                           (wsp_f, wsp_d), (gall, gall_d), (cw, cw_d), (sinkb, sink_d)):
            C.dma("sp", dst[:], src, (), (cst_b,), cst_b, indep=True)
        C.copy("dve", ident_b[:], ident_f[:], (cst_b,), (ident_bb,))
        C.memset("pool", ones_b[:], 1.0, (ones_bb,))
        C.memset("pool", neghalf[:], -0.5, (neghalf_b,))
        C.memset("pool", epsc[:], EPS, (eps_b,))
        C.copy("dve", wsp_b[:], wsp_f[:], (cst_b,), (wsp_bb,))
        C.memset("pool", vring[:], 1.0, vr_b)
        C.ts("dve", negc[:], sinkb[:], 0.0, -1.0, ALU.max, ALU.mult, (cst_b,), (att_b,))
        C.tt("dve", sinkterm[:], sinkb[:], negc[:], ALU.add, (cst_b, att_b), (att_b,))
        C.act(sinkterm[:], sinkterm[:], AF.Exp, (att_b,), (att_b,))

        proj = WPool(C, "wproj", 4, [128, KC, 128])
        gup = WPool(C, "wgu", 3, [128, 2, KC, 128])
        dnp = WPool(C, "wdn", 2, [128, 2, 11 * 128])
        wv0p = WPool(C, "wv0p", 1, [128, KC, 512])
        wv1p = WPool(C, "wv1p", 1, [128, KC, 256])

        def ws_req(idx):
            return (("ws", idx), ws_d[idx].rearrange("p (k j) -> p k j", k=KC), ws_s[idx].rearrange("p (k j) -> p k j", k=KC))

        def ffn_reqs(layer):
            for fc in range(FC):
                i = layer * FC + fc
                gup.req.append((("gu", i), gu_d[i].rearrange("t p (k j) -> p t k j", k=KC),
                                gu_s[i].rearrange("t p (k j) -> p t k j", k=KC)))
            for dc in range(8):
                i = layer * 8 + dc
                dnp.req.append((("wd", i), wd_d[i].rearrange("p (a b) j -> p a (b j)", a=2),
                                wd_s[i].rearrange("p (a b) j -> p a (b j)", a=2)))

        for s in range(NT + 1):
            if s < NT:
                wv0p.req.append((("wv0", 0), wv0_d, wv0_s))
                for j in range(4):
                    for t3 in range(3):
                        proj.req.append(ws_req(WS_IN + 4 + 4 * t3 + j))
                for g in range(4):
                    proj.req.append(ws_req(WS_IN + g))
                for dc in range(8):
                    proj.req.append(ws_req(WS_OUT0 + dc))
                ffn_reqs(0)
                if stages >= 2:
                    wv1p.req.append((("wv1", 0), wv1_d, wv1_s))
                    for k2 in range(2):
                        proj.req.append(ws_req(WS_QK + 8 + k2))
            if s >= 1 and stages >= 2:
                for dc in range(8):
                    proj.req.append(ws_req(WS_OUT1 + dc))
                ffn_reqs(1)
            if s < NT and stages >= 2:
                for c in range(8):
                    proj.req.append(ws_req(WS_QK + c))

        def proj_fm(w_t, w_b, rhs_of_kc, rhs_bufs, n, nk=KC, ps=None):
            if ps is None:
                ps = next_ps()
            pt, pb = ps
            for kc in range(nk):
                C.mm(pt[:, 0:n], w_t[:, kc, :], rhs_of_kc(kc), kc == 0, kc == nk - 1,
                     (w_b, rhs_bufs[kc]), (pb,))
            return pt, pb

        def resid_add(slot, dc, pt, pb):
            xa = xr[:, slot, dc, :]
            C.tt("dve", xa, xa, pt[:, 0:T], ALU.add, (pb,), (xr_b[slot][dc],))

        def norm(slot, gi, extra_slot=None, filler=None):
            pt, pb = next_ps()
            for c in range(KC):
                k = c % 3
                xa = xr[:, slot, c, :]
                if c % 2 == 0:
                    C.act(sq[k][:], xa, AF.Square, (xr_b[slot][c],), (sq_b[k],))
                else:
                    C.tt("dve", sq[k][:], xa, xa, ALU.mult, (xr_b[slot][c],), (sq_b[k],))
                C.mm(pt[:, 0:T], ones_b[:], sq[k][:], c == 0, c == KC - 1, (ones_bb, sq_b[k]), (pb,))
            C.act(tbuf[:], pt[:, 0:T], AF.Ln, (pb, eps_b), (tbuf_b,), bias=epsc[:, 0:1], scale=1.0 / D)
            C.act(rsb[:], tbuf[:], AF.Exp, (tbuf_b,), (rsb_b,), scale=-0.5)
            for c in range(KC):
                C.stt("dve", hT[:, c, 0:T], xr[:, slot, c, :], gall[:, gi, c:c + 1], rsb[:], ALU.mult, ALU.mult,
                      (xr_b[slot][c], rsb_b, cst_b), (hT_b[c],))
            if filler is not None:
                filler()
            if extra_slot is not None:
                xcol = xr[:, extra_slot, :, 0]
                xb = xr_b[extra_slot]
                C.tt("dve", sqx[:], xcol, xcol, ALU.mult, xb, (sqx_b,))
                C.tt("dve", sm_x8, xcol, gall[:, gi, :], ALU.mult, tuple(xb) + (cst_b,), (smx8_b,))
                return lambda: norm_extra(gi, extra_slot)
            return None

        def norm_extra(gi, extra_slot):
            pt2, pb2 = next_ps()
            for c in range(KC):
                C.mm(pt2[:, 0:1], ones_b[:], sqx[:, c:c + 1], c == 0, c == KC - 1, (ones_bb, sqx_b), (pb2,))
            t1 = sm_t1
            C.act(t1, pt2[:, 0:1], AF.Ln, (pb2, eps_b), (smx_b,), bias=epsc[:, 0:1], scale=1.0 / D)
            C.act(sm_rs1, t1, AF.Exp, (smx_b,), (smx_b,), scale=-0.5)
            C.ts("dve", hT[:, :, T], sm_x8, sm_rs1, None, ALU.mult, None, (smx_b, smx8_b), (hTx_b,))

        smx8_b = Buf("smx8")
        sm_t1 = sm(1)
        sm_rs1 = sm(1)
        sm_x8 = sm(8)
        smx_b = Buf("smx")

        def load_dma(t, tb):
            r0 = t * T + tb * 128
            C.dma("sp", xstage2[tb % 2][:], x_d[r0:r0 + 128, :], (), (xstage2_b[tb % 2],), xstage2_b[tb % 2])

        def load_tr(t, tb):
            slot = t % NRING
            xstage = xstage2[tb % 2]
            xstage_b = xstage2_b[tb % 2]
            for half in range(2):
                pt, pb = next_ps()
                for c4 in range(4):
                    c = half * 4 + c4
                    C.tr(pt[:, c4 * 128:(c4 + 1) * 128], xstage[:, c * 128:(c + 1) * 128], ident_f[:],
                         (xstage_b, ident_fb), (pb,))
                dst = xr[:, slot, half * 4:half * 4 + 4, tb * 128:(tb + 1) * 128]
                C.copy("act", dst, pt[:, 0:512].rearrange("p (c n) -> p c n", c=4), (pb,),
                       tuple(xr_b[slot][half * 4:half * 4 + 4]))

        def load_a(t):
            load_tr(t, 0)
            load_tr(t, 1)
            load_dma(t, 2)
            load_dma(t, 3)

        def load_b(t):
            load_tr(t, 2)
            load_tr(t, 3)
            if t + 1 < NT:
                load_dma(t + 1, 0)
                load_dma(t + 1, 1)

        def tmp_slot(i):
            p0 = 18 + 2 * (i % 6)
            return av_f32(p0, 2), (pg[p0], pg[p0 + 1])

        _tmpi = [0]

        def next_tmp():
            r = tmp_slot(_tmpi[0])
            _tmpi[0] += 1
            return r

        ln_st = [sm(6) for _ in range(4)]
        ln_mv = [sm(2) for _ in range(4)]
        ln_rt = [sm(1) for _ in range(4)]
        ln_rs = [sm(1) for _ in range(4)]
        ln_b = [Buf("ln%d" % i) for i in range(4)]
        xc_sb = sm(4)
        xc_b = [Buf("xc%d" % j) for j in range(4)]

        def mixer0(s):
            slot = s % NRING
            nslot = (s + 1) % NRING if s + 1 < NT else None
            extra = norm(slot, 0, nslot, filler=(lambda: load_a(s + 1)) if s + 1 < NT else None)
            gup.prefetch()
            wv1p.prefetch()
            hb = hT_b
            wv_t, wv_b = wv0p.next()
            for tb in range(4):
                pt, pb = next_ps()
                for kc in range(KC):
                    C.mm(pt[:, 0:512], hT[:, kc, tb * 128:(tb + 1) * 128], wv_t[:, kc, :], kc == 0, kc == KC - 1,
                         (hb[kc], wv_b), (pb,))
                gv, gvb = next_tmp()
                C.act(gv, pt[:, 0:512], AF.Gelu, (pb,), gvb)
                C.P.add("dve", lambda e, o=ln_st[tb], i=gv: e.bn_stats(o, i), gvb, (ln_b[tb],))
                C.P.add("dve", lambda e, o=ln_mv[tb], i=ln_st[tb]: e.bn_aggr(o, i), (ln_b[tb],), (ln_b[tb],))
                C.ts("dve", ln_rt[tb], ln_mv[tb][:, 1:2], EPS, None, ALU.add, None, (ln_b[tb],), (ln_b[tb],))
                C.tt("pool", ln_rs[tb], ln_rt[tb], neghalf[:, 0:1], ALU.pow, (ln_b[tb], neghalf_b), (ln_b[tb],))
                C.ts("dve", gv, gv, ln_mv[tb][:, 0:1], ln_rs[tb], ALU.subtract, ALU.mult, (ln_b[tb],), gvb)
                C.tt("pool", gv, gv, lng[:], ALU.mult, (cst_b,), gvb)
                vt = av_bf(8 + tb, 1)
                C.tt("pool", vt, gv, lnb[:], ALU.add, gvb + (cst_b,), (pg[8 + tb],))
            if extra is not None:
                extra()
            for j in range(4):
                z = zext[j % 2]
                zb = zext_b[j % 2]
                bb = av_f32(14 + 2 * (j % 2), 2)
                bbb = (pg[14 + 2 * (j % 2)], pg[15 + 2 * (j % 2)])
                xp = acc_t[2]

                def halo(col, wt_, wb_):
                    if nslot is None:
                        return
                    for kc in range(KC):
                        C.mm(xp[:, col:col + 1], wt_[:, kc, :], hT[:, kc, T:T + 1], kc == 0, kc == KC - 1,
                             (wb_, hTx_b), (xps_b,))

                wb_t, wb_b = proj.next()
                pt, pb = proj_fm(wb_t, wb_b, lambda kc: hT[:, kc, 0:T], hb, T)
                C.copy("act", bb, pt[:, 0:T], (pb,), bbb)
                wc_t, wc_b = proj.next()
                ptc, pbc = proj_fm(wc_t, wc_b, lambda kc: hT[:, kc, 0:T], hb, T)
                halo(496 + 2 * j, wc_t, wc_b)
                tc_, tcb = next_tmp()
                C.copy("act", tc_, ptc[:, 0:T], (pbc,), tcb)
                wh_t, wh_b = proj.next()
                pth, pbh = proj_fm(wh_t, wh_b, lambda kc: hT[:, kc, 0:T], hb, T)
                halo(497 + 2 * j, wh_t, wh_b)
                C.tt("dve", z[:, 1:T + 1], tc_, pth[:, 0:T], ALU.mult, tcb + (pbh,), (zb,))
                if s > 0:
                    C.copy("pool", z[:, 0:1], zprev[:, j:j + 1], (zprev_b[j],), (zb,))
                else:
                    C.memset("pool", z[:, 0:1], 0.0, (zb,))
                if nslot is not None:
                    C.copy("act", xc_sb[:, j:j + 1], xp[:, 496 + 2 * j:497 + 2 * j], (xps_b,), (xc_b[j],))
                    C.tt("dve", z[:, T + 1:T + 2], xc_sb[:, j:j + 1], xp[:, 497 + 2 * j:498 + 2 * j], ALU.mult,
                         (xc_b[j], xps_b), (zb,))
                else:
                    C.memset("pool", z[:, T + 1:T + 2], 0.0, (zb,))
                C.copy("pool", zprev[:, j:j + 1], z[:, T:T + 1], (zb,), (zprev_b[j],))
                ct, ctb = next_tmp()
                C.ts("dve", ct, z[:, 0:T], cw[:, 0, j:j + 1], None, ALU.mult, None, (zb, cst_b), ctb)
                C.stt("dve", ct, z[:, 1:T + 1], cw[:, 1, j:j + 1], ct, ALU.mult, ALU.add, (zb, cst_b), ctb)
                C.stt("dve", ct, z[:, 2:T + 2], cw[:, 2, j:j + 1], ct, ALU.mult, ALU.add, (zb, cst_b), ctb)
                C.tt("pool", av_bf(4 + j, 1), ct, bb, ALU.mult, ctb + bbb, (pg[4 + j],))
            for g in range(4):
                w_t, w_b = proj.next()
                pt, pb = proj_fm(w_t, w_b, lambda kc: hT[:, kc, 0:T], hb, T)
                au = av_bf(12 + g % 2, 1)
                aub = pg[12 + g % 2]
                C.act(au, pt[:, 0:T], AF.Gelu, (pb,), (aub,))
                pt2, pb2 = next_ps()
                for tb in range(4):
                    vt = av_bf(8 + tb, 1)
                    C.mm(pt2[:, tb * 128:(tb + 1) * 128], vt[:, g * 128:(g + 1) * 128], wsp_b[:, g, :], True, True,
                         (pg[8 + tb], wsp_bb), (pb2,))
                t1, t1b = next_tmp()
                C.tt("dve", t1.rearrange("p (t q) -> p t q", t=4), pt2[:, 0:512].rearrange("p (t q) -> p t q", t=4),
                     bsp[:, g:g + 1, :].broadcast_to([128, 4, 128]), ALU.add, (pb2, cst_b), t1b)
                C.tt("pool", av_bf(g, 1), t1, au, ALU.mult, t1b + (aub,), (pg[g],))
            for dc in range(8):
                w_t, w_b = proj.next()
                pt, pb = proj_fm(w_t, w_b, lambda kc: av_bf(kc, 1), pg[0:8], T)
                resid_add(slot, dc, pt, pb)

        def ffn(t, layer):
            slot = t % NRING
            norm(slot, 1 + 2 * layer, filler=(lambda: load_b(t + 1)) if (layer == 0 and t + 1 < NT) else None)
            dnp.prefetch()
            for fc in range(FC):
                gu_t, gu_b = gup.next()
                ptg, pbg = next_ps()
                ptu, pbu = next_ps()
                for kc in range(KC):
                    C.mm(ptg[:, 0:T], gu_t[:, 0, kc, :], hT[:, kc, 0:T], kc == 0, kc == KC - 1, (gu_b, hT_b[kc]), (pbg,))
                for kc in range(KC):
                    C.mm(ptu[:, 0:T], gu_t[:, 1, kc, :], hT[:, kc, 0:T], kc == 0, kc == KC - 1, (gu_b, hT_b[kc]), (pbu,))
                p0 = 22 + 2 * (fc % 2)
                sg = av_f32(p0, 2)
                sgb = (pg[p0], pg[p0 + 1])
                C.act(sg, ptg[:, 0:T], AF.Silu, (pbg,), sgb)
                C.tt("dve", av_bf(fc, 1), sg, ptu[:, 0:T], ALU.mult, sgb + (pbu,), (pg[fc],))
            proj.prefetch()
            for dc in range(8):
                wd_t, wd_b = dnp.next()
                pt, pb = next_ps()
                for fc in range(FC):
                    C.mm(pt[:, 0:T], wd_t[:, fc // 11, (fc % 11) * 128:(fc % 11 + 1) * 128], av_bf(fc, 1), fc == 0, fc == FC - 1, (wd_b, pg[fc]), (pb,))
                resid_add(slot, dc, pt, pb)

        def l1_kv(s):
            slot = s % NRING
            wv0p.prefetch()
            norm(slot, 2)
            wv_t, wv_b = wv1p.next()
            rb = (s % 4) * 4
            for k2 in range(2):
                w_t, w_b = proj.next()
                pt, pb = proj_fm(w_t, w_b, lambda kc: hT[:, kc, 0:T], hT_b, T)
                C.copy("act", kring[:, k2, rb * 128:rb * 128 + T], pt[:, 0:T], (pb,), tuple(kr_b[k2][rb:rb + 4]))
            for tb in range(4):
                pt, pb = next_ps()
                for kc in range(KC):
                    C.mm(pt[:, 0:256], hT[:, kc, tb * 128:(tb + 1) * 128], wv_t[:, kc, :], kc == 0, kc == KC - 1,
                         (hT_b[kc], wv_b), (pb,))
                C.copy("dve", vring[:, rb + tb, :, 0:64], pt[:, 0:256].rearrange("p (g d) -> p g d", g=4), (pb,),
                       (vr_b[rb + tb],))

        def l1_q(s):
            slot = s % NRING
            fin = s >= 1
            if fin:
                final_block(s - 1, 0)
            norm(slot, 2)
            if fin:
                final_block(s - 1, 1)
            for c in range(8):
                w_t, w_b = proj.next()
                pt, pb = proj_fm(w_t, w_b, lambda kc: hT[:, kc, 0:T], hT_b, T)
                C.act(qT[:, c, :], pt[:, 0:T], AF.Copy, (pb,), (qT_b[c],), scale=0.125)
                if fin and c == 1:
                    final_block(s - 1, 2)
                if fin and c == 3:
                    final_block(s - 1, 3)

        den = sm(16)
        rden = sm(16)
        den_b = Buf("den")

        def attention(t):
            slot = t % NRING
            proj.prefetch()
            gup.prefetch()
            for qb in range(4):
                gb = 4 * t + qb
                js = [j for j in range(3) if 0 <= gb - 1 + j < 32]
                nj = len(js)
                j0 = js[0]
                ao = av_bf(8 + 2 * (qb % 2), 2)
                aob = (pg[8 + 2 * (qb % 2)], pg[9 + 2 * (qb % 2)])
                def emit_st2(cq):
                    sts = []
                    for half in range(2):
                        hs = 2 * cq + half
                        st_t, st_b = next_ps()
                        sts.append((st_t, st_b))
                        C.mm(st_t[:, 0:nj * 128], ident_b[:], biasT[:, hs, j0 * 128:(j0 + nj) * 128], True, False,
                             (ident_bb, biasT_b), (st_b,))
                    for idx, j in enumerate(js):
                        kb = (gb - 1 + j) % 16
                        for half in range(2):
                            hs = 2 * cq + half
                            g = HEAD_OF_SLOT[hs] // 4
                            assert g % 2 == half
                            k2 = g // 2
                            st_t, st_b = sts[half]
                            C.mm(st_t[:, idx * 128:(idx + 1) * 128],
                                 kring[half * 64:(half + 1) * 64, k2, kb * 128:(kb + 1) * 128],
                                 qT[half * 64:(half + 1) * 64, cq, qb * 128:(qb + 1) * 128],
                                 False, idx == nj - 1, (kr_b[k2][kb], qT_b[cq]), (st_b,))
                    for half in range(2):
                        hs = 2 * cq + half
                        st_t, st_b = sts[half]
                        ptile = av_bf(12 + hs % 4, 1)
                        ptb = pg[12 + hs % 4]
                        C.act(ptile[:, 0:nj * 128], st_t[:, 0:nj * 128], AF.Exp, (st_b, att_b), (ptb,),
                              bias=negc[:, hs:hs + 1], scale=1.0)

                def emit_pv(hs):
                    g = HEAD_OF_SLOT[hs] // 4
                    ptile = av_bf(12 + hs % 4, 1)
                    ptb = pg[12 + hs % 4]
                    bank = hs // 7
                    col = (hs % 7) * 65
                    for idx, j in enumerate(js):
                        kb = (gb - 1 + j) % 16
                        C.mm(acc_t[bank][:, col:col + 65], ptile[:, idx * 128:(idx + 1) * 128], vring[:, kb, g, :],
                             idx == 0, idx == nj - 1, (ptb, vr_b[kb]), (acc_b[bank],))

                emit_st2(0)
                for cq in range(1, 8):
                    emit_st2(cq)
                    emit_pv(2 * cq - 2)
                    emit_pv(2 * cq - 1)
                emit_pv(14)
                emit_pv(15)
                for bank in range(3):
                    h0 = bank * 7
                    h1 = min(16, h0 + 7)
                    nh = h1 - h0
                    a3 = acc_t[bank][:, 0:nh * 65].rearrange("p (h e) -> p h e", e=65)
                    C.tt("dve", den[:, h0:h1], a3[:, :, 64], sinkterm[:, h0:h1], ALU.add, (acc_b[bank], att_b), (den_b,))
                    C.P.add("dve", lambda e, o=rden[:, h0:h1], i=den[:, h0:h1]: e.reciprocal(o, i), (den_b,), (den_b,))
                    C.tt("dve", ao[:, h0 * 64:h1 * 64].rearrange("p (h d) -> p h d", d=64), a3[:, :, 0:64],
                         rden[:, h0:h1].unsqueeze(2).broadcast_to([128, nh, 64]), ALU.mult, (acc_b[bank], den_b), aob)
                tp, tpb = next_ps()
                tpv = tp[:, 0:512].bitcast(BF16)
                for c in range(8):
                    C.tr(tpv[:, c * 128:(c + 1) * 128], ao[:, c * 128:(c + 1) * 128], ident_b[:], aob + (ident_bb,), (tpb,))
                dst = av_bf(0, 8).rearrange("p (c n) -> p c n", c=8)[:, :, qb * 128:(qb + 1) * 128]
                C.copy("act", dst, tpv.rearrange("p (c n) -> p c n", c=8), (tpb,), tuple(pg[0:8]))
            for dc in range(8):
                w_t, w_b = proj.next()
                pt, pb = proj_fm(w_t, w_b, lambda kc: av_bf(kc, 1), pg[0:8], T)
                resid_add(slot, dc, pt, pb)

        fss = [sm(1), sm(1)]
        fs_t = sm(1)
        fs_r = sm(1)
        fs_b = Buf("fs")

        def final_block(t, tb):
            slot = t % NRING
            pts = [(acc_t[0], acc_b[0]), (acc_t[1], acc_b[1])]
            for c in range(8):
                pt, pb = pts[c // 4]
                C.tr(pt[:, (c % 4) * 128:(c % 4 + 1) * 128], xr[:, slot, c, tb * 128:(tb + 1) * 128], ident_f[:],
                     (xr_b[slot][c], ident_fb), (pb,))
            for h2 in range(2):
                pt, pb = pts[h2]
                C.act(junk[:], pt[:, 0:512], AF.Square, (pb,), (junk_b, fs_b), accum_out=fss[h2])
            C.tt("dve", fs_t, fss[0], fss[1], ALU.add, (fs_b,), (fs_b,))
            C.ts("dve", fs_t, fs_t, 1.0 / D, EPS, ALU.mult, ALU.add, (fs_b,), (fs_b,))
            C.tt("pool", fs_r, fs_t, neghalf[:, 0:1], ALU.pow, (fs_b, neghalf_b), (fs_b,))
            for h2 in range(2):
                pt, pb = pts[h2]
                C.stt("dve", ostage[:, h2 * 512:(h2 + 1) * 512], pt[:, 0:512], fs_r, gfin[:, h2 * 512:(h2 + 1) * 512],
                      ALU.mult, ALU.mult, (pb, fs_b, cst_b, ostore_b), (ostage_b,))
            r0 = t * T + tb * 128
            C.dma("sp", out_d[r0:r0 + 128, :], ostage[:], (ostage_b,), (ostore_b,), ostore_b)

        def final(t):
            for tb in range(4):
                final_block(t, tb)

        def dump(k, t):
            if not debug:
                return
            slot = t % NRING
            b = C.dbuf("dbg%d_%d" % (k, t))
            C.dma("sp", dbg_d[k, t], xr[:, slot], tuple(xr_b[slot]), (), b)
            P.final_waits.append(b)

        wv0p.prefetch()
        load_dma(0, 0)
        load_dma(0, 1)
        load_a(0)
        load_b(0)
        for s in range(NT + 1):
            if s < NT:
                dump(0, s)
                mixer0(s)
                dump(1, s)
                ffn(s, 0)
                dump(2, s)
                if stages >= 2:
                    l1_kv(s)
            if s >= 1 and stages >= 2:
                attention(s - 1)
                dump(3, s - 1)
                ffn(s - 1, 1)
                dump(4, s - 1)
                if s == NT:
                    final(s - 1)
            if s < NT and stages >= 2:
                l1_q(s)

        fw = [(ostore_b.sem, ostore_b.semcnt)]
        for b in P.final_waits:
            fw.append((b.sem, b.semcnt))
        P.final_waits = fw
        print("SBUF bytes remaining per partition:", nc.sbuf_bytes_remaining, "ops:", {e: len(P.ops[e]) for e in ENGS})
        with nc.Block() as block:
            P.emit_all(nc, block, C.engsem)
    return nc


def _t5_bucket_table():
    nb = 16
    max_exact = 8
    rel = np.arange(-255, 256)
    ret = np.where(rel > 0, nb, 0)
    n = np.abs(rel)
    nf = np.maximum(n, 1).astype(np.float32)
    large = max_exact + (np.log(nf / max_exact) / np.log(128 / max_exact) * (nb - max_exact)).astype(np.int32)
    large = np.minimum(large, nb - 1)
    return ret + np.where(n < max_exact, n, large)


def _chunks_kmajor(W):
    K, E = W.shape
    a = W.reshape(K // 128, 128, E // 128, 128)
    return np.ascontiguousarray(a.transpose(2, 1, 0, 3)).reshape(E // 128, 128, (K // 128) * 128)


def prep_shared(inp):
    f = lambda a: np.ascontiguousarray(np.asarray(a, dtype=np.float32))
    w_in = f(inp["even_w_in"])[0]
    cols = np.concatenate([np.arange(0, 512), np.arange(1024, 2560)])
    ws_in = _chunks_kmajor(w_in[:, cols])
    ws_out0 = _chunks_kmajor(f(inp["even_w_out"])[0])
    wqkv = f(inp["attn_w_qkv"])[0]
    qcols = np.concatenate([np.arange(h * 64, h * 64 + 64) for h in HEAD_OF_SLOT])
    ws_q = _chunks_kmajor(wqkv[:, qcols])
    ws_k = _chunks_kmajor(wqkv[:, 1024:1280])
    wo1 = f(inp["attn_w_out"])[0][qcols, :]
    ws_out1 = _chunks_kmajor(wo1)
    ws = np.concatenate([ws_in, ws_out0, ws_q, ws_k, ws_out1], axis=0)
    assert ws.shape == (NWS, 128, 1024)
    gate = f(inp["ffn_w_gate"])
    up = f(inp["ffn_w_up"])
    down = f(inp["ffn_w_down"])
    gu = np.stack([np.stack([_chunks_kmajor(gate[l]), _chunks_kmajor(up[l])], axis=1) for l in range(2)], axis=0)
    gu = np.ascontiguousarray(gu.reshape(2 * FC, 2, 128, 1024))
    wd = np.stack([_chunks_kmajor(down[l]) for l in range(2)], axis=0).reshape(16, 128, FC, 128)
    wv0 = np.ascontiguousarray(w_in[:, 512:1024].reshape(8, 128, 512).transpose(1, 0, 2))
    wv1 = np.ascontiguousarray(wqkv[:, 1280:1536].reshape(8, 128, 256).transpose(1, 0, 2))
    nm = f(inp["norm_mix"])
    nf_ = f(inp["norm_ffn"])
    gl = np.stack([nm[0], nf_[0], nm[1], nf_[1]], axis=0)
    gall = np.ascontiguousarray(gl.reshape(4, 8, 128).transpose(2, 0, 1))
    rep = lambda v: np.ascontiguousarray(np.broadcast_to(v, (128,) + v.shape))
    gfin = rep(f(inp["final_norm"]))
    lng = rep(f(inp["even_v_ln_g"])[0])
    lnb = rep(f(inp["even_v_ln_b"])[0])
    bsp = rep(f(inp["even_b_spatial"])[0])
    wsp = np.ascontiguousarray(f(inp["even_w_spatial"])[0].transpose(2, 0, 1))
    cw = np.ascontiguousarray(f(inp["even_conv_w"])[0].reshape(3, 4, 128).transpose(2, 0, 1))
    tab = _t5_bucket_table()
    k = np.arange(128)[:, None, None]
    j = np.arange(3)[None, :, None]
    q = np.arange(128)[None, None, :]
    rel = (j - 1) * 128 + k - q
    bucket = tab[rel + 255]
    rb = f(inp["rel_bias"])[:, HEAD_OF_SLOT]
    bias = rb[bucket]
    band = (np.abs(rel) <= 128)[..., None]
    bias = np.where(band, bias, np.float32(NEG_MASK)).astype(np.float32)
    biasT = np.ascontiguousarray(bias.transpose(0, 3, 1, 2)).reshape(128, 16, 384)
    sinkb = rep(f(inp["attn_sink"])[0][HEAD_OF_SLOT])
    return {
        "ws": ws, "gu": gu, "wd": np.ascontiguousarray(wd), "wv0": wv0, "wv1": wv1, "gall": gall, "gfin": gfin,
        "lng": lng, "lnb": lnb, "bsp": bsp, "wsp": wsp, "cw": cw, "biasT": biasT, "sinkb": sinkb,
        "ident": np.eye(128, dtype=np.float32),
    }


_NC_CACHE = {}


def kernel(**inputs):
    x = np.ascontiguousarray(np.asarray(inputs["x"], dtype=np.float32))
    shared = prep_shared(inputs)
    if "nc" not in _NC_CACHE:
        _NC_CACHE["nc"] = build_program(False)
    nc = _NC_CACHE["nc"]
    in_maps = []
    for b in range(8):
        m = dict(shared)
        m["x"] = x[b]
        in_maps.append(m)
    res = run_bass_kernel_spmd(nc, in_maps, core_ids=list(range(8)))
    out = np.stack([np.asarray(r["out"], dtype=np.float32) for r in res.results], axis=0)
    return out
```

```python
import numpy as np
from contextlib import ExitStack
import concourse.bass as bass
import concourse.mybir as mybir
from concourse.bass_utils import run_bass_kernel_spmd

F32 = mybir.dt.float32
BF16 = mybir.dt.bfloat16
AF = mybir.ActivationFunctionType
ALU = mybir.AluOpType
AX = mybir.AxisListType

ENGS = ("pe", "act", "dve", "pool", "sp")


class Buf:
    __slots__ = ("name", "w", "r", "rd", "sem", "semcnt")

    def __init__(self, name, sem=None):
        self.name = name
        self.w = None
        self.r = {}
        self.rd = []
        self.sem = sem
        self.semcnt = 0


class Op:
    __slots__ = ("eng", "emit", "deps", "mile", "mileno", "sem", "semval", "is_dma", "seq")


class Prog:
    def __init__(self):
        self.ops = {e: [] for e in ENGS}
        self.final_waits = []

    def add(self, eng, emit, reads=(), writes=(), dma_buf=None, indep=False):
        op = Op()
        op.eng = eng
        op.emit = emit
        op.mile = False
        op.mileno = 0
        op.is_dma = dma_buf is not None
        op.sem = None
        op.semval = 0
        op.seq = len(self.ops[eng])
        deps = []
        wset = set(id(b) for b in writes)
        for b in reads:
            if b.w is not None:
                deps.append(b.w)
        for b in writes:
            if b.w is not None and not indep:
                deps.append(b.w)
            deps.extend(b.r.values())
            deps.extend(b.rd)
        best = {}
        dl = []
        seen = set()
        for d in deps:
            if d.is_dma:
                if id(d) not in seen:
                    seen.add(id(d))
                    dl.append(d)
            else:
                if d.eng == "pe" and eng == "pe" and not op.is_dma:
                    continue
                cur = best.get(d.eng)
                if cur is None or d.seq > cur.seq:
                    best[d.eng] = d
        for d in best.values():
            d.mile = True
            dl.append(d)
        op.deps = dl
        if op.is_dma:
            dma_buf.semcnt += 16
            op.sem = dma_buf.sem
            op.semval = dma_buf.semcnt
        for b in writes:
            b.w = op
            b.r = {}
            b.rd = []
        for b in reads:
            if id(b) in wset:
                continue
            if op.is_dma:
                b.rd.append(op)
            else:
                b.r[eng] = op
        self.ops[eng].append(op)
        return op

    def emit_all(self, nc, block, engsem):
        for e in ENGS:
            n = 0
            for op in self.ops[e]:
                if op.mile and not op.is_dma:
                    n += 1
                    op.mileno = n
        prog = self

        def run(ename, eobj):
            known = {}
            for op in prog.ops[ename]:
                need = {}
                for d in op.deps:
                    if d.is_dma:
                        s, v = d.sem, d.semval
                    else:
                        s, v = engsem[d.eng], d.mileno
                    k = s.num
                    if k not in need or need[k][1] < v:
                        need[k] = (s, v)
                for k, (s, v) in need.items():
                    if known.get(k, 0) < v:
                        eobj.wait_ge(s, v)
                        known[k] = v
                ins = op.emit(eobj)
                if op.is_dma:
                    ins.then_inc(op.sem, 16)
                elif op.mile:
                    ins.then_inc(engsem[ename], 1)
            if ename == "sp":
                for (s, v) in prog.final_waits:
                    eobj.wait_ge(s, v)

        @block.tensor
        def _(e):
            run("pe", e)

        @block.scalar
        def _(e):
            run("act", e)

        @block.vector
        def _(e):
            run("dve", e)

        @block.gpsimd
        def _(e):
            run("pool", e)

        @block.sync
        def _(e):
            run("sp", e)


class Ctx:
    def __init__(self, nc, st):
        self.nc = nc
        self.st = st
        self.P = Prog()
        self.nsem = 0
        self.engsem = {}
        for e in ("pe", "act", "dve", "pool"):
            self.engsem[e] = st.enter_context(nc.semaphore("prog_" + e))

    def sbuf(self, name, shape, dt):
        return self.st.enter_context(self.nc.sbuf_tensor("sb_" + name, list(shape), dt))

    def psum(self, name, shape, dt):
        return self.st.enter_context(self.nc.psum_tensor("pp_" + name, list(shape), dt))

    def dbuf(self, name):
        self.nsem += 1
        s = self.st.enter_context(self.nc.semaphore("d_" + name))
        return Buf(name, sem=s)

    def mm(self, out, lhsT, rhs, start, stop, reads, writes):
        return self.P.add("pe", lambda e: e.matmul(out, lhsT, rhs, start=start, stop=stop), reads, writes)

    def tr(self, out, in_, ident, reads, writes):
        return self.P.add("pe", lambda e: e.transpose(out, in_, ident), reads, writes)

    def act(self, out, in_, func, reads, writes, bias=None, scale=None, accum_out=None, eng="act"):
        kw = {}
        if bias is not None:
            kw["bias"] = bias
        if scale is not None:
            kw["scale"] = scale
        if accum_out is not None:
            kw["accum_out"] = accum_out
        return self.P.add("act", lambda e: e.activation(out, in_, func, **kw), reads, writes)

    def tt(self, eng, out, in0, in1, op, reads, writes):
        return self.P.add(eng, lambda e: e.tensor_tensor(out, in0, in1, op), reads, writes)

    def ts(self, eng, out, in0, s1, s2, op0, op1, reads, writes):
        if s2 is None:
            return self.P.add(eng, lambda e: e.tensor_scalar(out, in0, s1, None, op0), reads, writes)
        return self.P.add(eng, lambda e: e.tensor_scalar(out, in0, s1, s2, op0, op1), reads, writes)

    def stt(self, eng, out, in0, scalar, in1, op0, op1, reads, writes):
        return self.P.add(eng, lambda e: e.scalar_tensor_tensor(out, in0, scalar, in1, op0, op1), reads, writes)

    def copy(self, eng, out, in_, reads, writes):
        if eng == "act":
            return self.P.add("act", lambda e: e.copy(out, in_), reads, writes)
        return self.P.add(eng, lambda e: e.tensor_copy(out, in_), reads, writes)

    def memset(self, eng, ap, val, writes):
        return self.P.add(eng, lambda e: e.memset(ap, val), (), writes)

    def dma(self, eng, out, in_, reads, writes, dma_buf, indep=False, **kw):
        return self.P.add(eng, lambda e: e.dma_start(out, in_, **kw), reads, writes, dma_buf=dma_buf, indep=indep)


D = 1024
KC = 8
S = 4096
T = 512
NT = S // T
FF = 2816
FC = 22
EPS = 1e-6
NRING = 3
HEAD_OF_SLOT = []
for _c in range(8):
    HEAD_OF_SLOT.append([0, 1, 2, 3, 8, 9, 10, 11][_c])
    HEAD_OF_SLOT.append([4, 5, 6, 7, 12, 13, 14, 15][_c])
NEG_MASK = -30000.0

WS_IN = 0
WS_OUT0 = 16
WS_QK = 24
WS_OUT1 = 34
NWS = 42


class WPool:
    def __init__(self, C, name, nslots, shape):
        self.C = C
        self.n = nslots
        self.t = [C.sbuf("%s%d" % (name, i), shape, BF16) for i in range(nslots)]
        self.b = [C.dbuf("%s%d" % (name, i)) for i in range(nslots)]
        self.sb = [C.dbuf("%s%dst" % (name, i)) for i in range(nslots)]
        self.req = []
        self.chunk = {}
        self.emitted = 0
        self.cons = 0

    def _top(self, upto):
        C = self.C
        upto = min(upto, len(self.req))
        while self.emitted < upto:
            i = self.emitted
            key, src32, scr = self.req[i]
            k = i % self.n
            if key not in self.chunk:
                cb = Buf("chunk")
                self.chunk[key] = cb
                C.dma("pool", self.t[k][:], src32, (), (self.b[k],), self.b[k])
                C.dma("sp", scr, self.t[k][:], (self.b[k],), (cb,), self.sb[k])
            else:
                C.dma("sp", self.t[k][:], scr, (self.chunk[key],), (self.b[k],), self.b[k])
            self.emitted += 1

    def prefetch(self):
        self._top(self.cons + self.n)

    def next(self, held=0):
        i = self.cons
        assert i < len(self.req), "weight pool underflow"
        self._top(i + self.n - held)
        self.cons += 1
        return self.t[i % self.n], self.b[i % self.n]


def build_program(debug=False, NT=NT, stages=3):
    nc = bass.Bass("TRN2", target_bir_lowering=False)
    dt_in = lambda name, shape: nc.dram_tensor(name, list(shape), F32, kind="ExternalInput").ap()
    x_d = dt_in("x", [S, D])
    ws_d = dt_in("ws", [NWS, 128, 1024])
    gu_d = dt_in("gu", [2 * FC, 2, 128, 1024])
    wd_d = dt_in("wd", [16, 128, FC, 128])
    wv0_d = dt_in("wv0", [128, 8, 512])
    wv1_d = dt_in("wv1", [128, 8, 256])
    gall_d = dt_in("gall", [128, 4, 8])
    gfin_d = dt_in("gfin", [128, 1024])
    lng_d = dt_in("lng", [128, 512])
    lnb_d = dt_in("lnb", [128, 512])
    bsp_d = dt_in("bsp", [128, 4, 128])
    wsp_d = dt_in("wsp", [128, 4, 128])
    cw_d = dt_in("cw", [128, 3, 4])
    bias_d = dt_in("biasT", [128, 16, 384])
    sink_d = dt_in("sinkb", [128, 16])
    id_d = dt_in("ident", [128, 128])
    out_d = nc.dram_tensor("out", [S, D], F32, kind="ExternalOutput").ap()
    if debug:
        dbg_d = nc.dram_tensor("dbg", [5, NT, 128, 8, 512], F32, kind="ExternalOutput").ap()
    ws_s = nc.dram_tensor("ws_s", [NWS, 128, 1024], BF16).ap()
    gu_s = nc.dram_tensor("gu_s", [2 * FC, 2, 128, 1024], BF16).ap()
    wd_s = nc.dram_tensor("wd_s", [16, 128, FC, 128], BF16).ap()
    wv0_s = nc.dram_tensor("wv0_s", [128, 8, 512], BF16).ap()
    wv1_s = nc.dram_tensor("wv1_s", [128, 8, 256], BF16).ap()

    with ExitStack() as st:
        C = Ctx(nc, st)
        P = C.P
        xr = C.sbuf("xr", [128, NRING, KC, T], F32)
        xr_b = [[Buf("xr%d_%d" % (r, c)) for c in range(KC)] for r in range(NRING)]
        hT = C.sbuf("hT", [128, KC, T + 2], BF16)
        hT_b = [Buf("hT%d" % c) for c in range(KC)]
        hTx_b = Buf("hTx")
        arena = C.sbuf("arena", [128, 32 * 256], F32)
        pg = [Buf("pg%d" % i) for i in range(32)]

        def av_bf(p0, np_):
            return arena[:, p0 * 256:(p0 + np_) * 256].bitcast(BF16)

        def av_f32(p0, np_):
            return arena[:, p0 * 256:(p0 + np_) * 256]

        kring = C.sbuf("kring", [128, 2, 4 * T], BF16)
        kr_b = [[Buf("k%d_%d" % (c, b)) for b in range(16)] for c in range(2)]
        vring = C.sbuf("vring", [128, 16, 4, 65], BF16)
        vr_b = [Buf("v%d" % b) for b in range(16)]
        qT = C.sbuf("qT", [128, KC, T], BF16)
        qT_b = [Buf("qT%d" % c) for c in range(KC)]
        biasT = C.sbuf("biasTs", [128, 16, 384], BF16)
        biasT_b = C.dbuf("biasT")
        xstage2 = [C.sbuf("xstage%d" % i, [128, D], F32) for i in range(2)]
        xstage2_b = [C.dbuf("xstage%d" % i) for i in range(2)]
        ostage = C.sbuf("ostage", [128, D], F32)
        ostage_b = Buf("ostage")
        ostore_b = C.dbuf("ostore")
        sq = [C.sbuf("sq%d" % i, [128, T], BF16) for i in range(3)]
        sq_b = [Buf("sq%d" % i) for i in range(3)]
        nrm_t = sq
        nrm_b = sq_b
        sqx = C.sbuf("sqx", [128, 8], BF16)
        sqx_b = Buf("sqx")
        tbuf = C.sbuf("tbuf", [128, T], F32)
        tbuf_b = Buf("tbuf")
        rsb = C.sbuf("rsb", [128, T], F32)
        rsb_b = Buf("rsb")
        small = C.sbuf("small", [128, 256], F32)
        zext = [C.sbuf("zext%d" % i, [128, T + 4], F32) for i in range(2)]
        zext_b = [Buf("zext%d" % i) for i in range(2)]
        zprev = C.sbuf("zprev", [128, 4], F32)
        zprev_b = [Buf("zprev%d" % j) for j in range(4)]
        ident_f = C.sbuf("ident_f", [128, 128], F32)
        ident_fb = C.dbuf("ident_f")
        ident_b = C.sbuf("ident_b", [128, 128], BF16)
        ident_bb = Buf("ident_b")
        ones_b = C.sbuf("ones_b", [128, 128], BF16)
        ones_bb = Buf("ones_b")
        epsc = C.sbuf("epsc", [128, 1], F32)
        eps_b = Buf("epsc")
        neghalf = C.sbuf("neghalf", [128, 8], F32)
        neghalf_b = Buf("neghalf")
        gfin = C.sbuf("gfin", [128, D], F32)
        lng = C.sbuf("lng", [128, 512], F32)
        lnb = C.sbuf("lnb", [128, 512], F32)
        bsp = C.sbuf("bsp", [128, 4, 128], F32)
        wsp_f = C.sbuf("wsp_f", [128, 4, 128], F32)
        wsp_b = C.sbuf("wsp_b", [128, 4, 128], BF16)
        wsp_bb = Buf("wsp_b")
        gall = C.sbuf("gall", [128, 4, 8], F32)
        cw = C.sbuf("cw", [128, 3, 4], F32)
        sinkb = C.sbuf("sinkb", [128, 16], F32)
        negc = C.sbuf("negc", [128, 16], F32)
        sinkterm = C.sbuf("sinkterm", [128, 16], F32)
        att_b = Buf("attconst")
        cst_b = C.dbuf("consts")
        junk = C.sbuf("junk", [128, T], BF16)
        junk_b = Buf("junk")

        _sm = [0]

        def sm(n):
            a = small[:, _sm[0]:_sm[0] + n]
            _sm[0] += n
            assert _sm[0] <= 256
            return a

        psb = [C.psum("ps%d" % i, [128, 512], F32) for i in range(8)]
        ps_b = [Buf("ps%d" % i) for i in range(8)]
        _pr = [0]

        def next_ps():
            k = _pr[0] % 5
            _pr[0] += 1
            return psb[k], ps_b[k]

        acc_t = psb[5:8]
        acc_b = ps_b[5:8]
        xps_b = Buf("xps")

        C.dma("pool", biasT[:], bias_d, (), (biasT_b,), biasT_b)

        for (dst, src) in ((ident_f, id_d), (gfin, gfin_d), (lng, lng_d), (lnb, lnb_d), (bsp, bsp_d),
                           (wsp_f, wsp_d), (gall, gall_d), (cw, cw_d), (sinkb, sink_d)):
            C.dma("sp", dst[:], src, (), (cst_b,), cst_b, indep=True)
        C.copy("dve", ident_b[:], ident_f[:], (cst_b,), (ident_bb,))
        C.memset("pool", ones_b[:], 1.0, (ones_bb,))
        C.memset("pool", neghalf[:], -0.5, (neghalf_b,))
        C.memset("pool", epsc[:], EPS, (eps_b,))
        C.copy("dve", wsp_b[:], wsp_f[:], (cst_b,), (wsp_bb,))
        C.memset("pool", vring[:], 1.0, vr_b)
        C.ts("dve", negc[:], sinkb[:], 0.0, -1.0, ALU.max, ALU.mult, (cst_b,), (att_b,))
        C.tt("dve", sinkterm[:], sinkb[:], negc[:], ALU.add, (cst_b, att_b), (att_b,))
        C.act(sinkterm[:], sinkterm[:], AF.Exp, (att_b,), (att_b,))

        proj = WPool(C, "wproj", 4, [128, KC, 128])
        gup = WPool(C, "wgu", 3, [128, 2, KC, 128])
        dnp = WPool(C, "wdn", 2, [128, 2, 11 * 128])
        wv0p = WPool(C, "wv0p", 1, [128, KC, 512])
        wv1p = WPool(C, "wv1p", 1, [128, KC, 256])

        def ws_req(idx):
            return (("ws", idx), ws_d[idx].rearrange("p (k j) -> p k j", k=KC), ws_s[idx].rearrange("p (k j) -> p k j", k=KC))

        def ffn_reqs(layer):
            for fc in range(FC):
                i = layer * FC + fc
                gup.req.append((("gu", i), gu_d[i].rearrange("t p (k j) -> p t k j", k=KC),
                                gu_s[i].rearrange("t p (k j) -> p t k j", k=KC)))
            for dc in range(8):
                i = layer * 8 + dc
                dnp.req.append((("wd", i), wd_d[i].rearrange("p (a b) j -> p a (b j)", a=2),
                                wd_s[i].rearrange("p (a b) j -> p a (b j)", a=2)))

        for s in range(NT + 1):
            if s < NT:
                wv0p.req.append((("wv0", 0), wv0_d, wv0_s))
                for j in range(4):
                    for t3 in range(3):
                        proj.req.append(ws_req(WS_IN + 4 + 4 * t3 + j))
                for g in range(4):
                    proj.req.append(ws_req(WS_IN + g))
                for dc in range(8):
                    proj.req.append(ws_req(WS_OUT0 + dc))
                ffn_reqs(0)
                if stages >= 2:
                    wv1p.req.append((("wv1", 0), wv1_d, wv1_s))
                    for k2 in range(2):
                        proj.req.append(ws_req(WS_QK + 8 + k2))
            if s >= 1 and stages >= 2:
                for dc in range(8):
                    proj.req.append(ws_req(WS_OUT1 + dc))
                ffn_reqs(1)
            if s < NT and stages >= 2:
                for c in range(8):
                    proj.req.append(ws_req(WS_QK + c))

        def proj_fm(w_t, w_b, rhs_of_kc, rhs_bufs, n, nk=KC, ps=None):
            if ps is None:
                ps = next_ps()
            pt, pb = ps
            for kc in range(nk):
                C.mm(pt[:, 0:n], w_t[:, kc, :], rhs_of_kc(kc), kc == 0, kc == nk - 1,
                     (w_b, rhs_bufs[kc]), (pb,))
            return pt, pb

        def resid_add(slot, dc, pt, pb, stats=None):
            xa = xr[:, slot, dc, :]
            C.tt("dve", xa, xa, pt[:, 0:T], ALU.add, (pb,), (xr_b[slot][dc],))
            if stats is not None:
                stats_add(stats, slot, dc, "act")
                stats_flush(stats, keep=1)

        def multi_mm(groups, nk=KC):
            for kc in range(nk):
                for (out_ap, pb, lf, rf, rd) in groups:
                    C.mm(out_ap, lf(kc), rf(kc), kc == 0, kc == nk - 1, rd(kc), (pb,))

        pending = {}

        def stats_begin():
            return {"pt": acc_t[1], "pb": acc_b[1], "n": 0, "pend": []}

        def stats_add(st, slot, c, eng="act"):
            i = st["n"]
            k = i % 3
            xa = xr[:, slot, c, :]
            if eng == "act":
                C.act(sq[k][:], xa, AF.Square, (xr_b[slot][c],), (sq_b[k],))
            else:
                C.tt("dve", sq[k][:], xa, xa, ALU.mult, (xr_b[slot][c],), (sq_b[k],))
            st["pend"].append((k, i))
            st["n"] = i + 1

        def stats_flush(st, keep=0):
            while len(st["pend"]) > keep:
                k, i = st["pend"].pop(0)
                C.mm(st["pt"][:, 0:T], ones_b[:], sq[k][:], i == 0, i == KC - 1, (ones_bb, sq_b[k]), (st["pb"],))

        def norm(slot, gi, extra_slot=None, filler=None, stats=None, rs=None, reuse=False):
            rs_t, rs_b = rs if rs is not None else (rsb, rsb_b)
            if not reuse:
                if stats is None:
                    stats = stats_begin()
                    for c in range(KC):
                        stats_add(stats, slot, c, "act" if c % 2 == 0 else "dve")
                        stats_flush(stats)
                else:
                    stats_flush(stats)
                pt, pb = stats["pt"], stats["pb"]
                C.act(rsb[:], pt[:, 0:T], AF.Ln, (pb, eps_b), (rsb_b,), bias=epsc[:, 0:1], scale=1.0 / D)
                C.act(rs_t[:], rsb[:], AF.Exp, (rsb_b,), (rs_b,), scale=-0.5)
            for c in range(KC):
                C.stt("dve", hT[:, c, 0:T], xr[:, slot, c, :], gall[:, gi, c:c + 1], rs_t[:], ALU.mult, ALU.mult,
                      (xr_b[slot][c], rs_b, cst_b), (hT_b[c],))
            if filler is not None:
                filler()
            if extra_slot is not None:
                xcol = xr[:, extra_slot, :, 0]
                xb = xr_b[extra_slot]
                C.tt("dve", sqx[:], xcol, xcol, ALU.mult, xb, (sqx_b,))
                C.tt("dve", sm_x8, xcol, gall[:, gi, :], ALU.mult, tuple(xb) + (cst_b,), (smx8_b,))
                return lambda: norm_extra(gi, extra_slot)
            return None

        def norm_extra(gi, extra_slot):
            pt2, pb2 = next_ps()
            for c in range(KC):
                C.mm(pt2[:, 0:1], ones_b[:], sqx[:, c:c + 1], c == 0, c == KC - 1, (ones_bb, sqx_b), (pb2,))
            t1 = sm_t1
            C.act(t1, pt2[:, 0:1], AF.Ln, (pb2, eps_b), (smx_b,), bias=epsc[:, 0:1], scale=1.0 / D)
            C.act(sm_rs1, t1, AF.Exp, (smx_b,), (smx_b,), scale=-0.5)
            C.ts("pool", hT[:, :, T], sm_x8, sm_rs1, None, ALU.mult, None, (smx_b, smx8_b), (hTx_b,))

        smx8_b = Buf("smx8")
        sm_t1 = sm(1)
        sm_rs1 = sm(1)
        sm_x8 = sm(8)
        smx_b = Buf("smx")

        def load_dma(t, tb):
            r0 = t * T + tb * 128
            C.dma("sp", xstage2[tb % 2][:], x_d[r0:r0 + 128, :], (), (xstage2_b[tb % 2],), xstage2_b[tb % 2])

        def load_tr(t, tb):
            slot = t % NRING
            xstage = xstage2[tb % 2]
            xstage_b = xstage2_b[tb % 2]
            for half in range(2):
                pt, pb = next_ps()
                for c4 in range(4):
                    c = half * 4 + c4
                    C.tr(pt[:, c4 * 128:(c4 + 1) * 128], xstage[:, c * 128:(c + 1) * 128], ident_f[:],
                         (xstage_b, ident_fb), (pb,))
                dst = xr[:, slot, half * 4:half * 4 + 4, tb * 128:(tb + 1) * 128]
                C.copy("act", dst, pt[:, 0:512].rearrange("p (c n) -> p c n", c=4), (pb,),
                       tuple(xr_b[slot][half * 4:half * 4 + 4]))

        def load_a(t):
            load_tr(t, 0)
            load_tr(t, 1)
            load_dma(t, 2)
            load_dma(t, 3)

        def load_b(t):
            load_tr(t, 2)
            load_tr(t, 3)
            if t + 1 < NT:
                load_dma(t + 1, 0)
                load_dma(t + 1, 1)

        def tmp_slot(i):
            p0 = 18 + 2 * (i % 6)
            return av_f32(p0, 2), (pg[p0], pg[p0 + 1])

        _tmpi = [0]

        def next_tmp():
            r = tmp_slot(_tmpi[0])
            _tmpi[0] += 1
            return r

        ln_st = [sm(6) for _ in range(4)]
        ln_mv = [sm(2) for _ in range(4)]
        ln_rt = [sm(1) for _ in range(4)]
        ln_rs = [sm(1) for _ in range(4)]
        ln_b = [Buf("ln%d" % i) for i in range(4)]
        xc_sb = sm(4)
        xc_b = [Buf("xc%d" % j) for j in range(4)]

        def mixer0(s):
            slot = s % NRING
            nslot = (s + 1) % NRING if s + 1 < NT else None
            extra = norm(slot, 0, nslot, filler=(lambda: load_a(s + 1)) if s + 1 < NT else None)
            gup.prefetch()
            wv1p.prefetch()
            hb = hT_b
            wv_t, wv_b = wv0p.next()
            avps = [next_ps() for _ in range(4)]
            multi_mm([(avps[tb][0][:, 0:512], avps[tb][1],
                       (lambda kc, tb=tb: hT[:, kc, tb * 128:(tb + 1) * 128]),
                       (lambda kc: wv_t[:, kc, :]),
                       (lambda kc: (hb[kc], wv_b))) for tb in range(4)])
            for tb in range(4):
                pt, pb = avps[tb]
                gv, gvb = next_tmp()
                C.act(gv, pt[:, 0:512], AF.Gelu, (pb,), gvb)
                C.P.add("dve", lambda e, o=ln_st[tb], i=gv: e.bn_stats(o, i), gvb, (ln_b[tb],))
                C.P.add("dve", lambda e, o=ln_mv[tb], i=ln_st[tb]: e.bn_aggr(o, i), (ln_b[tb],), (ln_b[tb],))
                C.ts("dve", ln_rt[tb], ln_mv[tb][:, 1:2], EPS, None, ALU.add, None, (ln_b[tb],), (ln_b[tb],))
                C.tt("pool", ln_rs[tb], ln_rt[tb], neghalf[:, 0:1], ALU.pow, (ln_b[tb], neghalf_b), (ln_b[tb],))
                C.ts("dve", gv, gv, ln_mv[tb][:, 0:1], ln_rs[tb], ALU.subtract, ALU.mult, (ln_b[tb],), gvb)
                C.tt("pool", gv, gv, lng[:], ALU.mult, (cst_b,), gvb)
                vt = av_bf(8 + tb, 1)
                C.tt("pool", vt, gv, lnb[:], ALU.add, gvb + (cst_b,), (pg[8 + tb],))
            if extra is not None:
                extra()
            for j in range(4):
                z = zext[j % 2]
                zb = zext_b[j % 2]
                bb = av_f32(14 + 2 * (j % 2), 2)
                bbb = (pg[14 + 2 * (j % 2)], pg[15 + 2 * (j % 2)])
                xp = acc_t[2]

                def halo(col, wt_, wb_):
                    if nslot is None:
                        return
                    for kc in range(KC):
                        C.mm(xp[:, col:col + 1], wt_[:, kc, :], hT[:, kc, T:T + 1], kc == 0, kc == KC - 1,
                             (wb_, hTx_b), (xps_b,))

                wb_t, wb_b = proj.next()
                pt, pb = proj_fm(wb_t, wb_b, lambda kc: hT[:, kc, 0:T], hb, T)
                C.copy("act", bb, pt[:, 0:T], (pb,), bbb)
                wc_t, wc_b = proj.next()
                ptc, pbc = proj_fm(wc_t, wc_b, lambda kc: hT[:, kc, 0:T], hb, T)
                halo(496 + 2 * j, wc_t, wc_b)
                tc_, tcb = next_tmp()
                C.copy("act", tc_, ptc[:, 0:T], (pbc,), tcb)
                wh_t, wh_b = proj.next()
                pth, pbh = proj_fm(wh_t, wh_b, lambda kc: hT[:, kc, 0:T], hb, T)
                halo(497 + 2 * j, wh_t, wh_b)
                C.tt("dve", z[:, 1:T + 1], tc_, pth[:, 0:T], ALU.mult, tcb + (pbh,), (zb,))
                if s > 0:
                    C.copy("pool", z[:, 0:1], zprev[:, j:j + 1], (zprev_b[j],), (zb,))
                else:
                    C.memset("pool", z[:, 0:1], 0.0, (zb,))
                if nslot is not None:
                    C.copy("act", xc_sb[:, j:j + 1], xp[:, 496 + 2 * j:497 + 2 * j], (xps_b,), (xc_b[j],))
                    C.tt("dve", z[:, T + 1:T + 2], xc_sb[:, j:j + 1], xp[:, 497 + 2 * j:498 + 2 * j], ALU.mult,
                         (xc_b[j], xps_b), (zb,))
                else:
                    C.memset("pool", z[:, T + 1:T + 2], 0.0, (zb,))
                C.copy("pool", zprev[:, j:j + 1], z[:, T:T + 1], (zb,), (zprev_b[j],))
                ct, ctb = next_tmp()
                C.ts("dve", ct, z[:, 0:T], cw[:, 0, j:j + 1], None, ALU.mult, None, (zb, cst_b), ctb)
                C.stt("dve", ct, z[:, 1:T + 1], cw[:, 1, j:j + 1], ct, ALU.mult, ALU.add, (zb, cst_b), ctb)
                C.stt("dve", ct, z[:, 2:T + 2], cw[:, 2, j:j + 1], ct, ALU.mult, ALU.add, (zb, cst_b), ctb)
                C.tt("pool", av_bf(4 + j, 1), ct, bb, ALU.mult, ctb + bbb, (pg[4 + j],))
            for g in range(4):
                w_t, w_b = proj.next()
                pt, pb = proj_fm(w_t, w_b, lambda kc: hT[:, kc, 0:T], hb, T)
                au = av_bf(12 + g % 2, 1)
                aub = pg[12 + g % 2]
                C.act(au, pt[:, 0:T], AF.Gelu, (pb,), (aub,))
                pt2, pb2 = next_ps()
                for tb in range(4):
                    vt = av_bf(8 + tb, 1)
                    C.mm(pt2[:, tb * 128:(tb + 1) * 128], vt[:, g * 128:(g + 1) * 128], wsp_b[:, g, :], True, True,
                         (pg[8 + tb], wsp_bb), (pb2,))
                t1, t1b = next_tmp()
                C.tt("dve", t1.rearrange("p (t q) -> p t q", t=4), pt2[:, 0:512].rearrange("p (t q) -> p t q", t=4),
                     bsp[:, g:g + 1, :].broadcast_to([128, 4, 128]), ALU.add, (pb2, cst_b), t1b)
                C.tt("pool", av_bf(g, 1), t1, au, ALU.mult, t1b + (aub,), (pg[g],))
            st = stats_begin()
            for dc in range(8):
                w_t, w_b = proj.next()
                pt, pb = proj_fm(w_t, w_b, lambda kc: av_bf(kc, 1), pg[0:8], T)
                resid_add(slot, dc, pt, pb, stats=st)
            pending[slot] = st

        def ffn(t, layer):
            slot = t % NRING
            norm(slot, 1 + 2 * layer, filler=(lambda: load_b(t + 1)) if (layer == 0 and t + 1 < NT) else None,
                 stats=pending.pop(slot, None))
            dnp.prefetch()

            def act_part(fc, ptg, pbg, ptu, pbu):
                p0 = 22 + 2 * (fc % 2)
                sg = av_f32(p0, 2)
                sgb = (pg[p0], pg[p0 + 1])
                C.act(sg, ptg[:, 0:T], AF.Silu, (pbg,), sgb)
                C.tt("dve", av_bf(fc, 1), sg, ptu[:, 0:T], ALU.mult, sgb + (pbu,), (pg[fc],))

            gu0_t, gu0_b = gup.next()
            gu1_t, gu1_b = gup.next(held=1)
            b4 = [next_ps() for _ in range(4)]
            grp = []
            for i, (gt, gb_, tsel) in enumerate(((gu0_t, gu0_b, 0), (gu0_t, gu0_b, 1), (gu1_t, gu1_b, 0), (gu1_t, gu1_b, 1))):
                grp.append((b4[i][0][:, 0:T], b4[i][1],
                            (lambda kc, gt=gt, tsel=tsel: gt[:, tsel, kc, :]),
                            (lambda kc: hT[:, kc, 0:T]),
                            (lambda kc, gb_=gb_: (gb_, hT_b[kc]))))
            multi_mm(grp)
            act_part(0, b4[0][0], b4[0][1], b4[1][0], b4[1][1])
            act_part(1, b4[2][0], b4[2][1], b4[3][0], b4[3][1])
            for fc in range(2, FC):
                gu_t, gu_b = gup.next()
                ptg, pbg = next_ps()
                ptu, pbu = next_ps()
                for kc in range(KC):
                    C.mm(ptg[:, 0:T], gu_t[:, 0, kc, :], hT[:, kc, 0:T], kc == 0, kc == KC - 1, (gu_b, hT_b[kc]), (pbg,))
                for kc in range(KC):
                    C.mm(ptu[:, 0:T], gu_t[:, 1, kc, :], hT[:, kc, 0:T], kc == 0, kc == KC - 1, (gu_b, hT_b[kc]), (pbu,))
                act_part(fc, ptg, pbg, ptu, pbu)
            proj.prefetch()
            st = stats_begin() if layer == 0 else None
            for dc in range(8):
                wd_t, wd_b = dnp.next()
                pt, pb = next_ps()
                for fc in range(FC):
                    C.mm(pt[:, 0:T], wd_t[:, fc // 11, (fc % 11) * 128:(fc % 11 + 1) * 128], av_bf(fc, 1), fc == 0, fc == FC - 1, (wd_b, pg[fc]), (pb,))
                resid_add(slot, dc, pt, pb, stats=st)
            if st is not None:
                pending[slot] = st

        def l1_kv(s):
            slot = s % NRING
            wv0p.prefetch()
            norm(slot, 2, stats=pending.pop(slot, None), rs=(tbuf, tbuf_b))
            wv_t, wv_b = wv1p.next()
            rb = (s % 4) * 4
            wk0_t, wk0_b = proj.next()
            wk1_t, wk1_b = proj.next(held=1)
            kps = [next_ps(), next_ps()]
            multi_mm([(kps[i][0][:, 0:T], kps[i][1],
                       (lambda kc, wt=wt: wt[:, kc, :]),
                       (lambda kc: hT[:, kc, 0:T]),
                       (lambda kc, wb=wb: (wb, hT_b[kc]))) for i, (wt, wb) in enumerate(((wk0_t, wk0_b), (wk1_t, wk1_b)))])
            for k2 in range(2):
                pt, pb = kps[k2]
                C.copy("act", kring[:, k2, rb * 128:rb * 128 + T], pt[:, 0:T], (pb,), tuple(kr_b[k2][rb:rb + 4]))
            for tb in range(4):
                pt, pb = next_ps()
                for kc in range(KC):
                    C.mm(pt[:, 0:256], hT[:, kc, tb * 128:(tb + 1) * 128], wv_t[:, kc, :], kc == 0, kc == KC - 1,
                         (hT_b[kc], wv_b), (pb,))
                C.copy("dve", vring[:, rb + tb, :, 0:64], pt[:, 0:256].rearrange("p (g d) -> p g d", g=4), (pb,),
                       (vr_b[rb + tb],))

        def l1_q(s):
            slot = s % NRING
            fin = s >= 1
            norm(slot, 2, rs=(tbuf, tbuf_b), reuse=True)
            if fin:
                final_block(s - 1, 0)
            ws3 = [proj.next(), proj.next(held=1), proj.next(held=2)]
            qps = [next_ps() for _ in range(3)]
            multi_mm([(qps[i][0][:, 0:T], qps[i][1],
                       (lambda kc, wt=ws3[i][0]: wt[:, kc, :]),
                       (lambda kc: hT[:, kc, 0:T]),
                       (lambda kc, wb=ws3[i][1]: (wb, hT_b[kc]))) for i in range(3)])
            for c in range(8):
                if c < 3:
                    pt, pb = qps[c]
                else:
                    w_t, w_b = proj.next()
                    pt, pb = proj_fm(w_t, w_b, lambda kc: hT[:, kc, 0:T], hT_b, T)
                C.act(qT[:, c, :], pt[:, 0:T], AF.Copy, (pb,), (qT_b[c],), scale=0.125)
                if fin and c == 2:
                    final_block(s - 1, 1)
                if fin and c == 4:
                    final_block(s - 1, 2)
                if fin and c == 6:
                    final_block(s - 1, 3)

        den = sm(16)
        rden = sm(16)
        den_b = Buf("den")

        def attention(t):
            slot = t % NRING
            proj.prefetch()
            gup.prefetch()
            for qb in range(4):
                gb = 4 * t + qb
                js = [j for j in range(3) if 0 <= gb - 1 + j < 32]
                nj = len(js)
                j0 = js[0]
                ao = av_bf(8 + 2 * (qb % 2), 2)
                aob = (pg[8 + 2 * (qb % 2)], pg[9 + 2 * (qb % 2)])
                def emit_st2(cq):
                    sts = []
                    for half in range(2):
                        hs = 2 * cq + half
                        st_t, st_b = next_ps()
                        sts.append((st_t, st_b))
                        C.mm(st_t[:, 0:nj * 128], ident_b[:], biasT[:, hs, j0 * 128:(j0 + nj) * 128], True, False,
                             (ident_bb, biasT_b), (st_b,))
                    for idx, j in enumerate(js):
                        kb = (gb - 1 + j) % 16
                        for half in range(2):
                            hs = 2 * cq + half
                            g = HEAD_OF_SLOT[hs] // 4
                            assert g % 2 == half
                            k2 = g // 2
                            st_t, st_b = sts[half]
                            C.mm(st_t[:, idx * 128:(idx + 1) * 128],
                                 kring[half * 64:(half + 1) * 64, k2, kb * 128:(kb + 1) * 128],
                                 qT[half * 64:(half + 1) * 64, cq, qb * 128:(qb + 1) * 128],
                                 False, idx == nj - 1, (kr_b[k2][kb], qT_b[cq]), (st_b,))
                    for half in range(2):
                        hs = 2 * cq + half
                        st_t, st_b = sts[half]
                        ptile = av_bf(12 + hs % 4, 1)
                        ptb = pg[12 + hs % 4]
                        C.act(ptile[:, 0:nj * 128], st_t[:, 0:nj * 128], AF.Exp, (st_b, att_b), (ptb,),
                              bias=negc[:, hs:hs + 1], scale=1.0)

                def emit_pv(hs):
                    g = HEAD_OF_SLOT[hs] // 4
                    ptile = av_bf(12 + hs % 4, 1)
                    ptb = pg[12 + hs % 4]
                    bank = hs // 7
                    col = (hs % 7) * 65
                    for idx, j in enumerate(js):
                        kb = (gb - 1 + j) % 16
                        C.mm(acc_t[bank][:, col:col + 65], ptile[:, idx * 128:(idx + 1) * 128], vring[:, kb, g, :],
                             idx == 0, idx == nj - 1, (ptb, vr_b[kb]), (acc_b[bank],))

                emit_st2(0)
                for cq in range(1, 8):
                    emit_st2(cq)
                    emit_pv(2 * cq - 2)
                    emit_pv(2 * cq - 1)
                emit_pv(14)
                emit_pv(15)
                for bank in range(3):
                    h0 = bank * 7
                    h1 = min(16, h0 + 7)
                    nh = h1 - h0
                    a3 = acc_t[bank][:, 0:nh * 65].rearrange("p (h e) -> p h e", e=65)
                    C.tt("dve", den[:, h0:h1], a3[:, :, 64], sinkterm[:, h0:h1], ALU.add, (acc_b[bank], att_b), (den_b,))
                    C.P.add("dve", lambda e, o=rden[:, h0:h1], i=den[:, h0:h1]: e.reciprocal(o, i), (den_b,), (den_b,))
                    C.tt("dve", ao[:, h0 * 64:h1 * 64].rearrange("p (h d) -> p h d", d=64), a3[:, :, 0:64],
                         rden[:, h0:h1].unsqueeze(2).broadcast_to([128, nh, 64]), ALU.mult, (acc_b[bank], den_b), aob)
                tp, tpb = next_ps()
                tpv = tp[:, 0:512].bitcast(BF16)
                for c in range(8):
                    C.tr(tpv[:, c * 128:(c + 1) * 128], ao[:, c * 128:(c + 1) * 128], ident_b[:], aob + (ident_bb,), (tpb,))
                dst = av_bf(0, 8).rearrange("p (c n) -> p c n", c=8)[:, :, qb * 128:(qb + 1) * 128]
                C.copy("act", dst, tpv.rearrange("p (c n) -> p c n", c=8), (tpb,), tuple(pg[0:8]))
            st = stats_begin()
            for dc in range(8):
                w_t, w_b = proj.next()
                pt, pb = proj_fm(w_t, w_b, lambda kc: av_bf(kc, 1), pg[0:8], T)
                resid_add(slot, dc, pt, pb, stats=st)
            pending[slot] = st

        fss = [sm(1), sm(1)]
        fs_t = sm(1)
        fs_r = sm(1)
        fs_b = Buf("fs")

        def final_block(t, tb):
            slot = t % NRING
            pts = [(acc_t[0], acc_b[0]), (acc_t[1], acc_b[1])]
            for c in range(8):
                pt, pb = pts[c // 4]
                C.tr(pt[:, (c % 4) * 128:(c % 4 + 1) * 128], xr[:, slot, c, tb * 128:(tb + 1) * 128], ident_f[:],
                     (xr_b[slot][c], ident_fb), (pb,))
            for h2 in range(2):
                pt, pb = pts[h2]
                C.act(junk[:], pt[:, 0:512], AF.Square, (pb,), (junk_b, fs_b), accum_out=fss[h2])
            C.tt("dve", fs_t, fss[0], fss[1], ALU.add, (fs_b,), (fs_b,))
            C.ts("dve", fs_t, fs_t, 1.0 / D, EPS, ALU.mult, ALU.add, (fs_b,), (fs_b,))
            C.tt("pool", fs_r, fs_t, neghalf[:, 0:1], ALU.pow, (fs_b, neghalf_b), (fs_b,))
            for h2 in range(2):
                pt, pb = pts[h2]
                C.stt("dve", ostage[:, h2 * 512:(h2 + 1) * 512], pt[:, 0:512], fs_r, gfin[:, h2 * 512:(h2 + 1) * 512],
                      ALU.mult, ALU.mult, (pb, fs_b, cst_b, ostore_b), (ostage_b,))
            r0 = t * T + tb * 128
            C.dma("sp", out_d[r0:r0 + 128, :], ostage[:], (ostage_b,), (ostore_b,), ostore_b)

        def final(t):
            for tb in range(4):
                final_block(t, tb)

        def dump(k, t):
            if not debug:
                return
            slot = t % NRING
            b = C.dbuf("dbg%d_%d" % (k, t))
            C.dma("sp", dbg_d[k, t], xr[:, slot], tuple(xr_b[slot]), (), b)
            P.final_waits.append(b)

        wv0p.prefetch()
        load_dma(0, 0)
        load_dma(0, 1)
        load_a(0)
        load_b(0)
        for s in range(NT + 1):
            if s < NT:
                dump(0, s)
                mixer0(s)
                dump(1, s)
                ffn(s, 0)
                dump(2, s)
                if stages >= 2:
                    l1_kv(s)
            if s >= 1 and stages >= 2:
                attention(s - 1)
                dump(3, s - 1)
                ffn(s - 1, 1)
                dump(4, s - 1)
                if s == NT:
                    final(s - 1)
            if s < NT and stages >= 2:
                l1_q(s)

        fw = [(ostore_b.sem, ostore_b.semcnt)]
        for b in P.final_waits:
            fw.append((b.sem, b.semcnt))
        P.final_waits = fw
        print("SBUF bytes remaining per partition:", nc.sbuf_bytes_remaining, "ops:", {e: len(P.ops[e]) for e in ENGS})
        with nc.Block() as block:
            P.emit_all(nc, block, C.engsem)
    return nc


def _t5_bucket_table():
    nb = 16
    max_exact = 8
    rel = np.arange(-255, 256)
    ret = np.where(rel > 0, nb, 0)
    n = np.abs(rel)
    nf = np.maximum(n, 1).astype(np.float32)
    large = max_exact + (np.log(nf / max_exact) / np.log(128 / max_exact) * (nb - max_exact)).astype(np.int32)
    large = np.minimum(large, nb - 1)
    return ret + np.where(n < max_exact, n, large)


def _chunks_kmajor(W):
    K, E = W.shape
    a = W.reshape(K // 128, 128, E // 128, 128)
    return np.ascontiguousarray(a.transpose(2, 1, 0, 3)).reshape(E // 128, 128, (K // 128) * 128)


def prep_shared(inp):
    f = lambda a: np.ascontiguousarray(np.asarray(a, dtype=np.float32))
    w_in = f(inp["even_w_in"])[0]
    cols = np.concatenate([np.arange(0, 512), np.arange(1024, 2560)])
    ws_in = _chunks_kmajor(w_in[:, cols])
    ws_out0 = _chunks_kmajor(f(inp["even_w_out"])[0])
    wqkv = f(inp["attn_w_qkv"])[0]
    qcols = np.concatenate([np.arange(h * 64, h * 64 + 64) for h in HEAD_OF_SLOT])
    ws_q = _chunks_kmajor(wqkv[:, qcols])
    ws_k = _chunks_kmajor(wqkv[:, 1024:1280])
    wo1 = f(inp["attn_w_out"])[0][qcols, :]
    ws_out1 = _chunks_kmajor(wo1)
    ws = np.concatenate([ws_in, ws_out0, ws_q, ws_k, ws_out1], axis=0)
    assert ws.shape == (NWS, 128, 1024)
    gate = f(inp["ffn_w_gate"])
    up = f(inp["ffn_w_up"])
    down = f(inp["ffn_w_down"])
    gu = np.stack([np.stack([_chunks_kmajor(gate[l]), _chunks_kmajor(up[l])], axis=1) for l in range(2)], axis=0)
    gu = np.ascontiguousarray(gu.reshape(2 * FC, 2, 128, 1024))
    wd = np.stack([_chunks_kmajor(down[l]) for l in range(2)], axis=0).reshape(16, 128, FC, 128)
    wv0 = np.ascontiguousarray(w_in[:, 512:1024].reshape(8, 128, 512).transpose(1, 0, 2))
    wv1 = np.ascontiguousarray(wqkv[:, 1280:1536].reshape(8, 128, 256).transpose(1, 0, 2))
    nm = f(inp["norm_mix"])
    nf_ = f(inp["norm_ffn"])
    gl = np.stack([nm[0], nf_[0], nm[1], nf_[1]], axis=0)
    gall = np.ascontiguousarray(gl.reshape(4, 8, 128).transpose(2, 0, 1))
    rep = lambda v: np.ascontiguousarray(np.broadcast_to(v, (128,) + v.shape))
    gfin = rep(f(inp["final_norm"]))
    lng = rep(f(inp["even_v_ln_g"])[0])
    lnb = rep(f(inp["even_v_ln_b"])[0])
    bsp = rep(f(inp["even_b_spatial"])[0])
    wsp = np.ascontiguousarray(f(inp["even_w_spatial"])[0].transpose(2, 0, 1))
    cw = np.ascontiguousarray(f(inp["even_conv_w"])[0].reshape(3, 4, 128).transpose(2, 0, 1))
    tab = _t5_bucket_table()
    k = np.arange(128)[:, None, None]
    j = np.arange(3)[None, :, None]
    q = np.arange(128)[None, None, :]
    rel = (j - 1) * 128 + k - q
    bucket = tab[rel + 255]
    rb = f(inp["rel_bias"])[:, HEAD_OF_SLOT]
    bias = rb[bucket]
    band = (np.abs(rel) <= 128)[..., None]
    bias = np.where(band, bias, np.float32(NEG_MASK)).astype(np.float32)
    biasT = np.ascontiguousarray(bias.transpose(0, 3, 1, 2)).reshape(128, 16, 384)
    sinkb = rep(f(inp["attn_sink"])[0][HEAD_OF_SLOT])
    return {
        "ws": ws, "gu": gu, "wd": np.ascontiguousarray(wd), "wv0": wv0, "wv1": wv1, "gall": gall, "gfin": gfin,
        "lng": lng, "lnb": lnb, "bsp": bsp, "wsp": wsp, "cw": cw, "biasT": biasT, "sinkb": sinkb,
        "ident": np.eye(128, dtype=np.float32),
    }


_NC_CACHE = {}


def kernel(**inputs):
    x = np.ascontiguousarray(np.asarray(inputs["x"], dtype=np.float32))
    shared = prep_shared(inputs)
    if "nc" not in _NC_CACHE:
        _NC_CACHE["nc"] = build_program(False)
    nc = _NC_CACHE["nc"]
    in_maps = []
    for b in range(8):
        m = dict(shared)
        m["x"] = x[b]
        in_maps.append(m)
    res = run_bass_kernel_spmd(nc, in_maps, core_ids=list(range(8)))
    out = np.stack([np.asarray(r["out"], dtype=np.float32) for r in res.results], axis=0)
    return out
```

```python
import numpy as np
from contextlib import ExitStack
import concourse.bass as bass
import concourse.mybir as mybir
from concourse.bass_utils import run_bass_kernel_spmd

F32 = mybir.dt.float32
BF16 = mybir.dt.bfloat16
AF = mybir.ActivationFunctionType
ALU = mybir.AluOpType
AX = mybir.AxisListType

ENGS = ("pe", "act", "dve", "pool", "sp")


class Buf:
    __slots__ = ("name", "w", "r", "rd", "sem", "semcnt")

    def __init__(self, name, sem=None):
        self.name = name
        self.w = None
        self.r = {}
        self.rd = []
        self.sem = sem
        self.semcnt = 0


class Op:
    __slots__ = ("eng", "emit", "deps", "mile", "mileno", "sem", "semval", "is_dma", "seq")


class Prog:
    def __init__(self):
        self.ops = {e: [] for e in ENGS}
        self.final_waits = []

    def add(self, eng, emit, reads=(), writes=(), dma_buf=None, indep=False):
        op = Op()
        op.eng = eng
        op.emit = emit
        op.mile = False
        op.mileno = 0
        op.is_dma = dma_buf is not None
        op.sem = None
        op.semval = 0
        op.seq = len(self.ops[eng])
        deps = []
        wset = set(id(b) for b in writes)
        for b in reads:
            if b.w is not None:
                deps.append(b.w)
        for b in writes:
            if b.w is not None and not indep:
                deps.append(b.w)
            deps.extend(b.r.values())
            deps.extend(b.rd)
        best = {}
        dl = []
        seen = set()
        for d in deps:
            if d.is_dma:
                if id(d) not in seen:
                    seen.add(id(d))
                    dl.append(d)
            else:
                if d.eng == "pe" and eng == "pe" and not op.is_dma:
                    continue
                cur = best.get(d.eng)
                if cur is None or d.seq > cur.seq:
                    best[d.eng] = d
        for d in best.values():
            d.mile = True
            dl.append(d)
        op.deps = dl
        if op.is_dma:
            dma_buf.semcnt += 16
            op.sem = dma_buf.sem
            op.semval = dma_buf.semcnt
        for b in writes:
            b.w = op
            b.r = {}
            b.rd = []
        for b in reads:
            if id(b) in wset:
                continue
            if op.is_dma:
                b.rd.append(op)
            else:
                b.r[eng] = op
        self.ops[eng].append(op)
        return op

    def emit_all(self, nc, block, engsem):
        for e in ENGS:
            n = 0
            for op in self.ops[e]:
                if op.mile and not op.is_dma:
                    n += 1
                    op.mileno = n
        prog = self

        def run(ename, eobj):
            known = {}
            for op in prog.ops[ename]:
                need = {}
                for d in op.deps:
                    if d.is_dma:
                        s, v = d.sem, d.semval
                    else:
                        s, v = engsem[d.eng], d.mileno
                    k = s.num
                    if k not in need or need[k][1] < v:
                        need[k] = (s, v)
                for k, (s, v) in need.items():
                    if known.get(k, 0) < v:
                        eobj.wait_ge(s, v)
                        known[k] = v
                ins = op.emit(eobj)
                if op.is_dma:
                    ins.then_inc(op.sem, 16)
                elif op.mile:
                    ins.then_inc(engsem[ename], 1)
            if ename == "sp":
                for (s, v) in prog.final_waits:
                    eobj.wait_ge(s, v)

        @block.tensor
        def _(e):
            run("pe", e)

        @block.scalar
        def _(e):
            run("act", e)

        @block.vector
        def _(e):
            run("dve", e)

        @block.gpsimd
        def _(e):
            run("pool", e)

        @block.sync
        def _(e):
            run("sp", e)


class Ctx:
    def __init__(self, nc, st):
        self.nc = nc
        self.st = st
        self.P = Prog()
        self.nsem = 0
        self.engsem = {}
        for e in ("pe", "act", "dve", "pool"):
            self.engsem[e] = st.enter_context(nc.semaphore("prog_" + e))

    def sbuf(self, name, shape, dt):
        return self.st.enter_context(self.nc.sbuf_tensor("sb_" + name, list(shape), dt))

    def psum(self, name, shape, dt):
        return self.st.enter_context(self.nc.psum_tensor("pp_" + name, list(shape), dt))

    def dbuf(self, name):
        self.nsem += 1
        s = self.st.enter_context(self.nc.semaphore("d_" + name))
        return Buf(name, sem=s)

    def mm(self, out, lhsT, rhs, start, stop, reads, writes):
        return self.P.add("pe", lambda e: e.matmul(out, lhsT, rhs, start=start, stop=stop), reads, writes)

    def tr(self, out, in_, ident, reads, writes):
        return self.P.add("pe", lambda e: e.transpose(out, in_, ident), reads, writes)

    def act(self, out, in_, func, reads, writes, bias=None, scale=None, accum_out=None, eng="act"):
        kw = {}
        if bias is not None:
            kw["bias"] = bias
        if scale is not None:
            kw["scale"] = scale
        if accum_out is not None:
            kw["accum_out"] = accum_out
        return self.P.add("act", lambda e: e.activation(out, in_, func, **kw), reads, writes)

    def tt(self, eng, out, in0, in1, op, reads, writes):
        return self.P.add(eng, lambda e: e.tensor_tensor(out, in0, in1, op), reads, writes)

    def ts(self, eng, out, in0, s1, s2, op0, op1, reads, writes):
        if s2 is None:
            return self.P.add(eng, lambda e: e.tensor_scalar(out, in0, s1, None, op0), reads, writes)
        return self.P.add(eng, lambda e: e.tensor_scalar(out, in0, s1, s2, op0, op1), reads, writes)

    def stt(self, eng, out, in0, scalar, in1, op0, op1, reads, writes):
        return self.P.add(eng, lambda e: e.scalar_tensor_tensor(out, in0, scalar, in1, op0, op1), reads, writes)

    def copy(self, eng, out, in_, reads, writes):
        if eng == "act":
            return self.P.add("act", lambda e: e.copy(out, in_), reads, writes)
        return self.P.add(eng, lambda e: e.tensor_copy(out, in_), reads, writes)

    def memset(self, eng, ap, val, writes):
        return self.P.add(eng, lambda e: e.memset(ap, val), (), writes)

    def dma(self, eng, out, in_, reads, writes, dma_buf, indep=False, **kw):
        return self.P.add(eng, lambda e: e.dma_start(out, in_, **kw), reads, writes, dma_buf=dma_buf, indep=indep)


D = 1024
KC = 8
S = 4096
T = 512
NT = S // T
FF = 2816
FC = 22
EPS = 1e-6
NRING = 3
HEAD_OF_SLOT = []
for _c in range(8):
    HEAD_OF_SLOT.append([0, 1, 2, 3, 8, 9, 10, 11][_c])
    HEAD_OF_SLOT.append([4, 5, 6, 7, 12, 13, 14, 15][_c])
NEG_MASK = -30000.0

WS_IN = 0
WS_OUT0 = 16
WS_QK = 24
WS_OUT1 = 34
NWS = 42


class WPool:
    def __init__(self, C, name, nslots, shape):
        self.C = C
        self.n = nslots
        self.t = [C.sbuf("%s%d" % (name, i), shape, BF16) for i in range(nslots)]
        self.b = [C.dbuf("%s%d" % (name, i)) for i in range(nslots)]
        self.sb = [C.dbuf("%s%dst" % (name, i)) for i in range(nslots)]
        self.req = []
        self.chunk = {}
        self.emitted = 0
        self.cons = 0

    def _top(self, upto):
        C = self.C
        upto = min(upto, len(self.req))
        while self.emitted < upto:
            i = self.emitted
            key, src32, scr = self.req[i]
            k = i % self.n
            if key not in self.chunk:
                cb = Buf("chunk")
                self.chunk[key] = cb
                C.dma("pool", self.t[k][:], src32, (), (self.b[k],), self.b[k])
                C.dma("sp", scr, self.t[k][:], (self.b[k],), (cb,), self.sb[k])
            else:
                C.dma("sp", self.t[k][:], scr, (self.chunk[key],), (self.b[k],), self.b[k])
            self.emitted += 1

    def prefetch(self):
        self._top(self.cons + self.n)

    def next(self, held=0):
        i = self.cons
        assert i < len(self.req), "weight pool underflow"
        self._top(i + self.n - held)
        self.cons += 1
        return self.t[i % self.n], self.b[i % self.n]


def build_program(debug=False, NT=NT, stages=3):
    nc = bass.Bass("TRN2", target_bir_lowering=False)
    dt_in = lambda name, shape: nc.dram_tensor(name, list(shape), F32, kind="ExternalInput").ap()
    x_d = dt_in("x", [S, D])
    ws_d = dt_in("ws", [NWS, 128, 1024])
    gu_d = dt_in("gu", [2 * FC, 2, 128, 1024])
    wd_d = dt_in("wd", [16, 128, FC, 128])
    wv0_d = dt_in("wv0", [128, 8, 512])
    wv1_d = dt_in("wv1", [128, 8, 256])
    gall_d = dt_in("gall", [128, 4, 8])
    gfin_d = dt_in("gfin", [128, 1024])
    lng_d = dt_in("lng", [128, 512])
    lnb_d = dt_in("lnb", [128, 512])
    bsp_d = dt_in("bsp", [128, 4, 128])
    wsp_d = dt_in("wsp", [128, 4, 128])
    cw_d = dt_in("cw", [128, 3, 4])
    bias_d = dt_in("biasT", [128, 16, 384])
    sink_d = dt_in("sinkb", [128, 16])
    id_d = dt_in("ident", [128, 128])
    out_d = nc.dram_tensor("out", [S, D], F32, kind="ExternalOutput").ap()
    if debug:
        dbg_d = nc.dram_tensor("dbg", [5, NT, 128, 8, 512], F32, kind="ExternalOutput").ap()
    ws_s = nc.dram_tensor("ws_s", [NWS, 128, 1024], BF16).ap()
    gu_s = nc.dram_tensor("gu_s", [2 * FC, 2, 128, 1024], BF16).ap()
    wd_s = nc.dram_tensor("wd_s", [16, 128, FC, 128], BF16).ap()
    wv0_s = nc.dram_tensor("wv0_s", [128, 8, 512], BF16).ap()
    wv1_s = nc.dram_tensor("wv1_s", [128, 8, 256], BF16).ap()

    with ExitStack() as st:
        C = Ctx(nc, st)
        P = C.P
        xr = C.sbuf("xr", [128, NRING, KC, T], F32)
        xr_b = [[Buf("xr%d_%d" % (r, c)) for c in range(KC)] for r in range(NRING)]
        hT = C.sbuf("hT", [128, KC, T + 2], BF16)
        hT_b = [Buf("hT%d" % c) for c in range(KC)]
        hTx_b = Buf("hTx")
        arena = C.sbuf("arena", [128, 32 * 256], F32)
        pg = [Buf("pg%d" % i) for i in range(32)]

        def av_bf(p0, np_):
            return arena[:, p0 * 256:(p0 + np_) * 256].bitcast(BF16)

        def av_f32(p0, np_):
            return arena[:, p0 * 256:(p0 + np_) * 256]

        kring = C.sbuf("kring", [128, 2, 4 * T], BF16)
        kr_b = [[Buf("k%d_%d" % (c, b)) for b in range(16)] for c in range(2)]
        vring = C.sbuf("vring", [128, 16, 4, 65], BF16)
        vr_b = [Buf("v%d" % b) for b in range(16)]
        qT = C.sbuf("qT", [128, KC, T], BF16)
        qT_b = [Buf("qT%d" % c) for c in range(KC)]
        biasT = C.sbuf("biasTs", [128, 16, 384], BF16)
        biasT_b = C.dbuf("biasT")
        xstage2 = [C.sbuf("xstage%d" % i, [128, D], F32) for i in range(2)]
        xstage2_b = [C.dbuf("xstage%d" % i) for i in range(2)]
        ostage = C.sbuf("ostage", [128, D], F32)
        ostage_b = Buf("ostage")
        ostore_b = C.dbuf("ostore")
        sq = [C.sbuf("sq%d" % i, [128, T], BF16) for i in range(3)]
        sq_b = [Buf("sq%d" % i) for i in range(3)]
        nrm_t = sq
        nrm_b = sq_b
        sqx = C.sbuf("sqx", [128, 8], BF16)
        sqx_b = Buf("sqx")
        tbuf = C.sbuf("tbuf", [128, T], F32)
        tbuf_b = Buf("tbuf")
        rsb = C.sbuf("rsb", [128, T], F32)
        rsb_b = Buf("rsb")
        small = C.sbuf("small", [128, 256], F32)
        zext = [C.sbuf("zext%d" % i, [128, T + 4], F32) for i in range(2)]
        zext_b = [Buf("zext%d" % i) for i in range(2)]
        zprev = C.sbuf("zprev", [128, 4], F32)
        zprev_b = [Buf("zprev%d" % j) for j in range(4)]
        ident_f = C.sbuf("ident_f", [128, 128], F32)
        ident_fb = C.dbuf("ident_f")
        ident_b = C.sbuf("ident_b", [128, 128], BF16)
        ident_bb = Buf("ident_b")
        ones_b = C.sbuf("ones_b", [128, 128], BF16)
        ones_bb = Buf("ones_b")
        epsc = C.sbuf("epsc", [128, 1], F32)
        eps_b = Buf("epsc")
        neghalf = C.sbuf("neghalf", [128, 8], F32)
        neghalf_b = Buf("neghalf")
        gfin = C.sbuf("gfin", [128, D], F32)
        lng = C.sbuf("lng", [128, 512], F32)
        lnb = C.sbuf("lnb", [128, 512], F32)
        bsp = C.sbuf("bsp", [128, 4, 128], F32)
        wsp_f = C.sbuf("wsp_f", [128, 4, 128], F32)
        wsp_b = C.sbuf("wsp_b", [128, 4, 128], BF16)
        wsp_bb = Buf("wsp_b")
        gall = C.sbuf("gall", [128, 4, 8], F32)
        cw = C.sbuf("cw", [128, 3, 4], F32)
        sinkb = C.sbuf("sinkb", [128, 16], F32)
        negc = C.sbuf("negc", [128, 16], F32)
        sinkterm = C.sbuf("sinkterm", [128, 16], F32)
        att_b = Buf("attconst")
        cst_b = C.dbuf("consts")
        junk = C.sbuf("junk", [128, T], BF16)
        junk_b = Buf("junk")

        _sm = [0]

        def sm(n):
            a = small[:, _sm[0]:_sm[0] + n]
            _sm[0] += n
            assert _sm[0] <= 256
            return a

        psb = [C.psum("ps%d" % i, [128, 512], F32) for i in range(8)]
        ps_b = [Buf("ps%d" % i) for i in range(8)]
        _pr = [0]

        def next_ps():
            k = _pr[0] % 5
            _pr[0] += 1
            return psb[k], ps_b[k]

        acc_t = psb[5:8]
        acc_b = ps_b[5:8]
        xps_b = Buf("xps")

        C.dma("pool", biasT[:], bias_d, (), (biasT_b,), biasT_b)

        for (dst, src) in ((ident_f, id_d), (gfin, gfin_d), (lng, lng_d), (lnb, lnb_d), (bsp, bsp_d),
                           (wsp_f, wsp_d), (gall, gall_d), (cw, cw_d), (sinkb, sink_d)):
            C.dma("sp", dst[:], src, (), (cst_b,), cst_b, indep=True)
        C.copy("dve", ident_b[:], ident_f[:], (cst_b,), (ident_bb,))
        C.memset("pool", ones_b[:], 1.0, (ones_bb,))
        C.memset("pool", neghalf[:], -0.5, (neghalf_b,))
        C.memset("pool", epsc[:], EPS, (eps_b,))
        C.copy("dve", wsp_b[:], wsp_f[:], (cst_b,), (wsp_bb,))
        C.memset("pool", vring[:], 1.0, vr_b)
        C.ts("dve", negc[:], sinkb[:], 0.0, -1.0, ALU.max, ALU.mult, (cst_b,), (att_b,))
        C.tt("dve", sinkterm[:], sinkb[:], negc[:], ALU.add, (cst_b, att_b), (att_b,))
        C.act(sinkterm[:], sinkterm[:], AF.Exp, (att_b,), (att_b,))

        proj = WPool(C, "wproj", 4, [128, KC, 128])
        gup = WPool(C, "wgu", 3, [128, 2, KC, 128])
        dnp = WPool(C, "wdn", 2, [128, 2, 11 * 128])
        wv0p = WPool(C, "wv0p", 1, [128, KC, 512])
        wv1p = WPool(C, "wv1p", 1, [128, KC, 256])

        def ws_req(idx):
            return (("ws", idx), ws_d[idx].rearrange("p (k j) -> p k j", k=KC), ws_s[idx].rearrange("p (k j) -> p k j", k=KC))

        def ffn_reqs(layer):
            for fc in range(FC):
                i = layer * FC + fc
                gup.req.append((("gu", i), gu_d[i].rearrange("t p (k j) -> p t k j", k=KC),
                                gu_s[i].rearrange("t p (k j) -> p t k j", k=KC)))
            for dc in range(8):
                i = layer * 8 + dc
                dnp.req.append((("wd", i), wd_d[i].rearrange("p (a b) j -> p a (b j)", a=2),
                                wd_s[i].rearrange("p (a b) j -> p a (b j)", a=2)))

        for s in range(NT + 1):
            if s < NT:
                wv0p.req.append((("wv0", 0), wv0_d, wv0_s))
                for j in range(4):
                    for t3 in range(3):
                        proj.req.append(ws_req(WS_IN + 4 + 4 * t3 + j))
                for g in range(4):
                    proj.req.append(ws_req(WS_IN + g))
                for dc in range(8):
                    proj.req.append(ws_req(WS_OUT0 + dc))
                ffn_reqs(0)
                if stages >= 2:
                    wv1p.req.append((("wv1", 0), wv1_d, wv1_s))
                    for k2 in range(2):
                        proj.req.append(ws_req(WS_QK + 8 + k2))
            if s >= 1 and stages >= 2:
                for dc in range(8):
                    proj.req.append(ws_req(WS_OUT1 + dc))
                ffn_reqs(1)
            if s < NT and stages >= 2:
                for c in range(8):
                    proj.req.append(ws_req(WS_QK + c))

        def proj_fm(w_t, w_b, rhs_of_kc, rhs_bufs, n, nk=KC, ps=None):
            if ps is None:
                ps = next_ps()
            pt, pb = ps
            for kc in range(nk):
                C.mm(pt[:, 0:n], w_t[:, kc, :], rhs_of_kc(kc), kc == 0, kc == nk - 1,
                     (w_b, rhs_bufs[kc]), (pb,))
            return pt, pb

        def resid_add(slot, dc, pt, pb, stats=None):
            xa = xr[:, slot, dc, :]
            C.tt("dve", xa, xa, pt[:, 0:T], ALU.add, (pb,), (xr_b[slot][dc],))
            if stats is not None:
                stats_add(stats, slot, dc, "act")
                stats_flush(stats, keep=1)

        def multi_mm(groups, nk=KC):
            for kc in range(nk):
                for (out_ap, pb, lf, rf, rd) in groups:
                    C.mm(out_ap, lf(kc), rf(kc), kc == 0, kc == nk - 1, rd(kc), (pb,))

        pending = {}
        lndummy = sm(1)
        lnd_b = Buf("lnd")

        def stats_begin():
            C.act(lndummy, epsc[:, 0:1], AF.Ln, (eps_b,), (lnd_b,))
            return {"pt": acc_t[1], "pb": acc_b[1], "n": 0, "pend": []}

        def stats_add(st, slot, c, eng="act"):
            i = st["n"]
            k = i % 3
            xa = xr[:, slot, c, :]
            if eng == "act":
                C.act(sq[k][:], xa, AF.Square, (xr_b[slot][c],), (sq_b[k],))
            else:
                C.tt("dve", sq[k][:], xa, xa, ALU.mult, (xr_b[slot][c],), (sq_b[k],))
            st["pend"].append((k, i))
            st["n"] = i + 1

        def stats_flush(st, keep=0):
            while len(st["pend"]) > keep:
                k, i = st["pend"].pop(0)
                C.mm(st["pt"][:, 0:T], ones_b[:], sq[k][:], i == 0, i == KC - 1, (ones_bb, sq_b[k]), (st["pb"],))

        def norm(slot, gi, extra_slot=None, filler=None, stats=None, rs=None, reuse=False):
            rs_t, rs_b = rs if rs is not None else (rsb, rsb_b)
            if not reuse:
                if stats is None:
                    stats = stats_begin()
                    for c in range(KC):
                        stats_add(stats, slot, c, "act" if c % 2 == 0 else "dve")
                        stats_flush(stats)
                else:
                    stats_flush(stats)
                pt, pb = stats["pt"], stats["pb"]
                C.act(rsb[:], pt[:, 0:T], AF.Ln, (pb, eps_b), (rsb_b,), bias=epsc[:, 0:1], scale=1.0 / D)
                C.act(rs_t[:], rsb[:], AF.Exp, (rsb_b,), (rs_b,), scale=-0.5)
            for c in range(KC):
                C.stt("dve", hT[:, c, 0:T], xr[:, slot, c, :], gall[:, gi, c:c + 1], rs_t[:], ALU.mult, ALU.mult,
                      (xr_b[slot][c], rs_b, cst_b), (hT_b[c],))
            if filler is not None:
                filler()
            if extra_slot is not None:
                xcol = xr[:, extra_slot, :, 0]
                xb = xr_b[extra_slot]
                C.tt("dve", sqx[:], xcol, xcol, ALU.mult, xb, (sqx_b,))
                C.tt("dve", sm_x8, xcol, gall[:, gi, :], ALU.mult, tuple(xb) + (cst_b,), (smx8_b,))
                return lambda: norm_extra(gi, extra_slot)
            return None

        def norm_extra(gi, extra_slot):
            pt2, pb2 = next_ps()
            for c in range(KC):
                C.mm(pt2[:, 0:1], ones_b[:], sqx[:, c:c + 1], c == 0, c == KC - 1, (ones_bb, sqx_b), (pb2,))
            t1 = sm_t1
            C.act(t1, pt2[:, 0:1], AF.Ln, (pb2, eps_b), (smx_b,), bias=epsc[:, 0:1], scale=1.0 / D)
            C.act(sm_rs1, t1, AF.Exp, (smx_b,), (smx_b,), scale=-0.5)
            C.act(hT[:, :, T], sm_x8, AF.Copy, (smx_b, smx8_b), (hTx_b,), scale=sm_rs1)

        smx8_b = Buf("smx8")
        sm_t1 = sm(1)
        sm_rs1 = sm(1)
        sm_x8 = sm(8)
        smx_b = Buf("smx")

        def load_dma(t, tb):
            r0 = t * T + tb * 128
            C.dma("sp", xstage2[tb % 2][:], x_d[r0:r0 + 128, :], (), (xstage2_b[tb % 2],), xstage2_b[tb % 2])

        def load_tr(t, tb):
            slot = t % NRING
            xstage = xstage2[tb % 2]
            xstage_b = xstage2_b[tb % 2]
            for half in range(2):
                pt, pb = next_ps()
                for c4 in range(4):
                    c = half * 4 + c4
                    C.tr(pt[:, c4 * 128:(c4 + 1) * 128], xstage[:, c * 128:(c + 1) * 128], ident_f[:],
                         (xstage_b, ident_fb), (pb,))
                dst = xr[:, slot, half * 4:half * 4 + 4, tb * 128:(tb + 1) * 128]
                C.copy("act", dst, pt[:, 0:512].rearrange("p (c n) -> p c n", c=4), (pb,),
                       tuple(xr_b[slot][half * 4:half * 4 + 4]))

        def load_a(t):
            load_tr(t, 0)
            load_tr(t, 1)
            load_dma(t, 2)
            load_dma(t, 3)

        def load_b(t):
            load_tr(t, 2)
            load_tr(t, 3)
            if t + 1 < NT:
                load_dma(t + 1, 0)
                load_dma(t + 1, 1)

        def tmp_slot(i):
            p0 = 18 + 2 * (i % 6)
            return av_f32(p0, 2), (pg[p0], pg[p0 + 1])

        _tmpi = [0]

        def next_tmp():
            r = tmp_slot(_tmpi[0])
            _tmpi[0] += 1
            return r

        ln_st = [sm(6) for _ in range(4)]
        ln_mv = [sm(2) for _ in range(4)]
        ln_rt = [sm(1) for _ in range(4)]
        ln_rs = [sm(1) for _ in range(4)]
        ln_b = [Buf("ln%d" % i) for i in range(4)]
        xc_sb = sm(4)
        xc_b = [Buf("xc%d" % j) for j in range(4)]

        def mixer0(s):
            slot = s % NRING
            nslot = (s + 1) % NRING if s + 1 < NT else None
            extra = norm(slot, 0, nslot, filler=(lambda: load_a(s + 1)) if s + 1 < NT else None)
            gup.prefetch()
            wv1p.prefetch()
            hb = hT_b
            wv_t, wv_b = wv0p.next()
            avps = [next_ps() for _ in range(4)]
            multi_mm([(avps[tb][0][:, 0:512], avps[tb][1],
                       (lambda kc, tb=tb: hT[:, kc, tb * 128:(tb + 1) * 128]),
                       (lambda kc: wv_t[:, kc, :]),
                       (lambda kc: (hb[kc], wv_b))) for tb in range(4)])
            for tb in range(4):
                pt, pb = avps[tb]
                gv, gvb = next_tmp()
                C.act(gv, pt[:, 0:512], AF.Gelu, (pb,), gvb)
                C.P.add("dve", lambda e, o=ln_st[tb], i=gv: e.bn_stats(o, i), gvb, (ln_b[tb],))
                C.P.add("dve", lambda e, o=ln_mv[tb], i=ln_st[tb]: e.bn_aggr(o, i), (ln_b[tb],), (ln_b[tb],))
                C.ts("dve", ln_rt[tb], ln_mv[tb][:, 1:2], EPS, None, ALU.add, None, (ln_b[tb],), (ln_b[tb],))
                C.tt("pool", ln_rs[tb], ln_rt[tb], neghalf[:, 0:1], ALU.pow, (ln_b[tb], neghalf_b), (ln_b[tb],))
                C.ts("dve", gv, gv, ln_mv[tb][:, 0:1], ln_rs[tb], ALU.subtract, ALU.mult, (ln_b[tb],), gvb)
                C.tt("pool", gv, gv, lng[:], ALU.mult, (cst_b,), gvb)
                vt = av_bf(8 + tb, 1)
                C.tt("pool", vt, gv, lnb[:], ALU.add, gvb + (cst_b,), (pg[8 + tb],))
            if extra is not None:
                extra()
            for j in range(4):
                z = zext[j % 2]
                zb = zext_b[j % 2]
                bb = av_f32(14 + 2 * (j % 2), 2)
                bbb = (pg[14 + 2 * (j % 2)], pg[15 + 2 * (j % 2)])
                xp = acc_t[2]

                def halo(col, wt_, wb_):
                    if nslot is None:
                        return
                    for kc in range(KC):
                        C.mm(xp[:, col:col + 1], wt_[:, kc, :], hT[:, kc, T:T + 1], kc == 0, kc == KC - 1,
                             (wb_, hTx_b), (xps_b,))

                wb_t, wb_b = proj.next()
                pt, pb = proj_fm(wb_t, wb_b, lambda kc: hT[:, kc, 0:T], hb, T)
                C.copy("act", bb, pt[:, 0:T], (pb,), bbb)
                wc_t, wc_b = proj.next()
                ptc, pbc = proj_fm(wc_t, wc_b, lambda kc: hT[:, kc, 0:T], hb, T)
                halo(496 + 2 * j, wc_t, wc_b)
                tc_, tcb = next_tmp()
                C.copy("act", tc_, ptc[:, 0:T], (pbc,), tcb)
                wh_t, wh_b = proj.next()
                pth, pbh = proj_fm(wh_t, wh_b, lambda kc: hT[:, kc, 0:T], hb, T)
                halo(497 + 2 * j, wh_t, wh_b)
                C.tt("dve", z[:, 1:T + 1], tc_, pth[:, 0:T], ALU.mult, tcb + (pbh,), (zb,))
                if s > 0:
                    C.copy("pool", z[:, 0:1], zprev[:, j:j + 1], (zprev_b[j],), (zb,))
                else:
                    C.memset("pool", z[:, 0:1], 0.0, (zb,))
                if nslot is not None:
                    C.copy("act", xc_sb[:, j:j + 1], xp[:, 496 + 2 * j:497 + 2 * j], (xps_b,), (xc_b[j],))
                    C.tt("dve", z[:, T + 1:T + 2], xc_sb[:, j:j + 1], xp[:, 497 + 2 * j:498 + 2 * j], ALU.mult,
                         (xc_b[j], xps_b), (zb,))
                else:
                    C.memset("pool", z[:, T + 1:T + 2], 0.0, (zb,))
                C.copy("pool", zprev[:, j:j + 1], z[:, T:T + 1], (zb,), (zprev_b[j],))
                ct, ctb = next_tmp()
                C.ts("dve", ct, z[:, 0:T], cw[:, 0, j:j + 1], None, ALU.mult, None, (zb, cst_b), ctb)
                C.stt("dve", ct, z[:, 1:T + 1], cw[:, 1, j:j + 1], ct, ALU.mult, ALU.add, (zb, cst_b), ctb)
                C.stt("dve", ct, z[:, 2:T + 2], cw[:, 2, j:j + 1], ct, ALU.mult, ALU.add, (zb, cst_b), ctb)
                C.tt("pool", av_bf(4 + j, 1), ct, bb, ALU.mult, ctb + bbb, (pg[4 + j],))
            for g in range(4):
                w_t, w_b = proj.next()
                pt, pb = proj_fm(w_t, w_b, lambda kc: hT[:, kc, 0:T], hb, T)
                au = av_bf(12 + g % 2, 1)
                aub = pg[12 + g % 2]
                C.act(au, pt[:, 0:T], AF.Gelu, (pb,), (aub,))
                pt2, pb2 = next_ps()
                for tb in range(4):
                    vt = av_bf(8 + tb, 1)
                    C.mm(pt2[:, tb * 128:(tb + 1) * 128], vt[:, g * 128:(g + 1) * 128], wsp_b[:, g, :], True, True,
                         (pg[8 + tb], wsp_bb), (pb2,))
                t1, t1b = next_tmp()
                C.tt("dve", t1.rearrange("p (t q) -> p t q", t=4), pt2[:, 0:512].rearrange("p (t q) -> p t q", t=4),
                     bsp[:, g:g + 1, :].broadcast_to([128, 4, 128]), ALU.add, (pb2, cst_b), t1b)
                C.tt("pool", av_bf(g, 1), t1, au, ALU.mult, t1b + (aub,), (pg[g],))
            st = stats_begin()
            for dc in range(8):
                w_t, w_b = proj.next()
                pt, pb = proj_fm(w_t, w_b, lambda kc: av_bf(kc, 1), pg[0:8], T)
                resid_add(slot, dc, pt, pb, stats=st)
            pending[slot] = st

        def ffn(t, layer):
            slot = t % NRING
            norm(slot, 1 + 2 * layer, filler=(lambda: load_b(t + 1)) if (layer == 0 and t + 1 < NT) else None,
                 stats=pending.pop(slot, None))
            dnp.prefetch()

            def act_part(fc, ptg, pbg, ptu, pbu):
                p0 = 22 + 2 * (fc % 2)
                sg = av_f32(p0, 2)
                sgb = (pg[p0], pg[p0 + 1])
                C.act(sg, ptg[:, 0:T], AF.Silu, (pbg,), sgb)
                C.tt("dve", av_bf(fc, 1), sg, ptu[:, 0:T], ALU.mult, sgb + (pbu,), (pg[fc],))

            gu0_t, gu0_b = gup.next()
            gu1_t, gu1_b = gup.next(held=1)
            b4 = [next_ps() for _ in range(4)]
            grp = []
            for i, (gt, gb_, tsel) in enumerate(((gu0_t, gu0_b, 0), (gu0_t, gu0_b, 1), (gu1_t, gu1_b, 0), (gu1_t, gu1_b, 1))):
                grp.append((b4[i][0][:, 0:T], b4[i][1],
                            (lambda kc, gt=gt, tsel=tsel: gt[:, tsel, kc, :]),
                            (lambda kc: hT[:, kc, 0:T]),
                            (lambda kc, gb_=gb_: (gb_, hT_b[kc]))))
            multi_mm(grp)
            act_part(0, b4[0][0], b4[0][1], b4[1][0], b4[1][1])
            act_part(1, b4[2][0], b4[2][1], b4[3][0], b4[3][1])
            for fc in range(2, FC):
                gu_t, gu_b = gup.next()
                ptg, pbg = next_ps()
                ptu, pbu = next_ps()
                for kc in range(KC):
                    C.mm(ptg[:, 0:T], gu_t[:, 0, kc, :], hT[:, kc, 0:T], kc == 0, kc == KC - 1, (gu_b, hT_b[kc]), (pbg,))
                for kc in range(KC):
                    C.mm(ptu[:, 0:T], gu_t[:, 1, kc, :], hT[:, kc, 0:T], kc == 0, kc == KC - 1, (gu_b, hT_b[kc]), (pbu,))
                act_part(fc, ptg, pbg, ptu, pbu)
            proj.prefetch()
            st = stats_begin() if layer == 0 else None
            for dc in range(8):
                wd_t, wd_b = dnp.next()
                pt, pb = next_ps()
                for fc in range(FC):
                    C.mm(pt[:, 0:T], wd_t[:, fc // 11, (fc % 11) * 128:(fc % 11 + 1) * 128], av_bf(fc, 1), fc == 0, fc == FC - 1, (wd_b, pg[fc]), (pb,))
                resid_add(slot, dc, pt, pb, stats=st)
            if st is not None:
                pending[slot] = st

        def l1_kv(s):
            slot = s % NRING
            wv0p.prefetch()
            norm(slot, 2, stats=pending.pop(slot, None), rs=(tbuf, tbuf_b))
            wv_t, wv_b = wv1p.next()
            rb = (s % 4) * 4
            wk0_t, wk0_b = proj.next()
            wk1_t, wk1_b = proj.next(held=1)
            kps = [next_ps(), next_ps()]
            multi_mm([(kps[i][0][:, 0:T], kps[i][1],
                       (lambda kc, wt=wt: wt[:, kc, :]),
                       (lambda kc: hT[:, kc, 0:T]),
                       (lambda kc, wb=wb: (wb, hT_b[kc]))) for i, (wt, wb) in enumerate(((wk0_t, wk0_b), (wk1_t, wk1_b)))])
            for k2 in range(2):
                pt, pb = kps[k2]
                C.copy("act", kring[:, k2, rb * 128:rb * 128 + T], pt[:, 0:T], (pb,), tuple(kr_b[k2][rb:rb + 4]))
            for tb in range(4):
                pt, pb = next_ps()
                for kc in range(KC):
                    C.mm(pt[:, 0:256], hT[:, kc, tb * 128:(tb + 1) * 128], wv_t[:, kc, :], kc == 0, kc == KC - 1,
                         (hT_b[kc], wv_b), (pb,))
                C.copy("dve", vring[:, rb + tb, :, 0:64], pt[:, 0:256].rearrange("p (g d) -> p g d", g=4), (pb,),
                       (vr_b[rb + tb],))

        def l1_q(s):
            slot = s % NRING
            fin = s >= 1
            norm(slot, 2, rs=(tbuf, tbuf_b), reuse=True)
            if fin:
                final_block(s - 1, 0)
            ws3 = [proj.next(), proj.next(held=1)]
            qps = [next_ps() for _ in range(2)]
            multi_mm([(qps[i][0][:, 0:T], qps[i][1],
                       (lambda kc, wt=ws3[i][0]: wt[:, kc, :]),
                       (lambda kc: hT[:, kc, 0:T]),
                       (lambda kc, wb=ws3[i][1]: (wb, hT_b[kc]))) for i in range(2)])
            for c in range(8):
                if c < 2:
                    pt, pb = qps[c]
                else:
                    w_t, w_b = proj.next()
                    pt, pb = proj_fm(w_t, w_b, lambda kc: hT[:, kc, 0:T], hT_b, T)
                C.act(qT[:, c, :], pt[:, 0:T], AF.Copy, (pb,), (qT_b[c],), scale=0.125)
                if fin and c == 2:
                    final_block(s - 1, 1)
                if fin and c == 4:
                    final_block(s - 1, 2)
                if fin and c == 6:
                    final_block(s - 1, 3)

        den = sm(16)
        rden = sm(16)
        den_b = Buf("den")

        def attention(t):
            slot = t % NRING
            proj.prefetch()
            gup.prefetch()
            for qb in range(4):
                gb = 4 * t + qb
                js = [j for j in range(3) if 0 <= gb - 1 + j < 32]
                nj = len(js)
                j0 = js[0]
                ao = av_bf(8 + 2 * (qb % 2), 2)
                aob = (pg[8 + 2 * (qb % 2)], pg[9 + 2 * (qb % 2)])
                def emit_st2(cq):
                    sts = []
                    for half in range(2):
                        hs = 2 * cq + half
                        st_t, st_b = next_ps()
                        sts.append((st_t, st_b))
                        C.mm(st_t[:, 0:nj * 128], ident_b[:], biasT[:, hs, j0 * 128:(j0 + nj) * 128], True, False,
                             (ident_bb, biasT_b), (st_b,))
                    for idx, j in enumerate(js):
                        kb = (gb - 1 + j) % 16
                        for half in range(2):
                            hs = 2 * cq + half
                            g = HEAD_OF_SLOT[hs] // 4
                            assert g % 2 == half
                            k2 = g // 2
                            st_t, st_b = sts[half]
                            C.mm(st_t[:, idx * 128:(idx + 1) * 128],
                                 kring[half * 64:(half + 1) * 64, k2, kb * 128:(kb + 1) * 128],
                                 qT[half * 64:(half + 1) * 64, cq, qb * 128:(qb + 1) * 128],
                                 False, idx == nj - 1, (kr_b[k2][kb], qT_b[cq]), (st_b,))
                    for half in range(2):
                        hs = 2 * cq + half
                        st_t, st_b = sts[half]
                        ptile = av_bf(12 + hs % 4, 1)
                        ptb = pg[12 + hs % 4]
                        C.act(ptile[:, 0:nj * 128], st_t[:, 0:nj * 128], AF.Exp, (st_b, att_b), (ptb,),
                              bias=negc[:, hs:hs + 1], scale=1.0)

                def emit_pv(hs):
                    g = HEAD_OF_SLOT[hs] // 4
                    ptile = av_bf(12 + hs % 4, 1)
                    ptb = pg[12 + hs % 4]
                    bank = hs // 7
                    col = (hs % 7) * 65
                    for idx, j in enumerate(js):
                        kb = (gb - 1 + j) % 16
                        C.mm(acc_t[bank][:, col:col + 65], ptile[:, idx * 128:(idx + 1) * 128], vring[:, kb, g, :],
                             idx == 0, idx == nj - 1, (ptb, vr_b[kb]), (acc_b[bank],))

                emit_st2(0)
                for cq in range(1, 8):
                    emit_st2(cq)
                    emit_pv(2 * cq - 2)
                    emit_pv(2 * cq - 1)
                emit_pv(14)
                emit_pv(15)
                for bank in range(3):
                    h0 = bank * 7
                    h1 = min(16, h0 + 7)
                    nh = h1 - h0
                    a3 = acc_t[bank][:, 0:nh * 65].rearrange("p (h e) -> p h e", e=65)
                    C.tt("dve", den[:, h0:h1], a3[:, :, 64], sinkterm[:, h0:h1], ALU.add, (acc_b[bank], att_b), (den_b,))
                    C.P.add("dve", lambda e, o=rden[:, h0:h1], i=den[:, h0:h1]: e.reciprocal(o, i), (den_b,), (den_b,))
                    C.tt("dve", ao[:, h0 * 64:h1 * 64].rearrange("p (h d) -> p h d", d=64), a3[:, :, 0:64],
                         rden[:, h0:h1].unsqueeze(2).broadcast_to([128, nh, 64]), ALU.mult, (acc_b[bank], den_b), aob)
                tp, tpb = next_ps()
                tpv = tp[:, 0:512].bitcast(BF16)
                for c in range(8):
                    C.tr(tpv[:, c * 128:(c + 1) * 128], ao[:, c * 128:(c + 1) * 128], ident_b[:], aob + (ident_bb,), (tpb,))
                dst = av_bf(0, 8).rearrange("p (c n) -> p c n", c=8)[:, :, qb * 128:(qb + 1) * 128]
                C.copy("act", dst, tpv.rearrange("p (c n) -> p c n", c=8), (tpb,), tuple(pg[0:8]))
            st = stats_begin()
            for dc in range(8):
                w_t, w_b = proj.next()
                pt, pb = proj_fm(w_t, w_b, lambda kc: av_bf(kc, 1), pg[0:8], T)
                resid_add(slot, dc, pt, pb, stats=st)
            pending[slot] = st

        fss = [sm(1), sm(1)]
        fs_t = sm(1)
        fs_r = sm(1)
        fs_b = Buf("fs")

        def final_block(t, tb):
            slot = t % NRING
            pts = [(acc_t[0], acc_b[0]), (acc_t[1], acc_b[1])]
            for c in range(8):
                pt, pb = pts[c // 4]
                C.tr(pt[:, (c % 4) * 128:(c % 4 + 1) * 128], xr[:, slot, c, tb * 128:(tb + 1) * 128], ident_f[:],
                     (xr_b[slot][c], ident_fb), (pb,))
            for h2 in range(2):
                pt, pb = pts[h2]
                C.act(junk[:], pt[:, 0:512], AF.Square, (pb,), (junk_b, fs_b), accum_out=fss[h2])
            C.tt("dve", fs_t, fss[0], fss[1], ALU.add, (fs_b,), (fs_b,))
            C.ts("dve", fs_t, fs_t, 1.0 / D, EPS, ALU.mult, ALU.add, (fs_b,), (fs_b,))
            C.tt("pool", fs_r, fs_t, neghalf[:, 0:1], ALU.pow, (fs_b, neghalf_b), (fs_b,))
            for h2 in range(2):
                pt, pb = pts[h2]
                C.stt("dve", ostage[:, h2 * 512:(h2 + 1) * 512], pt[:, 0:512], fs_r, gfin[:, h2 * 512:(h2 + 1) * 512],
                      ALU.mult, ALU.mult, (pb, fs_b, cst_b, ostore_b), (ostage_b,))
            r0 = t * T + tb * 128
            C.dma("sp", out_d[r0:r0 + 128, :], ostage[:], (ostage_b,), (ostore_b,), ostore_b)

        def final(t):
            for tb in range(4):
                final_block(t, tb)

        def dump(k, t):
            if not debug:
                return
            slot = t % NRING
            b = C.dbuf("dbg%d_%d" % (k, t))
            C.dma("sp", dbg_d[k, t], xr[:, slot], tuple(xr_b[slot]), (), b)
            P.final_waits.append(b)

        wv0p.prefetch()
        load_dma(0, 0)
        load_dma(0, 1)
        load_a(0)
        load_b(0)
        for s in range(NT + 1):
            if s < NT:
                dump(0, s)
                mixer0(s)
                dump(1, s)
                ffn(s, 0)
                dump(2, s)
                if stages >= 2:
                    l1_kv(s)
            if s >= 1 and stages >= 2:
                attention(s - 1)
                dump(3, s - 1)
                ffn(s - 1, 1)
                dump(4, s - 1)
                if s == NT:
                    final(s - 1)
            if s < NT and stages >= 2:
                l1_q(s)

        fw = [(ostore_b.sem, ostore_b.semcnt)]
        for b in P.final_waits:
            fw.append((b.sem, b.semcnt))
        P.final_waits = fw
        print("SBUF bytes remaining per partition:", nc.sbuf_bytes_remaining, "ops:", {e: len(P.ops[e]) for e in ENGS})
        with nc.Block() as block:
            P.emit_all(nc, block, C.engsem)
    return nc


def _t5_bucket_table():
    nb = 16
    max_exact = 8
    rel = np.arange(-255, 256)
    ret = np.where(rel > 0, nb, 0)
    n = np.abs(rel)
    nf = np.maximum(n, 1).astype(np.float32)
    large = max_exact + (np.log(nf / max_exact) / np.log(128 / max_exact) * (nb - max_exact)).astype(np.int32)
    large = np.minimum(large, nb - 1)
    return ret + np.where(n < max_exact, n, large)


def _chunks_kmajor(W):
    K, E = W.shape
    a = W.reshape(K // 128, 128, E // 128, 128)
    return np.ascontiguousarray(a.transpose(2, 1, 0, 3)).reshape(E // 128, 128, (K // 128) * 128)


def prep_shared(inp):
    f = lambda a: np.ascontiguousarray(np.asarray(a, dtype=np.float32))
    w_in = f(inp["even_w_in"])[0]
    cols = np.concatenate([np.arange(0, 512), np.arange(1024, 2560)])
    ws_in = _chunks_kmajor(w_in[:, cols])
    ws_out0 = _chunks_kmajor(f(inp["even_w_out"])[0])
    wqkv = f(inp["attn_w_qkv"])[0]
    qcols = np.concatenate([np.arange(h * 64, h * 64 + 64) for h in HEAD_OF_SLOT])
    ws_q = _chunks_kmajor(wqkv[:, qcols])
    ws_k = _chunks_kmajor(wqkv[:, 1024:1280])
    wo1 = f(inp["attn_w_out"])[0][qcols, :]
    ws_out1 = _chunks_kmajor(wo1)
    ws = np.concatenate([ws_in, ws_out0, ws_q, ws_k, ws_out1], axis=0)
    assert ws.shape == (NWS, 128, 1024)
    gate = f(inp["ffn_w_gate"])
    up = f(inp["ffn_w_up"])
    down = f(inp["ffn_w_down"])
    gu = np.stack([np.stack([_chunks_kmajor(gate[l]), _chunks_kmajor(up[l])], axis=1) for l in range(2)], axis=0)
    gu = np.ascontiguousarray(gu.reshape(2 * FC, 2, 128, 1024))
    wd = np.stack([_chunks_kmajor(down[l]) for l in range(2)], axis=0).reshape(16, 128, FC, 128)
    wv0 = np.ascontiguousarray(w_in[:, 512:1024].reshape(8, 128, 512).transpose(1, 0, 2))
    wv1 = np.ascontiguousarray(wqkv[:, 1280:1536].reshape(8, 128, 256).transpose(1, 0, 2))
    nm = f(inp["norm_mix"])
    nf_ = f(inp["norm_ffn"])
    gl = np.stack([nm[0], nf_[0], nm[1], nf_[1]], axis=0)
    gall = np.ascontiguousarray(gl.reshape(4, 8, 128).transpose(2, 0, 1))
    rep = lambda v: np.ascontiguousarray(np.broadcast_to(v, (128,) + v.shape))
    gfin = rep(f(inp["final_norm"]))
    lng = rep(f(inp["even_v_ln_g"])[0])
    lnb = rep(f(inp["even_v_ln_b"])[0])
    bsp = rep(f(inp["even_b_spatial"])[0])
    wsp = np.ascontiguousarray(f(inp["even_w_spatial"])[0].transpose(2, 0, 1))
    cw = np.ascontiguousarray(f(inp["even_conv_w"])[0].reshape(3, 4, 128).transpose(2, 0, 1))
    tab = _t5_bucket_table()
    k = np.arange(128)[:, None, None]
    j = np.arange(3)[None, :, None]
    q = np.arange(128)[None, None, :]
    rel = (j - 1) * 128 + k - q
    bucket = tab[rel + 255]
    rb = f(inp["rel_bias"])[:, HEAD_OF_SLOT]
    bias = rb[bucket]
    band = (np.abs(rel) <= 128)[..., None]
    bias = np.where(band, bias, np.float32(NEG_MASK)).astype(np.float32)
    biasT = np.ascontiguousarray(bias.transpose(0, 3, 1, 2)).reshape(128, 16, 384)
    sinkb = rep(f(inp["attn_sink"])[0][HEAD_OF_SLOT])
    return {
        "ws": ws, "gu": gu, "wd": np.ascontiguousarray(wd), "wv0": wv0, "wv1": wv1, "gall": gall, "gfin": gfin,
        "lng": lng, "lnb": lnb, "bsp": bsp, "wsp": wsp, "cw": cw, "biasT": biasT, "sinkb": sinkb,
        "ident": np.eye(128, dtype=np.float32),
    }


_NC_CACHE = {}


def kernel(**inputs):
    x = np.ascontiguousarray(np.asarray(inputs["x"], dtype=np.float32))
    shared = prep_shared(inputs)
    if "nc" not in _NC_CACHE:
        _NC_CACHE["nc"] = build_program(False)
    nc = _NC_CACHE["nc"]
    in_maps = []
    for b in range(8):
        m = dict(shared)
        m["x"] = x[b]
        in_maps.append(m)
    res = run_bass_kernel_spmd(nc, in_maps, core_ids=list(range(8)))
    out = np.stack([np.asarray(r["out"], dtype=np.float32) for r in res.results], axis=0)
    return out
```

```python
import numpy as np
from contextlib import ExitStack
import concourse.bass as bass
import concourse.mybir as mybir
from concourse.bass_utils import run_bass_kernel_spmd

F32 = mybir.dt.float32
BF16 = mybir.dt.bfloat16
AF = mybir.ActivationFunctionType
ALU = mybir.AluOpType
AX = mybir.AxisListType

ENGS = ("pe", "act", "dve", "pool", "sp")


class Buf:
    __slots__ = ("name", "w", "r", "rd", "sem", "semcnt")

    def __init__(self, name, sem=None):
        self.name = name
        self.w = None
        self.r = {}
        self.rd = []
        self.sem = sem
        self.semcnt = 0


class Op:
    __slots__ = ("eng", "emit", "deps", "mile", "mileno", "sem", "semval", "is_dma", "seq")


class Prog:
    def __init__(self):
        self.ops = {e: [] for e in ENGS}
        self.final_waits = []

    def add(self, eng, emit, reads=(), writes=(), dma_buf=None, indep=False):
        op = Op()
        op.eng = eng
        op.emit = emit
        op.mile = False
        op.mileno = 0
        op.is_dma = dma_buf is not None
        op.sem = None
        op.semval = 0
        op.seq = len(self.ops[eng])
        deps = []
        wset = set(id(b) for b in writes)
        for b in reads:
            if b.w is not None:
                deps.append(b.w)
        for b in writes:
            if b.w is not None and not indep:
                deps.append(b.w)
            deps.extend(b.r.values())
            deps.extend(b.rd)
        best = {}
        dl = []
        seen = set()
        for d in deps:
            if d.is_dma:
                if id(d) not in seen:
                    seen.add(id(d))
                    dl.append(d)
            else:
                if d.eng == "pe" and eng == "pe" and not op.is_dma:
                    continue
                cur = best.get(d.eng)
                if cur is None or d.seq > cur.seq:
                    best[d.eng] = d
        for d in best.values():
            d.mile = True
            dl.append(d)
        op.deps = dl
        if op.is_dma:
            dma_buf.semcnt += 16
            op.sem = dma_buf.sem
            op.semval = dma_buf.semcnt
        for b in writes:
            b.w = op
            b.r = {}
            b.rd = []
        for b in reads:
            if id(b) in wset:
                continue
            if op.is_dma:
                b.rd.append(op)
            else:
                b.r[eng] = op
        self.ops[eng].append(op)
        return op

    def emit_all(self, nc, block, engsem):
        for e in ENGS:
            n = 0
            for op in self.ops[e]:
                if op.mile and not op.is_dma:
                    n += 1
                    op.mileno = n
        prog = self

        def run(ename, eobj):
            known = {}
            for op in prog.ops[ename]:
                need = {}
                for d in op.deps:
                    if d.is_dma:
                        s, v = d.sem, d.semval
                    else:
                        s, v = engsem[d.eng], d.mileno
                    k = s.num
                    if k not in need or need[k][1] < v:
                        need[k] = (s, v)
                for k, (s, v) in need.items():
                    if known.get(k, 0) < v:
                        eobj.wait_ge(s, v)
                        known[k] = v
                ins = op.emit(eobj)
                if op.is_dma:
                    ins.then_inc(op.sem, 16)
                elif op.mile:
                    ins.then_inc(engsem[ename], 1)
            if ename == "sp":
                for (s, v) in prog.final_waits:
                    eobj.wait_ge(s, v)

        @block.tensor
        def _(e):
            run("pe", e)

        @block.scalar
        def _(e):
            run("act", e)

        @block.vector
        def _(e):
            run("dve", e)

        @block.gpsimd
        def _(e):
            run("pool", e)

        @block.sync
        def _(e):
            run("sp", e)


class Ctx:
    def __init__(self, nc, st):
        self.nc = nc
        self.st = st
        self.P = Prog()
        self.nsem = 0
        self.engsem = {}
        for e in ("pe", "act", "dve", "pool"):
            self.engsem[e] = st.enter_context(nc.semaphore("prog_" + e))

    def sbuf(self, name, shape, dt):
        return self.st.enter_context(self.nc.sbuf_tensor("sb_" + name, list(shape), dt))

    def psum(self, name, shape, dt):
        return self.st.enter_context(self.nc.psum_tensor("pp_" + name, list(shape), dt))

    def dbuf(self, name):
        self.nsem += 1
        s = self.st.enter_context(self.nc.semaphore("d_" + name))
        return Buf(name, sem=s)

    def mm(self, out, lhsT, rhs, start, stop, reads, writes):
        return self.P.add("pe", lambda e: e.matmul(out, lhsT, rhs, start=start, stop=stop), reads, writes)

    def tr(self, out, in_, ident, reads, writes):
        return self.P.add("pe", lambda e: e.transpose(out, in_, ident), reads, writes)

    def act(self, out, in_, func, reads, writes, bias=None, scale=None, accum_out=None, eng="act"):
        kw = {}
        if bias is not None:
            kw["bias"] = bias
        if scale is not None:
            kw["scale"] = scale
        if accum_out is not None:
            kw["accum_out"] = accum_out
        return self.P.add("act", lambda e: e.activation(out, in_, func, **kw), reads, writes)

    def tt(self, eng, out, in0, in1, op, reads, writes):
        return self.P.add(eng, lambda e: e.tensor_tensor(out, in0, in1, op), reads, writes)

    def ts(self, eng, out, in0, s1, s2, op0, op1, reads, writes):
        if s2 is None:
            return self.P.add(eng, lambda e: e.tensor_scalar(out, in0, s1, None, op0), reads, writes)
        return self.P.add(eng, lambda e: e.tensor_scalar(out, in0, s1, s2, op0, op1), reads, writes)

    def stt(self, eng, out, in0, scalar, in1, op0, op1, reads, writes):
        return self.P.add(eng, lambda e: e.scalar_tensor_tensor(out, in0, scalar, in1, op0, op1), reads, writes)

    def copy(self, eng, out, in_, reads, writes):
        if eng == "act":
            return self.P.add("act", lambda e: e.copy(out, in_), reads, writes)
        return self.P.add(eng, lambda e: e.tensor_copy(out, in_), reads, writes)

    def memset(self, eng, ap, val, writes):
        return self.P.add(eng, lambda e: e.memset(ap, val), (), writes)

    def dma(self, eng, out, in_, reads, writes, dma_buf, indep=False, **kw):
        return self.P.add(eng, lambda e: e.dma_start(out, in_, **kw), reads, writes, dma_buf=dma_buf, indep=indep)


D = 1024
KC = 8
S = 4096
T = 512
NT = S // T
FF = 2816
FC = 22
EPS = 1e-6
NRING = 3
HEAD_OF_SLOT = []
for _c in range(8):
    HEAD_OF_SLOT.append([0, 1, 2, 3, 8, 9, 10, 11][_c])
    HEAD_OF_SLOT.append([4, 5, 6, 7, 12, 13, 14, 15][_c])
NEG_MASK = -30000.0

WS_IN = 0
WS_OUT0 = 16
WS_QK = 24
WS_OUT1 = 34
NWS = 42


class WPool:
    def __init__(self, C, name, nslots, shape):
        self.C = C
        self.n = nslots
        self.t = [C.sbuf("%s%d" % (name, i), shape, BF16) for i in range(nslots)]
        self.b = [C.dbuf("%s%d" % (name, i)) for i in range(nslots)]
        self.sb = [C.dbuf("%s%dst" % (name, i)) for i in range(nslots)]
        self.req = []
        self.chunk = {}
        self.emitted = 0
        self.cons = 0

    def _top(self, upto):
        C = self.C
        upto = min(upto, len(self.req))
        while self.emitted < upto:
            i = self.emitted
            key, src32, scr = self.req[i]
            k = i % self.n
            if key not in self.chunk:
                cb = Buf("chunk")
                self.chunk[key] = cb
                C.dma("pool", self.t[k][:], src32, (), (self.b[k],), self.b[k])
                C.dma("sp", scr, self.t[k][:], (self.b[k],), (cb,), self.sb[k])
            else:
                C.dma("sp", self.t[k][:], scr, (self.chunk[key],), (self.b[k],), self.b[k])
            self.emitted += 1

    def prefetch(self):
        self._top(self.cons + self.n)

    def next(self, held=0):
        i = self.cons
        assert i < len(self.req), "weight pool underflow"
        self._top(i + self.n - held)
        self.cons += 1
        return self.t[i % self.n], self.b[i % self.n]


def build_program(debug=False, NT=NT, stages=3):
    nc = bass.Bass("TRN2", target_bir_lowering=False)
    dt_in = lambda name, shape: nc.dram_tensor(name, list(shape), F32, kind="ExternalInput").ap()
    x_d = dt_in("x", [S, D])
    ws_d = dt_in("ws", [NWS, 128, 1024])
    gu_d = dt_in("gu", [2 * FC, 2, 128, 1024])
    wd_d = dt_in("wd", [16, 128, FC, 128])
    wv0_d = dt_in("wv0", [128, 8, 512])
    wv1_d = dt_in("wv1", [128, 8, 256])
    gall_d = dt_in("gall", [128, 4, 8])
    gfin_d = dt_in("gfin", [128, 1024])
    lng_d = dt_in("lng", [128, 512])
    lnb_d = dt_in("lnb", [128, 512])
    bsp_d = dt_in("bsp", [128, 4, 128])
    wsp_d = dt_in("wsp", [128, 4, 128])
    cw_d = dt_in("cw", [128, 3, 4])
    bias_d = dt_in("biasT", [128, 16, 384])
    sink_d = dt_in("sinkb", [128, 16])
    id_d = dt_in("ident", [128, 128])
    out_d = nc.dram_tensor("out", [S, D], F32, kind="ExternalOutput").ap()
    if debug:
        dbg_d = nc.dram_tensor("dbg", [5, NT, 128, 8, 512], F32, kind="ExternalOutput").ap()
    ws_s = nc.dram_tensor("ws_s", [NWS, 128, 1024], BF16).ap()
    gu_s = nc.dram_tensor("gu_s", [2 * FC, 2, 128, 1024], BF16).ap()
    wd_s = nc.dram_tensor("wd_s", [16, 128, FC, 128], BF16).ap()
    wv0_s = nc.dram_tensor("wv0_s", [128, 8, 512], BF16).ap()
    wv1_s = nc.dram_tensor("wv1_s", [128, 8, 256], BF16).ap()

    with ExitStack() as st:
        C = Ctx(nc, st)
        P = C.P
        xr = C.sbuf("xr", [128, NRING, KC, T], F32)
        xr_b = [[Buf("xr%d_%d" % (r, c)) for c in range(KC)] for r in range(NRING)]
        hT = C.sbuf("hT", [128, KC, T + 2], BF16)
        hT_b = [Buf("hT%d" % c) for c in range(KC)]
        hTx_b = Buf("hTx")
        arena = C.sbuf("arena", [128, 32 * 256], F32)
        pg = [Buf("pg%d" % i) for i in range(32)]

        def av_bf(p0, np_):
            return arena[:, p0 * 256:(p0 + np_) * 256].bitcast(BF16)

        def av_f32(p0, np_):
            return arena[:, p0 * 256:(p0 + np_) * 256]

        kring = C.sbuf("kring", [128, 2, 4 * T], BF16)
        kr_b = [[Buf("k%d_%d" % (c, b)) for b in range(16)] for c in range(2)]
        vring = C.sbuf("vring", [128, 16, 4, 65], BF16)
        vr_b = [Buf("v%d" % b) for b in range(16)]
        qT = C.sbuf("qT", [128, KC, T], BF16)
        qT_b = [Buf("qT%d" % c) for c in range(KC)]
        biasT = C.sbuf("biasTs", [128, 16, 384], BF16)
        biasT_b = C.dbuf("biasT")
        xstage2 = [C.sbuf("xstage%d" % i, [128, D], F32) for i in range(2)]
        xstage2_b = [C.dbuf("xstage%d" % i) for i in range(2)]
        ostage = C.sbuf("ostage", [128, D], F32)
        ostage_b = Buf("ostage")
        ostore_b = C.dbuf("ostore")
        sq = [C.sbuf("sq%d" % i, [128, T], BF16) for i in range(3)]
        sq_b = [Buf("sq%d" % i) for i in range(3)]
        nrm_t = sq
        nrm_b = sq_b
        sqx = C.sbuf("sqx", [128, 8], BF16)
        sqx_b = Buf("sqx")
        tbuf = C.sbuf("tbuf", [128, T], F32)
        tbuf_b = Buf("tbuf")
        rsb = C.sbuf("rsb", [128, T], F32)
        rsb_b = Buf("rsb")
        small = C.sbuf("small", [128, 256], F32)
        zext = [C.sbuf("zext%d" % i, [128, T + 4], F32) for i in range(2)]
        zext_b = [Buf("zext%d" % i) for i in range(2)]
        zprev = C.sbuf("zprev", [128, 4], F32)
        zprev_b = [Buf("zprev%d" % j) for j in range(4)]
        ident_f = C.sbuf("ident_f", [128, 128], F32)
        ident_fb = None
        ident_b = C.sbuf("ident_b", [128, 128], BF16)
        ident_bb = Buf("ident_b")
        ones_b = C.sbuf("ones_b", [128, 128], BF16)
        ones_bb = Buf("ones_b")
        epsc = C.sbuf("epsc", [128, 1], F32)
        eps_b = Buf("epsc")
        neghalf = C.sbuf("neghalf", [128, 8], F32)
        neghalf_b = Buf("neghalf")
        gfin = C.sbuf("gfin", [128, D], F32)
        lng = C.sbuf("lng", [128, 512], F32)
        lnb = C.sbuf("lnb", [128, 512], F32)
        bsp = C.sbuf("bsp", [128, 4, 128], F32)
        wsp_f = C.sbuf("wsp_f", [128, 4, 128], F32)
        wsp_b = C.sbuf("wsp_b", [128, 4, 128], BF16)
        wsp_bb = Buf("wsp_b")
        gall = C.sbuf("gall", [128, 4, 8], F32)
        cw = C.sbuf("cw", [128, 3, 4], F32)
        sinkb = C.sbuf("sinkb", [128, 16], F32)
        negc = C.sbuf("negc", [128, 16], F32)
        sinkterm = C.sbuf("sinkterm", [128, 16], F32)
        att_b = Buf("attconst")
        cst_b = C.dbuf("consts")
        ident_fb = cst_b
        junk = C.sbuf("junk", [128, T], BF16)
        junk_b = Buf("junk")

        _sm = [0]

        def sm(n):
            a = small[:, _sm[0]:_sm[0] + n]
            _sm[0] += n
            assert _sm[0] <= 256
            return a

        psb = [C.psum("ps%d" % i, [128, 512], F32) for i in range(8)]
        ps_b = [Buf("ps%d" % i) for i in range(8)]
        _pr = [0]

        def next_ps():
            k = _pr[0] % 5
            _pr[0] += 1
            return psb[k], ps_b[k]

        acc_t = psb[5:8]
        acc_b = ps_b[5:8]
        xps_b = Buf("xps")

        C.dma("pool", biasT[:], bias_d, (), (biasT_b,), biasT_b)

        for (dst, src) in ((ident_f, id_d), (gfin, gfin_d), (lng, lng_d), (lnb, lnb_d), (bsp, bsp_d),
                           (wsp_f, wsp_d), (gall, gall_d), (cw, cw_d), (sinkb, sink_d)):
            C.dma("sp", dst[:], src, (), (cst_b,), cst_b, indep=True)
        C.copy("dve", ident_b[:], ident_f[:], (cst_b,), (ident_bb,))
        C.memset("pool", ones_b[:], 1.0, (ones_bb,))
        C.memset("pool", neghalf[:], -0.5, (neghalf_b,))
        C.memset("pool", epsc[:], EPS, (eps_b,))
        C.copy("dve", wsp_b[:], wsp_f[:], (cst_b,), (wsp_bb,))
        C.memset("pool", vring[:], 1.0, vr_b)
        C.ts("dve", negc[:], sinkb[:], 0.0, -1.0, ALU.max, ALU.mult, (cst_b,), (att_b,))
        C.tt("dve", sinkterm[:], sinkb[:], negc[:], ALU.add, (cst_b, att_b), (att_b,))
        C.act(sinkterm[:], sinkterm[:], AF.Exp, (att_b,), (att_b,))

        proj = WPool(C, "wproj", 4, [128, KC, 128])
        gup = WPool(C, "wgu", 3, [128, 2, KC, 128])
        dnp = WPool(C, "wdn", 2, [128, 2, 11 * 128])
        wv0p = WPool(C, "wv0p", 1, [128, KC, 512])
        wv1p = WPool(C, "wv1p", 1, [128, KC, 256])

        def ws_req(idx):
            return (("ws", idx), ws_d[idx].rearrange("p (k j) -> p k j", k=KC), ws_s[idx].rearrange("p (k j) -> p k j", k=KC))

        def ffn_reqs(layer):
            for fc in range(FC):
                i = layer * FC + fc
                gup.req.append((("gu", i), gu_d[i].rearrange("t p (k j) -> p t k j", k=KC),
                                gu_s[i].rearrange("t p (k j) -> p t k j", k=KC)))
            for dc in range(8):
                i = layer * 8 + dc
                dnp.req.append((("wd", i), wd_d[i].rearrange("p (a b) j -> p a (b j)", a=2),
                                wd_s[i].rearrange("p (a b) j -> p a (b j)", a=2)))

        for s in range(NT + 1):
            if s < NT:
                wv0p.req.append((("wv0", 0), wv0_d, wv0_s))
                for g in range(4):
                    proj.req.append(ws_req(WS_IN + g))
                for j in range(4):
                    for t3 in range(3):
                        proj.req.append(ws_req(WS_IN + 4 + 4 * t3 + j))
                for dc in range(8):
                    proj.req.append(ws_req(WS_OUT0 + dc))
                ffn_reqs(0)
                if stages >= 2:
                    wv1p.req.append((("wv1", 0), wv1_d, wv1_s))
                    for k2 in range(2):
                        proj.req.append(ws_req(WS_QK + 8 + k2))
            if s >= 1 and stages >= 2:
                for dc in range(8):
                    proj.req.append(ws_req(WS_OUT1 + dc))
                ffn_reqs(1)
            if s < NT and stages >= 2:
                for c in range(8):
                    proj.req.append(ws_req(WS_QK + c))

        def proj_fm(w_t, w_b, rhs_of_kc, rhs_bufs, n, nk=KC, ps=None):
            if ps is None:
                ps = next_ps()
            pt, pb = ps
            for kc in range(nk):
                C.mm(pt[:, 0:n], w_t[:, kc, :], rhs_of_kc(kc), kc == 0, kc == nk - 1,
                     (w_b, rhs_bufs[kc]), (pb,))
            return pt, pb

        def resid_add(slot, dc, pt, pb, stats=None):
            xa = xr[:, slot, dc, :]
            C.tt("dve", xa, xa, pt[:, 0:T], ALU.add, (pb,), (xr_b[slot][dc],))
            if stats is not None:
                stats_add(stats, slot, dc, "act")
                stats_flush(stats, keep=1)

        def multi_mm(groups, nk=KC):
            for kc in range(nk):
                for (out_ap, pb, lf, rf, rd) in groups:
                    C.mm(out_ap, lf(kc), rf(kc), kc == 0, kc == nk - 1, rd(kc), (pb,))

        pending = {}
        lndummy = sm(1)
        lnd_b = Buf("lnd")

        def stats_begin():
            C.act(lndummy, epsc[:, 0:1], AF.Ln, (eps_b,), (lnd_b,))
            return {"pt": acc_t[1], "pb": acc_b[1], "n": 0, "pend": []}

        def stats_add(st, slot, c, eng="act"):
            i = st["n"]
            k = i % 3
            xa = xr[:, slot, c, :]
            if eng == "act":
                C.act(sq[k][:], xa, AF.Square, (xr_b[slot][c],), (sq_b[k],))
            else:
                C.tt("dve", sq[k][:], xa, xa, ALU.mult, (xr_b[slot][c],), (sq_b[k],))
            st["pend"].append((k, i))
            st["n"] = i + 1

        def stats_flush(st, keep=0):
            while len(st["pend"]) > keep:
                k, i = st["pend"].pop(0)
                C.mm(st["pt"][:, 0:T], ones_b[:], sq[k][:], i == 0, i == KC - 1, (ones_bb, sq_b[k]), (st["pb"],))

        def norm(slot, gi, extra_slot=None, filler=None, stats=None, rs=None, reuse=False):
            rs_t, rs_b = rs if rs is not None else (rsb, rsb_b)
            if not reuse:
                if stats is None:
                    stats = stats_begin()
                    for c in range(KC):
                        stats_add(stats, slot, c, "act" if c % 2 == 0 else "dve")
                        stats_flush(stats)
                else:
                    stats_flush(stats)
                pt, pb = stats["pt"], stats["pb"]
                C.act(rsb[:], pt[:, 0:T], AF.Ln, (pb, eps_b), (rsb_b,), bias=epsc[:, 0:1], scale=1.0 / D)
                C.act(rs_t[:], rsb[:], AF.Exp, (rsb_b,), (rs_b,), scale=-0.5)
            for c in range(KC):
                C.stt("dve", hT[:, c, 0:T], xr[:, slot, c, :], gall[:, gi, c:c + 1], rs_t[:], ALU.mult, ALU.mult,
                      (xr_b[slot][c], rs_b, cst_b), (hT_b[c],))
            if filler is not None:
                filler()
            if extra_slot is not None:
                xcol = xr[:, extra_slot, :, 0]
                xb = xr_b[extra_slot]
                C.tt("dve", sqx[:], xcol, xcol, ALU.mult, xb, (sqx_b,))
                C.tt("dve", sm_x8, xcol, gall[:, gi, :], ALU.mult, tuple(xb) + (cst_b,), (smx8_b,))
                return lambda: norm_extra(gi, extra_slot)
            return None

        def norm_extra(gi, extra_slot):
            pt2, pb2 = next_ps()
            for c in range(KC):
                C.mm(pt2[:, 0:1], ones_b[:], sqx[:, c:c + 1], c == 0, c == KC - 1, (ones_bb, sqx_b), (pb2,))
            t1 = sm_t1
            C.act(t1, pt2[:, 0:1], AF.Ln, (pb2, eps_b), (smx_b,), bias=epsc[:, 0:1], scale=1.0 / D)
            C.act(sm_rs1, t1, AF.Exp, (smx_b,), (smx_b,), scale=-0.5)
            C.act(hT[:, :, T], sm_x8, AF.Copy, (smx_b, smx8_b), (hTx_b,), scale=sm_rs1)

        smx8_b = Buf("smx8")
        sm_t1 = sm(1)
        sm_rs1 = sm(1)
        sm_x8 = sm(8)
        smx_b = Buf("smx")

        def load_dma(t, tb):
            r0 = t * T + tb * 128
            C.dma("sp", xstage2[tb % 2][:], x_d[r0:r0 + 128, :], (), (xstage2_b[tb % 2],), xstage2_b[tb % 2])

        def load_tr(t, tb):
            slot = t % NRING
            xstage = xstage2[tb % 2]
            xstage_b = xstage2_b[tb % 2]
            for half in range(2):
                pt, pb = next_ps()
                for c4 in range(4):
                    c = half * 4 + c4
                    C.tr(pt[:, c4 * 128:(c4 + 1) * 128], xstage[:, c * 128:(c + 1) * 128], ident_f[:],
                         (xstage_b, ident_fb), (pb,))
                dst = xr[:, slot, half * 4:half * 4 + 4, tb * 128:(tb + 1) * 128]
                C.copy("act", dst, pt[:, 0:512].rearrange("p (c n) -> p c n", c=4), (pb,),
                       tuple(xr_b[slot][half * 4:half * 4 + 4]))

        def load_a(t):
            load_tr(t, 0)
            load_tr(t, 1)
            load_dma(t, 2)
            load_dma(t, 3)

        def load_b(t):
            load_tr(t, 2)
            load_tr(t, 3)
            if t + 1 < NT:
                load_dma(t + 1, 0)
                load_dma(t + 1, 1)

        def tmp_slot(i):
            p0 = 18 + 2 * (i % 6)
            return av_f32(p0, 2), (pg[p0], pg[p0 + 1])

        _tmpi = [0]

        def next_tmp():
            r = tmp_slot(_tmpi[0])
            _tmpi[0] += 1
            return r

        ln_st = [sm(6) for _ in range(4)]
        ln_mv = [sm(2) for _ in range(4)]
        ln_rt = [sm(1) for _ in range(4)]
        ln_rs = [sm(1) for _ in range(4)]
        ln_b = [Buf("ln%d" % i) for i in range(4)]
        xc_sb = sm(4)
        xc_b = [Buf("xc%d" % j) for j in range(4)]

        AU_PG = [12, 13, 30, 31]

        def mixer0(s):
            slot = s % NRING
            nslot = (s + 1) % NRING if s + 1 < NT else None
            extra = norm(slot, 0, nslot, filler=(lambda: load_a(s + 1)) if s + 1 < NT else None)
            gup.prefetch()
            wv1p.prefetch()
            hb = hT_b
            wv_t, wv_b = wv0p.next()
            avps = [next_ps() for _ in range(4)]
            multi_mm([(avps[tb][0][:, 0:512], avps[tb][1],
                       (lambda kc, tb=tb: hT[:, kc, tb * 128:(tb + 1) * 128]),
                       (lambda kc: wv_t[:, kc, :]),
                       (lambda kc: (hb[kc], wv_b))) for tb in range(4)])
            for tb in range(4):
                pt, pb = avps[tb]
                gv, gvb = next_tmp()
                C.act(gv, pt[:, 0:512], AF.Gelu, (pb,), gvb)
                C.P.add("dve", lambda e, o=ln_st[tb], i=gv: e.bn_stats(o, i), gvb, (ln_b[tb],))
                C.P.add("dve", lambda e, o=ln_mv[tb], i=ln_st[tb]: e.bn_aggr(o, i), (ln_b[tb],), (ln_b[tb],))
                C.ts("dve", ln_rt[tb], ln_mv[tb][:, 1:2], EPS, None, ALU.add, None, (ln_b[tb],), (ln_b[tb],))
                C.tt("pool", ln_rs[tb], ln_rt[tb], neghalf[:, 0:1], ALU.pow, (ln_b[tb], neghalf_b), (ln_b[tb],))
                C.ts("dve", gv, gv, ln_mv[tb][:, 0:1], ln_rs[tb], ALU.subtract, ALU.mult, (ln_b[tb],), gvb)
                C.tt("pool", gv, gv, lng[:], ALU.mult, (cst_b,), gvb)
                vt = av_bf(8 + tb, 1)
                C.tt("pool", vt, gv, lnb[:], ALU.add, gvb + (cst_b,), (pg[8 + tb],))
            if extra is not None:
                extra()
            for g in range(4):
                w_t, w_b = proj.next()
                pt, pb = proj_fm(w_t, w_b, lambda kc: hT[:, kc, 0:T], hb, T)
                C.act(av_bf(AU_PG[g], 1), pt[:, 0:T], AF.Gelu, (pb,), (pg[AU_PG[g]],))
            for j in range(4):
                z = zext[j % 2]
                zb = zext_b[j % 2]
                bb = av_f32(14 + 2 * (j % 2), 2)
                bbb = (pg[14 + 2 * (j % 2)], pg[15 + 2 * (j % 2)])
                xp = acc_t[2]

                def halo(col, wt_, wb_):
                    if nslot is None:
                        return
                    for kc in range(KC):
                        C.mm(xp[:, col:col + 1], wt_[:, kc, :], hT[:, kc, T:T + 1], kc == 0, kc == KC - 1,
                             (wb_, hTx_b), (xps_b,))

                wb_t, wb_b = proj.next()
                pt, pb = proj_fm(wb_t, wb_b, lambda kc: hT[:, kc, 0:T], hb, T)
                C.copy("act", bb, pt[:, 0:T], (pb,), bbb)
                wc_t, wc_b = proj.next()
                ptc, pbc = proj_fm(wc_t, wc_b, lambda kc: hT[:, kc, 0:T], hb, T)
                halo(496 + 2 * j, wc_t, wc_b)
                tc_, tcb = next_tmp()
                C.copy("act", tc_, ptc[:, 0:T], (pbc,), tcb)
                wh_t, wh_b = proj.next()
                pth, pbh = proj_fm(wh_t, wh_b, lambda kc: hT[:, kc, 0:T], hb, T)
                halo(497 + 2 * j, wh_t, wh_b)
                C.tt("dve", z[:, 1:T + 1], tc_, pth[:, 0:T], ALU.mult, tcb + (pbh,), (zb,))
                if s > 0:
                    C.copy("pool", z[:, 0:1], zprev[:, j:j + 1], (zprev_b[j],), (zb,))
                else:
                    C.memset("pool", z[:, 0:1], 0.0, (zb,))
                if nslot is not None:
                    C.copy("act", xc_sb[:, j:j + 1], xp[:, 496 + 2 * j:497 + 2 * j], (xps_b,), (xc_b[j],))
                    C.tt("dve", z[:, T + 1:T + 2], xc_sb[:, j:j + 1], xp[:, 497 + 2 * j:498 + 2 * j], ALU.mult,
                         (xc_b[j], xps_b), (zb,))
                else:
                    C.memset("pool", z[:, T + 1:T + 2], 0.0, (zb,))
                C.copy("pool", zprev[:, j:j + 1], z[:, T:T + 1], (zb,), (zprev_b[j],))
                ct, ctb = next_tmp()
                C.ts("dve", ct, z[:, 0:T], cw[:, 0, j:j + 1], None, ALU.mult, None, (zb, cst_b), ctb)
                C.stt("dve", ct, z[:, 1:T + 1], cw[:, 1, j:j + 1], ct, ALU.mult, ALU.add, (zb, cst_b), ctb)
                C.stt("dve", ct, z[:, 2:T + 2], cw[:, 2, j:j + 1], ct, ALU.mult, ALU.add, (zb, cst_b), ctb)
                C.tt("pool", av_bf(4 + j, 1), ct, bb, ALU.mult, ctb + bbb, (pg[4 + j],))
            for g in range(4):
                au = av_bf(AU_PG[g], 1)
                aub = pg[AU_PG[g]]
                pt2, pb2 = next_ps()
                for tb in range(4):
                    vt = av_bf(8 + tb, 1)
                    C.mm(pt2[:, tb * 128:(tb + 1) * 128], vt[:, g * 128:(g + 1) * 128], wsp_b[:, g, :], True, True,
                         (pg[8 + tb], wsp_bb), (pb2,))
                t1, t1b = next_tmp()
                C.tt("dve", t1.rearrange("p (t q) -> p t q", t=4), pt2[:, 0:512].rearrange("p (t q) -> p t q", t=4),
                     bsp[:, g:g + 1, :].broadcast_to([128, 4, 128]), ALU.add, (pb2, cst_b), t1b)
                C.tt("pool", av_bf(g, 1), t1, au, ALU.mult, t1b + (aub,), (pg[g],))
            st = stats_begin()
            for dc in range(8):
                w_t, w_b = proj.next()
                pt, pb = proj_fm(w_t, w_b, lambda kc: av_bf(kc, 1), pg[0:8], T)
                resid_add(slot, dc, pt, pb, stats=st)
            pending[slot] = st

        def ffn(t, layer):
            slot = t % NRING
            norm(slot, 1 + 2 * layer, filler=(lambda: load_b(t + 1)) if (layer == 0 and t + 1 < NT) else None,
                 stats=pending.pop(slot, None))
            dnp.prefetch()

            def act_part(fc, ptg, pbg, ptu, pbu):
                p0 = 22 + 2 * (fc % 2)
                sg = av_f32(p0, 2)
                sgb = (pg[p0], pg[p0 + 1])
                C.act(sg, ptg[:, 0:T], AF.Silu, (pbg,), sgb)
                C.tt("dve", av_bf(fc, 1), sg, ptu[:, 0:T], ALU.mult, sgb + (pbu,), (pg[fc],))

            gu0_t, gu0_b = gup.next()
            gu1_t, gu1_b = gup.next(held=1)
            b4 = [next_ps() for _ in range(4)]
            grp = []
            for i, (gt, gb_, tsel) in enumerate(((gu0_t, gu0_b, 0), (gu0_t, gu0_b, 1), (gu1_t, gu1_b, 0), (gu1_t, gu1_b, 1))):
                grp.append((b4[i][0][:, 0:T], b4[i][1],
                            (lambda kc, gt=gt, tsel=tsel: gt[:, tsel, kc, :]),
                            (lambda kc: hT[:, kc, 0:T]),
                            (lambda kc, gb_=gb_: (gb_, hT_b[kc]))))
            multi_mm(grp)
            act_part(0, b4[0][0], b4[0][1], b4[1][0], b4[1][1])
            act_part(1, b4[2][0], b4[2][1], b4[3][0], b4[3][1])
            for fc in range(2, FC):
                gu_t, gu_b = gup.next()
                ptg, pbg = next_ps()
                ptu, pbu = next_ps()
                for kc in range(KC):
                    C.mm(ptg[:, 0:T], gu_t[:, 0, kc, :], hT[:, kc, 0:T], kc == 0, kc == KC - 1, (gu_b, hT_b[kc]), (pbg,))
                for kc in range(KC):
                    C.mm(ptu[:, 0:T], gu_t[:, 1, kc, :], hT[:, kc, 0:T], kc == 0, kc == KC - 1, (gu_b, hT_b[kc]), (pbu,))
                act_part(fc, ptg, pbg, ptu, pbu)
            proj.prefetch()
            st = stats_begin() if layer == 0 else None
            for dc in range(8):
                wd_t, wd_b = dnp.next()
                pt, pb = next_ps()
                for fc in range(FC):
                    C.mm(pt[:, 0:T], wd_t[:, fc // 11, (fc % 11) * 128:(fc % 11 + 1) * 128], av_bf(fc, 1), fc == 0, fc == FC - 1, (wd_b, pg[fc]), (pb,))
                resid_add(slot, dc, pt, pb, stats=st)
            if st is not None:
                pending[slot] = st

        def l1_kv(s):
            slot = s % NRING
            wv0p.prefetch()
            norm(slot, 2, stats=pending.pop(slot, None), rs=(tbuf, tbuf_b))
            wv_t, wv_b = wv1p.next()
            rb = (s % 4) * 4
            wk0_t, wk0_b = proj.next()
            wk1_t, wk1_b = proj.next(held=1)
            kps = [next_ps(), next_ps()]
            multi_mm([(kps[i][0][:, 0:T], kps[i][1],
                       (lambda kc, wt=wt: wt[:, kc, :]),
                       (lambda kc: hT[:, kc, 0:T]),
                       (lambda kc, wb=wb: (wb, hT_b[kc]))) for i, (wt, wb) in enumerate(((wk0_t, wk0_b), (wk1_t, wk1_b)))])
            for k2 in range(2):
                pt, pb = kps[k2]
                C.copy("act", kring[:, k2, rb * 128:rb * 128 + T], pt[:, 0:T], (pb,), tuple(kr_b[k2][rb:rb + 4]))
            for tb in range(4):
                pt, pb = next_ps()
                for kc in range(KC):
                    C.mm(pt[:, 0:256], hT[:, kc, tb * 128:(tb + 1) * 128], wv_t[:, kc, :], kc == 0, kc == KC - 1,
                         (hT_b[kc], wv_b), (pb,))
                C.copy("dve", vring[:, rb + tb, :, 0:64], pt[:, 0:256].rearrange("p (g d) -> p g d", g=4), (pb,),
                       (vr_b[rb + tb],))

        def l1_q(s):
            slot = s % NRING
            fin = s >= 1
            norm(slot, 2, rs=(tbuf, tbuf_b), reuse=True)
            if fin:
                final_block(s - 1, 0)
            ws3 = [proj.next(), proj.next(held=1)]
            qps = [next_ps() for _ in range(2)]
            multi_mm([(qps[i][0][:, 0:T], qps[i][1],
                       (lambda kc, wt=ws3[i][0]: wt[:, kc, :]),
                       (lambda kc: hT[:, kc, 0:T]),
                       (lambda kc, wb=ws3[i][1]: (wb, hT_b[kc]))) for i in range(2)])
            for c in range(8):
                if c < 2:
                    pt, pb = qps[c]
                else:
                    w_t, w_b = proj.next()
                    pt, pb = proj_fm(w_t, w_b, lambda kc: hT[:, kc, 0:T], hT_b, T)
                C.act(qT[:, c, :], pt[:, 0:T], AF.Copy, (pb,), (qT_b[c],), scale=0.125)
                if fin and c == 2:
                    final_block(s - 1, 1)
                if fin and c == 4:
                    final_block(s - 1, 2)
                if fin and c == 6:
                    final_block(s - 1, 3)

        den = sm(16)
        rden = sm(16)
        den_b = Buf("den")

        def attention(t):
            slot = t % NRING
            proj.prefetch()
            gup.prefetch()

            def qinfo(qb):
                gb = 4 * t + qb
                js = [j for j in range(3) if 0 <= gb - 1 + j < 32]
                return gb, js

            def pt_slot(P, half):
                k = 12 + (2 * P + half) % 6
                return av_bf(k, 1), pg[k]

            def emit_st2(P):
                qb, cq = divmod(P, 8)
                gb, js = qinfo(qb)
                nj = len(js)
                j0 = js[0]
                sts = []
                for half in range(2):
                    hs = 2 * cq + half
                    st_t, st_b = next_ps()
                    sts.append((st_t, st_b))
                    C.mm(st_t[:, 0:nj * 128], ident_b[:], biasT[:, hs, j0 * 128:(j0 + nj) * 128], True, False,
                         (ident_bb, biasT_b), (st_b,))
                for idx, j in enumerate(js):
                    kb = (gb - 1 + j) % 16
                    for half in range(2):
                        hs = 2 * cq + half
                        g = HEAD_OF_SLOT[hs] // 4
                        assert g % 2 == half
                        k2 = g // 2
                        st_t, st_b = sts[half]
                        C.mm(st_t[:, idx * 128:(idx + 1) * 128],
                             kring[half * 64:(half + 1) * 64, k2, kb * 128:(kb + 1) * 128],
                             qT[half * 64:(half + 1) * 64, cq, qb * 128:(qb + 1) * 128],
                             False, idx == nj - 1, (kr_b[k2][kb], qT_b[cq]), (st_b,))
                for half in range(2):
                    hs = 2 * cq + half
                    st_t, st_b = sts[half]
                    ptile, ptb = pt_slot(P, half)
                    C.act(ptile[:, 0:nj * 128], st_t[:, 0:nj * 128], AF.Exp, (st_b, att_b), (ptb,),
                          bias=negc[:, hs:hs + 1], scale=1.0)

            def emit_pv(P):
                qb, cq = divmod(P, 8)
                gb, js = qinfo(qb)
                nj = len(js)
                for half in range(2):
                    hs = 2 * cq + half
                    g = HEAD_OF_SLOT[hs] // 4
                    ptile, ptb = pt_slot(P, half)
                    bank = hs // 7
                    col = (hs % 7) * 65
                    for idx, j in enumerate(js):
                        kb = (gb - 1 + j) % 16
                        C.mm(acc_t[bank][:, col:col + 65], ptile[:, idx * 128:(idx + 1) * 128], vring[:, kb, g, :],
                             idx == 0, idx == nj - 1, (ptb, vr_b[kb]), (acc_b[bank],))

            def ao_of(qb):
                return av_bf(8 + 2 * (qb % 2), 2), (pg[8 + 2 * (qb % 2)], pg[9 + 2 * (qb % 2)])

            def normalize(qb):
                ao, aob = ao_of(qb)
                for bank in range(3):
                    h0 = bank * 7
                    h1 = min(16, h0 + 7)
                    nh = h1 - h0
                    a3 = acc_t[bank][:, 0:nh * 65].rearrange("p (h e) -> p h e", e=65)
                    C.tt("dve", den[:, h0:h1], a3[:, :, 64], sinkterm[:, h0:h1], ALU.add, (acc_b[bank], att_b), (den_b,))
                    C.P.add("dve", lambda e, o=rden[:, h0:h1], i=den[:, h0:h1]: e.reciprocal(o, i), (den_b,), (den_b,))
                    C.tt("dve", ao[:, h0 * 64:h1 * 64].rearrange("p (h d) -> p h d", d=64), a3[:, :, 0:64],
                         rden[:, h0:h1].unsqueeze(2).broadcast_to([128, nh, 64]), ALU.mult, (acc_b[bank], den_b), aob)

            def transposes(qb):
                ao, aob = ao_of(qb)
                tp, tpb = next_ps()
                tpv = tp[:, 0:512].bitcast(BF16)
                for c in range(8):
                    C.tr(tpv[:, c * 128:(c + 1) * 128], ao[:, c * 128:(c + 1) * 128], ident_b[:], aob + (ident_bb,), (tpb,))
                dst = av_bf(0, 8).rearrange("p (c n) -> p c n", c=8)[:, :, qb * 128:(qb + 1) * 128]
                C.copy("act", dst, tpv.rearrange("p (c n) -> p c n", c=8), (tpb,), tuple(pg[0:8]))

            NPAIR = 32
            pend_tr = None
            for P in range(NPAIR + 2):
                if P < NPAIR:
                    emit_st2(P)
                if pend_tr is not None:
                    transposes(pend_tr)
                    pend_tr = None
                if P >= 2:
                    emit_pv(P - 2)
                    if (P - 2) % 8 == 7:
                        normalize((P - 2) // 8)
                        pend_tr = (P - 2) // 8
            if pend_tr is not None:
                transposes(pend_tr)
            st = stats_begin()
            for dc in range(8):
                w_t, w_b = proj.next()
                pt, pb = proj_fm(w_t, w_b, lambda kc: av_bf(kc, 1), pg[0:8], T)
                resid_add(slot, dc, pt, pb, stats=st)
            pending[slot] = st

        fss = [sm(1), sm(1)]
        fs_t = sm(1)
        fs_r = sm(1)
        fs_b = Buf("fs")

        def final_block(t, tb):
            slot = t % NRING
            pts = [(acc_t[0], acc_b[0]), (acc_t[1], acc_b[1])]
            for c in range(8):
                pt, pb = pts[c // 4]
                C.tr(pt[:, (c % 4) * 128:(c % 4 + 1) * 128], xr[:, slot, c, tb * 128:(tb + 1) * 128], ident_f[:],
                     (xr_b[slot][c], ident_fb), (pb,))
            for h2 in range(2):
                pt, pb = pts[h2]
                C.act(junk[:], pt[:, 0:512], AF.Square, (pb,), (junk_b, fs_b), accum_out=fss[h2])
            C.tt("dve", fs_t, fss[0], fss[1], ALU.add, (fs_b,), (fs_b,))
            C.ts("dve", fs_t, fs_t, 1.0 / D, EPS, ALU.mult, ALU.add, (fs_b,), (fs_b,))
            C.tt("pool", fs_r, fs_t, neghalf[:, 0:1], ALU.pow, (fs_b, neghalf_b), (fs_b,))
            for h2 in range(2):
                pt, pb = pts[h2]
                C.stt("dve", ostage[:, h2 * 512:(h2 + 1) * 512], pt[:, 0:512], fs_r, gfin[:, h2 * 512:(h2 + 1) * 512],
                      ALU.mult, ALU.mult, (pb, fs_b, cst_b, ostore_b), (ostage_b,))
            r0 = t * T + tb * 128
            C.dma("sp", out_d[r0:r0 + 128, :], ostage[:], (ostage_b,), (ostore_b,), ostore_b)

        def final(t):
            for tb in range(4):
                final_block(t, tb)

        def dump(k, t):
            if not debug:
                return
            slot = t % NRING
            b = C.dbuf("dbg%d_%d" % (k, t))
            C.dma("sp", dbg_d[k, t], xr[:, slot], tuple(xr_b[slot]), (), b)
            P.final_waits.append(b)

        wv0p.prefetch()
        proj.prefetch()
        gup.prefetch()
        dnp.prefetch()
        load_dma(0, 0)
        load_dma(0, 1)
        load_a(0)
        load_b(0)
        for s in range(NT + 1):
            if s < NT:
                dump(0, s)
                mixer0(s)
                dump(1, s)
                ffn(s, 0)
                dump(2, s)
                if stages >= 2:
                    l1_kv(s)
            if s >= 1 and stages >= 2:
                attention(s - 1)
                dump(3, s - 1)
                ffn(s - 1, 1)
                dump(4, s - 1)
                if s == NT:
                    final(s - 1)
            if s < NT and stages >= 2:
                l1_q(s)

        fw = [(ostore_b.sem, ostore_b.semcnt)]
        for b in P.final_waits:
            fw.append((b.sem, b.semcnt))
        P.final_waits = fw
        print("SBUF bytes remaining per partition:", nc.sbuf_bytes_remaining, "ops:", {e: len(P.ops[e]) for e in ENGS})
        with nc.Block() as block:
            P.emit_all(nc, block, C.engsem)
    return nc


def _t5_bucket_table():
    nb = 16
    max_exact = 8
    rel = np.arange(-255, 256)
    ret = np.where(rel > 0, nb, 0)
    n = np.abs(rel)
    nf = np.maximum(n, 1).astype(np.float32)
    large = max_exact + (np.log(nf / max_exact) / np.log(128 / max_exact) * (nb - max_exact)).astype(np.int32)
    large = np.minimum(large, nb - 1)
    return ret + np.where(n < max_exact, n, large)


def _chunks_kmajor(W):
    K, E = W.shape
    a = W.reshape(K // 128, 128, E // 128, 128)
    return np.ascontiguousarray(a.transpose(2, 1, 0, 3)).reshape(E // 128, 128, (K // 128) * 128)


def prep_shared(inp):
    f = lambda a: np.ascontiguousarray(np.asarray(a, dtype=np.float32))
    w_in = f(inp["even_w_in"])[0]
    cols = np.concatenate([np.arange(0, 512), np.arange(1024, 2560)])
    ws_in = _chunks_kmajor(w_in[:, cols])
    ws_out0 = _chunks_kmajor(f(inp["even_w_out"])[0])
    wqkv = f(inp["attn_w_qkv"])[0]
    qcols = np.concatenate([np.arange(h * 64, h * 64 + 64) for h in HEAD_OF_SLOT])
    ws_q = _chunks_kmajor(wqkv[:, qcols])
    ws_k = _chunks_kmajor(wqkv[:, 1024:1280])
    wo1 = f(inp["attn_w_out"])[0][qcols, :]
    ws_out1 = _chunks_kmajor(wo1)
    ws = np.concatenate([ws_in, ws_out0, ws_q, ws_k, ws_out1], axis=0)
    assert ws.shape == (NWS, 128, 1024)
    gate = f(inp["ffn_w_gate"])
    up = f(inp["ffn_w_up"])
    down = f(inp["ffn_w_down"])
    gu = np.stack([np.stack([_chunks_kmajor(gate[l]), _chunks_kmajor(up[l])], axis=1) for l in range(2)], axis=0)
    gu = np.ascontiguousarray(gu.reshape(2 * FC, 2, 128, 1024))
    wd = np.stack([_chunks_kmajor(down[l]) for l in range(2)], axis=0).reshape(16, 128, FC, 128)
    wv0 = np.ascontiguousarray(w_in[:, 512:1024].reshape(8, 128, 512).transpose(1, 0, 2))
    wv1 = np.ascontiguousarray(wqkv[:, 1280:1536].reshape(8, 128, 256).transpose(1, 0, 2))
    nm = f(inp["norm_mix"])
    nf_ = f(inp["norm_ffn"])
    gl = np.stack([nm[0], nf_[0], nm[1], nf_[1]], axis=0)
    gall = np.ascontiguousarray(gl.reshape(4, 8, 128).transpose(2, 0, 1))
    rep = lambda v: np.ascontiguousarray(np.broadcast_to(v, (128,) + v.shape))
    gfin = rep(f(inp["final_norm"]))
    lng = rep(f(inp["even_v_ln_g"])[0])
    lnb = rep(f(inp["even_v_ln_b"])[0])
    bsp = rep(f(inp["even_b_spatial"])[0])
    wsp = np.ascontiguousarray(f(inp["even_w_spatial"])[0].transpose(2, 0, 1))
    cw = np.ascontiguousarray(f(inp["even_conv_w"])[0].reshape(3, 4, 128).transpose(2, 0, 1))
    tab = _t5_bucket_table()
    k = np.arange(128)[:, None, None]
    j = np.arange(3)[None, :, None]
    q = np.arange(128)[None, None, :]
    rel = (j - 1) * 128 + k - q
    bucket = tab[rel + 255]
    rb = f(inp["rel_bias"])[:, HEAD_OF_SLOT]
    bias = rb[bucket]
    band = (np.abs(rel) <= 128)[..., None]
    bias = np.where(band, bias, np.float32(NEG_MASK)).astype(np.float32)
    biasT = np.ascontiguousarray(bias.transpose(0, 3, 1, 2)).reshape(128, 16, 384)
    sinkb = rep(f(inp["attn_sink"])[0][HEAD_OF_SLOT])
    return {
        "ws": ws, "gu": gu, "wd": np.ascontiguousarray(wd), "wv0": wv0, "wv1": wv1, "gall": gall, "gfin": gfin,
        "lng": lng, "lnb": lnb, "bsp": bsp, "wsp": wsp, "cw": cw, "biasT": biasT, "sinkb": sinkb,
        "ident": np.eye(128, dtype=np.float32),
    }


_NC_CACHE = {}


def kernel(**inputs):
    x = np.ascontiguousarray(np.asarray(inputs["x"], dtype=np.float32))
    shared = prep_shared(inputs)
    if "nc" not in _NC_CACHE:
        _NC_CACHE["nc"] = build_program(False)
    nc = _NC_CACHE["nc"]
    in_maps = []
    for b in range(8):
        m = dict(shared)
        m["x"] = x[b]
        in_maps.append(m)
    res = run_bass_kernel_spmd(nc, in_maps, core_ids=list(range(8)))
    out = np.stack([np.asarray(r["out"], dtype=np.float32) for r in res.results], axis=0)
    return out
```

```python
import numpy as np
from contextlib import ExitStack
import concourse.bass as bass
import concourse.mybir as mybir
from concourse.bass_utils import run_bass_kernel_spmd

F32 = mybir.dt.float32
BF16 = mybir.dt.bfloat16
AF = mybir.ActivationFunctionType
ALU = mybir.AluOpType
AX = mybir.AxisListType

ENGS = ("pe", "act", "dve", "pool", "sp")


class Buf:
    __slots__ = ("name", "w", "r", "rd", "sem", "semcnt")

    def __init__(self, name, sem=None):
        self.name = name
        self.w = None
        self.r = {}
        self.rd = []
        self.sem = sem
        self.semcnt = 0


class Op:
    __slots__ = ("eng", "emit", "deps", "mile", "mileno", "sem", "semval", "is_dma", "seq")


class Prog:
    def __init__(self):
        self.ops = {e: [] for e in ENGS}
        self.final_waits = []

    def add(self, eng, emit, reads=(), writes=(), dma_buf=None, indep=False):
        op = Op()
        op.eng = eng
        op.emit = emit
        op.mile = False
        op.mileno = 0
        op.is_dma = dma_buf is not None
        op.sem = None
        op.semval = 0
        op.seq = len(self.ops[eng])
        deps = []
        wset = set(id(b) for b in writes)
        for b in reads:
            if b.w is not None:
                deps.append(b.w)
        for b in writes:
            if b.w is not None and not indep:
                deps.append(b.w)
            deps.extend(b.r.values())
            deps.extend(b.rd)
        best = {}
        dl = []
        seen = set()
        for d in deps:
            if d.is_dma:
                if id(d) not in seen:
                    seen.add(id(d))
                    dl.append(d)
            else:
                if d.eng == "pe" and eng == "pe" and not op.is_dma:
                    continue
                cur = best.get(d.eng)
                if cur is None or d.seq > cur.seq:
                    best[d.eng] = d
        for d in best.values():
            d.mile = True
            dl.append(d)
        op.deps = dl
        if op.is_dma:
            dma_buf.semcnt += 16
            op.sem = dma_buf.sem
            op.semval = dma_buf.semcnt
        for b in writes:
            b.w = op
            b.r = {}
            b.rd = []
        for b in reads:
            if id(b) in wset:
                continue
            if op.is_dma:
                b.rd.append(op)
            else:
                b.r[eng] = op
        self.ops[eng].append(op)
        return op

    def emit_all(self, nc, block, engsem):
        for e in ENGS:
            n = 0
            for op in self.ops[e]:
                if op.mile and not op.is_dma:
                    n += 1
                    op.mileno = n
        prog = self

        def run(ename, eobj):
            known = {}
            for op in prog.ops[ename]:
                need = {}
                for d in op.deps:
                    if d.is_dma:
                        s, v = d.sem, d.semval
                    else:
                        s, v = engsem[d.eng], d.mileno
                    k = s.num
                    if k not in need or need[k][1] < v:
                        need[k] = (s, v)
                for k, (s, v) in need.items():
                    if known.get(k, 0) < v:
                        eobj.wait_ge(s, v)
                        known[k] = v
                ins = op.emit(eobj)
                if op.is_dma:
                    ins.then_inc(op.sem, 16)
                elif op.mile:
                    ins.then_inc(engsem[ename], 1)
            if ename == "sp":
                for (s, v) in prog.final_waits:
                    eobj.wait_ge(s, v)

        @block.tensor
        def _(e):
            run("pe", e)

        @block.scalar
        def _(e):
            run("act", e)

        @block.vector
        def _(e):
            run("dve", e)

        @block.gpsimd
        def _(e):
            run("pool", e)

        @block.sync
        def _(e):
            run("sp", e)


class Ctx:
    def __init__(self, nc, st):
        self.nc = nc
        self.st = st
        self.P = Prog()
        self.nsem = 0
        self.engsem = {}
        for e in ("pe", "act", "dve", "pool"):
            self.engsem[e] = st.enter_context(nc.semaphore("prog_" + e))

    def sbuf(self, name, shape, dt):
        return self.st.enter_context(self.nc.sbuf_tensor("sb_" + name, list(shape), dt))

    def psum(self, name, shape, dt):
        return self.st.enter_context(self.nc.psum_tensor("pp_" + name, list(shape), dt))

    def dbuf(self, name):
        self.nsem += 1
        s = self.st.enter_context(self.nc.semaphore("d_" + name))
        return Buf(name, sem=s)

    def mm(self, out, lhsT, rhs, start, stop, reads, writes):
        return self.P.add("pe", lambda e: e.matmul(out, lhsT, rhs, start=start, stop=stop), reads, writes)

    def tr(self, out, in_, ident, reads, writes):
        return self.P.add("pe", lambda e: e.transpose(out, in_, ident), reads, writes)

    def act(self, out, in_, func, reads, writes, bias=None, scale=None, accum_out=None, eng="act"):
        kw = {}
        if bias is not None:
            kw["bias"] = bias
        if scale is not None:
            kw["scale"] = scale
        if accum_out is not None:
            kw["accum_out"] = accum_out
        return self.P.add("act", lambda e: e.activation(out, in_, func, **kw), reads, writes)

    def tt(self, eng, out, in0, in1, op, reads, writes):
        return self.P.add(eng, lambda e: e.tensor_tensor(out, in0, in1, op), reads, writes)

    def ts(self, eng, out, in0, s1, s2, op0, op1, reads, writes):
        if s2 is None:
            return self.P.add(eng, lambda e: e.tensor_scalar(out, in0, s1, None, op0), reads, writes)
        return self.P.add(eng, lambda e: e.tensor_scalar(out, in0, s1, s2, op0, op1), reads, writes)

    def stt(self, eng, out, in0, scalar, in1, op0, op1, reads, writes):
        return self.P.add(eng, lambda e: e.scalar_tensor_tensor(out, in0, scalar, in1, op0, op1), reads, writes)

    def copy(self, eng, out, in_, reads, writes):
        if eng == "act":
            return self.P.add("act", lambda e: e.copy(out, in_), reads, writes)
        return self.P.add(eng, lambda e: e.tensor_copy(out, in_), reads, writes)

    def memset(self, eng, ap, val, writes):
        return self.P.add(eng, lambda e: e.memset(ap, val), (), writes)

    def dma(self, eng, out, in_, reads, writes, dma_buf, indep=False, **kw):
        return self.P.add(eng, lambda e: e.dma_start(out, in_, **kw), reads, writes, dma_buf=dma_buf, indep=indep)


D = 1024
KC = 8
S = 4096
T = 512
NT = S // T
FF = 2816
FC = 22
EPS = 1e-6
NRING = 3
HEAD_OF_SLOT = []
for _c in range(8):
    HEAD_OF_SLOT.append([0, 1, 2, 3, 8, 9, 10, 11][_c])
    HEAD_OF_SLOT.append([4, 5, 6, 7, 12, 13, 14, 15][_c])
NEG_MASK = -30000.0

WS_IN = 0
WS_OUT0 = 16
WS_QK = 24
WS_OUT1 = 34
NWS = 42


class WPool:
    def __init__(self, C, name, nslots, shape):
        self.C = C
        self.n = nslots
        self.t = [C.sbuf("%s%d" % (name, i), shape, BF16) for i in range(nslots)]
        self.b = [C.dbuf("%s%d" % (name, i)) for i in range(nslots)]
        self.sb = [C.dbuf("%s%dst" % (name, i)) for i in range(nslots)]
        self.swb = [C.dbuf("%s%dsw" % (name, i)) for i in range(nslots)]
        self.req = []
        self.chunk = {}
        self.emitted = 0
        self.cons = 0

    def _top(self, upto):
        C = self.C
        upto = min(upto, len(self.req))
        while self.emitted < upto:
            i = self.emitted
            key, src32, scr = self.req[i]
            k = i % self.n
            if key not in self.chunk:
                cb = Buf("chunk")
                self.chunk[key] = cb
                C.dma("pool", self.t[k][:], src32, (), (self.b[k],), self.swb[k], max_dma_last_dim=4096)
                C.dma("sp", scr, self.t[k][:], (self.b[k],), (cb,), self.sb[k])
            else:
                C.dma("sp", self.t[k][:], scr, (self.chunk[key],), (self.b[k],), self.b[k])
            self.emitted += 1

    def prefetch(self):
        self._top(self.cons + self.n)

    def next(self, held=0):
        i = self.cons
        assert i < len(self.req), "weight pool underflow"
        self._top(i + self.n - held)
        self.cons += 1
        return self.t[i % self.n], self.b[i % self.n]


def build_program(debug=False, NT=NT, stages=3):
    nc = bass.Bass("TRN2", target_bir_lowering=False)
    dt_in = lambda name, shape: nc.dram_tensor(name, list(shape), F32, kind="ExternalInput").ap()
    x_d = dt_in("x", [S, D])
    ws_d = dt_in("ws", [NWS, 128, 1024])
    gu_d = dt_in("gu", [2 * FC, 2, 128, 1024])
    wd_d = dt_in("wd", [16, 128, FC, 128])
    wv0_d = dt_in("wv0", [128, 8, 512])
    wv1_d = dt_in("wv1", [128, 8, 256])
    gall_d = dt_in("gall", [128, 4, 8])
    gfin_d = dt_in("gfin", [128, 1024])
    lng_d = dt_in("lng", [128, 512])
    lnb_d = dt_in("lnb", [128, 512])
    bsp_d = dt_in("bsp", [128, 4, 128])
    wsp_d = dt_in("wsp", [128, 4, 128])
    cw_d = dt_in("cw", [128, 3, 4])
    bias_d = dt_in("biasT", [128, 16, 384])
    sink_d = dt_in("sinkb", [128, 16])
    id_d = dt_in("ident", [128, 128])
    out_d = nc.dram_tensor("out", [S, D], F32, kind="ExternalOutput").ap()
    if debug:
        dbg_d = nc.dram_tensor("dbg", [5, NT, 128, 8, 512], F32, kind="ExternalOutput").ap()
    ws_s = nc.dram_tensor("ws_s", [NWS, 128, 1024], BF16).ap()
    gu_s = nc.dram_tensor("gu_s", [2 * FC, 2, 128, 1024], BF16).ap()
    wd_s = nc.dram_tensor("wd_s", [16, 128, FC, 128], BF16).ap()
    wv0_s = nc.dram_tensor("wv0_s", [128, 8, 512], BF16).ap()
    wv1_s = nc.dram_tensor("wv1_s", [128, 8, 256], BF16).ap()

    with ExitStack() as st:
        C = Ctx(nc, st)
        P = C.P
        xr = C.sbuf("xr", [128, NRING, KC, T], F32)
        xr_b = [[Buf("xr%d_%d" % (r, c)) for c in range(KC)] for r in range(NRING)]
        hT = C.sbuf("hT", [128, KC, T + 2], BF16)
        hT_b = [Buf("hT%d" % c) for c in range(KC)]
        hTx_b = Buf("hTx")
        arena = C.sbuf("arena", [128, 32 * 256], F32)
        pg = [Buf("pg%d" % i) for i in range(32)]

        def av_bf(p0, np_):
            return arena[:, p0 * 256:(p0 + np_) * 256].bitcast(BF16)

        def av_f32(p0, np_):
            return arena[:, p0 * 256:(p0 + np_) * 256]

        kring = C.sbuf("kring", [128, 2, 4 * T], BF16)
        kr_b = [[Buf("k%d_%d" % (c, b)) for b in range(16)] for c in range(2)]
        vring = C.sbuf("vring", [128, 16, 4, 65], BF16)
        vr_b = [Buf("v%d" % b) for b in range(16)]
        qT = C.sbuf("qT", [128, KC, T], BF16)
        qT_b = [Buf("qT%d" % c) for c in range(KC)]
        biasT = C.sbuf("biasTs", [128, 16, 384], BF16)
        biasT_b = C.dbuf("biasT")
        xstage2 = [C.sbuf("xstage%d" % i, [128, D], F32) for i in range(2)]
        xstage2_b = [C.dbuf("xstage%d" % i) for i in range(2)]
        ostage = C.sbuf("ostage", [128, D], F32)
        ostage_b = Buf("ostage")
        ostore_b = C.dbuf("ostore")
        sq = [C.sbuf("sq%d" % i, [128, T], BF16) for i in range(3)]
        sq_b = [Buf("sq%d" % i) for i in range(3)]
        nrm_t = sq
        nrm_b = sq_b
        sqx = C.sbuf("sqx", [128, 8], BF16)
        sqx_b = Buf("sqx")
        tbuf = C.sbuf("tbuf", [128, T], F32)
        tbuf_b = Buf("tbuf")
        rsb = C.sbuf("rsb", [128, T], F32)
        rsb_b = Buf("rsb")
        small = C.sbuf("small", [128, 256], F32)
        zext = [C.sbuf("zext%d" % i, [128, T + 4], F32) for i in range(2)]
        zext_b = [Buf("zext%d" % i) for i in range(2)]
        zprev = C.sbuf("zprev", [128, 4], F32)
        zprev_b = [Buf("zprev%d" % j) for j in range(4)]
        ident_f = C.sbuf("ident_f", [128, 128], F32)
        ident_fb = None
        ident_b = C.sbuf("ident_b", [128, 128], BF16)
        ident_bb = Buf("ident_b")
        ones_b = C.sbuf("ones_b", [128, 128], BF16)
        ones_bb = Buf("ones_b")
        epsc = C.sbuf("epsc", [128, 1], F32)
        eps_b = Buf("epsc")
        neghalf = C.sbuf("neghalf", [128, 8], F32)
        neghalf_b = Buf("neghalf")
        gfin = C.sbuf("gfin", [128, D], F32)
        lng = C.sbuf("lng", [128, 512], F32)
        lnb = C.sbuf("lnb", [128, 512], F32)
        bsp = C.sbuf("bsp", [128, 4, 128], F32)
        wsp_f = C.sbuf("wsp_f", [128, 4, 128], F32)
        wsp_b = C.sbuf("wsp_b", [128, 4, 128], BF16)
        wsp_bb = Buf("wsp_b")
        gall = C.sbuf("gall", [128, 4, 8], F32)
        cw = C.sbuf("cw", [128, 3, 4], F32)
        sinkb = C.sbuf("sinkb", [128, 16], F32)
        negc = C.sbuf("negc", [128, 16], F32)
        sinkterm = C.sbuf("sinkterm", [128, 16], F32)
        att_b = Buf("attconst")
        cst_b = C.dbuf("consts")
        ident_fb = cst_b
        junk = C.sbuf("junk", [128, T], BF16)
        junk_b = Buf("junk")

        _sm = [0]

        def sm(n):
            a = small[:, _sm[0]:_sm[0] + n]
            _sm[0] += n
            assert _sm[0] <= 256
            return a

        psb = [C.psum("ps%d" % i, [128, 512], F32) for i in range(8)]
        ps_b = [Buf("ps%d" % i) for i in range(8)]
        _pr = [0]

        def next_ps():
            k = _pr[0] % 5
            _pr[0] += 1
            return psb[k], ps_b[k]

        acc_t = psb[5:8]
        acc_b = ps_b[5:8]
        xps_b = Buf("xps")

        C.dma("pool", biasT[:], bias_d, (), (biasT_b,), biasT_b, max_dma_last_dim=4096)

        for (dst, src) in ((ident_f, id_d), (gfin, gfin_d), (lng, lng_d), (lnb, lnb_d), (bsp, bsp_d),
                           (wsp_f, wsp_d), (gall, gall_d), (cw, cw_d), (sinkb, sink_d)):
            C.dma("sp", dst[:], src, (), (cst_b,), cst_b, indep=True)
        C.copy("dve", ident_b[:], ident_f[:], (cst_b,), (ident_bb,))
        C.memset("pool", ones_b[:], 1.0, (ones_bb,))
        C.memset("pool", neghalf[:], -0.5, (neghalf_b,))
        C.memset("pool", epsc[:], EPS, (eps_b,))
        C.copy("dve", wsp_b[:], wsp_f[:], (cst_b,), (wsp_bb,))
        C.memset("pool", vring[:], 1.0, vr_b)
        C.ts("dve", negc[:], sinkb[:], 0.0, -1.0, ALU.max, ALU.mult, (cst_b,), (att_b,))
        C.tt("dve", sinkterm[:], sinkb[:], negc[:], ALU.add, (cst_b, att_b), (att_b,))
        C.act(sinkterm[:], sinkterm[:], AF.Exp, (att_b,), (att_b,))

        proj = WPool(C, "wproj", 4, [128, KC, 128])
        gup = WPool(C, "wgu", 3, [128, 2, KC, 128])
        dnp = WPool(C, "wdn", 2, [128, 2, 11 * 128])
        wv0p = WPool(C, "wv0p", 1, [128, KC, 512])
        wv1p = WPool(C, "wv1p", 1, [128, KC, 256])

        def ws_req(idx):
            return (("ws", idx), ws_d[idx].rearrange("p (k j) -> p k j", k=KC), ws_s[idx].rearrange("p (k j) -> p k j", k=KC))

        def ffn_reqs(layer):
            for fc in range(FC):
                i = layer * FC + fc
                gup.req.append((("gu", i), gu_d[i].rearrange("t p (k j) -> p t k j", k=KC),
                                gu_s[i].rearrange("t p (k j) -> p t k j", k=KC)))
            for dc in range(8):
                i = layer * 8 + dc
                dnp.req.append((("wd", i), wd_d[i].rearrange("p (a b) j -> p a (b j)", a=2),
                                wd_s[i].rearrange("p (a b) j -> p a (b j)", a=2)))

        for s in range(NT + 1):
            if s < NT:
                wv0p.req.append((("wv0", 0), wv0_d, wv0_s))
                for g in range(4):
                    proj.req.append(ws_req(WS_IN + g))
                for j in range(4):
                    for t3 in range(3):
                        proj.req.append(ws_req(WS_IN + 4 + 4 * t3 + j))
                for dc in range(8):
                    proj.req.append(ws_req(WS_OUT0 + dc))
                ffn_reqs(0)
                if stages >= 2:
                    wv1p.req.append((("wv1", 0), wv1_d, wv1_s))
                    for k2 in range(2):
                        proj.req.append(ws_req(WS_QK + 8 + k2))
            if s >= 1 and stages >= 2:
                for dc in range(8):
                    proj.req.append(ws_req(WS_OUT1 + dc))
                ffn_reqs(1)
            if s < NT and stages >= 2:
                for c in range(8):
                    proj.req.append(ws_req(WS_QK + c))

        def proj_fm(w_t, w_b, rhs_of_kc, rhs_bufs, n, nk=KC, ps=None, order=None):
            if ps is None:
                ps = next_ps()
            pt, pb = ps
            order = list(range(nk)) if order is None else order
            for i, kc in enumerate(order):
                C.mm(pt[:, 0:n], w_t[:, kc, :], rhs_of_kc(kc), i == 0, i == nk - 1,
                     (w_b, rhs_bufs[kc]), (pb,))
            return pt, pb

        def resid_add(slot, dc, pt, pb, stats=None):
            xa = xr[:, slot, dc, :]
            C.tt("dve", xa, xa, pt[:, 0:T], ALU.add, (pb,), (xr_b[slot][dc],))
            if stats is not None:
                stats_add(stats, slot, dc, "act")
                stats_flush(stats, keep=1)

        def multi_mm(groups, nk=KC):
            for kc in range(nk):
                for (out_ap, pb, lf, rf, rd) in groups:
                    C.mm(out_ap, lf(kc), rf(kc), kc == 0, kc == nk - 1, rd(kc), (pb,))

        pending = {}
        lndummy = sm(1)
        lnd_b = Buf("lnd")

        def preload_ln():
            C.act(lndummy, epsc[:, 0:1], AF.Ln, (eps_b,), (lnd_b,))

        def stats_begin(preload=True):
            if preload:
                preload_ln()
            return {"pt": acc_t[1], "pb": acc_b[1], "n": 0, "pend": []}

        def stats_add(st, slot, c, eng="act"):
            i = st["n"]
            k = i % 3
            xa = xr[:, slot, c, :]
            if eng == "act":
                C.act(sq[k][:], xa, AF.Square, (xr_b[slot][c],), (sq_b[k],))
            else:
                C.tt("dve", sq[k][:], xa, xa, ALU.mult, (xr_b[slot][c],), (sq_b[k],))
            st["pend"].append((k, i))
            st["n"] = i + 1

        def stats_flush(st, keep=0):
            while len(st["pend"]) > keep:
                k, i = st["pend"].pop(0)
                C.mm(st["pt"][:, 0:T], ones_b[:], sq[k][:], i == 0, i == KC - 1, (ones_bb, sq_b[k]), (st["pb"],))

        def norm(slot, gi, extra_slot=None, filler=None, stats=None, rs=None, reuse=False):
            rs_t, rs_b = rs if rs is not None else (rsb, rsb_b)
            if not reuse:
                if stats is None:
                    stats = stats_begin()
                    for c in range(KC):
                        stats_add(stats, slot, c, "act" if c % 2 == 0 else "dve")
                        stats_flush(stats)
                else:
                    stats_flush(stats)
                pt, pb = stats["pt"], stats["pb"]
                C.act(rsb[:], pt[:, 0:T], AF.Ln, (pb, eps_b), (rsb_b,), bias=epsc[:, 0:1], scale=1.0 / D)
                C.act(rs_t[:], rsb[:], AF.Exp, (rsb_b,), (rs_b,), scale=-0.5)
            for c in range(KC):
                C.stt("dve", hT[:, c, 0:T], xr[:, slot, c, :], gall[:, gi, c:c + 1], rs_t[:], ALU.mult, ALU.mult,
                      (xr_b[slot][c], rs_b, cst_b), (hT_b[c],))
            if filler is not None:
                filler()
            if extra_slot is not None:
                xcol = xr[:, extra_slot, :, 0]
                xb = xr_b[extra_slot]
                C.tt("dve", sqx[:], xcol, xcol, ALU.mult, xb, (sqx_b,))
                C.tt("dve", sm_x8, xcol, gall[:, gi, :], ALU.mult, tuple(xb) + (cst_b,), (smx8_b,))
                return lambda: norm_extra(gi, extra_slot)
            return None

        def norm_extra(gi, extra_slot):
            pt2, pb2 = next_ps()
            for c in range(KC):
                C.mm(pt2[:, 0:1], ones_b[:], sqx[:, c:c + 1], c == 0, c == KC - 1, (ones_bb, sqx_b), (pb2,))
            t1 = sm_t1
            C.act(t1, pt2[:, 0:1], AF.Ln, (pb2, eps_b), (smx_b,), bias=epsc[:, 0:1], scale=1.0 / D)
            C.act(sm_rs1, t1, AF.Exp, (smx_b,), (smx_b,), scale=-0.5)
            C.act(hT[:, :, T], sm_x8, AF.Copy, (smx_b, smx8_b), (hTx_b,), scale=sm_rs1)

        smx8_b = Buf("smx8")
        sm_t1 = sm(1)
        sm_rs1 = sm(1)
        sm_x8 = sm(8)
        smx_b = Buf("smx")

        def load_dma(t, tb):
            r0 = t * T + tb * 128
            C.dma("sp", xstage2[tb % 2][:], x_d[r0:r0 + 128, :], (), (xstage2_b[tb % 2],), xstage2_b[tb % 2])

        def load_tr(t, tb):
            slot = t % NRING
            xstage = xstage2[tb % 2]
            xstage_b = xstage2_b[tb % 2]
            for half in range(2):
                pt, pb = next_ps()
                for c4 in range(4):
                    c = half * 4 + c4
                    C.tr(pt[:, c4 * 128:(c4 + 1) * 128], xstage[:, c * 128:(c + 1) * 128], ident_f[:],
                         (xstage_b, ident_fb), (pb,))
                dst = xr[:, slot, half * 4:half * 4 + 4, tb * 128:(tb + 1) * 128]
                C.copy("act", dst, pt[:, 0:512].rearrange("p (c n) -> p c n", c=4), (pb,),
                       tuple(xr_b[slot][half * 4:half * 4 + 4]))

        def load_a(t):
            load_tr(t, 0)
            load_tr(t, 1)
            load_dma(t, 2)
            load_dma(t, 3)

        def load_b(t):
            load_tr(t, 2)
            load_tr(t, 3)
            if t + 1 < NT:
                load_dma(t + 1, 0)
                load_dma(t + 1, 1)

        def tmp_slot(i):
            p0 = 18 + 2 * (i % 6)
            return av_f32(p0, 2), (pg[p0], pg[p0 + 1])

        _tmpi = [0]

        def next_tmp():
            r = tmp_slot(_tmpi[0])
            _tmpi[0] += 1
            return r

        ln_st = [sm(6) for _ in range(4)]
        ln_mv = [sm(2) for _ in range(4)]
        ln_rt = [sm(1) for _ in range(4)]
        ln_rs = [sm(1) for _ in range(4)]
        ln_b = [Buf("ln%d" % i) for i in range(4)]
        xc_sb = sm(4)
        xc_b = [Buf("xc%d" % j) for j in range(4)]

        AU_PG = [12, 13, 30, 31]

        def mixer0(s):
            slot = s % NRING
            nslot = (s + 1) % NRING if s + 1 < NT else None
            extra = norm(slot, 0, nslot, filler=(lambda: load_a(s + 1)) if s + 1 < NT else None)
            gup.prefetch()
            wv1p.prefetch()
            hb = hT_b
            wv_t, wv_b = wv0p.next()
            avps = [next_ps() for _ in range(4)]
            multi_mm([(avps[tb][0][:, 0:512], avps[tb][1],
                       (lambda kc, tb=tb: hT[:, kc, tb * 128:(tb + 1) * 128]),
                       (lambda kc: wv_t[:, kc, :]),
                       (lambda kc: (hb[kc], wv_b))) for tb in range(4)])
            for tb in range(4):
                pt, pb = avps[tb]
                gv, gvb = next_tmp()
                C.act(gv, pt[:, 0:512], AF.Gelu, (pb,), gvb)
                C.P.add("dve", lambda e, o=ln_st[tb], i=gv: e.bn_stats(o, i), gvb, (ln_b[tb],))
                C.P.add("dve", lambda e, o=ln_mv[tb], i=ln_st[tb]: e.bn_aggr(o, i), (ln_b[tb],), (ln_b[tb],))
                C.ts("dve", ln_rt[tb], ln_mv[tb][:, 1:2], EPS, None, ALU.add, None, (ln_b[tb],), (ln_b[tb],))
                C.tt("pool", ln_rs[tb], ln_rt[tb], neghalf[:, 0:1], ALU.pow, (ln_b[tb], neghalf_b), (ln_b[tb],))
                C.ts("dve", gv, gv, ln_mv[tb][:, 0:1], ln_rs[tb], ALU.subtract, ALU.mult, (ln_b[tb],), gvb)
                C.tt("pool", gv, gv, lng[:], ALU.mult, (cst_b,), gvb)
                vt = av_bf(8 + tb, 1)
                C.tt("pool", vt, gv, lnb[:], ALU.add, gvb + (cst_b,), (pg[8 + tb],))
            if extra is not None:
                extra()
            for g in range(4):
                w_t, w_b = proj.next()
                pt, pb = proj_fm(w_t, w_b, lambda kc: hT[:, kc, 0:T], hb, T)
                C.act(av_bf(AU_PG[g], 1), pt[:, 0:T], AF.Gelu, (pb,), (pg[AU_PG[g]],))
            preload_ln()
            for j in range(4):
                z = zext[j % 2]
                zb = zext_b[j % 2]
                bb = av_f32(14 + 2 * (j % 2), 2)
                bbb = (pg[14 + 2 * (j % 2)], pg[15 + 2 * (j % 2)])
                xp = acc_t[2]

                def halo(col, wt_, wb_):
                    if nslot is None:
                        return
                    for kc in range(KC):
                        C.mm(xp[:, col:col + 1], wt_[:, kc, :], hT[:, kc, T:T + 1], kc == 0, kc == KC - 1,
                             (wb_, hTx_b), (xps_b,))

                wb_t, wb_b = proj.next()
                pt, pb = proj_fm(wb_t, wb_b, lambda kc: hT[:, kc, 0:T], hb, T)
                C.copy("act", bb, pt[:, 0:T], (pb,), bbb)
                wc_t, wc_b = proj.next()
                ptc, pbc = proj_fm(wc_t, wc_b, lambda kc: hT[:, kc, 0:T], hb, T)
                halo(496 + 2 * j, wc_t, wc_b)
                tc_, tcb = next_tmp()
                C.copy("act", tc_, ptc[:, 0:T], (pbc,), tcb)
                wh_t, wh_b = proj.next()
                pth, pbh = proj_fm(wh_t, wh_b, lambda kc: hT[:, kc, 0:T], hb, T)
                halo(497 + 2 * j, wh_t, wh_b)
                C.tt("dve", z[:, 1:T + 1], tc_, pth[:, 0:T], ALU.mult, tcb + (pbh,), (zb,))
                if s > 0:
                    C.copy("pool", z[:, 0:1], zprev[:, j:j + 1], (zprev_b[j],), (zb,))
                else:
                    C.memset("pool", z[:, 0:1], 0.0, (zb,))
                if nslot is not None:
                    C.copy("act", xc_sb[:, j:j + 1], xp[:, 496 + 2 * j:497 + 2 * j], (xps_b,), (xc_b[j],))
                    C.tt("dve", z[:, T + 1:T + 2], xc_sb[:, j:j + 1], xp[:, 497 + 2 * j:498 + 2 * j], ALU.mult,
                         (xc_b[j], xps_b), (zb,))
                else:
                    C.memset("pool", z[:, T + 1:T + 2], 0.0, (zb,))
                C.copy("pool", zprev[:, j:j + 1], z[:, T:T + 1], (zb,), (zprev_b[j],))
                ct, ctb = next_tmp()
                C.ts("dve", ct, z[:, 0:T], cw[:, 0, j:j + 1], None, ALU.mult, None, (zb, cst_b), ctb)
                C.stt("dve", ct, z[:, 1:T + 1], cw[:, 1, j:j + 1], ct, ALU.mult, ALU.add, (zb, cst_b), ctb)
                C.stt("dve", ct, z[:, 2:T + 2], cw[:, 2, j:j + 1], ct, ALU.mult, ALU.add, (zb, cst_b), ctb)
                C.tt("pool", av_bf(4 + j, 1), ct, bb, ALU.mult, ctb + bbb, (pg[4 + j],))
            for g in range(4):
                au = av_bf(AU_PG[g], 1)
                aub = pg[AU_PG[g]]
                pt2, pb2 = next_ps()
                for tb in range(4):
                    vt = av_bf(8 + tb, 1)
                    C.mm(pt2[:, tb * 128:(tb + 1) * 128], vt[:, g * 128:(g + 1) * 128], wsp_b[:, g, :], True, True,
                         (pg[8 + tb], wsp_bb), (pb2,))
                t1, t1b = next_tmp()
                C.tt("dve", t1.rearrange("p (t q) -> p t q", t=4), pt2[:, 0:512].rearrange("p (t q) -> p t q", t=4),
                     bsp[:, g:g + 1, :].broadcast_to([128, 4, 128]), ALU.add, (pb2, cst_b), t1b)
                C.tt("pool", av_bf(g, 1), t1, au, ALU.mult, t1b + (aub,), (pg[g],))
            st = stats_begin(preload=False)
            for dc in range(8):
                w_t, w_b = proj.next()
                pt, pb = proj_fm(w_t, w_b, lambda kc: av_bf(kc, 1), pg[0:8], T,
                                 order=[4, 5, 6, 7, 0, 1, 2, 3] if dc == 0 else None)
                resid_add(slot, dc, pt, pb, stats=st)
            pending[slot] = st

        def ffn(t, layer):
            slot = t % NRING
            norm(slot, 1 + 2 * layer, filler=(lambda: load_b(t + 1)) if (layer == 0 and t + 1 < NT) else None,
                 stats=pending.pop(slot, None))
            dnp.prefetch()

            def act_part(fc, ptg, pbg, ptu, pbu):
                p0 = 22 + 2 * (fc % 2)
                sg = av_f32(p0, 2)
                sgb = (pg[p0], pg[p0 + 1])
                C.act(sg, ptg[:, 0:T], AF.Silu, (pbg,), sgb)
                C.tt("dve", av_bf(fc, 1), sg, ptu[:, 0:T], ALU.mult, sgb + (pbu,), (pg[fc],))

            gu0_t, gu0_b = gup.next()
            gu1_t, gu1_b = gup.next(held=1)
            b4 = [next_ps() for _ in range(4)]
            grp = []
            for i, (gt, gb_, tsel) in enumerate(((gu0_t, gu0_b, 0), (gu0_t, gu0_b, 1), (gu1_t, gu1_b, 0), (gu1_t, gu1_b, 1))):
                grp.append((b4[i][0][:, 0:T], b4[i][1],
                            (lambda kc, gt=gt, tsel=tsel: gt[:, tsel, kc, :]),
                            (lambda kc: hT[:, kc, 0:T]),
                            (lambda kc, gb_=gb_: (gb_, hT_b[kc]))))
            multi_mm(grp)
            act_part(0, b4[0][0], b4[0][1], b4[1][0], b4[1][1])
            act_part(1, b4[2][0], b4[2][1], b4[3][0], b4[3][1])
            for fc in range(2, FC):
                gu_t, gu_b = gup.next()
                ptg, pbg = next_ps()
                ptu, pbu = next_ps()
                for kc in range(KC):
                    C.mm(ptg[:, 0:T], gu_t[:, 0, kc, :], hT[:, kc, 0:T], kc == 0, kc == KC - 1, (gu_b, hT_b[kc]), (pbg,))
                for kc in range(KC):
                    C.mm(ptu[:, 0:T], gu_t[:, 1, kc, :], hT[:, kc, 0:T], kc == 0, kc == KC - 1, (gu_b, hT_b[kc]), (pbu,))
                act_part(fc, ptg, pbg, ptu, pbu)
            proj.prefetch()
            st = stats_begin() if layer == 0 else None
            for dc in range(8):
                wd_t, wd_b = dnp.next()
                pt, pb = next_ps()
                for fc in range(FC):
                    C.mm(pt[:, 0:T], wd_t[:, fc // 11, (fc % 11) * 128:(fc % 11 + 1) * 128], av_bf(fc, 1), fc == 0, fc == FC - 1, (wd_b, pg[fc]), (pb,))
                resid_add(slot, dc, pt, pb, stats=st)
            if st is not None:
                pending[slot] = st

        def l1_kv(s):
            slot = s % NRING
            wv0p.prefetch()
            norm(slot, 2, stats=pending.pop(slot, None), rs=(tbuf, tbuf_b))
            wv_t, wv_b = wv1p.next()
            rb = (s % 4) * 4
            wk0_t, wk0_b = proj.next()
            wk1_t, wk1_b = proj.next(held=1)
            kps = [next_ps(), next_ps()]
            multi_mm([(kps[i][0][:, 0:T], kps[i][1],
                       (lambda kc, wt=wt: wt[:, kc, :]),
                       (lambda kc: hT[:, kc, 0:T]),
                       (lambda kc, wb=wb: (wb, hT_b[kc]))) for i, (wt, wb) in enumerate(((wk0_t, wk0_b), (wk1_t, wk1_b)))])
            for k2 in range(2):
                pt, pb = kps[k2]
                C.copy("act", kring[:, k2, rb * 128:rb * 128 + T], pt[:, 0:T], (pb,), tuple(kr_b[k2][rb:rb + 4]))
            for tb in range(4):
                pt, pb = next_ps()
                for kc in range(KC):
                    C.mm(pt[:, 0:256], hT[:, kc, tb * 128:(tb + 1) * 128], wv_t[:, kc, :], kc == 0, kc == KC - 1,
                         (hT_b[kc], wv_b), (pb,))
                C.copy("dve", vring[:, rb + tb, :, 0:64], pt[:, 0:256].rearrange("p (g d) -> p g d", g=4), (pb,),
                       (vr_b[rb + tb],))

        def l1_q(s):
            slot = s % NRING
            fin = s >= 1
            norm(slot, 2, rs=(tbuf, tbuf_b), reuse=True)
            if fin:
                final_block(s - 1, 0)
            ws3 = [proj.next(), proj.next(held=1)]
            qps = [next_ps() for _ in range(2)]
            multi_mm([(qps[i][0][:, 0:T], qps[i][1],
                       (lambda kc, wt=ws3[i][0]: wt[:, kc, :]),
                       (lambda kc: hT[:, kc, 0:T]),
                       (lambda kc, wb=ws3[i][1]: (wb, hT_b[kc]))) for i in range(2)])
            for c in range(8):
                if c < 2:
                    pt, pb = qps[c]
                else:
                    w_t, w_b = proj.next()
                    pt, pb = proj_fm(w_t, w_b, lambda kc: hT[:, kc, 0:T], hT_b, T)
                C.act(qT[:, c, :], pt[:, 0:T], AF.Copy, (pb,), (qT_b[c],), scale=0.125)
                if fin and c == 2:
                    final_block(s - 1, 1)
                if fin and c == 4:
                    final_block(s - 1, 2)
                if fin and c == 6:
                    final_block(s - 1, 3)

        den = sm(16)
        rden = sm(16)
        den_b = Buf("den")

        def attention(t):
            slot = t % NRING
            proj.prefetch()
            gup.prefetch()

            def qinfo(qb):
                gb = 4 * t + qb
                js = [j for j in range(3) if 0 <= gb - 1 + j < 32]
                return gb, js

            def pt_slot(P, half):
                k = 12 + (2 * P + half) % 6
                return av_bf(k, 1), pg[k]

            def emit_st2(P):
                qb, cq = divmod(P, 8)
                gb, js = qinfo(qb)
                nj = len(js)
                j0 = js[0]
                sts = []
                for half in range(2):
                    hs = 2 * cq + half
                    st_t, st_b = next_ps()
                    sts.append((st_t, st_b))
                    C.mm(st_t[:, 0:nj * 128], ident_b[:], biasT[:, hs, j0 * 128:(j0 + nj) * 128], True, False,
                         (ident_bb, biasT_b), (st_b,))
                for idx, j in enumerate(js):
                    kb = (gb - 1 + j) % 16
                    for half in range(2):
                        hs = 2 * cq + half
                        g = HEAD_OF_SLOT[hs] // 4
                        assert g % 2 == half
                        k2 = g // 2
                        st_t, st_b = sts[half]
                        C.mm(st_t[:, idx * 128:(idx + 1) * 128],
                             kring[half * 64:(half + 1) * 64, k2, kb * 128:(kb + 1) * 128],
                             qT[half * 64:(half + 1) * 64, cq, qb * 128:(qb + 1) * 128],
                             False, idx == nj - 1, (kr_b[k2][kb], qT_b[cq]), (st_b,))
                for half in range(2):
                    hs = 2 * cq + half
                    st_t, st_b = sts[half]
                    ptile, ptb = pt_slot(P, half)
                    C.act(ptile[:, 0:nj * 128], st_t[:, 0:nj * 128], AF.Exp, (st_b, att_b), (ptb,),
                          bias=negc[:, hs:hs + 1], scale=1.0)

            def emit_pv(P):
                qb, cq = divmod(P, 8)
                gb, js = qinfo(qb)
                nj = len(js)
                for half in range(2):
                    hs = 2 * cq + half
                    g = HEAD_OF_SLOT[hs] // 4
                    ptile, ptb = pt_slot(P, half)
                    bank = hs // 7
                    col = (hs % 7) * 65
                    for idx, j in enumerate(js):
                        kb = (gb - 1 + j) % 16
                        C.mm(acc_t[bank][:, col:col + 65], ptile[:, idx * 128:(idx + 1) * 128], vring[:, kb, g, :],
                             idx == 0, idx == nj - 1, (ptb, vr_b[kb]), (acc_b[bank],))

            def ao_of(qb):
                return av_bf(8 + 2 * (qb % 2), 2), (pg[8 + 2 * (qb % 2)], pg[9 + 2 * (qb % 2)])

            def normalize(qb):
                ao, aob = ao_of(qb)
                for bank in range(3):
                    h0 = bank * 7
                    h1 = min(16, h0 + 7)
                    nh = h1 - h0
                    a3 = acc_t[bank][:, 0:nh * 65].rearrange("p (h e) -> p h e", e=65)
                    C.tt("dve", den[:, h0:h1], a3[:, :, 64], sinkterm[:, h0:h1], ALU.add, (acc_b[bank], att_b), (den_b,))
                    C.P.add("dve", lambda e, o=rden[:, h0:h1], i=den[:, h0:h1]: e.reciprocal(o, i), (den_b,), (den_b,))
                    C.tt("dve", ao[:, h0 * 64:h1 * 64].rearrange("p (h d) -> p h d", d=64), a3[:, :, 0:64],
                         rden[:, h0:h1].unsqueeze(2).broadcast_to([128, nh, 64]), ALU.mult, (acc_b[bank], den_b), aob)

            def transposes(qb):
                ao, aob = ao_of(qb)
                tp, tpb = next_ps()
                tpv = tp[:, 0:512].bitcast(BF16)
                for c in range(8):
                    C.tr(tpv[:, c * 128:(c + 1) * 128], ao[:, c * 128:(c + 1) * 128], ident_b[:], aob + (ident_bb,), (tpb,))
                dst = av_bf(0, 8).rearrange("p (c n) -> p c n", c=8)[:, :, qb * 128:(qb + 1) * 128]
                C.copy("act", dst, tpv.rearrange("p (c n) -> p c n", c=8), (tpb,), tuple(pg[0:8]))

            NPAIR = 32
            pend_tr = None
            for P in range(NPAIR + 2):
                if P < NPAIR:
                    emit_st2(P)
                if pend_tr is not None:
                    transposes(pend_tr)
                    pend_tr = None
                if P >= 2:
                    emit_pv(P - 2)
                    if (P - 2) % 8 == 7:
                        normalize((P - 2) // 8)
                        pend_tr = (P - 2) // 8
            if pend_tr is not None:
                transposes(pend_tr)
            st = stats_begin()
            for dc in range(8):
                w_t, w_b = proj.next()
                pt, pb = proj_fm(w_t, w_b, lambda kc: av_bf(kc, 1), pg[0:8], T)
                resid_add(slot, dc, pt, pb, stats=st)
            pending[slot] = st

        fss = [sm(1), sm(1)]
        fs_t = sm(1)
        fs_r = sm(1)
        fs_b = Buf("fs")

        def final_block(t, tb):
            slot = t % NRING
            pts = [(acc_t[0], acc_b[0]), (acc_t[1], acc_b[1])]
            for c in range(8):
                pt, pb = pts[c // 4]
                C.tr(pt[:, (c % 4) * 128:(c % 4 + 1) * 128], xr[:, slot, c, tb * 128:(tb + 1) * 128], ident_f[:],
                     (xr_b[slot][c], ident_fb), (pb,))
            for h2 in range(2):
                pt, pb = pts[h2]
                C.act(junk[:], pt[:, 0:512], AF.Square, (pb,), (junk_b, fs_b), accum_out=fss[h2])
            C.tt("dve", fs_t, fss[0], fss[1], ALU.add, (fs_b,), (fs_b,))
            C.ts("dve", fs_t, fs_t, 1.0 / D, EPS, ALU.mult, ALU.add, (fs_b,), (fs_b,))
            C.tt("pool", fs_r, fs_t, neghalf[:, 0:1], ALU.pow, (fs_b, neghalf_b), (fs_b,))
            for h2 in range(2):
                pt, pb = pts[h2]
                C.stt("dve", ostage[:, h2 * 512:(h2 + 1) * 512], pt[:, 0:512], fs_r, gfin[:, h2 * 512:(h2 + 1) * 512],
                      ALU.mult, ALU.mult, (pb, fs_b, cst_b, ostore_b), (ostage_b,))
            r0 = t * T + tb * 128
            C.dma("sp", out_d[r0:r0 + 128, :], ostage[:], (ostage_b,), (ostore_b,), ostore_b)

        def final(t):
            for tb in range(4):
                final_block(t, tb)

        def dump(k, t):
            if not debug:
                return
            slot = t % NRING
            b = C.dbuf("dbg%d_%d" % (k, t))
            C.dma("sp", dbg_d[k, t], xr[:, slot], tuple(xr_b[slot]), (), b)
            P.final_waits.append(b)

        wv0p.prefetch()
        proj.prefetch()
        gup.prefetch()
        dnp.prefetch()
        load_dma(0, 0)
        load_dma(0, 1)
        load_a(0)
        load_b(0)
        for s in range(NT + 1):
            if s < NT:
                dump(0, s)
                mixer0(s)
                dump(1, s)
                ffn(s, 0)
                dump(2, s)
                if stages >= 2:
                    l1_kv(s)
            if s >= 1 and stages >= 2:
                attention(s - 1)
                dump(3, s - 1)
                ffn(s - 1, 1)
                dump(4, s - 1)
                if s == NT:
                    final(s - 1)
            if s < NT and stages >= 2:
                l1_q(s)

        fw = [(ostore_b.sem, ostore_b.semcnt)]
        for b in P.final_waits:
            fw.append((b.sem, b.semcnt))
        P.final_waits = fw
        print("SBUF bytes remaining per partition:", nc.sbuf_bytes_remaining, "ops:", {e: len(P.ops[e]) for e in ENGS})
        with nc.Block() as block:
            P.emit_all(nc, block, C.engsem)
    return nc


def _t5_bucket_table():
    nb = 16
    max_exact = 8
    rel = np.arange(-255, 256)
    ret = np.where(rel > 0, nb, 0)
    n = np.abs(rel)
    nf = np.maximum(n, 1).astype(np.float32)
    large = max_exact + (np.log(nf / max_exact) / np.log(128 / max_exact) * (nb - max_exact)).astype(np.int32)
    large = np.minimum(large, nb - 1)
    return ret + np.where(n < max_exact, n, large)


def _chunks_kmajor(W):
    K, E = W.shape
    a = W.reshape(K // 128, 128, E // 128, 128)
    return np.ascontiguousarray(a.transpose(2, 1, 0, 3)).reshape(E // 128, 128, (K // 128) * 128)


def prep_shared(inp):
    f = lambda a: np.ascontiguousarray(np.asarray(a, dtype=np.float32))
    w_in = f(inp["even_w_in"])[0]
    cols = np.concatenate([np.arange(0, 512), np.arange(1024, 2560)])
    ws_in = _chunks_kmajor(w_in[:, cols])
    ws_out0 = _chunks_kmajor(f(inp["even_w_out"])[0])
    wqkv = f(inp["attn_w_qkv"])[0]
    qcols = np.concatenate([np.arange(h * 64, h * 64 + 64) for h in HEAD_OF_SLOT])
    ws_q = _chunks_kmajor(wqkv[:, qcols])
    ws_k = _chunks_kmajor(wqkv[:, 1024:1280])
    wo1 = f(inp["attn_w_out"])[0][qcols, :]
    ws_out1 = _chunks_kmajor(wo1)
    ws = np.concatenate([ws_in, ws_out0, ws_q, ws_k, ws_out1], axis=0)
    assert ws.shape == (NWS, 128, 1024)
    gate = f(inp["ffn_w_gate"])
    up = f(inp["ffn_w_up"])
    down = f(inp["ffn_w_down"])
    gu = np.stack([np.stack([_chunks_kmajor(gate[l]), _chunks_kmajor(up[l])], axis=1) for l in range(2)], axis=0)
    gu = np.ascontiguousarray(gu.reshape(2 * FC, 2, 128, 1024))
    wd = np.stack([_chunks_kmajor(down[l]) for l in range(2)], axis=0).reshape(16, 128, FC, 128)
    wv0 = np.ascontiguousarray(w_in[:, 512:1024].reshape(8, 128, 512).transpose(1, 0, 2))
    wv1 = np.ascontiguousarray(wqkv[:, 1280:1536].reshape(8, 128, 256).transpose(1, 0, 2))
    nm = f(inp["norm_mix"])
    nf_ = f(inp["norm_ffn"])
    gl = np.stack([nm[0], nf_[0], nm[1], nf_[1]], axis=0)
    gall = np.ascontiguousarray(gl.reshape(4, 8, 128).transpose(2, 0, 1))
    rep = lambda v: np.ascontiguousarray(np.broadcast_to(v, (128,) + v.shape))
    gfin = rep(f(inp["final_norm"]))
    lng = rep(f(inp["even_v_ln_g"])[0])
    lnb = rep(f(inp["even_v_ln_b"])[0])
    bsp = rep(f(inp["even_b_spatial"])[0])
    wsp = np.ascontiguousarray(f(inp["even_w_spatial"])[0].transpose(2, 0, 1))
    cw = np.ascontiguousarray(f(inp["even_conv_w"])[0].reshape(3, 4, 128).transpose(2, 0, 1))
    tab = _t5_bucket_table()
    k = np.arange(128)[:, None, None]
    j = np.arange(3)[None, :, None]
    q = np.arange(128)[None, None, :]
    rel = (j - 1) * 128 + k - q
    bucket = tab[rel + 255]
    rb = f(inp["rel_bias"])[:, HEAD_OF_SLOT]
    bias = rb[bucket]
    band = (np.abs(rel) <= 128)[..., None]
    bias = np.where(band, bias, np.float32(NEG_MASK)).astype(np.float32)
    biasT = np.ascontiguousarray(bias.transpose(0, 3, 1, 2)).reshape(128, 16, 384)
    sinkb = rep(f(inp["attn_sink"])[0][HEAD_OF_SLOT])
    return {
        "ws": ws, "gu": gu, "wd": np.ascontiguousarray(wd), "wv0": wv0, "wv1": wv1, "gall": gall, "gfin": gfin,
        "lng": lng, "lnb": lnb, "bsp": bsp, "wsp": wsp, "cw": cw, "biasT": biasT, "sinkb": sinkb,
        "ident": np.eye(128, dtype=np.float32),
    }


_NC_CACHE = {}


def kernel(**inputs):
    x = np.ascontiguousarray(np.asarray(inputs["x"], dtype=np.float32))
    shared = prep_shared(inputs)
    if "nc" not in _NC_CACHE:
        _NC_CACHE["nc"] = build_program(False)
    nc = _NC_CACHE["nc"]
    in_maps = []
    for b in range(8):
        m = dict(shared)
        m["x"] = x[b]
        in_maps.append(m)
    res = run_bass_kernel_spmd(nc, in_maps, core_ids=list(range(8)))
    out = np.stack([np.asarray(r["out"], dtype=np.float32) for r in res.results], axis=0)
    return out
```
